# Optimizing a Trainium2 kernel written in Bass

```python
import math
import jax, jax.numpy as jnp
from jax import lax
import numpy as np

D_MODEL = 1024
BATCH = 16
SEQ = 4096
DEPTH = 2
DEC_BATCH = 16
DEC_SEQ = 16
PAST_LEN = 2048

CHUNK = 64
N_META = 16
N_EVEN = (DEPTH + 1) // 2
N_ODD = DEPTH // 2
EPS = 1e-6
CONV_K = 4
S5_GROUP = 16
S5_STATE = 64
W_A = D_MODEL // 2
G_A = W_A // S5_GROUP
W_B = D_MODEL
H_B = 4
DH_B = W_B // H_B
W_C = D_MODEL
H_C = 4
DV_C = W_C // H_C
DK_C = DV_C // 2
QK_C = H_C * DK_C
ROPE_BASE = 10000.0
W_D = D_MODEL
H_D = 8
BD_D = W_D // H_D
LRU_C = 8.0
EV_IN = 2 * W_A + 2 * W_B
EV_MIX = W_A + W_B
OD_IN = 2 * QK_C + 2 * W_C + 2 * W_D
OD_MIX = W_C + W_D
F32 = jnp.float32

kernel_name = 'hybrid_s5_mlstm_retention_rglru_stream_step'


def rmsnorm(x, g):
    xf = x.astype(F32)
    y = xf * lax.rsqrt(jnp.mean(xf * xf, axis=-1, keepdims=True) + EPS)
    return (y * g.astype(F32)).astype(x.dtype)


def head_norm(h, g):
    mu = jnp.mean(h, axis=-1, keepdims=True)
    var = jnp.mean(jnp.square(h - mu), axis=-1, keepdims=True)
    y = (h - mu) * lax.rsqrt(var + EPS)
    b, t, nh, d = h.shape
    return y.reshape(b, t, nh * d) * g.astype(F32)


def causal_conv(x, buf, w, b):
    L = x.shape[1]
    xp = jnp.concatenate([buf.astype(x.dtype), x], axis=1)
    out = b.astype(F32)
    for tap in range(CONV_K):
        out = out + xp[:, tap:tap + L].astype(F32) * w[tap].astype(F32)
    return out, xp[:, L:]


def _real_combine(e1, e2):
    a1, b1 = e1
    a2, b2 = e2
    return a1 * a2, a2 * b1 + b2


def _cplx_combine(e1, e2):
    ar1, ai1, br1, bi1 = e1
    ar2, ai2, br2, bi2 = e2
    return (ar1 * ar2 - ai1 * ai2, ar1 * ai2 + ai1 * ar2,
            ar2 * br1 - ai2 * bi1 + br2, ar2 * bi1 + ai2 * br1 + bi2)


def to_chunks(a, c):
    b, L, nh = a.shape[:3]
    a = a.reshape((b, L // c, c, nh) + a.shape[3:])
    return jnp.swapaxes(jnp.moveaxis(a, 1, 0), 2, 3)


def from_chunks(o):
    o = jnp.moveaxis(jnp.swapaxes(o, 2, 3), 0, 1)
    b, n, c, nh, d = o.shape
    return o.reshape(b, n * c, nh, d)


def run_chunked(step, seqs, state, segments):
    outs = []
    start = 0
    for length, c in segments:
        xs = tuple(to_chunks(s[:, start:start + length], c) for s in seqs)
        state, o = lax.scan(step, state, xs)
        outs.append(from_chunks(o))
        start += length
    return jnp.concatenate(outs, axis=1), state


def s5_mixer(u, h0_re, h0_im, lam_re, lam_im, log_dt, b_re, b_im, c_re, c_im, d, w_glu, b_glu):
    bsz, T, _ = u.shape
    uf = u.astype(F32)
    ug = uf.reshape(bsz, T, G_A, S5_GROUP)
    lam_re = lam_re.astype(F32)
    lam_im = lam_im.astype(F32)
    dt = jnp.exp(log_dt.astype(F32))[:, None]
    mag = jnp.exp(lam_re * dt)
    ab_re = mag * jnp.cos(lam_im * dt)
    ab_im = mag * jnp.sin(lam_im * dt)
    den = lam_re * lam_re + lam_im * lam_im
    nr = ab_re - 1.0
    k_re = (nr * lam_re + ab_im * lam_im) / den
    k_im = (ab_im * lam_re - nr * lam_im) / den
    b_re = b_re.astype(F32)
    b_im = b_im.astype(F32)
    bb_re = k_re[..., None] * b_re - k_im[..., None] * b_im
    bb_im = k_re[..., None] * b_im + k_im[..., None] * b_re
    bu_re = jnp.einsum('btgc,gpc->btgp', ug, bb_re)
    bu_im = jnp.einsum('btgc,gpc->btgp', ug, bb_im)
    h0_re = h0_re.astype(F32)
    h0_im = h0_im.astype(F32)
    bu_re = bu_re.at[:, 0].add(ab_re * h0_re - ab_im * h0_im)
    bu_im = bu_im.at[:, 0].add(ab_re * h0_im + ab_im * h0_re)
    a_re = jnp.broadcast_to(ab_re[None, None], (1, T, G_A, S5_STATE))
    a_im = jnp.broadcast_to(ab_im[None, None], (1, T, G_A, S5_STATE))
    _, _, h_re, h_im = lax.associative_scan(_cplx_combine, (a_re, a_im, bu_re, bu_im), axis=1)
    y = (jnp.einsum('btgp,gcp->btgc', h_re, c_re.astype(F32))
         - jnp.einsum('btgp,gcp->btgc', h_im, c_im.astype(F32)))
    y = y.reshape(bsz, T, W_A) + d.astype(F32) * uf
    y = jax.nn.gelu(y)
    y = y * jax.nn.sigmoid(y @ w_glu.astype(F32) + b_glu.astype(F32))
    return y, h_re[:, -1], h_im[:, -1]


def mlstm_step(carry, xs):
    C0, n0, m0 = carry
    q, k, v, ig, lf = xs
    c = q.shape[2]
    b = jnp.cumsum(lf, axis=-1)
    causal = jnp.tril(jnp.ones((c, c), dtype=bool))
    dlog = jnp.where(causal, b[..., :, None] - b[..., None, :] + ig[..., None, :], -jnp.inf)
    inter = b + m0[..., None]
    m = jnp.maximum(inter, jnp.max(dlog, axis=-1))
    s = jnp.einsum('bhtd,bhsd->bhts', q, k) * jnp.exp(dlog - m[..., None])
    w_inter = jnp.exp(inter - m)
    num = jnp.einsum('bhts,bhse->bhte', s, v) + w_inter[..., None] * jnp.einsum('bhtd,bhde->bhte', q, C0)
    den = jnp.sum(s, axis=-1) + w_inter * jnp.einsum('bhtd,bhd->bht', q, n0)
    h = num / jnp.maximum(jnp.abs(den), jnp.exp(-m))[..., None]
    m_end = m[..., -1]
    w_end = jnp.exp(b[..., -1:] - b + ig - m_end[..., None])
    decay = jnp.exp(inter[..., -1] - m_end)
    kw = k * w_end[..., None]
    C1 = decay[..., None, None] * C0 + jnp.einsum('bhsd,bhse->bhde', kw, v)
    n1 = decay[..., None] * n0 + jnp.sum(kw, axis=2)
    return (C1, n1, m_end), h


def mlstm_mixer(x, conv_buf, C0, n0, m0, conv_w, conv_b, wq, wk, wv, w_if, b_if, norm_w, skip, segments):
    bsz, T, _ = x.shape
    xc, new_buf = causal_conv(x, conv_buf, conv_w, conv_b)
    xc = jax.nn.silu(xc)
    xh = x.astype(F32).reshape(bsz, T, H_B, DH_B)
    ch = xc.reshape(bsz, T, H_B, DH_B)
    q = jnp.einsum('bthd,hde->bthe', ch, wq.astype(F32))
    k = jnp.einsum('bthd,hde->bthe', ch, wk.astype(F32))
    v = jnp.einsum('bthd,hde->bthe', xh, wv.astype(F32))
    qkv = jnp.concatenate([q.reshape(bsz, T, W_B), k.reshape(bsz, T, W_B), v.reshape(bsz, T, W_B)], axis=-1)
    gates = qkv @ w_if.astype(F32) + b_if.astype(F32)
    ig = gates[..., :H_B]
    lf = jax.nn.log_sigmoid(gates[..., H_B:])
    k = k * (DH_B ** -0.5)
    h, (C1, n1, m1) = run_chunked(mlstm_step, (q, k, v, ig, lf),
                                  (C0.astype(F32), n0.astype(F32), m0.astype(F32)), segments)
    y = head_norm(h, norm_w) + skip.astype(F32) * xc
    return y, C1, n1, m1, new_buf


def retention_log_decay():
    return jnp.log1p(-jnp.exp2(-5.0 - jnp.arange(H_C, dtype=F32)))


def retention_step(S, xs):
    q, k, v = xs
    c = q.shape[2]
    log_g = retention_log_decay()
    idx = jnp.arange(c, dtype=F32)
    diff = idx[:, None] - idx[None, :]
    dmask = jnp.where(diff >= 0, jnp.exp(log_g[:, None, None] * jnp.maximum(diff, 0.0)), 0.0)
    inner = jnp.einsum('bhtd,bhsd->bhts', q, k) * dmask
    xi = jnp.exp(log_g[:, None] * (idx + 1.0))[..., None]
    o = jnp.einsum('bhts,bhse->bhte', inner, v) + xi * jnp.einsum('bhtd,bhde->bhte', q, S)
    zeta = jnp.exp(log_g[:, None] * (c - 1.0 - idx))[..., None]
    S1 = jnp.exp(log_g * c)[:, None, None] * S + jnp.einsum('bhsd,bhse->bhde', k * zeta, v)
    return S1, o


def rope(x, pos):
    half = x.shape[-1] // 2
    inv = ROPE_BASE ** (-jnp.arange(half, dtype=F32) / half)
    ang = pos[:, None] * inv[None, :]
    cos = jnp.cos(ang)[None, :, None, :]
    sin = jnp.sin(ang)[None, :, None, :]
    x1, x2 = x[..., :half], x[..., half:]
    return jnp.concatenate([x1 * cos - x2 * sin, x1 * sin + x2 * cos], axis=-1)


def retention_mixer(q, k, v, S0, pos, norm_w, segments):
    q = rope(q.astype(F32), pos)
    k = rope(k.astype(F32), pos) * (DK_C ** -0.5)
    o, S1 = run_chunked(retention_step, (q, k, v.astype(F32)), S0.astype(F32), segments)
    return head_norm(o, norm_w), S1


def rglru_mixer(x, conv_buf, h0, conv_w, conv_b, w_a, b_a, w_x, b_x, lam):
    bsz, T, _ = x.shape
    xc, new_buf = causal_conv(x, conv_buf, conv_w, conv_b)
    xb = xc.reshape(bsz, T, H_D, BD_D)
    r = jax.nn.sigmoid(jnp.einsum('bthd,hde->bthe', xb, w_a.astype(F32)).reshape(bsz, T, W_D) + b_a.astype(F32))
    i = jax.nn.sigmoid(jnp.einsum('bthd,hde->bthe', xb, w_x.astype(F32)).reshape(bsz, T, W_D) + b_x.astype(F32))
    log_a = -LRU_C * r * jax.nn.softplus(-lam.astype(F32))
    a = jnp.exp(log_a)
    bx = jnp.sqrt(-jnp.expm1(2.0 * log_a)) * (i * xc)
    bx = bx.at[:, 0].add(a[:, 0] * h0.astype(F32))
    _, h = lax.associative_scan(_real_combine, (a, bx), axis=1)
    return h, h[:, -1], new_buf


def even_layer(h, g, i, st, P, segments):
    u = rmsnorm(h, g)
    ua, za, xb, zb = jnp.split(u @ P['ev_w_in'][i], [W_A, 2 * W_A, 2 * W_A + W_B], axis=-1)
    ya, s_re, s_im = s5_mixer(ua, st['s5_re'][i], st['s5_im'][i], P['s5_lambda_re'][i], P['s5_lambda_im'][i],
                              P['s5_log_dt'][i], P['s5_b_re'][i], P['s5_b_im'][i], P['s5_c_re'][i],
                              P['s5_c_im'][i], P['s5_d'][i], P['s5_w_glu'][i], P['s5_b_glu'][i])
    yb, mc, mn, mm, mbuf = mlstm_mixer(xb, st['ml_conv'][i], st['ml_c'][i], st['ml_n'][i], st['ml_m'][i],
                                       P['ml_conv_w'][i], P['ml_conv_b'][i], P['ml_wq'][i], P['ml_wk'][i],
                                       P['ml_wv'][i], P['ml_w_if'][i], P['ml_b_if'][i], P['ml_norm_w'][i],
                                       P['ml_skip'][i], segments)
    gated = jnp.concatenate([ya * jax.nn.silu(za.astype(F32)), yb * jax.nn.silu(zb.astype(F32))], axis=-1)
    h = h + (gated.astype(h.dtype) @ P['ev_w_out'][i]).astype(h.dtype)
    return h, dict(s5_re=s_re, s5_im=s_im, ml_c=mc, ml_n=mn, ml_m=mm, ml_conv=mbuf)


def odd_layer(h, g, i, st, P, pos, segments):
    u = rmsnorm(h, g)
    bsz, T, _ = u.shape
    cuts = [QK_C, 2 * QK_C, 2 * QK_C + W_C, 2 * QK_C + 2 * W_C, 2 * QK_C + 2 * W_C + W_D]
    qc, kc, vc, zc, xd, zd = jnp.split(u @ P['od_w_in'][i], cuts, axis=-1)
    yc, S1 = retention_mixer(qc.reshape(bsz, T, H_C, DK_C), kc.reshape(bsz, T, H_C, DK_C),
                             vc.reshape(bsz, T, H_C, DV_C), st['ret'][i], pos, P['ret_norm_w'][i], segments)
    yd, hd, dbuf = rglru_mixer(xd, st['lru_conv'][i], st['lru_h'][i], P['lru_conv_w'][i], P['lru_conv_b'][i],
                               P['lru_w_a'][i], P['lru_b_a'][i], P['lru_w_x'][i], P['lru_b_x'][i],
                               P['lru_lambda'][i])
    gated = jnp.concatenate([yc * jax.nn.silu(zc.astype(F32)), yd * jax.nn.silu(zd.astype(F32))], axis=-1)
    h = h + (gated.astype(h.dtype) @ P['od_w_out'][i]).astype(h.dtype)
    return h, dict(ret=S1, lru_h=hd, lru_conv=dbuf)


def run_trunk(h, pos, segments, st, P):
    new = {}
    for layer in range(DEPTH):
        i = layer // 2
        g = P['norm_w'][layer]
        if layer % 2 == 0:
            h, ns = even_layer(h, g, i, st, P, segments)
        else:
            h, ns = odd_layer(h, g, i, st, P, pos, segments)
        for name, val in ns.items():
            new.setdefault(name, []).append(val)
    stacked = {name: jnp.stack(vals) for name, vals in new.items()}
    return rmsnorm(h, P['final_norm_w']), stacked


def zero_states(bsz, act_dtype):
    return dict(
        s5_re=jnp.zeros((N_EVEN, bsz, G_A, S5_STATE), F32),
        s5_im=jnp.zeros((N_EVEN, bsz, G_A, S5_STATE), F32),
        ml_c=jnp.zeros((N_EVEN, bsz, H_B, DH_B, DH_B), F32),
        ml_n=jnp.zeros((N_EVEN, bsz, H_B, DH_B), F32),
        ml_m=jnp.zeros((N_EVEN, bsz, H_B), F32),
        ml_conv=jnp.zeros((N_EVEN, bsz, CONV_K - 1, W_B), act_dtype),
        ret=jnp.zeros((N_ODD, bsz, H_C, DK_C, DV_C), F32),
        lru_h=jnp.zeros((N_ODD, bsz, W_D), F32),
        lru_conv=jnp.zeros((N_ODD, bsz, CONV_K - 1, W_D), act_dtype),
    )


def setup_inputs(seed: int = 0) -> dict:
    key = jax.random.key(seed)
    ks = iter(jax.random.split(key, 64))

    def nrm(shape, scale):
        return scale * jax.random.normal(next(ks), shape, F32)

    def unif(shape, lo, hi):
        return jax.random.uniform(next(ks), shape, F32, lo, hi)

    NE, NO = N_EVEN, N_ODD
    lru_s = unif((NO, W_D), 0.9, 0.999) ** (1.0 / LRU_C)
    return {
        'x_prompt': nrm((BATCH, SEQ, D_MODEL), 1.0),
        'x_sample': nrm((DEC_BATCH, DEC_SEQ, D_MODEL), 1.0),
        'state_s5_re': nrm((NE, DEC_BATCH, G_A, S5_STATE), 0.5),
        'state_s5_im': nrm((NE, DEC_BATCH, G_A, S5_STATE), 0.5),
        'state_ml_c': nrm((NE, DEC_BATCH, H_B, DH_B, DH_B), 0.1),
        'state_ml_n': nrm((NE, DEC_BATCH, H_B, DH_B), 0.1),
        'state_ml_m': nrm((NE, DEC_BATCH, H_B), 1.0),
        'state_ml_conv': nrm((NE, DEC_BATCH, CONV_K - 1, W_B), 1.0),
        'state_ret': nrm((NO, DEC_BATCH, H_C, DK_C, DV_C), 0.3),
        'state_lru_h': nrm((NO, DEC_BATCH, W_D), 0.5),
        'state_lru_conv': nrm((NO, DEC_BATCH, CONV_K - 1, W_D), 1.0),
        'meta': nrm((N_META, D_MODEL), 1.0),
        'norm_w': 1.0 + nrm((DEPTH, D_MODEL), 0.01),
        'final_norm_w': 1.0 + nrm((D_MODEL,), 0.01),
        'ev_w_in': nrm((NE, D_MODEL, EV_IN), D_MODEL ** -0.5),
        'ev_w_out': nrm((NE, EV_MIX, D_MODEL), EV_MIX ** -0.5),
        's5_lambda_re': -0.5 + nrm((NE, G_A, S5_STATE), 0.01),
        's5_lambda_im': jnp.pi * jnp.arange(S5_STATE, dtype=F32) + nrm((NE, G_A, S5_STATE), 0.01),
        's5_log_dt': unif((NE, G_A), math.log(1e-3), math.log(1e-1)),
        's5_b_re': nrm((NE, G_A, S5_STATE, S5_GROUP), (2 * S5_GROUP) ** -0.5),
        's5_b_im': nrm((NE, G_A, S5_STATE, S5_GROUP), (2 * S5_GROUP) ** -0.5),
        's5_c_re': nrm((NE, G_A, S5_GROUP, S5_STATE), S5_STATE ** -0.5),
        's5_c_im': nrm((NE, G_A, S5_GROUP, S5_STATE), S5_STATE ** -0.5),
        's5_d': nrm((NE, W_A), 1.0),
        's5_w_glu': nrm((NE, W_A, W_A), W_A ** -0.5),
        's5_b_glu': nrm((NE, W_A), 0.01),
        'ml_conv_w': nrm((NE, CONV_K, W_B), CONV_K ** -0.5),
        'ml_conv_b': nrm((NE, W_B), 0.01),
        'ml_wq': nrm((NE, H_B, DH_B, DH_B), DH_B ** -0.5),
        'ml_wk': nrm((NE, H_B, DH_B, DH_B), DH_B ** -0.5),
        'ml_wv': nrm((NE, H_B, DH_B, DH_B), DH_B ** -0.5),
        'ml_w_if': nrm((NE, 3 * W_B, 2 * H_B), 0.1 * (3 * W_B) ** -0.5),
        'ml_b_if': jnp.concatenate([nrm((NE, H_B), 0.1),
                                    jnp.linspace(3.0, 6.0, H_B, dtype=F32) + nrm((NE, H_B), 0.01)], axis=-1),
        'ml_norm_w': 1.0 + nrm((NE, W_B), 0.01),
        'ml_skip': 1.0 + nrm((NE, W_B), 0.01),
        'od_w_in': nrm((NO, D_MODEL, OD_IN), D_MODEL ** -0.5),
        'od_w_out': nrm((NO, OD_MIX, D_MODEL), OD_MIX ** -0.5),
        'ret_norm_w': 1.0 + nrm((NO, W_C), 0.01),
        'lru_conv_w': nrm((NO, CONV_K, W_D), CONV_K ** -0.5),
        'lru_conv_b': nrm((NO, W_D), 0.01),
        'lru_w_a': nrm((NO, H_D, BD_D, BD_D), BD_D ** -0.5),
        'lru_b_a': nrm((NO, W_D), 0.1),
        'lru_w_x': nrm((NO, H_D, BD_D, BD_D), BD_D ** -0.5),
        'lru_b_x': nrm((NO, W_D), 0.1),
        'lru_lambda': jnp.log(lru_s) - jnp.log1p(-lru_s),
    }


def reference(x_prompt, x_sample, state_s5_re, state_s5_im, state_ml_c, state_ml_n, state_ml_m, state_ml_conv,
              state_ret, state_lru_h, state_lru_conv, meta, norm_w, final_norm_w, ev_w_in, ev_w_out,
              s5_lambda_re, s5_lambda_im, s5_log_dt, s5_b_re, s5_b_im, s5_c_re, s5_c_im, s5_d, s5_w_glu, s5_b_glu,
              ml_conv_w, ml_conv_b, ml_wq, ml_wk, ml_wv, ml_w_if, ml_b_if, ml_norm_w, ml_skip,
              od_w_in, od_w_out, ret_norm_w, lru_conv_w, lru_conv_b, lru_w_a, lru_b_a, lru_w_x, lru_b_x,
              lru_lambda):
    P = dict(norm_w=norm_w, final_norm_w=final_norm_w, ev_w_in=ev_w_in, ev_w_out=ev_w_out,
             s5_lambda_re=s5_lambda_re, s5_lambda_im=s5_lambda_im, s5_log_dt=s5_log_dt,
             s5_b_re=s5_b_re, s5_b_im=s5_b_im, s5_c_re=s5_c_re, s5_c_im=s5_c_im, s5_d=s5_d,
             s5_w_glu=s5_w_glu, s5_b_glu=s5_b_glu, ml_conv_w=ml_conv_w, ml_conv_b=ml_conv_b,
             ml_wq=ml_wq, ml_wk=ml_wk, ml_wv=ml_wv, ml_w_if=ml_w_if, ml_b_if=ml_b_if,
             ml_norm_w=ml_norm_w, ml_skip=ml_skip, od_w_in=od_w_in, od_w_out=od_w_out,
             ret_norm_w=ret_norm_w, lru_conv_w=lru_conv_w, lru_conv_b=lru_conv_b,
             lru_w_a=lru_w_a, lru_b_a=lru_b_a, lru_w_x=lru_w_x, lru_b_x=lru_b_x, lru_lambda=lru_lambda)

    bp, tp, _ = x_prompt.shape
    meta_b = jnp.broadcast_to(meta.astype(x_prompt.dtype)[None], (bp, N_META, D_MODEL))
    h_p = jnp.concatenate([meta_b, x_prompt], axis=1)
    pos_p = jnp.arange(N_META + tp, dtype=F32)
    seg_p = ((N_META, N_META), (tp, CHUNK))
    out_p, new_p = run_trunk(h_p, pos_p, seg_p, zero_states(bp, x_prompt.dtype), P)
    y_prompt = out_p[:, N_META:]

    ts = x_sample.shape[1]
    pos_s = (N_META + PAST_LEN) + jnp.arange(ts, dtype=F32)
    st_s = dict(s5_re=state_s5_re, s5_im=state_s5_im, ml_c=state_ml_c, ml_n=state_ml_n, ml_m=state_ml_m,
                ml_conv=state_ml_conv, ret=state_ret, lru_h=state_lru_h, lru_conv=state_lru_conv)
    y_sample, new_s = run_trunk(x_sample, pos_s, ((ts, ts),), st_s, P)

    return (y_prompt, y_sample,
            new_p['s5_re'], new_p['s5_im'], new_p['ml_c'], new_p['ml_n'], new_p['ml_m'], new_p['ml_conv'],
            new_p['ret'], new_p['lru_h'], new_p['lru_conv'],
            new_s['s5_re'], new_s['s5_im'], new_s['ml_c'], new_s['ml_n'], new_s['ml_m'], new_s['ml_conv'],
            new_s['ret'], new_s['lru_h'], new_s['lru_conv'])
```

```python
import math
from contextlib import ExitStack
import numpy as np
import concourse.bass as bass
import concourse.mybir as mybir
from concourse.bass_utils import run_bass_kernel_spmd

F32 = mybir.dt.float32
BF16 = mybir.dt.bfloat16
AF = mybir.ActivationFunctionType
ALU = mybir.AluOpType

D = 1024
NPIECE = 256
EPS = 1e-6
PI = math.pi


class Prog:
    def __init__(self, nc, ndma=6):
        self.nc = nc
        self.e = dict(pe=nc.tensor, dve=nc.vector, act=nc.scalar, pool=nc.gpsimd, sp=nc.sync)
        self.streams = {k: [] for k in self.e}
        self.cnt = {k: 0 for k in self.e}
        self.waited = {k: {} for k in self.e}
        self.lastw = {}
        self.reads = {}
        self.sems = {}
        self.ndma = ndma
        self.dmai = {q: 0 for q in ('sp', 'act', 'pool')}
        self.dma_last = {}
        self.enabled = True
        self.semnames = ['c_' + k for k in ('pe', 'dve', 'act', 'pool')] + \
            ['d_%s_%d' % (q, j) for q in ('sp', 'act', 'pool') for j in range(ndma)]

    def _wait(self, eng, tok):
        sem, val = tok
        if self.waited[eng].get(sem, 0) >= val:
            return
        self.waited[eng][sem] = val
        self.e[eng].wait_ge(self.sems[sem], val)

    def _deps(self, eng, reads, writes):
        deps = []
        own = 'c_' + eng
        for k in reads:
            t = self.lastw.get(k)
            if t is not None:
                if not (eng == 'pe' and t[0] == own):
                    deps.append(t)
        for k in writes:
            t = self.lastw.get(k)
            if t is not None and t[0] != own:
                deps.append(t)
            for t in self.reads.get(k, ()):
                if t[0] != own:
                    deps.append(t)
        for t in deps:
            self._wait(eng, t)

    def _commit(self, tok, reads, writes):
        for k in writes:
            self.lastw[k] = tok
            self.reads[k] = []
        for k in reads:
            if k not in writes:
                self.reads.setdefault(k, []).append(tok)

    def op(self, eng, fn, reads=(), writes=()):
        if not self.enabled:
            return None
        self._deps(eng, reads, writes)
        self.cnt[eng] += 1
        tok = ('c_' + eng, self.cnt[eng])
        fn().then_inc(self.sems['c_' + eng], 1)
        self._commit(tok, reads, writes)
        return tok

    def dma(self, q, out, in_, reads=(), writes=(), **kw):
        if not self.enabled:
            return None
        self._deps(q, reads, writes)
        i = self.dmai[q]
        self.dmai[q] += 1
        j, r = i % self.ndma, i // self.ndma
        sem = 'd_%s_%d' % (q, j)
        if r > 0:
            self._wait(q, (sem, 16 * r))
        e = self.e[q]
        e.dma_start(out=out, in_=in_, **kw).then_inc(self.sems[sem], 16)
        tok = (sem, 16 * (r + 1))
        self.dma_last[sem] = tok
        self._commit(tok, reads, writes)
        return tok

    def barrier(self):
        toks = set(self.lastw.values())
        for lst in self.reads.values():
            toks.update(lst)
        toks.update(self.dma_last.values())
        for k in ('pe', 'dve', 'act', 'pool'):
            if self.cnt[k] > 0:
                toks.add(('c_' + k, self.cnt[k]))
        for eng in self.e:
            for t in toks:
                self._wait(eng, t)
        self.lastw = {}
        self.reads = {}

    def emit(self, block):
        sems = self.sems

        def run(engname, engobj):
            for it in self.streams[engname]:
                if it[0] == 'w':
                    engobj.wait_ge(sems[it[1]], it[2])
                else:
                    inst = it[1]()
                    inst.then_inc(sems[it[2]], it[3])

        @block.tensor
        def _(x):
            run('pe', x)

        @block.vector
        def _(x):
            run('dve', x)

        @block.scalar
        def _(x):
            run('act', x)

        @block.gpsimd
        def _(x):
            run('pool', x)

        @block.sync
        def _(x):
            run('sp', x)


def host_consts():
    c = {}
    c['c_identb'] = np.eye(128, dtype=np.float32)
    c['c_identf'] = np.eye(128, dtype=np.float32)
    s = np.arange(128)
    c['c_negmask'] = np.where(s[:, None] <= s[None, :], 1.0, 0.0).astype(np.float32)
    sel = np.zeros((4, 4, 128), np.float32)
    for h in range(4):
        sel[h, h, :] = 1.0
    c['c_sel'] = sel
    c['c_ones4'] = np.ones((4, 128), np.float32)
    mA = np.zeros((128, 8), np.float32)
    for g8 in range(8):
        mA[g8 * 16:(g8 + 1) * 16, g8] = 1.0
    c['c_maskA'] = mA
    lg = np.log1p(-np.exp2(-5.0 - np.arange(4, dtype=np.float64)))
    for L in (16, 128):
        idx = np.arange(L, dtype=np.float64)
        diff = idx[None, :] - idx[:, None]
        dm = np.where(diff >= 0, np.exp(lg[:, None, None] * np.maximum(diff, 0.0)), 0.0)
        dmp = np.zeros((128, 4, 128), np.float32)
        dmp[:L, :, :L] = np.transpose(dm, (1, 0, 2))
        c['c_rmask%d' % L] = dmp
        xi = np.exp(lg[:, None] * (idx + 1.0))
        c['c_rxi%d' % L] = np.ascontiguousarray(np.broadcast_to(xi[None], (128, 4, L))).astype(np.float32)
        zeta = np.exp(lg[:, None] * (L - 1.0 - idx))
        zp = np.zeros((128, 4), np.float32)
        zp[:L] = zeta.T
        c['c_rzeta%d' % L] = zp
        c['c_rgl%d' % L] = np.ascontiguousarray(np.broadcast_to(np.exp(lg * L)[None], (128, 4))).astype(np.float32)
    return c


def rope_tables(pos):
    half = 64
    inv = (10000.0 ** (-np.arange(half, dtype=np.float32) / half)).astype(np.float32)
    ang = pos.astype(np.float32)[:, None] * inv[None, :]
    cos = np.cos(ang).astype(np.float32)
    sin = np.sin(ang).astype(np.float32)
    cf = np.concatenate([cos.T, cos.T], axis=0)
    sf = np.concatenate([-sin.T, sin.T], axis=0)
    return np.ascontiguousarray(cf), np.ascontiguousarray(sf), cos, sin


class Cfg:
    def __init__(self, n_p=2, T_x=4096, n_s=2, past=2048, layers=2, debug=False, stage=99):
        self.stage = stage
        self.n_p, self.T_x, self.n_s, self.past, self.layers, self.debug = n_p, T_x, n_s, past, layers, debug
        assert T_x % NPIECE == 0


IN_SPECS = None


def in_specs(cfg):
    n_p, T_x, n_s = cfg.n_p, cfg.T_x, cfg.n_s
    T = 16 + T_x
    sp = {
        'xp': [n_p, T_x, D], 'xs': [n_s, 16, D], 'meta': [16, D],
        's5re_in': [n_s, 2048], 's5im_in': [n_s, 2048],
        'mlc_in': [n_s, 4, 256, 256], 'mln_in': [n_s, 4, 256], 'mlm_in': [n_s, 4], 'mlconv_in': [n_s, 3, D],
        'ret_in': [n_s, 4, 128, 256], 'lruh_in': [n_s, D], 'lruconv_in': [n_s, 3, D],
        'norm_w': [2, D], 'final_norm_w': [D],
        'ev_w_in': [D, 3072], 'ev_w_out': [1536, D],
        'lamre_A': [128, 4, 64], 'lamim_A': [128, 4, 64], 'logdt_A': [128, 4, 64],
        'lamre_S': [128, 16], 'lamim_S': [128, 16], 'logdt_S': [128, 16],
        'bre_A': [128, 4, 64], 'bim_A': [128, 4, 64], 'cre_S': [128, 16, 16], 'cim_S': [128, 16, 16],
        's5_d': [512], 's5_w_glu': [512, 512], 's5_b_glu': [512],
        'ml_conv_w': [4, D], 'ml_conv_b': [D], 'ml_wq': [4, 256, 256], 'ml_wk': [4, 256, 256], 'ml_wv': [4, 256, 256],
        'ml_wqT': [4, 256, 256], 'ml_wkT': [4, 256, 256], 'ml_wvT': [4, 256, 256],
        'ml_w_if': [3072, 8], 'ml_b_if': [8], 'ml_norm_w': [D], 'ml_skip': [D],
        'od_w_in': [D, 5120], 'od_w_qksw': [D, 1024], 'fnw_b': [128, D], 'od_w_out': [2048, D], 'ret_norm_w': [D],
        'lru_conv_w': [4, D], 'lru_conv_b': [D], 'lru_w_a': [8, 128, 128], 'lru_b_a': [D],
        'lru_w_x': [8, 128, 128], 'lru_b_x': [D], 'lru_lambda': [D],
        'ropeF_c': [128, T + 16], 'ropeF_s': [128, T + 16], 'ropeT_c': [T + 16, 64], 'ropeT_s': [T + 16, 64],
    }
    for k, v in host_consts().items():
        sp[k] = list(v.shape)
    return sp


def out_specs(cfg):
    n_p, T_x, n_s = cfg.n_p, cfg.T_x, cfg.n_s
    o = {'yp': [n_p, T_x, D], 'ys': [n_s, 16, D]}
    for g, n in (('p', n_p), ('s', n_s)):
        o['o_s5re_' + g] = [n, 2048]
        o['o_s5im_' + g] = [n, 2048]
        o['o_mlc_' + g] = [n, 4, 256, 256]
        o['o_mln_' + g] = [n, 4, 256]
        o['o_mlm_' + g] = [n, 4]
        o['o_mlconv_' + g] = [n, 3, D]
        o['o_ret_' + g] = [n, 4, 128, 256]
        o['o_lruh_' + g] = [n, D]
        o['o_lruconv_' + g] = [n, 3, D]
    return o


def build(cfg):
    nc = bass.Bass("TRN2", target_bir_lowering=False)
    n_p, T_x, n_s = cfg.n_p, cfg.T_x, cfg.n_s
    T = 16 + T_x
    I = {k: nc.dram_tensor(k, v, F32, kind="ExternalInput").ap() for k, v in in_specs(cfg).items()}
    O = {k: nc.dram_tensor(k, v, F32, kind="ExternalOutput").ap() for k, v in out_specs(cfg).items()}
    hk = "ExternalOutput" if cfg.debug else "Internal"
    h1p = nc.dram_tensor("h1p", [n_p, T, D], F32, kind=hk).ap()
    h1s = nc.dram_tensor("h1s", [n_s, 16, D], F32, kind=hk).ap()
    dbg_o = nc.dram_tensor("dbg_o", [4, 128, 256], F32, kind=hk).ap()

    seqs = []
    for i in range(n_p):
        pcs = [(0, 16, I['meta'][:, :], O['yp'], None)]
        for t in range(0, T_x, NPIECE):
            pcs.append((16 + t, NPIECE, I['xp'][i, t:t + NPIECE, :], O['yp'][i, t:t + NPIECE, :], None))
        seqs.append(('p', i, pcs))
    for i in range(n_s):
        seqs.append(('s', i, [(0, 16, I['xs'][i, :, :], O['ys'][i, :, :], None)]))

    def h1rows(g, i, t0, n):
        return (h1p if g == 'p' else h1s)[i, t0:t0 + n, :]

    with ExitStack() as top:
        top.enter_context(nc.allow_non_contiguous_dma(reason='small strided parameter/state layouts'))
        P = Prog(nc)
        for s in P.semnames:
            P.sems[s] = top.enter_context(nc.semaphore(s))
        pb = [top.enter_context(nc.psum_tensor("pb%d" % i, [128, 512], F32)) for i in range(8)]

        V, G, A, PL = nc.vector, nc.gpsimd, nc.scalar, nc.tensor

        def mm(out, lhsT, rhs, start, stop, reads, writes):
            P.op('pe', lambda: PL.matmul(out, lhsT, rhs, start=start, stop=stop), reads=reads, writes=writes)

        sb0 = lambda n, s, d: top.enter_context(nc.sbuf_tensor(n, s, d))
        identb = sb0("identb", [128, 128], BF16)
        identf = sb0("identf", [128, 128], F32)
        negh = sb0("negh", [128, 1], F32)
        P.dma('pool', identb[:], I['c_identb'][:, :], writes=['identb'])
        P.dma('sp', identf[:], I['c_identf'][:, :], writes=['identf'])
        P.op('pool', lambda: G.memset(negh[:], -0.5), writes=['negh'])

        def rmsnorm_T(hin, tp, tt, gcol, ub, uT, ss, hkey='hin', gkey='gcol'):
            P.op('pool', lambda: G.memset(ss[0:tp, 0:1], 0.0), writes=['ss'])
            P.op('act', lambda: A.activation(out=ub[0:tp, :], in_=hin[0:tp, :], func=AF.Square, accum_out=ss[0:tp, 0:1]),
                 reads=[hkey], writes=['ub', 'ss'])
            P.op('dve', lambda: V.tensor_scalar(ss[0:tp, 1:2], ss[0:tp, 0:1], 1.0 / D, EPS, ALU.mult, ALU.add),
                 reads=['ss'], writes=['ss'])
            P.op('pool', lambda: G.tensor_tensor(ss[0:tp, 2:3], ss[0:tp, 1:2], negh[0:tp, :], ALU.pow),
                 reads=['ss', 'negh'], writes=['ss'])
            P.op('act', lambda: A.activation(out=ub[0:tp, :], in_=hin[0:tp, :], func=AF.Copy, scale=ss[0:tp, 2:3]),
                 reads=[hkey, 'ss'], writes=['ub'])
            pT = pb[2][:].bitcast(BF16)
            for k in range(8):
                P.op('pe', lambda k=k: PL.transpose(pT[:, k * 128:k * 128 + tp], ub[0:tp, k * 128:(k + 1) * 128],
                                                    identb[0:tp, 0:tp]),
                     reads=['ub', 'identb'], writes=['pb2'])
            for k in range(8):
                P.op('act', lambda k=k: A.activation(out=uT[:, k, tt * 128:tt * 128 + tp], in_=pT[:, k * 128:k * 128 + tp],
                                                     func=AF.Copy, scale=gcol[:, k:k + 1]),
                     reads=['pb2', gkey], writes=['uT'])

        with ExitStack() as l0:
            sb = lambda n, s, d: l0.enter_context(nc.sbuf_tensor(n, s, d))
            NP = NPIECE
            w_in = sb("w_in0", [128, 8, 3072], BF16)
            w_out = sb("w_out0", [128, 12, D], BF16)
            w_glu = sb("w_glu", [128, 4, 512], BF16)
            wq = sb("wq", [128, 4, 2, 256], BF16)
            wk = sb("wk", [128, 4, 2, 256], BF16)
            wv = sb("wv", [128, 4, 2, 256], BF16)
            wcif = sb("wcif", [128, 8, 8], BF16)
            wxif = sb("wxif", [128, 8, 8], BF16)
            s5B = sb("s5B", [128, 2, 16, 128], BF16)
            s5C = sb("s5C", [128, 2, 16, 128], BF16)
            LR = 64
            tabc = sb("tabc", [128, 16, LR], F32)
            tabs = sb("tabs", [128, 16, LR], F32)
            rtab = sb("rtab", [128, 16, LR], F32)
            gcol = sb("gcol0", [128, 8], F32)
            dcol = sb("dcol", [128, 4], F32)
            bgluh = sb("bgluh", [128, 4], F32)
            convw = sb("convw", [128, 8, 4], F32)
            convb = sb("convb", [128, 8], F32)
            mlnw = sb("mlnw", [128, 8], F32)
            skipc = sb("skipc", [128, 8], F32)
            b_i = sb("b_i", [4, 1], F32)
            b_f = sb("b_f", [4, 1], F32)
            negmask = sb("negmask", [128, 128], F32)
            sel = sb("sel", [4, 4, 128], F32)
            ones4 = sb("ones4", [4, 128], F32)
            onesrow = sb("onesrow", [4, NP], F32)

            for k in range(8):
                P.dma('pool', w_in[:, k, :], I['ev_w_in'][k * 128:(k + 1) * 128, :], writes=['w_in'])
            for k in range(12):
                P.dma('pool', w_out[:, k, :], I['ev_w_out'][k * 128:(k + 1) * 128, :], writes=['w_out'])
            P.dma('pool', w_glu[:], I['s5_w_glu'].rearrange("(k p) f -> p k f", p=128), writes=['w_glu'])
            for nm, t_ in (('ml_wq', wq), ('ml_wk', wk), ('ml_wv', wv)):
                for h in range(4):
                    P.dma('pool', t_[:, h, :, :], I[nm][h].rearrange("(k p) e -> p k e", p=128), writes=[nm])
            P.dma('sp', gcol[:], I['norm_w'][0, :].rearrange("(k p) -> p k", p=128), writes=['gcol'])
            P.dma('sp', dcol[:], I['s5_d'].rearrange("(k p) -> p k", p=128), writes=['dcol'])
            P.dma('sp', bgluh[:], I['s5_b_glu'].rearrange("(k p) -> p k", p=128), writes=['bgluh'])
            for tap in range(4):
                P.dma('sp', convw[:, :, tap], I['ml_conv_w'][tap, :].rearrange("(k p) -> p k", p=128), writes=['convw'])
            P.dma('sp', convb[:], I['ml_conv_b'].rearrange("(k p) -> p k", p=128), writes=['convb'])
            P.dma('sp', mlnw[:], I['ml_norm_w'].rearrange("(k p) -> p k", p=128), writes=['mlnw'])
            P.dma('sp', skipc[:], I['ml_skip'].rearrange("(k p) -> p k", p=128), writes=['skipc'])
            P.dma('sp', b_i[:], I['ml_b_if'][0:4].rearrange("(p o) -> p o", o=1), writes=['b_i'])
            P.dma('sp', b_f[:], I['ml_b_if'][4:8].rearrange("(p o) -> p o", o=1), writes=['b_f'])
            P.dma('sp', negmask[:], I['c_negmask'][:, :], writes=['negmask'])
            P.dma('sp', sel[:], I['c_sel'][:, :, :], writes=['sel'])
            P.dma('sp', ones4[:], I['c_ones4'][:, :], writes=['ones4'])
            P.op('pool', lambda: G.memset(onesrow[:], 1.0), writes=['onesrow'])
            P.op('dve', lambda: V.tensor_scalar(bgluh[:], bgluh[:], 0.5, None, ALU.mult), reads=['bgluh'], writes=['bgluh'])
            P.op('dve', lambda: V.tensor_scalar(b_f[:], b_f[:], -1.0, None, ALU.mult), reads=['b_f'], writes=['b_f'])

            print('SBUF remaining before setup', nc.sbuf_bytes_remaining)
            with ExitStack() as su:
                sbs = lambda n, s, d: su.enter_context(nc.sbuf_tensor(n, s, d))
                wqT = sbs("wqT", [128, 4, 2, 256], BF16)
                wkT = sbs("wkT", [128, 4, 2, 256], BF16)
                wvT = sbs("wvT", [128, 4, 2, 256], BF16)
                wif = sbs("wif", [128, 24, 8], BF16)
                for nm, t_ in (('ml_wqT', wqT), ('ml_wkT', wkT), ('ml_wvT', wvT)):
                    for h in range(4):
                        P.dma('pool', t_[:, h, :, :], I[nm][h].rearrange("(k p) e -> p k e", p=128), writes=[nm])
                P.dma('pool', wif[:], I['ml_w_if'].rearrange("(k p) g -> p k g", p=128), writes=['wif'])
                for h in range(4):
                    for dt_ in range(2):
                        o_ = pb[0][:, 0:8]
                        i_ = 0
                        for (wT, nm, off) in ((wqT, 'ml_wqT', 0), (wkT, 'ml_wkT', 8)):
                            for ke in range(2):
                                mm(o_, wT[:, h, ke, dt_ * 128:(dt_ + 1) * 128], wif[:, off + h * 2 + ke, :],
                                   i_ == 0, i_ == 3, [nm, 'wif'], ['pb0'])
                                i_ += 1
                        P.op('act', lambda h=h, dt_=dt_: A.copy(wcif[:, h * 2 + dt_, :], pb[0][:, 0:8]),
                             reads=['pb0'], writes=['wcif'])
                        o2 = pb[1][:, 0:8]
                        for ke in range(2):
                            mm(o2, wvT[:, h, ke, dt_ * 128:(dt_ + 1) * 128], wif[:, 16 + h * 2 + ke, :],
                               ke == 0, ke == 1, ['ml_wvT', 'wif'], ['pb1'])
                        P.op('act', lambda h=h, dt_=dt_: A.copy(wxif[:, h * 2 + dt_, :], pb[1][:, 0:8]),
                             reads=['pb1'], writes=['wxif'])

                def S(n, shp):
                    return sbs(n, shp, F32)

                def trig(name, ang_ap, shp, key):
                    cosT = S(name + "_cos", shp)
                    sinT = S(name + "_sin", shp)
                    sh = S(name + "_sh", shp)
                    acc = S(name + "_acc", shp)
                    tmp = S(name + "_tm", shp)
                    for (dst, shift) in ((cosT, PI / 2), (sinT, 0.0)):
                        P.op('dve', lambda shift=shift: V.tensor_scalar(sh[:], ang_ap, shift, None, ALU.add),
                             reads=[key], writes=[name + 'sh'])
                        P.op('dve', lambda: V.tensor_copy(acc[:], sh[:]), reads=[name + 'sh'], writes=[name + 'win_re'])
                        for m_ in range(1, 7):
                            thr = (2 * m_ - 1) * PI
                            P.op('dve', lambda thr=thr: V.tensor_scalar(tmp[:], sh[:], thr, -2 * PI, ALU.is_ge, ALU.mult),
                                 reads=[name + 'sh'], writes=[name + 'tm'])
                            P.op('dve', lambda: V.tensor_tensor(acc[:], acc[:], tmp[:], ALU.add),
                                 reads=[name + 'win_re', name + 'tm'], writes=[name + 'win_re'])
                        P.op('act', lambda dst=dst: A.activation(out=dst[:], in_=acc[:], func=AF.Sin),
                             reads=[name + 'win_re'], writes=[name + ('c' if dst is cosT else 's')])
                    return cosT, sinT

                shA = [128, 4, 64]
                lamreA = S("lamreA", shA); lamimA = S("lamimA", shA); dtA = S("dtA", shA)
                breA = S("breA", shA); bimA = S("bimA", shA)
                P.dma('sp', lamreA[:], I['lamre_A'][:, :, :], writes=['lamreA'])
                P.dma('sp', lamimA[:], I['lamim_A'][:, :, :], writes=['lamimA'])
                P.dma('sp', dtA[:], I['logdt_A'][:, :, :], writes=['dtA'])
                P.dma('sp', breA[:], I['bre_A'][:, :, :], writes=['breA'])
                P.dma('sp', bimA[:], I['bim_A'][:, :, :], writes=['bimA'])
                P.op('act', lambda: A.activation(out=dtA[:], in_=dtA[:], func=AF.Exp), reads=['dtA'], writes=['dtA'])
                magA = S("magA", shA); angA = S("angA", shA)
                P.op('dve', lambda: V.tensor_tensor(magA[:], lamreA[:], dtA[:], ALU.mult), reads=['lamreA', 'dtA'], writes=['magA'])
                P.op('act', lambda: A.activation(out=magA[:], in_=magA[:], func=AF.Exp), reads=['magA'], writes=['magA'])
                P.op('dve', lambda: V.tensor_tensor(angA[:], lamimA[:], dtA[:], ALU.mult), reads=['lamimA', 'dtA'], writes=['angA'])
                cosA, sinA = trig("tA", angA[:], shA, 'angA')
                abre = S("abre", shA); abim = S("abim", shA)
                P.op('dve', lambda: V.tensor_tensor(abre[:], magA[:], cosA[:], ALU.mult), reads=['magA', 'tAc'], writes=['abre'])
                P.op('dve', lambda: V.tensor_tensor(abim[:], magA[:], sinA[:], ALU.mult), reads=['magA', 'tAs'], writes=['abim'])
                den = S("denA", shA); t1 = S("t1A", shA); t2 = S("t2A", shA); kre = S("kreA", shA); kim = S("kimA", shA)
                P.op('dve', lambda: V.tensor_tensor(den[:], lamreA[:], lamreA[:], ALU.mult), reads=['lamreA'], writes=['denA'])
                P.op('dve', lambda: V.tensor_tensor(t1[:], lamimA[:], lamimA[:], ALU.mult), reads=['lamimA'], writes=['t1A'])
                P.op('dve', lambda: V.tensor_tensor(den[:], den[:], t1[:], ALU.add), reads=['denA', 't1A'], writes=['denA'])
                P.op('dve', lambda: V.reciprocal(den[:], den[:]), reads=['denA'], writes=['denA'])
                P.op('dve', lambda: V.tensor_scalar(abre[:], abre[:], -1.0, None, ALU.add), reads=['abre'], writes=['abre'])
                P.op('dve', lambda: V.tensor_tensor(t1[:], abre[:], lamreA[:], ALU.mult), reads=['abre', 'lamreA'], writes=['t1A'])
                P.op('dve', lambda: V.tensor_tensor(t2[:], abim[:], lamimA[:], ALU.mult), reads=['abim', 'lamimA'], writes=['t2A'])
                P.op('dve', lambda: V.tensor_tensor(t1[:], t1[:], t2[:], ALU.add), reads=['t1A', 't2A'], writes=['t1A'])
                P.op('dve', lambda: V.tensor_tensor(kre[:], t1[:], den[:], ALU.mult), reads=['t1A', 'denA'], writes=['kreA'])
                P.op('dve', lambda: V.tensor_tensor(t1[:], abim[:], lamreA[:], ALU.mult), reads=['abim', 'lamreA'], writes=['t1A'])
                P.op('dve', lambda: V.tensor_tensor(t2[:], abre[:], lamimA[:], ALU.mult), reads=['abre', 'lamimA'], writes=['t2A'])
                P.op('dve', lambda: V.tensor_tensor(t1[:], t1[:], t2[:], ALU.subtract), reads=['t1A', 't2A'], writes=['t1A'])
                P.op('dve', lambda: V.tensor_tensor(kim[:], t1[:], den[:], ALU.mult), reads=['t1A', 'denA'], writes=['kimA'])
                bbre = S("bbre", shA); bbim = S("bbim", shA)
                P.op('dve', lambda: V.tensor_tensor(t1[:], kre[:], breA[:], ALU.mult), reads=['kreA', 'breA'], writes=['t1A'])
                P.op('dve', lambda: V.tensor_tensor(t2[:], kim[:], bimA[:], ALU.mult), reads=['kimA', 'bimA'], writes=['t2A'])
                P.op('dve', lambda: V.tensor_tensor(bbre[:], t1[:], t2[:], ALU.subtract), reads=['t1A', 't2A'], writes=['bbre'])
                P.op('dve', lambda: V.tensor_tensor(t1[:], kre[:], bimA[:], ALU.mult), reads=['kreA', 'bimA'], writes=['t1A'])
                P.op('dve', lambda: V.tensor_tensor(t2[:], kim[:], breA[:], ALU.mult), reads=['kimA', 'breA'], writes=['t2A'])
                P.op('dve', lambda: V.tensor_tensor(bbim[:], t1[:], t2[:], ALU.add), reads=['t1A', 't2A'], writes=['bbim'])
                maskA = S("maskA", [128, 8])
                P.dma('sp', maskA[:], I['c_maskA'][:, :], writes=['maskA'])
                for j in range(16):
                    kc, jl = j // 4, j % 4
                    for h in range(2):
                        g8 = 2 * jl + h
                        for part, src, key in ((0, bbre, 'bbre'), (1, bbim, 'bbim')):
                            P.op('dve', lambda j=j, kc=kc, h=h, g8=g8, part=part, src=src: V.tensor_scalar(
                                s5B[:, part, j, h * 64:(h + 1) * 64], src[:, kc, :], maskA[:, g8:g8 + 1], None, ALU.mult),
                                reads=[key, 'maskA'], writes=['s5B'])
                creS = S("creS", [128, 16, 16]); cimS = S("cimS", [128, 16, 16])
                P.dma('sp', creS[:], I['cre_S'][:, :, :], writes=['creS'])
                P.dma('sp', cimS[:], I['cim_S'][:, :, :], writes=['cimS'])
                P.op('pool', lambda: G.memset(s5C[:], 0.0), writes=['s5C'])
                for j in range(16):
                    jl = j % 4
                    for h in range(2):
                        g8 = 2 * jl + h
                        P.op('dve', lambda j=j, h=h, g8=g8: V.tensor_copy(
                            s5C[h * 64:(h + 1) * 64, 0, j, g8 * 16:(g8 + 1) * 16], creS[h * 64:(h + 1) * 64, j, :]),
                            reads=['creS'], writes=['s5C'])
                        P.op('dve', lambda j=j, h=h, g8=g8: V.tensor_scalar(
                            s5C[h * 64:(h + 1) * 64, 1, j, g8 * 16:(g8 + 1) * 16], cimS[h * 64:(h + 1) * 64, j, :],
                            -1.0, None, ALU.mult), reads=['cimS'], writes=['s5C'])
                shS = [128, 16]
                lamreS = S("lamreS", shS); lamimS = S("lamimS", shS); dtS = S("dtS", shS)
                P.dma('sp', lamreS[:], I['lamre_S'][:, :], writes=['lamreS'])
                P.dma('sp', lamimS[:], I['lamim_S'][:, :], writes=['lamimS'])
                P.dma('sp', dtS[:], I['logdt_S'][:, :], writes=['dtS'])
                P.op('act', lambda: A.activation(out=dtS[:], in_=dtS[:], func=AF.Exp), reads=['dtS'], writes=['dtS'])
                rmag = S("rmag", shS); angS = S("angS", shS)
                P.op('dve', lambda: V.tensor_tensor(rmag[:], lamreS[:], dtS[:], ALU.mult), reads=['lamreS', 'dtS'], writes=['rmag'])
                P.op('act', lambda: A.activation(out=rmag[:], in_=rmag[:], func=AF.Exp), reads=['rmag'], writes=['rmag'])
                P.op('dve', lambda: V.tensor_tensor(angS[:], lamimS[:], dtS[:], ALU.mult), reads=['lamimS', 'dtS'], writes=['angS'])
                cosS, sinS = trig("tS", angS[:], shS, 'angS')
                onesL = S("onesL", [128, LR])
                P.op('pool', lambda: G.memset(onesL[:], 1.0), writes=['onesL'])
                for j in range(16):
                    P.op('dve', lambda j=j: V.tensor_scalar(rtab[:, j, :], onesL[:], rmag[:, j:j + 1], None, ALU.mult),
                         reads=['rmag', 'onesL'], writes=['rtab'])
                P.op('dve', lambda: V.tensor_copy(tabc[:, :, 0:1], cosS[:].unsqueeze(2)), reads=['tSc'], writes=['tabc'])
                P.op('dve', lambda: V.tensor_copy(tabs[:, :, 0:1], sinS[:].unsqueeze(2)), reads=['tSs'], writes=['tabs'])
                d1 = S("dbl1", [128, LR // 2])
                m_ = 1
                while m_ < LR:
                    for j in range(16):
                        cm, sm = tabc[:, j, m_ - 1:m_], tabs[:, j, m_ - 1:m_]
                        lo_c, lo_s = tabc[:, j, 0:m_], tabs[:, j, 0:m_]
                        hi_c, hi_s = tabc[:, j, m_:2 * m_], tabs[:, j, m_:2 * m_]
                        a1 = d1[:, 0:m_]
                        P.op('dve', lambda: V.tensor_scalar(a1, lo_s, sm, None, ALU.mult), reads=['tabs'], writes=['dbl1'])
                        P.op('dve', lambda: V.scalar_tensor_tensor(hi_c, lo_c, cm, a1, ALU.mult, ALU.subtract),
                             reads=['tabc', 'dbl1'], writes=['tabc'])
                        P.op('dve', lambda: V.tensor_scalar(a1, lo_c, sm, None, ALU.mult), reads=['tabc', 'tabs'], writes=['dbl1'])
                        P.op('dve', lambda: V.scalar_tensor_tensor(hi_s, lo_s, cm, a1, ALU.mult, ALU.add),
                             reads=['tabs', 'tabc', 'dbl1'], writes=['tabs'])
                    m_ *= 2
                P.barrier()
            print('SBUF remaining after weights/setup', nc.sbuf_bytes_remaining)
            L_ = {}
            hin = sb("hin", [128, D], F32)
            ss = sb("ss", [128, 4], F32)
            ub = sb("ub", [128, D], BF16)
            uT = sb("uT", [128, 8, NP], BF16)
            gated = sb("gated", [128, 12, NP], BF16)
            ua = sb("ua", [128, 4, NP], BF16)
            sza = sb("sza", [128, 4, NP], BF16)
            tA = sb("tA_", [128, NP], F32)
            tB = sb("tB_", [128, NP], F32)
            wre = sb("wre", [128, 4, NP], F32)
            wim = sb("wim", [128, 4, NP], F32)
            tA2 = sb("tA2_", [128, NP], F32)
            tB2 = sb("tB2_", [128, NP], F32)
            hsre = sb("hsre", [128, 4, NP], BF16)
            hsim = sb("hsim", [128, 4, NP], BF16)
            Hre = sb("Hre", [128, 16], F32)
            Him = sb("Him", [128, 16], F32)
            ch1 = sb("ch1", [128, 4], F32)
            ch2 = sb("ch2", [128, 4], F32)
            ch3 = sb("ch3", [128, 4], F32)
            ch4 = sb("ch4", [128, 4], F32)
            y1 = sb("y1", [128, NP], F32)
            y2 = sb("y2", [128, NP + 1], F32)
            y3 = sb("y3", [128, NP + 1], F32)
            yg = sb("yg", [128, 4, NP], BF16)
            xb = sb("xb", [128, 8, 3 + NP], BF16)
            szb = sb("szb", [128, 8, NP], BF16)
            acc = sb("acc", [128, NP], F32)
            ctmp = sb("ctmp", [128, NP], F32)
            xc = sb("xc", [128, 8, NP], BF16)
            qT = uT
            kT = sb("kT", [128, 8, NP], BF16)
            vtok = sb("vtok", [128, 4, 264], BF16)
            ktok = sb("ktok", [128, 4, 256], BF16)
            Cst = sb("Cst", [128, 4, 2, 264], F32)
            Cbf = sb("Cbf", [128, 4, 2, 264], BF16)
            r_ig = sb("r_ig", [4, NP], F32)
            r_lf = sb("r_lf", [4, NP], F32)
            Mfull = sb("Mfull", [4, NP + 1], F32)
            negM = sb("negM", [4, NP + 1], F32)
            r_wi = sb("r_wi", [4, NP], F32)
            r_g = sb("r_g", [4, NP], F32)
            r_emm = sb("r_emm", [4, NP], F32)
            r_wend = sb("r_wend", [4, NP], F32)
            r_e = r_wend
            mcarry = sb("mcarry", [4, 1], F32)
            dg4 = sb("dg4", [4, 4], F32)
            cols = sb("cols", [128, 20], F32)
            Esb = y1[:, 0:128]
            Dm = y1[:, 128:256]
            PT = sb("PT", [128, 128], BF16)
            kw = sb("kw", [128, 256], BF16)
            inter = y2
            tot = y3
            st6 = sb("st6", [128, 6], F32)
            st2 = sb("st2", [128, 8], F32)
            hn = ub
            ytmp = sb("ytmp", [128, 128], F32)
            xbo = sb("xbo", [128, 8, 3], F32)

            print('SBUF remaining after L0 activations', nc.sbuf_bytes_remaining)
            P.op('pool', lambda: G.memset(vtok[:], 1.0), writes=['vtok'])

            mmreg = [(pb[0], 'pb0'), (pb[1], 'pb1')]
            mmi = [0]

            def nextmm():
                r_ = mmreg[mmi[0] % 2]
                mmi[0] += 1
                return r_

            def inproj(f, n):
                ps, key = nextmm()
                for k in range(8):
                    mm(ps[:, 0:n], w_in[:, k, f * 128:(f + 1) * 128], uT[:, k, 0:n], k == 0, k == 7, ['w_in', 'uT'], [key])
                return ps, key

            for (g, si, pcs) in seqs:
                if g == 'p':
                    P.op('pool', lambda: G.memset(Hre[:], 0.0), writes=['Hre'])
                    P.op('pool', lambda: G.memset(Him[:], 0.0), writes=['Him'])
                    P.op('pool', lambda: G.memset(Cst[:], 0.0), writes=['Cst%d' % h_ for h_ in range(4)])
                    P.op('pool', lambda: G.memset(mcarry[:], 0.0), writes=['mcarry'])
                    P.op('pool', lambda: G.memset(xb[:, :, 0:3], 0.0), writes=['xb'])
                else:
                    P.dma('sp', Hre[:], I['s5re_in'][si, :].rearrange("(j q) -> q j", q=128), writes=['Hre'])
                    P.dma('sp', Him[:], I['s5im_in'][si, :].rearrange("(j q) -> q j", q=128), writes=['Him'])
                    for h in range(4):
                        P.dma('sp', Cst[:, h, :, 0:256], I['mlc_in'][si, h].rearrange("(k p) e -> p k e", p=128), writes=['Cst%d' % h])
                        P.dma('sp', Cst[:, h, :, 256], I['mln_in'][si, h].rearrange("(k p) -> p k", p=128), writes=['Cst%d' % h])
                    P.dma('sp', mcarry[:], I['mlm_in'][si, :].rearrange("(p o) -> p o", o=1), writes=['mcarry'])
                    for t_ in range(3):
                        P.dma('pool', xb[:, :, t_], I['mlconv_in'][si, t_].rearrange("(k p) -> p k", p=128), writes=['xb'])
                P.op('act', lambda: A.copy(Cbf[:], Cst[:]), reads=['Cst%d' % h_ for h_ in range(4)], writes=['Cbf%d' % h_ for h_ in range(4)])

                for (t0, n, src, _, _) in pcs:
                    tp = min(n, 128)
                    ntt = (n + 127) // 128
                    L = tp
                    nch = n // L
                    for tt in range(ntt):
                        P.dma('sp', hin[0:tp, :], src[tt * 128:tt * 128 + tp, :], writes=['hin'])
                        rmsnorm_T(hin, tp, tt, gcol, ub, uT, ss)
                    for f in range(4):
                        ps, key = inproj(f, n)
                        P.op('act', lambda f=f, ps=ps: A.copy(ua[:, f, 0:n], ps[:, 0:n]), reads=[key], writes=['ua'])
                    for f in range(4):
                        ps, key = inproj(4 + f, n)
                        P.op('act', lambda f=f, ps=ps: A.activation(out=sza[:, f, 0:n], in_=ps[:, 0:n], func=AF.Silu),
                             reads=[key], writes=['sza'])

                    def gen_s5():
                        Ls = min(n, LR)
                        ncs = n // Ls
                        def v3(ap2):
                            return ap2.rearrange("p (c l) -> p c l", c=ncs)

                        for kc in range(4):
                            tcs, tss = [], []
                            for jl in range(4):
                                j = 4 * kc + jl
                                sreg = (pb[4], 'pb4') if j % 2 == 0 else (pb[5], 'pb5')
                                pre, pim = sreg[0][:, 0:n], sreg[0][:, 256:256 + n]
                                mm(pre, s5B[:, 0, j, :], ua[:, kc, 0:n], True, True, ['s5B', 'ua'], [sreg[1]])
                                mm(pim, s5B[:, 1, j, :], ua[:, kc, 0:n], True, True, ['s5B', 'ua'], [sreg[1]])
                                tc_ = tabc[:, j, 0:Ls].unsqueeze(1).to_broadcast([128, ncs, Ls])
                                ts_ = tabs[:, j, 0:Ls].unsqueeze(1).to_broadcast([128, ncs, Ls])
                                tcs.append(tc_); tss.append(ts_)
                                P.op('dve', lambda pre=pre, tc_=tc_: V.tensor_tensor(v3(tA[:, 0:n]), v3(pre), tc_, ALU.mult),
                                     reads=[sreg[1], 'tabc'], writes=['tA'])
                                P.op('dve', lambda pim=pim, ts_=ts_: V.tensor_tensor(v3(tB[:, 0:n]), v3(pim), ts_, ALU.mult),
                                     reads=[sreg[1], 'tabs'], writes=['tB'])
                                P.op('pool', lambda jl=jl: G.tensor_tensor(wre[:, jl, 0:n], tA[:, 0:n], tB[:, 0:n], ALU.add),
                                     reads=['tA', 'tB'], writes=['wre%d' % jl])
                                P.op('dve', lambda pim=pim, tc_=tc_: V.tensor_tensor(v3(tA2[:, 0:n]), v3(pim), tc_, ALU.mult),
                                     reads=[sreg[1], 'tabc'], writes=['tA2'])
                                P.op('dve', lambda pre=pre, ts_=ts_: V.tensor_tensor(v3(tB2[:, 0:n]), v3(pre), ts_, ALU.mult),
                                     reads=[sreg[1], 'tabs'], writes=['tB2'])
                                P.op('pool', lambda jl=jl: G.tensor_tensor(wim[:, jl, 0:n], tA2[:, 0:n], tB2[:, 0:n], ALU.subtract),
                                     reads=['tA2', 'tB2'], writes=['wim%d' % jl])
                                if jl % 2 == 1:
                                    yield
                            j0 = 4 * kc
                            for c in range(ncs):
                                cs = slice(c * Ls, (c + 1) * Ls)
                                for jl in range(4):
                                    j = j0 + jl
                                    if c == 0:
                                        ire, iim, rk = Hre[:, j:j + 1], Him[:, j:j + 1], ['Hre', 'Him']
                                    else:
                                        ire, iim, rk = ch1[:, jl:jl + 1], ch2[:, jl:jl + 1], ['ch1', 'ch2']
                                    P.op('dve', lambda j=j, jl=jl, cs=cs, ire=ire: V.tensor_tensor_scan(
                                        wre[:, jl, cs], rtab[:, j, 0:Ls], wre[:, jl, cs], ire, ALU.mult, ALU.add),
                                        reads=['wre%d' % jl, 'rtab'] + rk, writes=['wre%d' % jl])
                                    P.op('dve', lambda j=j, jl=jl, cs=cs, iim=iim: V.tensor_tensor_scan(
                                        wim[:, jl, cs], rtab[:, j, 0:Ls], wim[:, jl, cs], iim, ALU.mult, ALU.add),
                                        reads=['wim%d' % jl, 'rtab'] + rk, writes=['wim%d' % jl])
                                e_ = (c + 1) * Ls - 1
                                last = (c == ncs - 1)
                                wlr, wli = wre[:, :, e_], wim[:, :, e_]
                                cL4, sL4 = tabc[:, j0:j0 + 4, Ls - 1], tabs[:, j0:j0 + 4, Ls - 1]
                                dre = Hre[:, j0:j0 + 4] if last else ch1[:, 0:4]
                                dim_ = Him[:, j0:j0 + 4] if last else ch2[:, 0:4]
                                kre_ = 'Hre' if last else 'ch1'
                                kim_ = 'Him' if last else 'ch2'
                                w4r = ['wre%d' % i_ for i_ in range(4)]
                                w4i = ['wim%d' % i_ for i_ in range(4)]
                                P.op('dve', lambda: V.tensor_tensor(ch3[:, 0:4], wli, sL4, ALU.mult), reads=w4i + ['tabs'], writes=['ch3'])
                                P.op('dve', lambda: V.tensor_tensor(ch4[:, 0:4], wlr, sL4, ALU.mult), reads=w4r + ['tabs'], writes=['ch4'])
                                P.op('dve', lambda: V.tensor_tensor(dre, wlr, cL4, ALU.mult), reads=w4r + ['tabc', kre_], writes=[kre_])
                                P.op('dve', lambda: V.tensor_tensor(dim_, wli, cL4, ALU.mult), reads=w4i + ['tabc', kim_], writes=[kim_])
                                P.op('dve', lambda: V.tensor_tensor(dre, dre, ch3[:, 0:4], ALU.subtract), reads=[kre_, 'ch3'], writes=[kre_])
                                P.op('dve', lambda: V.tensor_tensor(dim_, dim_, ch4[:, 0:4], ALU.add), reads=[kim_, 'ch4'], writes=[kim_])
                                if c % 2 == 1 or last:
                                    yield
                            for jl in range(4):
                                tc_, ts_ = tcs[jl], tss[jl]
                                P.op('dve', lambda jl=jl, tc_=tc_: V.tensor_tensor(v3(tA[:, 0:n]), v3(wre[:, jl, 0:n]), tc_, ALU.mult),
                                     reads=['wre%d' % jl, 'tabc'], writes=['tA'])
                                P.op('dve', lambda jl=jl, ts_=ts_: V.tensor_tensor(v3(tB[:, 0:n]), v3(wim[:, jl, 0:n]), ts_, ALU.mult),
                                     reads=['wim%d' % jl, 'tabs'], writes=['tB'])
                                P.op('pool', lambda jl=jl: G.tensor_tensor(hsre[:, jl, 0:n], tA[:, 0:n], tB[:, 0:n], ALU.subtract),
                                     reads=['tA', 'tB'], writes=['hsre%d' % jl])
                                P.op('dve', lambda jl=jl, tc_=tc_: V.tensor_tensor(v3(tA2[:, 0:n]), v3(wim[:, jl, 0:n]), tc_, ALU.mult),
                                     reads=['wim%d' % jl, 'tabc'], writes=['tA2'])
                                P.op('dve', lambda jl=jl, ts_=ts_: V.tensor_tensor(v3(tB2[:, 0:n]), v3(wre[:, jl, 0:n]), ts_, ALU.mult),
                                     reads=['wre%d' % jl, 'tabs'], writes=['tB2'])
                                P.op('pool', lambda jl=jl: G.tensor_tensor(hsim[:, jl, 0:n], tA2[:, 0:n], tB2[:, 0:n], ALU.add),
                                     reads=['tA2', 'tB2'], writes=['hsim%d' % jl])
                                if jl % 2 == 1:
                                    yield
                            ps, key = nextmm()
                            for jl in range(4):
                                j = 4 * kc + jl
                                mm(ps[:, 0:n], s5C[:, 0, j, :], hsre[:, jl, 0:n], jl == 0, False, ['s5C', 'hsre%d' % jl], [key])
                                mm(ps[:, 0:n], s5C[:, 1, j, :], hsim[:, jl, 0:n], False, jl == 3, ['s5C', 'hsim%d' % jl], [key])
                            P.op('dve', lambda kc=kc, ps=ps: V.scalar_tensor_tensor(y1[:, 0:n], ua[:, kc, 0:n], dcol[:, kc:kc + 1],
                                                                                   ps[:, 0:n], ALU.mult, ALU.add),
                                 reads=['ua', 'dcol', key], writes=['y1'])
                            P.op('pool', lambda: G.tensor_tensor(y2[:, 0:n], y1[:, 0:n], y1[:, 0:n], ALU.mult), reads=['y1'], writes=['y2'])
                            P.op('pool', lambda: G.tensor_scalar(y2[:, 0:n], y2[:, 0:n], 0.044715, 1.0, ALU.mult, ALU.add),
                                 reads=['y2'], writes=['y2'])
                            P.op('pool', lambda: G.tensor_tensor(y2[:, 0:n], y2[:, 0:n], y1[:, 0:n], ALU.mult), reads=['y2', 'y1'], writes=['y2'])
                            P.op('act', lambda: A.activation(out=y3[:, 0:n], in_=y2[:, 0:n], func=AF.Tanh, scale=math.sqrt(2.0 / PI)),
                                 reads=['y2'], writes=['y3'])
                            P.op('dve', lambda: V.tensor_scalar(y3[:, 0:n], y3[:, 0:n], 0.5, 0.5, ALU.mult, ALU.add), reads=['y3'], writes=['y3'])
                            P.op('dve', lambda kc=kc: V.tensor_tensor(yg[:, kc, 0:n], y3[:, 0:n], y1[:, 0:n], ALU.mult),
                                 reads=['y3', 'y1'], writes=['yg%d' % kc])
                            yield
                        for m_ in range(4):
                            ps, key = nextmm()
                            for kk in range(4):
                                mm(ps[:, 0:n], w_glu[:, kk, m_ * 128:(m_ + 1) * 128], yg[:, kk, 0:n], kk == 0, kk == 3,
                                   ['w_glu', 'yg%d' % kk], [key])
                            P.op('act', lambda m_=m_, ps=ps: A.activation(out=y3[:, 0:n], in_=ps[:, 0:n], func=AF.Tanh, scale=0.5,
                                                                          bias=bgluh[:, m_:m_ + 1]),
                                 reads=[key, 'bgluh'], writes=['y3'])
                            P.op('dve', lambda: V.tensor_scalar(y3[:, 0:n], y3[:, 0:n], 0.5, 0.5, ALU.mult, ALU.add), reads=['y3'], writes=['y3'])
                            P.op('dve', lambda m_=m_: V.tensor_tensor(y3[:, 0:n], y3[:, 0:n], yg[:, m_, 0:n], ALU.mult),
                                 reads=['y3', 'yg%d' % m_], writes=['y3'])
                            P.op('dve', lambda m_=m_: V.tensor_tensor(gated[:, m_, 0:n], y3[:, 0:n], sza[:, m_, 0:n], ALU.mult),
                                 reads=['y3', 'sza'], writes=['gated%d' % m_])
                            yield

                    def gen_ml():
                        for f in range(8):
                            ps, key = inproj(8 + f, n)
                            P.op('act', lambda f=f, ps=ps: A.copy(xb[:, f, 3:3 + n], ps[:, 0:n]), reads=[key], writes=['xb'])
                            if f % 2 == 1:
                                yield
                        for f in range(8):
                            ps, key = inproj(16 + f, n)
                            P.op('act', lambda f=f, ps=ps: A.activation(out=szb[:, f, 0:n], in_=ps[:, 0:n], func=AF.Silu),
                                 reads=[key], writes=['szb'])
                            if f % 2 == 1:
                                yield
                        for k in range(8):
                            P.op('pool', lambda k=k: G.tensor_scalar(acc[:, 0:n], xb[:, k, 0:n], convw[:, k, 0:1], convb[:, k:k + 1],
                                                                     ALU.mult, ALU.add),
                                 reads=['xb', 'convw', 'convb'], writes=['acc'])
                            for tap in range(1, 4):
                                P.op('pool', lambda k=k, tap=tap: G.tensor_scalar(ctmp[:, 0:n], xb[:, k, tap:tap + n], convw[:, k, tap:tap + 1], None, ALU.mult),
                                     reads=['xb', 'convw'], writes=['ctmp'])
                                P.op('pool', lambda: G.tensor_tensor(acc[:, 0:n], acc[:, 0:n], ctmp[:, 0:n], ALU.add),
                                     reads=['acc', 'ctmp'], writes=['acc'])
                            P.op('act', lambda k=k: A.activation(out=xc[:, k, 0:n], in_=acc[:, 0:n], func=AF.Silu),
                                 reads=['acc'], writes=['xc'])
                            yield
                        gip, gikey = nextmm()
                        gfp, gfkey = nextmm()
                        gi, gf = gip[0:4, 0:n], gfp[0:4, 0:n]
                        for gsel, o_, gkey_ in ((0, gi, gikey), (4, gf, gfkey)):
                            for k in range(8):
                                mm(o_, wcif[:, k, gsel:gsel + 4], xc[:, k, 0:n], k == 0, False, ['wcif', 'xc'], [gkey_])
                            for k in range(8):
                                mm(o_, wxif[:, k, gsel:gsel + 4], xb[:, k, 3:3 + n], False, k == 7, ['wxif', 'xb'], [gkey_])
                        P.op('act', lambda: A.activation(out=r_ig[:, 0:n], in_=gi, func=AF.Identity, bias=b_i[:, 0:1]),
                             reads=[gikey, 'b_i'], writes=['r_ig'])
                        P.op('act', lambda: A.activation(out=r_e[:, 0:n], in_=gf, func=AF.Exp, scale=-1.0, bias=b_f[:, 0:1]),
                             reads=[gfkey, 'b_f'], writes=['r_wend'])
                        P.op('act', lambda: A.activation(out=r_e[:, 0:n], in_=r_e[:, 0:n], func=AF.Ln, bias=1.0),
                             reads=['r_wend'], writes=['r_wend'])
                        P.op('dve', lambda: V.tensor_tensor_scan(r_lf[:, 0:n], onesrow[:, 0:n], r_e[:, 0:n], 0.0, ALU.mult, ALU.add),
                             reads=['r_wend', 'onesrow'], writes=['r_lf'])
                        P.op('dve', lambda: V.tensor_tensor(r_ig[:, 0:n], r_ig[:, 0:n], r_lf[:, 0:n], ALU.add),
                             reads=['r_ig', 'r_lf'], writes=['r_ig'])
                        P.op('dve', lambda: V.tensor_copy(Mfull[:, 0:1], mcarry[:, 0:1]), reads=['mcarry'], writes=['Mfull'])
                        P.op('dve', lambda: V.tensor_tensor_scan(Mfull[:, 1:n + 1], onesrow[:, 0:n], r_ig[:, 0:n], mcarry[:, 0:1],
                                                                 ALU.mult, ALU.max),
                             reads=['r_ig', 'onesrow', 'mcarry', 'Mfull'], writes=['Mfull'])
                        P.op('dve', lambda: V.tensor_scalar(negM[:, 0:n + 1], Mfull[:, 0:n + 1], -1.0, None, ALU.mult),
                             reads=['Mfull'], writes=['negM'])
                        P.op('dve', lambda: V.tensor_tensor(r_emm[:, 0:n], r_lf[:, 0:n], Mfull[:, 1:n + 1], ALU.subtract),
                             reads=['r_lf', 'Mfull'], writes=['r_emm'])
                        P.op('act', lambda: A.activation(out=r_emm[:, 0:n], in_=r_emm[:, 0:n], func=AF.Exp), reads=['r_emm'], writes=['r_emm'])
                        P.op('dve', lambda: V.tensor_tensor(mcarry[:, 0:1], Mfull[:, n:n + 1], r_lf[:, n - 1:n], ALU.subtract),
                             reads=['Mfull', 'r_lf'], writes=['mcarry'])
                        for c in range(nch):
                            cs = slice(c * L, (c + 1) * L)
                            P.op('act', lambda c=c, cs=cs: A.activation(out=r_wi[:, cs], in_=Mfull[:, 1 + c * L:1 + (c + 1) * L], func=AF.Exp,
                                                                        scale=-1.0, bias=Mfull[:, c * L:c * L + 1]),
                                 reads=['Mfull'], writes=['r_wi'])
                            P.op('act', lambda c=c, cs=cs: A.activation(out=r_g[:, cs], in_=Mfull[:, 1 + c * L:1 + (c + 1) * L], func=AF.Exp,
                                                                        scale=-1.0, bias=Mfull[:, (c + 1) * L:(c + 1) * L + 1]),
                                 reads=['Mfull'], writes=['r_g'])
                            P.op('act', lambda c=c, cs=cs: A.activation(out=r_wend[:, cs], in_=r_ig[:, cs], func=AF.Exp,
                                                                        bias=negM[:, (c + 1) * L:(c + 1) * L + 1]),
                                 reads=['r_ig', 'negM'], writes=['r_wend'])
                        yield
                        for h in range(4):
                            for e in range(2):
                                for (w_, dst, sc, wkey) in ((wq, qT, 1.0, 'ml_wq'), (wk, kT, 1.0 / 16.0, 'ml_wk')):
                                    ps, key = nextmm()
                                    for kd in range(2):
                                        mm(ps[:, 0:n], w_[:, h, kd, e * 128:(e + 1) * 128], xc[:, 2 * h + kd, 0:n], kd == 0, kd == 1,
                                           [wkey, 'xc'], [key])
                                    P.op('act', lambda ps=ps, dst=dst, h=h, e=e, sc=sc: A.mul(dst[:, 2 * h + e, 0:n], ps[:, 0:n], sc),
                                         reads=[key], writes=['uT'] if dst is qT else ['kT'])
                        for c in range(nch):
                            cs = slice(c * L, (c + 1) * L)
                            tt = c
                            ts0 = c * 128
                            for h in range(4):
                                ps, key = nextmm()
                                for kd in range(2):
                                    mm(ps[0:tp, 0:256], xb[:, 2 * h + kd, 3 + ts0:3 + ts0 + tp], wv[:, h, kd, :], kd == 0, kd == 1,
                                       ['xb', 'ml_wv'], [key])
                                P.op('act', lambda ps=ps, h=h: A.copy(vtok[0:tp, h, 0:256], ps[0:tp, 0:256]),
                                     reads=[key], writes=['vtok'])
                                ps, key = nextmm()
                                for kd in range(2):
                                    mm(ps[0:tp, 0:256], xc[:, 2 * h + kd, ts0:ts0 + tp], wk[:, h, kd, :], kd == 0, kd == 1,
                                       ['xc', 'ml_wk'], [key])
                                P.op('act', lambda ps=ps, h=h: A.mul(ktok[0:tp, h, :], ps[0:tp, 0:256], 1.0 / 16.0),
                                     reads=[key], writes=['ktok'])
                            pc = pb[6]
                            for i_, row in enumerate((r_wi, r_emm, r_wend, r_g)):
                                mm(pc[0:L, 384 + 4 * i_:388 + 4 * i_], row[0:4, cs], identf[0:4, 0:4], True, True,
                                   ['r_wi', 'r_emm', 'r_wend', 'r_g', 'identf'], ['pb6c'])
                            P.op('dve', lambda c=c: V.tensor_scalar(dg4[:, :], identf[0:4, 0:4], r_wi[0:4, (c + 1) * L - 1:(c + 1) * L], None, ALU.mult),
                                 reads=['r_wi', 'identf'], writes=['dg4'])
                            mm(pc[:, 400:404], ones4[0:4, :], dg4[0:4, :], True, True, ['ones4', 'dg4'], ['pb6c'])
                            P.op('dve', lambda pc=pc: V.tensor_copy(cols[:, 0:20], pc[:, 384:404]), reads=['pb6c'], writes=['cols'])
                            yield
                            for h in range(4):
                                stp, stkey = nextmm()
                                ST = stp[0:L, 0:L]
                                for ke in range(2):
                                    mm(ST, kT[:, 2 * h + ke, cs], qT[:, 2 * h + ke, cs], ke == 0, ke == 1, ['kT', 'uT'], [stkey])
                                P.op('dve', lambda ST=ST, h=h: V.scalar_tensor_tensor(PT[0:L, 0:L], ST, cols[0:L, 8 + h:9 + h], negmask[0:L, 0:L],
                                                                                     ALU.mult, ALU.mult),
                                     reads=[stkey, 'cols', 'negmask'], writes=['PT'])
                                mm(pb[7][0:L, 0:257], PT[0:L, 0:L], vtok[0:L, h, 0:257], True, True, ['PT', 'vtok'], ['pb7'])
                                for kd in range(2):
                                    mm(pb[3][0:L, 0:257], qT[:, 2 * h + kd, cs], Cbf[:, h, kd, 0:257], kd == 0, kd == 1, ['uT', 'Cbf%d' % h], ['pb3'])
                                P.op('act', lambda h=h: A.activation(out=inter[0:L, 0:257], in_=pb[3][0:L, 0:257], func=AF.Copy,
                                                                     scale=cols[0:L, h:h + 1]),
                                     reads=['pb3', 'cols'], writes=['y2'])
                                P.op('dve', lambda h=h: V.scalar_tensor_tensor(tot[0:L, 0:257], pb[7][0:L, 0:257], cols[0:L, 12 + h:13 + h],
                                                                            inter[0:L, 0:257], ALU.mult, ALU.add),
                                     reads=['pb7', 'y2', 'cols'], writes=['y3'])
                                P.op('dve', lambda: V.scalar_tensor_tensor(st2[0:L, 2:3], tot[0:L, 256:257], -1.0, tot[0:L, 256:257],
                                                                           ALU.mult, ALU.max), reads=['y3'], writes=['st2c'])
                                P.op('dve', lambda h=h: V.tensor_tensor(st2[0:L, 2:3], st2[0:L, 2:3], cols[0:L, 4 + h:5 + h], ALU.max),
                                     reads=['st2c', 'cols'], writes=['st2c'])
                                P.op('dve', lambda: V.reciprocal(st2[0:L, 3:4], st2[0:L, 2:3]), reads=['st2c'], writes=['st2d'])
                                P.op('dve', lambda: V.bn_stats(st6[0:L, :], tot[0:L, 0:256]), reads=['y3'], writes=['st6'])
                                P.op('dve', lambda: V.bn_aggr(st2[0:L, 4:6], st6[0:L, :]), reads=['st6'], writes=['st2e'])
                                P.op('dve', lambda: V.tensor_tensor(st2[0:L, 6:7], st2[0:L, 3:4], st2[0:L, 3:4], ALU.mult),
                                     reads=['st2d'], writes=['st2f'])
                                P.op('dve', lambda: V.tensor_scalar(st2[0:L, 6:7], st2[0:L, 6:7], st2[0:L, 5:6], EPS, ALU.mult, ALU.add),
                                     reads=['st2f', 'st2e'], writes=['st2f'])
                                P.op('pool', lambda: G.tensor_tensor(st2[0:L, 7:8], st2[0:L, 6:7], negh[0:L, :], ALU.pow),
                                     reads=['st2f', 'negh'], writes=['st2g'])
                                P.op('dve', lambda: V.tensor_tensor(st2[0:L, 7:8], st2[0:L, 7:8], st2[0:L, 3:4], ALU.mult),
                                     reads=['st2g', 'st2d'], writes=['st2g'])
                                P.op('dve', lambda tt=tt, h=h: V.tensor_scalar(hn[0:L, h * 256:(h + 1) * 256], tot[0:L, 0:256],
                                                                              st2[0:L, 4:5], st2[0:L, 7:8], ALU.subtract, ALU.mult),
                                     reads=['y3', 'st2e', 'st2g'], writes=['ub'])
                                P.op('act', lambda tt=tt, h=h: A.activation(out=kw[0:L, :], in_=ktok[0:L, h, :], func=AF.Copy, scale=cols[0:L, 8 + h:9 + h]),
                                     reads=['ktok', 'cols'], writes=['kw'])
                                for kd in range(2):
                                    cup = (pb[4], 'pb4') if kd == 0 else (pb[5], 'pb5')
                                    mm(cup[0][:, 0:257], kw[0:L, kd * 128:(kd + 1) * 128], vtok[0:L, h, 0:257], True, True, ['kw', 'vtok'], [cup[1]])
                                    P.op('dve', lambda h=h, kd=kd, cup=cup: V.scalar_tensor_tensor(
                                        Cst[:, h, kd, 0:257], Cst[:, h, kd, 0:257], cols[:, 16 + h:17 + h], cup[0][:, 0:257], ALU.mult, ALU.add),
                                        reads=['Cst%d' % h, 'cols', cup[1]], writes=['Cst%d' % h])
                                P.op('act', lambda h=h: A.copy(Cbf[:, h, :, :], Cst[:, h, :, :]), reads=['Cst%d' % h], writes=['Cbf%d' % h])
                                yield
                            pT = pb[2][:].bitcast(BF16)
                            for k in range(8):
                                P.op('pe', lambda k=k: PL.transpose(pT[:, k * 128:k * 128 + tp], hn[0:tp, k * 128:(k + 1) * 128],
                                                                    identb[0:tp, 0:tp]),
                                     reads=['ub', 'identb'], writes=['pb2'])
                            for k in range(8):
                                P.op('dve', lambda k=k: V.tensor_scalar(ytmp[:, 0:tp], pT[:, k * 128:k * 128 + tp], mlnw[:, k:k + 1], None, ALU.mult),
                                     reads=['pb2', 'mlnw'], writes=['ytmp'])
                                P.op('dve', lambda k=k: V.scalar_tensor_tensor(ytmp[:, 0:tp], xc[:, k, ts0:ts0 + tp], skipc[:, k:k + 1],
                                                                               ytmp[:, 0:tp], ALU.mult, ALU.add),
                                     reads=['xc', 'skipc', 'ytmp'], writes=['ytmp'])
                                P.op('dve', lambda k=k: V.tensor_tensor(gated[:, 4 + k, ts0:ts0 + tp], ytmp[:, 0:tp], szb[:, k, ts0:ts0 + tp], ALU.mult),
                                     reads=['ytmp', 'szb'], writes=['gatedB'])
                            yield
                        P.op('dve', lambda n=n: V.tensor_copy(xbo[:, :, :], xb[:, :, n:n + 3]), reads=['xb'], writes=['xbo'])
                        P.op('dve', lambda n=n: V.tensor_copy(xb[:, :, 0:3], xbo[:, :, :]), reads=['xbo'], writes=['xb'])
                    gens = [gen_s5(), gen_ml()]
                    while gens:
                        for g_ in list(gens):
                            try:
                                next(g_)
                            except StopIteration:
                                gens.remove(g_)
                    for tt in range(ntt):
                        ts0 = tt * 128
                        P.dma('sp', hin[0:tp, :], src[ts0:ts0 + tp, :], writes=['hin'])
                        for nh in range(2):
                            po = ((pb[7], 'pb7'), (pb[3], 'pb3'), (pb[4], 'pb4'), (pb[5], 'pb5'))[2 * (tt % 2) + nh]
                            for cch in range(12):
                                gk = ('gated%d' % cch) if cch < 4 else 'gatedB'
                                mm(po[0][0:tp, :], gated[:, cch, ts0:ts0 + tp], w_out[:, cch, nh * 512:(nh + 1) * 512], cch == 0, cch == 11,
                                   [gk, 'w_out'], [po[1]])
                            P.op('dve', lambda tt=tt, nh=nh, po=po: V.tensor_tensor(hin[0:tp, nh * 512:(nh + 1) * 512],
                                                                                   hin[0:tp, nh * 512:(nh + 1) * 512], po[0][0:tp, :], ALU.add),
                                 reads=['hin', po[1]], writes=['hin'])
                        P.dma('sp', h1rows(g, si, t0 + ts0, tp), hin[0:tp, :], reads=['hin'], writes=['h1'])
                P.enabled = True
                sfx = '_' + g
                P.dma('sp', O['o_s5re' + sfx][si, :].rearrange("(j q) -> q j", q=128), Hre[:], reads=['Hre'], writes=['o1'])
                P.dma('sp', O['o_s5im' + sfx][si, :].rearrange("(j q) -> q j", q=128), Him[:], reads=['Him'], writes=['o2'])
                for h in range(4):
                    P.dma('sp', O['o_mlc' + sfx][si, h].rearrange("(k p) e -> p k e", p=128), Cst[:, h, :, 0:256],
                          reads=['Cst%d' % h], writes=['o3'])
                    P.dma('sp', O['o_mln' + sfx][si, h].rearrange("(k p) -> p k", p=128), Cst[:, h, :, 256],
                          reads=['Cst%d' % h], writes=['o4'])
                P.dma('sp', O['o_mlm' + sfx][si, :].rearrange("(p o) -> p o", o=1), mcarry[:], reads=['mcarry'], writes=['o5'])
                for t_ in range(3):
                    P.dma('sp', O['o_mlconv' + sfx][si, t_].rearrange("(k p) -> p k", p=128), xbo[:, :, t_], reads=['xbo'], writes=['o6'])
            P.barrier()


        if cfg.layers >= 2:
          with ExitStack() as l1:
            sb = lambda n, s, d: l1.enter_context(nc.sbuf_tensor(n, s, d))
            NP = NPIECE
            w_in = sb("w_in1", [128, 8, 5120], BF16)
            w_sw = sb("w_sw1", [128, 8, 1024], BF16)
            w_out = sb("w_out1", [128, 16, D], BF16)
            lwa = sb("lwa", [128, 8, 128], BF16)
            lwx = sb("lwx", [128, 8, 128], BF16)
            for k in range(8):
                P.dma('pool', w_in[:, k, :], I['od_w_in'][k * 128:(k + 1) * 128, :], writes=['w_in'])
                P.dma('pool', w_sw[:, k, :], I['od_w_qksw'][k * 128:(k + 1) * 128, :], writes=['w_sw'])
            for k in range(16):
                P.dma('pool', w_out[:, k, :], I['od_w_out'][k * 128:(k + 1) * 128, :], writes=['w_out'])
            P.dma('pool', lwa[:], I['lru_w_a'].rearrange("h d e -> d h e"), writes=['lwa'])
            P.dma('pool', lwx[:], I['lru_w_x'].rearrange("h d e -> d h e"), writes=['lwx'])
            gcol = sb("gcol1", [128, 8], F32)
            rnw = sb("rnw", [128, 8], F32)
            cw = sb("cw1", [128, 8, 4], F32)
            cb = sb("cb1", [128, 8], F32)
            bah = sb("bah", [128, 8], F32)
            bxh = sb("bxh", [128, 8], F32)
            ccol = sb("ccol", [128, 8], F32)
            ccol2 = sb("ccol2", [128, 8], F32)
            fnw = sb("fnw", [128, D], F32)
            rmaskL = sb("rmaskL", [128, 4, 128], F32)
            rmaskS = sb("rmaskS", [128, 4, 128], F32)
            rxiL = sb("rxiL", [128, 4, 128], F32)
            rxiS = sb("rxiS", [128, 4, 16], F32)
            rzL = sb("rzL", [128, 4], F32)
            rzS = sb("rzS", [128, 4], F32)
            rglL = sb("rglL", [128, 4], F32)
            rglS = sb("rglS", [128, 4], F32)
            col = lambda nm: I[nm].rearrange("(k p) -> p k", p=128)
            P.dma('sp', gcol[:], I['norm_w'][1, :].rearrange("(k p) -> p k", p=128), writes=['gcol1'])
            P.dma('sp', rnw[:], col('ret_norm_w'), writes=['rnw'])
            for tap in range(4):
                P.dma('sp', cw[:, :, tap], I['lru_conv_w'][tap, :].rearrange("(k p) -> p k", p=128), writes=['cw1'])
            P.dma('sp', cb[:], col('lru_conv_b'), writes=['cb1'])
            P.dma('sp', bah[:], col('lru_b_a'), writes=['bah'])
            P.dma('sp', bxh[:], col('lru_b_x'), writes=['bxh'])
            P.dma('sp', ccol[:], col('lru_lambda'), writes=['ccol'])
            P.dma('sp', fnw[:], I['fnw_b'][:, :], writes=['fnw'])
            P.dma('sp', rmaskL[:], I['c_rmask128'][:, :, :], writes=['rmaskL'])
            P.dma('sp', rmaskS[:], I['c_rmask16'][:, :, :], writes=['rmaskS'])
            P.dma('sp', rxiL[:], I['c_rxi128'][:, :, :], writes=['rxiL'])
            P.dma('sp', rxiS[:], I['c_rxi16'][:, :, :], writes=['rxiS'])
            P.dma('sp', rzL[:], I['c_rzeta128'][:, :], writes=['rzL'])
            P.dma('sp', rzS[:], I['c_rzeta16'][:, :], writes=['rzS'])
            P.dma('sp', rglL[:], I['c_rgl128'][:, :], writes=['rglL'])
            P.dma('sp', rglS[:], I['c_rgl16'][:, :], writes=['rglS'])
            P.op('dve', lambda: V.tensor_scalar(bah[:], bah[:], 0.5, None, ALU.mult), reads=['bah'], writes=['bah'])
            P.op('dve', lambda: V.tensor_scalar(bxh[:], bxh[:], 0.5, None, ALU.mult), reads=['bxh'], writes=['bxh'])
            P.op('act', lambda: A.activation(out=ccol[:], in_=ccol[:], func=AF.Exp, scale=-1.0), reads=['ccol'], writes=['ccol'])
            P.op('act', lambda: A.activation(out=ccol[:], in_=ccol[:], func=AF.Ln, bias=1.0), reads=['ccol'], writes=['ccol'])
            P.op('dve', lambda: V.tensor_scalar(ccol2[:], ccol[:], -4.0, None, ALU.mult), reads=['ccol'], writes=['ccol2'])
            P.op('dve', lambda: V.tensor_scalar(ccol[:], ccol[:], -8.0, None, ALU.mult), reads=['ccol'], writes=['ccol'])

            hin = sb("hin1", [128, D], F32)
            ss = sb("ss1", [128, 4], F32)
            ub = sb("ub1", [128, D], BF16)
            hn = ub
            uT = sb("uT1", [128, 8, NP], BF16)
            gated = sb("gated1", [128, 16, NP], BF16)
            cosF = sb("cosF", [128, NP], F32)
            sinF = sb("sinF", [128, NP], F32)
            qT = sb("qT1", [128, 4, NP], BF16)
            kT = sb("kT1", [128, 4, NP], BF16)
            qxi = sb("qxi", [128, 128], BF16)
            kz = sb("kz", [128, 128], BF16)
            vtok = sb("vtok1", [128, 4, 256], BF16)
            szc = sb("szc", [128, 8, NP], BF16)
            xd = sb("xd", [128, 8, 3 + NP], BF16)
            szd = sb("szd", [128, 8, NP], BF16)
            Sst = sb("Sst", [128, 4, 256], F32)
            Sbf = sb("Sbf", [128, 4, 256], BF16)
            hst = sb("hst", [128, 8], F32)
            f1 = sb("f1", [128, NP], F32)
            f2 = sb("f2", [128, NP], F32)
            f3 = sb("f3", [128, NP], F32)
            f4 = sb("f4", [128, NP], F32)
            f5 = sb("f5", [128, NP], F32)
            xcb = sb("xcb", [128, NP], BF16)
            f1b = sb("f1b", [128, NP], F32)
            f2b = sb("f2b", [128, NP], F32)
            f3b = sb("f3b", [128, NP], F32)
            f4b = sb("f4b", [128, NP], F32)
            f5b = sb("f5b", [128, NP], F32)
            xcbb = sb("xcbb", [128, NP], BF16)
            ob = sb("ob", [128, 256], F32)
            halfT = sb("halfT", [128, NP], F32)
            P.op('pool', lambda: G.memset(halfT[:], 0.5), writes=['halfT'])
            PT = sb("PT1", [128, 128], BF16)
            st6 = sb("st6b", [128, 6], F32)
            st2 = sb("st2b", [128, 8], F32)
            xdo = sb("xdo", [128, 8, 3], F32)
            yout = sb("yout", [128, D], F32)

            print('SBUF remaining after L1 activations', nc.sbuf_bytes_remaining)
            mmreg = [(pb[0], 'pb0'), (pb[1], 'pb1'), (pb[4], 'pb4'), (pb[5], 'pb5')]
            mmi = [0]

            def nextmm():
                r_ = mmreg[mmi[0] % 4]
                mmi[0] += 1
                return r_

            def inproj(f, n, wt=None, wkey='w_in'):
                ps, key = nextmm()
                wt = w_in if wt is None else wt
                for k in range(8):
                    mm(ps[:, 0:n], wt[:, k, f * 128:(f + 1) * 128], uT[:, k, 0:n], k == 0, k == 7, [wkey, 'uT'], [key])
                return ps, key

            for (g, si, pcs) in seqs:
                if g == 'p':
                    P.op('pool', lambda: G.memset(Sst[:], 0.0), writes=['Sst%d' % h_ for h_ in range(4)])
                    P.op('pool', lambda: G.memset(hst[:], 0.0), writes=['hst'])
                    P.op('pool', lambda: G.memset(xd[:, :, 0:3], 0.0), writes=['xd'])
                else:
                    for h in range(4):
                        P.dma('sp', Sst[:, h, :], I['ret_in'][si, h], writes=['Sst%d' % h])
                    P.dma('sp', hst[:], I['lruh_in'][si, :].rearrange("(k p) -> p k", p=128), writes=['hst'])
                    for t_ in range(3):
                        P.dma('pool', xd[:, :, t_], I['lruconv_in'][si, t_].rearrange("(k p) -> p k", p=128), writes=['xd'])
                P.op('act', lambda: A.copy(Sbf[:], Sst[:]), reads=['Sst%d' % h_ for h_ in range(4)], writes=['Sbf%d' % h_ for h_ in range(4)])
                for (t0, n, src, dst, _) in pcs:
                    tp = min(n, 128)
                    ntt = (n + 127) // 128
                    L = tp
                    nch = n // L
                    rmask, rxi, rz, rgl = (rmaskL, rxiL, rzL, rglL) if L == 128 else (rmaskS, rxiS, rzS, rglS)
                    rp0 = t0 if g == 'p' else T
                    P.dma('sp', cosF[:, 0:n], I['ropeF_c'][:, rp0:rp0 + n], writes=['cosF'])
                    P.dma('sp', sinF[:, 0:n], I['ropeF_s'][:, rp0:rp0 + n], writes=['sinF'])
                    for tt in range(ntt):
                        P.dma('sp', hin[0:tp, :], h1rows(g, si, t0 + tt * 128, tp), reads=['h1'], writes=['hin1'])
                        rmsnorm_T(hin, tp, tt, gcol, ub, uT, ss, hkey='hin1', gkey='gcol1')
                    for f in range(8):
                        ps, key = inproj(24 + f, n)
                        P.op('act', lambda f=f, ps=ps: A.copy(xd[:, f, 3:3 + n], ps[:, 0:n]), reads=[key], writes=['xd'])
                    for f in range(8):
                        ps, key = inproj(32 + f, n)
                        P.op('act', lambda f=f, ps=ps: A.activation(out=szd[:, f, 0:n], in_=ps[:, 0:n], func=AF.Silu), reads=[key], writes=['szd'])
                    F1, F2, F3, F4, F5, XCB = f1, f2, f3, f4, f5, xcb
                    def gen_ret():
                        for (base, dstT, sc, dkey) in ((0, qT, 1.0, 'qT1'), (4, kT, 1.0 / math.sqrt(128.0), 'kT1')):
                            for h in range(4):
                                psA, keyA = inproj(base + h, n)
                                psB, keyB = inproj(base + h, n, wt=w_sw, wkey='w_sw')
                                P.op('dve', lambda psA=psA: V.tensor_tensor(yout[:, 0:n], psA[:, 0:n], cosF[:, 0:n], ALU.mult),
                                     reads=[keyA, 'cosF'], writes=['yout'])
                                P.op('dve', lambda psB=psB: V.tensor_tensor(yout[:, 256:256 + n], psB[:, 0:n], sinF[:, 0:n], ALU.mult),
                                     reads=[keyB, 'sinF'], writes=['yout'])
                                P.op('dve', lambda: V.tensor_tensor(yout[:, 0:n], yout[:, 0:n], yout[:, 256:256 + n], ALU.add), reads=['yout', 'yout'], writes=['yout'])
                                P.op('act', lambda h=h, dstT=dstT, sc=sc: A.mul(dstT[:, h, 0:n], yout[:, 0:n], sc), reads=['yout'], writes=[dkey])
                                yield
                        for f in range(8):
                            ps, key = inproj(16 + f, n)
                            P.op('act', lambda f=f, ps=ps: A.activation(out=szc[:, f, 0:n], in_=ps[:, 0:n], func=AF.Silu), reads=[key], writes=['szc'])
                            if f % 2 == 1:
                                yield
                        for c in range(nch):
                            cs = slice(c * L, (c + 1) * L)
                            ts0 = c * 128
                            for h in range(4):
                                ps, key = nextmm()
                                for k in range(8):
                                    mm(ps[0:tp, 0:256], uT[:, k, ts0:ts0 + tp], w_in[:, k, 1024 + h * 256:1024 + (h + 1) * 256], k == 0, k == 7,
                                       ['uT', 'w_in'], [key])
                                P.op('act', lambda ps=ps, h=h: A.copy(vtok[0:tp, h, :], ps[0:tp, 0:256]), reads=[key], writes=['vtok1'])
                            yield
                            for h in range(4):
                                stp, stkey = nextmm()
                                ST = stp[0:L, 0:L]
                                mm(ST, kT[:, h, cs], qT[:, h, cs], True, True, ['kT1', 'qT1'], [stkey])
                                P.op('dve', lambda ST=ST, h=h: V.tensor_tensor(PT[0:L, 0:L], ST, rmask[0:L, h, 0:L], ALU.mult),
                                     reads=[stkey, 'rmaskL', 'rmaskS'], writes=['PT1'])
                                P.op('dve', lambda h=h: V.tensor_tensor(qxi[:, 0:L], qT[:, h, cs], rxi[:, h, 0:L], ALU.mult),
                                     reads=['qT1', 'rxiL', 'rxiS'], writes=['qxi'])
                                mm(pb[7][0:L, 0:256], PT[0:L, 0:L], vtok[0:L, h, :], True, False, ['PT1', 'vtok1'], ['pb7'])
                                mm(pb[7][0:L, 0:256], qxi[:, 0:L], Sbf[:, h, :], False, True, ['qxi', 'Sbf%d' % h], ['pb7'])
                                P.op('act', lambda: A.copy(ob[0:L, 0:256], pb[7][0:L, 0:256]), reads=['pb7'], writes=['ob'])
                                P.op('dve', lambda: V.bn_stats(st6[0:L, :], ob[0:L, 0:256]), reads=['ob'], writes=['st6b'])
                                P.op('dve', lambda: V.bn_aggr(st2[0:L, 0:2], st6[0:L, :]), reads=['st6b'], writes=['st2x'])
                                P.op('dve', lambda: V.tensor_scalar(st2[0:L, 2:3], st2[0:L, 1:2], EPS, None, ALU.add), reads=['st2x'], writes=['st2y'])
                                P.op('pool', lambda: G.tensor_tensor(st2[0:L, 3:4], st2[0:L, 2:3], negh[0:L, :], ALU.pow),
                                     reads=['st2y', 'negh'], writes=['st2z'])
                                P.op('dve', lambda h=h: V.tensor_scalar(hn[0:L, h * 256:(h + 1) * 256], ob[0:L, 0:256], st2[0:L, 0:1], st2[0:L, 3:4],
                                                                        ALU.subtract, ALU.mult),
                                     reads=['ob', 'st2x', 'st2z'], writes=['ub'])
                                pT = pb[2][:].bitcast(BF16)
                                P.op('pe', lambda h=h: PL.transpose(pT[0:L, 0:128], kT[:, h, cs], identb[:, :]),
                                     reads=['kT1', 'identb'], writes=['pb2'])
                                P.op('dve', lambda h=h: V.tensor_scalar(kz[0:L, :], pT[0:L, 0:128], rz[0:L, h:h + 1], None, ALU.mult),
                                     reads=['pb2', 'rzL', 'rzS'], writes=['kz'])
                                mm(pb[3][:, 0:256], kz[0:L, :], vtok[0:L, h, :], True, True, ['kz', 'vtok1'], ['pb3'])
                                P.op('dve', lambda h=h: V.scalar_tensor_tensor(Sst[:, h, :], Sst[:, h, :], rgl[:, h:h + 1], pb[3][:, 0:256], ALU.mult, ALU.add),
                                     reads=['Sst%d' % h, 'rglL', 'rglS', 'pb3'], writes=['Sst%d' % h])
                                P.op('act', lambda h=h: A.copy(Sbf[:, h, :], Sst[:, h, :]), reads=['Sst%d' % h], writes=['Sbf%d' % h])
                                yield
                            pT = pb[2][:].bitcast(BF16)
                            for k in range(8):
                                P.op('pe', lambda k=k: PL.transpose(pT[:, k * 128:k * 128 + tp], hn[0:tp, k * 128:(k + 1) * 128], identb[0:tp, 0:tp]),
                                     reads=['ub', 'identb'], writes=['pb2'])
                            for k in range(8):
                                P.op('dve', lambda k=k: V.scalar_tensor_tensor(gated[:, k, ts0:ts0 + tp], pT[:, k * 128:k * 128 + tp], rnw[:, k:k + 1],
                                                                               szc[:, k, ts0:ts0 + tp], ALU.mult, ALU.mult),
                                     reads=['pb2', 'rnw', 'szc'], writes=['gatedA'])
                            yield
                    def gen_lru():
                        def bufs(k):
                            return ((F1, F2, F3, F4, F5, XCB), '') if k % 2 == 0 else ((f1b, f2b, f3b, f4b, f5b, xcbb), 'b')

                        def ph1(k):
                            (f1, f2, f3, f4, f5, xcb), sfx_ = bufs(k)
                            P.op('dve', lambda: V.tensor_scalar(f1[:, 0:n], xd[:, k, 0:n], cw[:, k, 0:1], cb[:, k:k + 1], ALU.mult, ALU.add),
                                 reads=['xd', 'cw1', 'cb1'], writes=['f1' + sfx_])
                            for tap in range(1, 4):
                                P.op('dve', lambda tap=tap: V.scalar_tensor_tensor(f1[:, 0:n], xd[:, k, tap:tap + n], cw[:, k, tap:tap + 1], f1[:, 0:n],
                                                                                   ALU.mult, ALU.add),
                                     reads=['xd', 'cw1', 'f1' + sfx_], writes=['f1' + sfx_])
                            P.op('act', lambda: A.copy(xcb[:, 0:n], f1[:, 0:n]), reads=['f1' + sfx_], writes=['xcb' + sfx_])
                            psr, keyr = nextmm()
                            mm(psr[:, 0:n], lwa[:, k, :], xcb[:, 0:n], True, True, ['lwa', 'xcb' + sfx_], [keyr])
                            psi, keyi = nextmm()
                            mm(psi[:, 0:n], lwx[:, k, :], xcb[:, 0:n], True, True, ['lwx', 'xcb' + sfx_], [keyi])
                            P.op('act', lambda: A.activation(out=f2[:, 0:n], in_=psr[:, 0:n], func=AF.Tanh, scale=0.5, bias=bah[:, k:k + 1]),
                                 reads=[keyr, 'bah'], writes=['f2' + sfx_])
                            P.op('act', lambda: A.activation(out=f3[:, 0:n], in_=psi[:, 0:n], func=AF.Tanh, scale=0.5, bias=bxh[:, k:k + 1]),
                                 reads=[keyi, 'bxh'], writes=['f3' + sfx_])
                            P.op('act', lambda: A.activation(out=f4[:, 0:n], in_=f2[:, 0:n], func=AF.Exp, scale=ccol2[:, k:k + 1], bias=ccol2[:, k:k + 1]),
                                 reads=['f2' + sfx_, 'ccol2'], writes=['f4' + sfx_])
                            P.op('pool', lambda: G.tensor_scalar(f3[:, 0:n], f3[:, 0:n], 0.5, 0.5, ALU.mult, ALU.add), reads=['f3' + sfx_], writes=['f3' + sfx_])
                            P.op('pool', lambda: G.tensor_tensor(f3[:, 0:n], f3[:, 0:n], f1[:, 0:n], ALU.mult), reads=['f3' + sfx_, 'f1' + sfx_], writes=['f3' + sfx_])
                            P.op('pool', lambda: G.tensor_tensor(f5[:, 0:n], f4[:, 0:n], f4[:, 0:n], ALU.mult), reads=['f4' + sfx_], writes=['f5' + sfx_])
                            P.op('pool', lambda: G.tensor_scalar(f5[:, 0:n], f5[:, 0:n], -1.0, 1.0, ALU.mult, ALU.add), reads=['f5' + sfx_], writes=['f5' + sfx_])

                        def ph2(k):
                            (f1, f2, f3, f4, f5, xcb), sfx_ = bufs(k)
                            P.op('act', lambda: A.activation(out=f5[:, 0:n], in_=f5[:, 0:n], func=AF.Ln), reads=['f5' + sfx_], writes=['f5' + sfx_])
                            P.op('act', lambda: A.activation(out=f5[:, 0:n], in_=f5[:, 0:n], func=AF.Exp, scale=0.5), reads=['f5' + sfx_], writes=['f5' + sfx_])

                        def ph3(k):
                            (f1, f2, f3, f4, f5, xcb), sfx_ = bufs(k)
                            P.op('dve', lambda: V.tensor_tensor(f3[:, 0:n], f3[:, 0:n], f5[:, 0:n], ALU.mult), reads=['f3' + sfx_, 'f5' + sfx_], writes=['f3' + sfx_])
                            P.op('dve', lambda: V.tensor_tensor_scan(f2[:, 0:n], f4[:, 0:n], f3[:, 0:n], hst[:, k:k + 1], ALU.mult, ALU.add),
                                 reads=['f4' + sfx_, 'f3' + sfx_, 'hst', 'f2' + sfx_], writes=['f2' + sfx_])
                            P.op('dve', lambda: V.tensor_copy(hst[:, k:k + 1], f2[:, n - 1:n]), reads=['f2' + sfx_], writes=['hst'])
                            P.op('dve', lambda: V.tensor_tensor(gated[:, 8 + k, 0:n], f2[:, 0:n], szd[:, k, 0:n], ALU.mult),
                                 reads=['f2' + sfx_, 'szd'], writes=['gatedB%d' % k])

                        for p_ in range(4):
                            ph1(2 * p_)
                            yield
                            ph1(2 * p_ + 1)
                            yield
                            ph2(2 * p_)
                            ph2(2 * p_ + 1)
                            yield
                            ph3(2 * p_)
                            ph3(2 * p_ + 1)
                            yield
                    gens = [gen_ret(), gen_lru()]
                    while gens:
                        for g_ in list(gens):
                            try:
                                next(g_)
                            except StopIteration:
                                gens.remove(g_)
                    P.op('dve', lambda n=n: V.tensor_copy(xdo[:, :, :], xd[:, :, n:n + 3]), reads=['xd'], writes=['xdo'])
                    P.op('dve', lambda: V.tensor_copy(xd[:, :, 0:3], xdo[:, :, :]), reads=['xdo'], writes=['xd'])
                    for tt in range(ntt):
                        ts0 = tt * 128
                        P.dma('sp', hin[0:tp, :], h1rows(g, si, t0 + ts0, tp), reads=['h1'], writes=['hin1'])
                        for nh in range(2):
                            po = ((pb[7], 'pb7'), (pb[3], 'pb3'), (pb[6], 'pb6'), (pb[2], 'pb2'))[2 * (tt % 2) + nh]
                            for cch in range(16):
                                gk = 'gatedA' if cch < 8 else 'gatedB%d' % (cch - 8)
                                mm(po[0][0:tp, :], gated[:, cch, ts0:ts0 + tp], w_out[:, cch, nh * 512:(nh + 1) * 512], cch == 0, cch == 15,
                                   [gk, 'w_out'], [po[1]])
                            P.op('dve', lambda nh=nh, po=po: V.tensor_tensor(hin[0:tp, nh * 512:(nh + 1) * 512],
                                                                             hin[0:tp, nh * 512:(nh + 1) * 512], po[0][0:tp, :], ALU.add),
                                 reads=['hin1', po[1]], writes=['hin1'])
                        if not (g == 'p' and t0 == 0):
                            P.op('pool', lambda: G.memset(ss[0:tp, 0:1], 0.0), writes=['ss'])
                            P.op('act', lambda: A.activation(out=ub[0:tp, :], in_=hin[0:tp, :], func=AF.Square, accum_out=ss[0:tp, 0:1]),
                                 reads=['hin1'], writes=['ub', 'ss'])
                            P.op('dve', lambda: V.tensor_scalar(ss[0:tp, 1:2], ss[0:tp, 0:1], 1.0 / D, EPS, ALU.mult, ALU.add), reads=['ss'], writes=['ss'])
                            P.op('pool', lambda: G.tensor_tensor(ss[0:tp, 2:3], ss[0:tp, 1:2], negh[0:tp, :], ALU.pow), reads=['ss', 'negh'], writes=['ss'])
                            P.op('dve', lambda: V.scalar_tensor_tensor(yout[0:tp, :], hin[0:tp, :], ss[0:tp, 2:3], fnw[0:tp, :], ALU.mult, ALU.mult),
                                 reads=['hin1', 'ss', 'fnw'], writes=['yout'])
                            P.dma('sp', dst[ts0:ts0 + tp, :], yout[0:tp, :], reads=['yout'], writes=['yo'])
                sfx = '_' + g
                for h in range(4):
                    P.dma('sp', O['o_ret' + sfx][si, h], Sst[:, h, :], reads=['Sst%d' % h], writes=['o7'])
                P.dma('sp', O['o_lruh' + sfx][si, :].rearrange("(k p) -> p k", p=128), hst[:], reads=['hst'], writes=['o8'])
                for t_ in range(3):
                    P.dma('sp', O['o_lruconv' + sfx][si, t_].rearrange("(k p) -> p k", p=128), xdo[:, :, t_], reads=['xdo'], writes=['o9'])
            P.barrier()
        P.barrier()
    return nc


def host_layout(inp):
    f = lambda a: np.ascontiguousarray(np.asarray(a, dtype=np.float32))
    d = {}
    d['norm_w'] = f(inp['norm_w'])
    d['final_norm_w'] = f(inp['final_norm_w'])
    d['ev_w_in'] = f(inp['ev_w_in'][0])
    d['ev_w_out'] = f(inp['ev_w_out'][0])
    lre, lim, ldt = inp['s5_lambda_re'][0], inp['s5_lambda_im'][0], inp['s5_log_dt'][0]
    def A_from_gp(x):
        x4 = np.asarray(x).reshape(4, 8, 64)
        o = np.broadcast_to(x4[:, :, None, :], (4, 8, 16, 64))
        return f(np.transpose(o, (1, 2, 0, 3)).reshape(128, 4, 64))
    d['lamre_A'] = A_from_gp(lre)
    d['lamim_A'] = A_from_gp(lim)
    d['logdt_A'] = A_from_gp(np.broadcast_to(np.asarray(ldt)[:, None], (32, 64)))
    def A_from_gpc(x):
        x5 = np.asarray(x).reshape(4, 8, 64, 16)
        return f(np.transpose(x5, (1, 3, 0, 2)).reshape(128, 4, 64))
    d['bre_A'] = A_from_gpc(inp['s5_b_re'][0])
    d['bim_A'] = A_from_gpc(inp['s5_b_im'][0])
    def S_from_gp(x):
        x3 = np.asarray(x).reshape(16, 2, 64)
        return f(np.transpose(x3, (1, 2, 0)).reshape(128, 16))
    d['lamre_S'] = S_from_gp(lre)
    d['lamim_S'] = S_from_gp(lim)
    d['logdt_S'] = S_from_gp(np.broadcast_to(np.asarray(ldt)[:, None], (32, 64)))
    def S_from_gcp(x):
        x4 = np.asarray(x).reshape(16, 2, 16, 64)
        return f(np.transpose(x4, (1, 3, 0, 2)).reshape(128, 16, 16))
    d['cre_S'] = S_from_gcp(inp['s5_c_re'][0])
    d['cim_S'] = S_from_gcp(inp['s5_c_im'][0])
    d['s5_d'] = f(inp['s5_d'][0]); d['s5_w_glu'] = f(inp['s5_w_glu'][0]); d['s5_b_glu'] = f(inp['s5_b_glu'][0])
    d['ml_conv_w'] = f(inp['ml_conv_w'][0]); d['ml_conv_b'] = f(inp['ml_conv_b'][0])
    for k in ('ml_wq', 'ml_wk', 'ml_wv'):
        d[k] = f(inp[k][0])
        d[k + 'T'] = f(np.transpose(np.asarray(inp[k][0]), (0, 2, 1)))
    d['ml_w_if'] = f(inp['ml_w_if'][0]); d['ml_b_if'] = f(inp['ml_b_if'][0])
    d['ml_norm_w'] = f(inp['ml_norm_w'][0]); d['ml_skip'] = f(inp['ml_skip'][0])
    w1 = np.asarray(inp['od_w_in'][0])
    d['od_w_in'] = f(w1)
    qk = w1[:, 0:1024].reshape(D, 8, 2, 64)
    d['od_w_qksw'] = f(qk[:, :, ::-1, :].reshape(D, 1024))
    d['fnw_b'] = f(np.broadcast_to(np.asarray(inp['final_norm_w'])[None, :], (128, D)))
    d['od_w_out'] = f(inp['od_w_out'][0]); d['ret_norm_w'] = f(inp['ret_norm_w'][0])
    d['lru_conv_w'] = f(inp['lru_conv_w'][0]); d['lru_conv_b'] = f(inp['lru_conv_b'][0])
    d['lru_w_a'] = f(inp['lru_w_a'][0]); d['lru_b_a'] = f(inp['lru_b_a'][0])
    d['lru_w_x'] = f(inp['lru_w_x'][0]); d['lru_b_x'] = f(inp['lru_b_x'][0]); d['lru_lambda'] = f(inp['lru_lambda'][0])
    d['meta'] = f(inp['meta'])
    return d


def make_in_maps(cfg, inputs, ncores):
    shared = host_layout(inputs)
    shared.update(host_consts())
    T = 16 + cfg.T_x
    pos = np.concatenate([np.arange(T, dtype=np.float32), (16 + cfg.past) + np.arange(16, dtype=np.float32)])
    cf, sf, ct, st = rope_tables(pos)
    shared['ropeF_c'], shared['ropeF_s'], shared['ropeT_c'], shared['ropeT_s'] = cf, sf, np.ascontiguousarray(ct), np.ascontiguousarray(st)
    f = lambda a: np.ascontiguousarray(np.asarray(a, dtype=np.float32))
    maps = []
    for c in range(ncores):
        ps = slice(c * cfg.n_p, (c + 1) * cfg.n_p)
        ss_ = slice(c * cfg.n_s, (c + 1) * cfg.n_s)
        m = dict(shared)
        m['xp'] = f(inputs['x_prompt'][ps])
        m['xs'] = f(inputs['x_sample'][ss_])
        m['s5re_in'] = f(inputs['state_s5_re'][0][ss_]).reshape(cfg.n_s, 2048)
        m['s5im_in'] = f(inputs['state_s5_im'][0][ss_]).reshape(cfg.n_s, 2048)
        m['mlc_in'] = f(inputs['state_ml_c'][0][ss_])
        m['mln_in'] = f(inputs['state_ml_n'][0][ss_])
        m['mlm_in'] = f(inputs['state_ml_m'][0][ss_])
        m['mlconv_in'] = f(inputs['state_ml_conv'][0][ss_])
        m['ret_in'] = f(inputs['state_ret'][0][ss_])
        m['lruh_in'] = f(inputs['state_lru_h'][0][ss_])
        m['lruconv_in'] = f(inputs['state_lru_conv'][0][ss_])
        maps.append(m)
    return maps


def gather(cfg, results):
    cat = lambda k: np.concatenate([np.asarray(r[k]) for r in results], axis=0)
    outs = [cat('yp'), cat('ys')]
    for g in ('p', 's'):
        n = cat('o_s5re_' + g).shape[0]
        outs += [cat('o_s5re_' + g).reshape(1, n, 32, 64), cat('o_s5im_' + g).reshape(1, n, 32, 64),
                 cat('o_mlc_' + g)[None], cat('o_mln_' + g)[None], cat('o_mlm_' + g)[None], cat('o_mlconv_' + g)[None],
                 cat('o_ret_' + g)[None], cat('o_lruh_' + g)[None], cat('o_lruconv_' + g)[None]]
    return tuple(np.ascontiguousarray(o.astype(np.float32)) for o in outs)


def kernel(**inputs):
    ncores = 8
    cfg = Cfg(n_p=2, T_x=4096, n_s=2, past=2048, layers=2)
    nc = build(cfg)
    maps = make_in_maps(cfg, inputs, ncores)
    res = run_bass_kernel_spmd(nc, maps, core_ids=list(range(ncores)))
    return gather(cfg, res.results)
```

```python
import math
from contextlib import ExitStack
import numpy as np
import concourse.bass as bass
import concourse.mybir as mybir
from concourse.bass_utils import run_bass_kernel_spmd

F32 = mybir.dt.float32
BF16 = mybir.dt.bfloat16
AF = mybir.ActivationFunctionType
ALU = mybir.AluOpType

D = 1024
NPIECE = 256
EPS = 1e-6
PI = math.pi


class Prog:
    def __init__(self, nc, ndma=6):
        self.nc = nc
        self.e = dict(pe=nc.tensor, dve=nc.vector, act=nc.scalar, pool=nc.gpsimd, sp=nc.sync)
        self.streams = {k: [] for k in self.e}
        self.cnt = {k: 0 for k in self.e}
        self.waited = {k: {} for k in self.e}
        self.lastw = {}
        self.reads = {}
        self.sems = {}
        self.ndma = ndma
        self.dmai = {q: 0 for q in ('sp', 'act', 'pool')}
        self.dma_last = {}
        self.enabled = True
        self.semnames = ['c_' + k for k in ('pe', 'dve', 'act', 'pool')] + \
            ['d_%s_%d' % (q, j) for q in ('sp', 'act', 'pool') for j in range(ndma)]

    def _wait(self, eng, tok):
        sem, val = tok
        if self.waited[eng].get(sem, 0) >= val:
            return
        self.waited[eng][sem] = val
        self.e[eng].wait_ge(self.sems[sem], val)

    def _deps(self, eng, reads, writes):
        deps = []
        own = 'c_' + eng
        for k in reads:
            t = self.lastw.get(k)
            if t is not None:
                if not (eng == 'pe' and t[0] == own):
                    deps.append(t)
        for k in writes:
            t = self.lastw.get(k)
            if t is not None and t[0] != own:
                deps.append(t)
            for t in self.reads.get(k, ()):
                if t[0] != own:
                    deps.append(t)
        for t in deps:
            self._wait(eng, t)

    def _commit(self, tok, reads, writes):
        for k in writes:
            self.lastw[k] = tok
            self.reads[k] = []
        for k in reads:
            if k not in writes:
                self.reads.setdefault(k, []).append(tok)

    def op(self, eng, fn, reads=(), writes=()):
        if not self.enabled:
            return None
        self._deps(eng, reads, writes)
        self.cnt[eng] += 1
        tok = ('c_' + eng, self.cnt[eng])
        fn().then_inc(self.sems['c_' + eng], 1)
        self._commit(tok, reads, writes)
        return tok

    def dma(self, q, out, in_, reads=(), writes=(), **kw):
        if not self.enabled:
            return None
        self._deps(q, reads, writes)
        i = self.dmai[q]
        self.dmai[q] += 1
        j, r = i % self.ndma, i // self.ndma
        sem = 'd_%s_%d' % (q, j)
        if r > 0:
            self._wait(q, (sem, 16 * r))
        e = self.e[q]
        e.dma_start(out=out, in_=in_, **kw).then_inc(self.sems[sem], 16)
        tok = (sem, 16 * (r + 1))
        self.dma_last[sem] = tok
        self._commit(tok, reads, writes)
        return tok

    def barrier(self):
        toks = set(self.lastw.values())
        for lst in self.reads.values():
            toks.update(lst)
        toks.update(self.dma_last.values())
        for k in ('pe', 'dve', 'act', 'pool'):
            if self.cnt[k] > 0:
                toks.add(('c_' + k, self.cnt[k]))
        for eng in self.e:
            for t in toks:
                self._wait(eng, t)
        self.lastw = {}
        self.reads = {}

    def emit(self, block):
        sems = self.sems

        def run(engname, engobj):
            for it in self.streams[engname]:
                if it[0] == 'w':
                    engobj.wait_ge(sems[it[1]], it[2])
                else:
                    inst = it[1]()
                    inst.then_inc(sems[it[2]], it[3])

        @block.tensor
        def _(x):
            run('pe', x)

        @block.vector
        def _(x):
            run('dve', x)

        @block.scalar
        def _(x):
            run('act', x)

        @block.gpsimd
        def _(x):
            run('pool', x)

        @block.sync
        def _(x):
            run('sp', x)


def host_consts():
    c = {}
    c['c_identb'] = np.eye(128, dtype=np.float32)
    c['c_identf'] = np.eye(128, dtype=np.float32)
    s = np.arange(128)
    c['c_negmask'] = np.where(s[:, None] <= s[None, :], 1.0, 0.0).astype(np.float32)
    sel = np.zeros((4, 4, 128), np.float32)
    for h in range(4):
        sel[h, h, :] = 1.0
    c['c_sel'] = sel
    c['c_ones4'] = np.ones((4, 128), np.float32)
    mA = np.zeros((128, 8), np.float32)
    for g8 in range(8):
        mA[g8 * 16:(g8 + 1) * 16, g8] = 1.0
    c['c_maskA'] = mA
    lg = np.log1p(-np.exp2(-5.0 - np.arange(4, dtype=np.float64)))
    for L in (16, 128):
        idx = np.arange(L, dtype=np.float64)
        diff = idx[None, :] - idx[:, None]
        dm = np.where(diff >= 0, np.exp(lg[:, None, None] * np.maximum(diff, 0.0)), 0.0)
        dmp = np.zeros((128, 4, 128), np.float32)
        dmp[:L, :, :L] = np.transpose(dm, (1, 0, 2))
        c['c_rmask%d' % L] = dmp
        xi = np.exp(lg[:, None] * (idx + 1.0))
        c['c_rxi%d' % L] = np.ascontiguousarray(np.broadcast_to(xi[None], (128, 4, L))).astype(np.float32)
        zeta = np.exp(lg[:, None] * (L - 1.0 - idx))
        zp = np.zeros((128, 4), np.float32)
        zp[:L] = zeta.T
        c['c_rzeta%d' % L] = zp
        c['c_rgl%d' % L] = np.ascontiguousarray(np.broadcast_to(np.exp(lg * L)[None], (128, 4))).astype(np.float32)
    return c


def rope_tables(pos):
    half = 64
    inv = (10000.0 ** (-np.arange(half, dtype=np.float32) / half)).astype(np.float32)
    ang = pos.astype(np.float32)[:, None] * inv[None, :]
    cos = np.cos(ang).astype(np.float32)
    sin = np.sin(ang).astype(np.float32)
    cf = np.concatenate([cos.T, cos.T], axis=0)
    sf = np.concatenate([-sin.T, sin.T], axis=0)
    return np.ascontiguousarray(cf), np.ascontiguousarray(sf), cos, sin


class Cfg:
    def __init__(self, n_p=2, T_x=4096, n_s=2, past=2048, layers=2, debug=False, stage=99):
        self.stage = stage
        self.n_p, self.T_x, self.n_s, self.past, self.layers, self.debug = n_p, T_x, n_s, past, layers, debug
        assert T_x % NPIECE == 0


IN_SPECS = None


def in_specs(cfg):
    n_p, T_x, n_s = cfg.n_p, cfg.T_x, cfg.n_s
    T = 16 + T_x
    sp = {
        'xp': [n_p, T_x, D], 'xs': [n_s, 16, D], 'meta': [16, D],
        's5re_in': [n_s, 2048], 's5im_in': [n_s, 2048],
        'mlc_in': [n_s, 4, 256, 256], 'mln_in': [n_s, 4, 256], 'mlm_in': [n_s, 4], 'mlconv_in': [n_s, 3, D],
        'ret_in': [n_s, 4, 128, 256], 'lruh_in': [n_s, D], 'lruconv_in': [n_s, 3, D],
        'norm_w': [2, D], 'final_norm_w': [D],
        'ev_w_in': [D, 3072], 'ev_w_out': [1536, D],
        'lamre_A': [128, 4, 64], 'lamim_A': [128, 4, 64], 'logdt_A': [128, 4, 64],
        'lamre_S': [128, 16], 'lamim_S': [128, 16], 'logdt_S': [128, 16],
        'bre_A': [128, 4, 64], 'bim_A': [128, 4, 64], 'cre_S': [128, 16, 16], 'cim_S': [128, 16, 16],
        's5_d': [512], 's5_w_glu': [512, 512], 's5_b_glu': [512],
        'ml_conv_w': [4, D], 'ml_conv_b': [D], 'ml_wq': [4, 256, 256], 'ml_wk': [4, 256, 256], 'ml_wv': [4, 256, 256],
        'ml_wqT': [4, 256, 256], 'ml_wkT': [4, 256, 256], 'ml_wvT': [4, 256, 256],
        'ml_w_if': [3072, 8], 'ml_b_if': [8], 'ml_norm_w': [D], 'ml_skip': [D],
        'od_w_in': [D, 5120], 'od_w_qksw': [D, 1024], 'fnw_b': [128, D], 'od_w_out': [2048, D], 'ret_norm_w': [D],
        'lru_conv_w': [4, D], 'lru_conv_b': [D], 'lru_w_a': [8, 128, 128], 'lru_b_a': [D],
        'lru_w_x': [8, 128, 128], 'lru_b_x': [D], 'lru_lambda': [D],
        'ropeF_c': [128, T + 16], 'ropeF_s': [128, T + 16], 'ropeT_c': [T + 16, 64], 'ropeT_s': [T + 16, 64],
    }
    for k, v in host_consts().items():
        sp[k] = list(v.shape)
    return sp


def out_specs(cfg):
    n_p, T_x, n_s = cfg.n_p, cfg.T_x, cfg.n_s
    o = {'yp': [n_p, T_x, D], 'ys': [n_s, 16, D]}
    for g, n in (('p', n_p), ('s', n_s)):
        o['o_s5re_' + g] = [n, 2048]
        o['o_s5im_' + g] = [n, 2048]
        o['o_mlc_' + g] = [n, 4, 256, 256]
        o['o_mln_' + g] = [n, 4, 256]
        o['o_mlm_' + g] = [n, 4]
        o['o_mlconv_' + g] = [n, 3, D]
        o['o_ret_' + g] = [n, 4, 128, 256]
        o['o_lruh_' + g] = [n, D]
        o['o_lruconv_' + g] = [n, 3, D]
    return o


def build(cfg):
    nc = bass.Bass("TRN2", target_bir_lowering=False)
    n_p, T_x, n_s = cfg.n_p, cfg.T_x, cfg.n_s
    T = 16 + T_x
    I = {k: nc.dram_tensor(k, v, F32, kind="ExternalInput").ap() for k, v in in_specs(cfg).items()}
    O = {k: nc.dram_tensor(k, v, F32, kind="ExternalOutput").ap() for k, v in out_specs(cfg).items()}
    hk = "ExternalOutput" if cfg.debug else "Internal"
    h1p = nc.dram_tensor("h1p", [n_p, T, D], F32, kind=hk).ap()
    h1s = nc.dram_tensor("h1s", [n_s, 16, D], F32, kind=hk).ap()
    dbg_o = nc.dram_tensor("dbg_o", [4, 128, 256], F32, kind=hk).ap()

    seqs = []
    for i in range(n_p):
        pcs = [(0, 16, I['meta'][:, :], O['yp'], None)]
        for t in range(0, T_x, NPIECE):
            pcs.append((16 + t, NPIECE, I['xp'][i, t:t + NPIECE, :], O['yp'][i, t:t + NPIECE, :], None))
        seqs.append(('p', i, pcs))
    for i in range(n_s):
        seqs.append(('s', i, [(0, 16, I['xs'][i, :, :], O['ys'][i, :, :], None)]))

    def h1rows(g, i, t0, n):
        return (h1p if g == 'p' else h1s)[i, t0:t0 + n, :]

    with ExitStack() as top:
        top.enter_context(nc.allow_non_contiguous_dma(reason='small strided parameter/state layouts'))
        P = Prog(nc)
        for s in P.semnames:
            P.sems[s] = top.enter_context(nc.semaphore(s))
        pb = [top.enter_context(nc.psum_tensor("pb%d" % i, [128, 512], F32)) for i in range(8)]

        V, G, A, PL = nc.vector, nc.gpsimd, nc.scalar, nc.tensor

        def mm(out, lhsT, rhs, start, stop, reads, writes):
            P.op('pe', lambda: PL.matmul(out, lhsT, rhs, start=start, stop=stop), reads=reads, writes=writes)

        sb0 = lambda n, s, d: top.enter_context(nc.sbuf_tensor(n, s, d))
        identb = sb0("identb", [128, 128], BF16)
        identf = sb0("identf", [128, 128], F32)
        negh = sb0("negh", [128, 1], F32)
        P.dma('pool', identb[:], I['c_identb'][:, :], writes=['identb'])
        P.dma('sp', identf[:], I['c_identf'][:, :], writes=['identf'])
        P.op('pool', lambda: G.memset(negh[:], -0.5), writes=['negh'])

        def rmsnorm_T(hin, tp, tt, gcol, ub, uT, ss, hkey='hin', gkey='gcol'):
            P.op('pool', lambda: G.memset(ss[0:tp, 0:1], 0.0), writes=['ss'])
            P.op('act', lambda: A.activation(out=ub[0:tp, :], in_=hin[0:tp, :], func=AF.Square, accum_out=ss[0:tp, 0:1]),
                 reads=[hkey], writes=['ub', 'ss'])
            P.op('dve', lambda: V.tensor_scalar(ss[0:tp, 1:2], ss[0:tp, 0:1], 1.0 / D, EPS, ALU.mult, ALU.add),
                 reads=['ss'], writes=['ss'])
            P.op('pool', lambda: G.tensor_tensor(ss[0:tp, 2:3], ss[0:tp, 1:2], negh[0:tp, :], ALU.pow),
                 reads=['ss', 'negh'], writes=['ss'])
            P.op('act', lambda: A.activation(out=ub[0:tp, :], in_=hin[0:tp, :], func=AF.Copy, scale=ss[0:tp, 2:3]),
                 reads=[hkey, 'ss'], writes=['ub'])
            pT = pb[2][:].bitcast(BF16)
            for k in range(8):
                P.op('pe', lambda k=k: PL.transpose(pT[:, k * 128:k * 128 + tp], ub[0:tp, k * 128:(k + 1) * 128],
                                                    identb[0:tp, 0:tp]),
                     reads=['ub', 'identb'], writes=['pb2'])
            for k in range(8):
                P.op('act', lambda k=k: A.activation(out=uT[:, k, tt * 128:tt * 128 + tp], in_=pT[:, k * 128:k * 128 + tp],
                                                     func=AF.Copy, scale=gcol[:, k:k + 1]),
                     reads=['pb2', gkey], writes=['uT'])

        with ExitStack() as l0:
            sb = lambda n, s, d: l0.enter_context(nc.sbuf_tensor(n, s, d))
            NP = NPIECE
            w_in = sb("w_in0", [128, 8, 3072], BF16)
            w_out = sb("w_out0", [128, 12, D], BF16)
            w_glu = sb("w_glu", [128, 4, 512], BF16)
            wq = sb("wq", [128, 4, 2, 256], BF16)
            wk = sb("wk", [128, 4, 2, 256], BF16)
            wv = sb("wv", [128, 4, 2, 256], BF16)
            wcif = sb("wcif", [128, 8, 8], BF16)
            wxif = sb("wxif", [128, 8, 8], BF16)
            s5B = sb("s5B", [128, 2, 16, 128], BF16)
            s5C = sb("s5C", [128, 2, 16, 128], BF16)
            LR = 64
            tabc = sb("tabc", [128, 16, LR], F32)
            tabs = sb("tabs", [128, 16, LR], F32)
            rtab = sb("rtab", [128, 16, LR], F32)
            gcol = sb("gcol0", [128, 8], F32)
            dcol = sb("dcol", [128, 4], F32)
            bgluh = sb("bgluh", [128, 4], F32)
            convw = sb("convw", [128, 8, 4], F32)
            convb = sb("convb", [128, 8], F32)
            mlnw = sb("mlnw", [128, 8], F32)
            skipc = sb("skipc", [128, 8], F32)
            b_i = sb("b_i", [4, 1], F32)
            b_f = sb("b_f", [4, 1], F32)
            negmask = sb("negmask", [128, 128], F32)
            sel = sb("sel", [4, 4, 128], F32)
            ones4 = sb("ones4", [4, 128], F32)
            onesrow = sb("onesrow", [4, NP], F32)

            for k in range(8):
                P.dma('pool', w_in[:, k, :], I['ev_w_in'][k * 128:(k + 1) * 128, :], writes=['w_in'])
            for k in range(12):
                P.dma('pool', w_out[:, k, :], I['ev_w_out'][k * 128:(k + 1) * 128, :], writes=['w_out'])
            P.dma('pool', w_glu[:], I['s5_w_glu'].rearrange("(k p) f -> p k f", p=128), writes=['w_glu'])
            for nm, t_ in (('ml_wq', wq), ('ml_wk', wk), ('ml_wv', wv)):
                for h in range(4):
                    P.dma('pool', t_[:, h, :, :], I[nm][h].rearrange("(k p) e -> p k e", p=128), writes=[nm])
            P.dma('sp', gcol[:], I['norm_w'][0, :].rearrange("(k p) -> p k", p=128), writes=['gcol'])
            P.dma('sp', dcol[:], I['s5_d'].rearrange("(k p) -> p k", p=128), writes=['dcol'])
            P.dma('sp', bgluh[:], I['s5_b_glu'].rearrange("(k p) -> p k", p=128), writes=['bgluh'])
            for tap in range(4):
                P.dma('sp', convw[:, :, tap], I['ml_conv_w'][tap, :].rearrange("(k p) -> p k", p=128), writes=['convw'])
            P.dma('sp', convb[:], I['ml_conv_b'].rearrange("(k p) -> p k", p=128), writes=['convb'])
            P.dma('sp', mlnw[:], I['ml_norm_w'].rearrange("(k p) -> p k", p=128), writes=['mlnw'])
            P.dma('sp', skipc[:], I['ml_skip'].rearrange("(k p) -> p k", p=128), writes=['skipc'])
            P.dma('sp', b_i[:], I['ml_b_if'][0:4].rearrange("(p o) -> p o", o=1), writes=['b_i'])
            P.dma('sp', b_f[:], I['ml_b_if'][4:8].rearrange("(p o) -> p o", o=1), writes=['b_f'])
            P.dma('sp', negmask[:], I['c_negmask'][:, :], writes=['negmask'])
            P.dma('sp', sel[:], I['c_sel'][:, :, :], writes=['sel'])
            P.dma('sp', ones4[:], I['c_ones4'][:, :], writes=['ones4'])
            P.op('pool', lambda: G.memset(onesrow[:], 1.0), writes=['onesrow'])
            P.op('dve', lambda: V.tensor_scalar(bgluh[:], bgluh[:], 0.5, None, ALU.mult), reads=['bgluh'], writes=['bgluh'])
            P.op('dve', lambda: V.tensor_scalar(b_f[:], b_f[:], -1.0, None, ALU.mult), reads=['b_f'], writes=['b_f'])

            print('SBUF remaining before setup', nc.sbuf_bytes_remaining)
            with ExitStack() as su:
                sbs = lambda n, s, d: su.enter_context(nc.sbuf_tensor(n, s, d))
                wqT = sbs("wqT", [128, 4, 2, 256], BF16)
                wkT = sbs("wkT", [128, 4, 2, 256], BF16)
                wvT = sbs("wvT", [128, 4, 2, 256], BF16)
                wif = sbs("wif", [128, 24, 8], BF16)
                for nm, t_ in (('ml_wqT', wqT), ('ml_wkT', wkT), ('ml_wvT', wvT)):
                    for h in range(4):
                        P.dma('pool', t_[:, h, :, :], I[nm][h].rearrange("(k p) e -> p k e", p=128), writes=[nm])
                P.dma('pool', wif[:], I['ml_w_if'].rearrange("(k p) g -> p k g", p=128), writes=['wif'])
                for h in range(4):
                    for dt_ in range(2):
                        o_ = pb[0][:, 0:8]
                        i_ = 0
                        for (wT, nm, off) in ((wqT, 'ml_wqT', 0), (wkT, 'ml_wkT', 8)):
                            for ke in range(2):
                                mm(o_, wT[:, h, ke, dt_ * 128:(dt_ + 1) * 128], wif[:, off + h * 2 + ke, :],
                                   i_ == 0, i_ == 3, [nm, 'wif'], ['pb0'])
                                i_ += 1
                        P.op('act', lambda h=h, dt_=dt_: A.copy(wcif[:, h * 2 + dt_, :], pb[0][:, 0:8]),
                             reads=['pb0'], writes=['wcif'])
                        o2 = pb[1][:, 0:8]
                        for ke in range(2):
                            mm(o2, wvT[:, h, ke, dt_ * 128:(dt_ + 1) * 128], wif[:, 16 + h * 2 + ke, :],
                               ke == 0, ke == 1, ['ml_wvT', 'wif'], ['pb1'])
                        P.op('act', lambda h=h, dt_=dt_: A.copy(wxif[:, h * 2 + dt_, :], pb[1][:, 0:8]),
                             reads=['pb1'], writes=['wxif'])

                def S(n, shp):
                    return sbs(n, shp, F32)

                def trig(name, ang_ap, shp, key):
                    cosT = S(name + "_cos", shp)
                    sinT = S(name + "_sin", shp)
                    sh = S(name + "_sh", shp)
                    acc = S(name + "_acc", shp)
                    tmp = S(name + "_tm", shp)
                    for (dst, shift) in ((cosT, PI / 2), (sinT, 0.0)):
                        P.op('dve', lambda shift=shift: V.tensor_scalar(sh[:], ang_ap, shift, None, ALU.add),
                             reads=[key], writes=[name + 'sh'])
                        P.op('dve', lambda: V.tensor_copy(acc[:], sh[:]), reads=[name + 'sh'], writes=[name + 'win_re'])
                        for m_ in range(1, 7):
                            thr = (2 * m_ - 1) * PI
                            P.op('dve', lambda thr=thr: V.tensor_scalar(tmp[:], sh[:], thr, -2 * PI, ALU.is_ge, ALU.mult),
                                 reads=[name + 'sh'], writes=[name + 'tm'])
                            P.op('dve', lambda: V.tensor_tensor(acc[:], acc[:], tmp[:], ALU.add),
                                 reads=[name + 'win_re', name + 'tm'], writes=[name + 'win_re'])
                        P.op('act', lambda dst=dst: A.activation(out=dst[:], in_=acc[:], func=AF.Sin),
                             reads=[name + 'win_re'], writes=[name + ('c' if dst is cosT else 's')])
                    return cosT, sinT

                shA = [128, 4, 64]
                lamreA = S("lamreA", shA); lamimA = S("lamimA", shA); dtA = S("dtA", shA)
                breA = S("breA", shA); bimA = S("bimA", shA)
                P.dma('sp', lamreA[:], I['lamre_A'][:, :, :], writes=['lamreA'])
                P.dma('sp', lamimA[:], I['lamim_A'][:, :, :], writes=['lamimA'])
                P.dma('sp', dtA[:], I['logdt_A'][:, :, :], writes=['dtA'])
                P.dma('sp', breA[:], I['bre_A'][:, :, :], writes=['breA'])
                P.dma('sp', bimA[:], I['bim_A'][:, :, :], writes=['bimA'])
                P.op('act', lambda: A.activation(out=dtA[:], in_=dtA[:], func=AF.Exp), reads=['dtA'], writes=['dtA'])
                magA = S("magA", shA); angA = S("angA", shA)
                P.op('dve', lambda: V.tensor_tensor(magA[:], lamreA[:], dtA[:], ALU.mult), reads=['lamreA', 'dtA'], writes=['magA'])
                P.op('act', lambda: A.activation(out=magA[:], in_=magA[:], func=AF.Exp), reads=['magA'], writes=['magA'])
                P.op('dve', lambda: V.tensor_tensor(angA[:], lamimA[:], dtA[:], ALU.mult), reads=['lamimA', 'dtA'], writes=['angA'])
                cosA, sinA = trig("tA", angA[:], shA, 'angA')
                abre = S("abre", shA); abim = S("abim", shA)
                P.op('dve', lambda: V.tensor_tensor(abre[:], magA[:], cosA[:], ALU.mult), reads=['magA', 'tAc'], writes=['abre'])
                P.op('dve', lambda: V.tensor_tensor(abim[:], magA[:], sinA[:], ALU.mult), reads=['magA', 'tAs'], writes=['abim'])
                den = S("denA", shA); t1 = S("t1A", shA); t2 = S("t2A", shA); kre = S("kreA", shA); kim = S("kimA", shA)
                P.op('dve', lambda: V.tensor_tensor(den[:], lamreA[:], lamreA[:], ALU.mult), reads=['lamreA'], writes=['denA'])
                P.op('dve', lambda: V.tensor_tensor(t1[:], lamimA[:], lamimA[:], ALU.mult), reads=['lamimA'], writes=['t1A'])
                P.op('dve', lambda: V.tensor_tensor(den[:], den[:], t1[:], ALU.add), reads=['denA', 't1A'], writes=['denA'])
                P.op('dve', lambda: V.reciprocal(den[:], den[:]), reads=['denA'], writes=['denA'])
                P.op('dve', lambda: V.tensor_scalar(abre[:], abre[:], -1.0, None, ALU.add), reads=['abre'], writes=['abre'])
                P.op('dve', lambda: V.tensor_tensor(t1[:], abre[:], lamreA[:], ALU.mult), reads=['abre', 'lamreA'], writes=['t1A'])
                P.op('dve', lambda: V.tensor_tensor(t2[:], abim[:], lamimA[:], ALU.mult), reads=['abim', 'lamimA'], writes=['t2A'])
                P.op('dve', lambda: V.tensor_tensor(t1[:], t1[:], t2[:], ALU.add), reads=['t1A', 't2A'], writes=['t1A'])
                P.op('dve', lambda: V.tensor_tensor(kre[:], t1[:], den[:], ALU.mult), reads=['t1A', 'denA'], writes=['kreA'])
                P.op('dve', lambda: V.tensor_tensor(t1[:], abim[:], lamreA[:], ALU.mult), reads=['abim', 'lamreA'], writes=['t1A'])
                P.op('dve', lambda: V.tensor_tensor(t2[:], abre[:], lamimA[:], ALU.mult), reads=['abre', 'lamimA'], writes=['t2A'])
                P.op('dve', lambda: V.tensor_tensor(t1[:], t1[:], t2[:], ALU.subtract), reads=['t1A', 't2A'], writes=['t1A'])
                P.op('dve', lambda: V.tensor_tensor(kim[:], t1[:], den[:], ALU.mult), reads=['t1A', 'denA'], writes=['kimA'])
                bbre = S("bbre", shA); bbim = S("bbim", shA)
                P.op('dve', lambda: V.tensor_tensor(t1[:], kre[:], breA[:], ALU.mult), reads=['kreA', 'breA'], writes=['t1A'])
                P.op('dve', lambda: V.tensor_tensor(t2[:], kim[:], bimA[:], ALU.mult), reads=['kimA', 'bimA'], writes=['t2A'])
                P.op('dve', lambda: V.tensor_tensor(bbre[:], t1[:], t2[:], ALU.subtract), reads=['t1A', 't2A'], writes=['bbre'])
                P.op('dve', lambda: V.tensor_tensor(t1[:], kre[:], bimA[:], ALU.mult), reads=['kreA', 'bimA'], writes=['t1A'])
                P.op('dve', lambda: V.tensor_tensor(t2[:], kim[:], breA[:], ALU.mult), reads=['kimA', 'breA'], writes=['t2A'])
                P.op('dve', lambda: V.tensor_tensor(bbim[:], t1[:], t2[:], ALU.add), reads=['t1A', 't2A'], writes=['bbim'])
                maskA = S("maskA", [128, 8])
                P.dma('sp', maskA[:], I['c_maskA'][:, :], writes=['maskA'])
                for j in range(16):
                    kc, jl = j // 4, j % 4
                    for h in range(2):
                        g8 = 2 * jl + h
                        for part, src, key in ((0, bbre, 'bbre'), (1, bbim, 'bbim')):
                            P.op('dve', lambda j=j, kc=kc, h=h, g8=g8, part=part, src=src: V.tensor_scalar(
                                s5B[:, part, j, h * 64:(h + 1) * 64], src[:, kc, :], maskA[:, g8:g8 + 1], None, ALU.mult),
                                reads=[key, 'maskA'], writes=['s5B'])
                creS = S("creS", [128, 16, 16]); cimS = S("cimS", [128, 16, 16])
                P.dma('sp', creS[:], I['cre_S'][:, :, :], writes=['creS'])
                P.dma('sp', cimS[:], I['cim_S'][:, :, :], writes=['cimS'])
                P.op('pool', lambda: G.memset(s5C[:], 0.0), writes=['s5C'])
                for j in range(16):
                    jl = j % 4
                    for h in range(2):
                        g8 = 2 * jl + h
                        P.op('dve', lambda j=j, h=h, g8=g8: V.tensor_copy(
                            s5C[h * 64:(h + 1) * 64, 0, j, g8 * 16:(g8 + 1) * 16], creS[h * 64:(h + 1) * 64, j, :]),
                            reads=['creS'], writes=['s5C'])
                        P.op('dve', lambda j=j, h=h, g8=g8: V.tensor_scalar(
                            s5C[h * 64:(h + 1) * 64, 1, j, g8 * 16:(g8 + 1) * 16], cimS[h * 64:(h + 1) * 64, j, :],
                            -1.0, None, ALU.mult), reads=['cimS'], writes=['s5C'])
                shS = [128, 16]
                lamreS = S("lamreS", shS); lamimS = S("lamimS", shS); dtS = S("dtS", shS)
                P.dma('sp', lamreS[:], I['lamre_S'][:, :], writes=['lamreS'])
                P.dma('sp', lamimS[:], I['lamim_S'][:, :], writes=['lamimS'])
                P.dma('sp', dtS[:], I['logdt_S'][:, :], writes=['dtS'])
                P.op('act', lambda: A.activation(out=dtS[:], in_=dtS[:], func=AF.Exp), reads=['dtS'], writes=['dtS'])
                rmag = S("rmag", shS); angS = S("angS", shS)
                P.op('dve', lambda: V.tensor_tensor(rmag[:], lamreS[:], dtS[:], ALU.mult), reads=['lamreS', 'dtS'], writes=['rmag'])
                P.op('act', lambda: A.activation(out=rmag[:], in_=rmag[:], func=AF.Exp), reads=['rmag'], writes=['rmag'])
                P.op('dve', lambda: V.tensor_tensor(angS[:], lamimS[:], dtS[:], ALU.mult), reads=['lamimS', 'dtS'], writes=['angS'])
                cosS, sinS = trig("tS", angS[:], shS, 'angS')
                onesL = S("onesL", [128, LR])
                P.op('pool', lambda: G.memset(onesL[:], 1.0), writes=['onesL'])
                for j in range(16):
                    P.op('dve', lambda j=j: V.tensor_scalar(rtab[:, j, :], onesL[:], rmag[:, j:j + 1], None, ALU.mult),
                         reads=['rmag', 'onesL'], writes=['rtab'])
                P.op('dve', lambda: V.tensor_copy(tabc[:, :, 0:1], cosS[:].unsqueeze(2)), reads=['tSc'], writes=['tabc'])
                P.op('dve', lambda: V.tensor_copy(tabs[:, :, 0:1], sinS[:].unsqueeze(2)), reads=['tSs'], writes=['tabs'])
                d1 = S("dbl1", [128, LR // 2])
                m_ = 1
                while m_ < LR:
                    for j in range(16):
                        cm, sm = tabc[:, j, m_ - 1:m_], tabs[:, j, m_ - 1:m_]
                        lo_c, lo_s = tabc[:, j, 0:m_], tabs[:, j, 0:m_]
                        hi_c, hi_s = tabc[:, j, m_:2 * m_], tabs[:, j, m_:2 * m_]
                        a1 = d1[:, 0:m_]
                        P.op('dve', lambda: V.tensor_scalar(a1, lo_s, sm, None, ALU.mult), reads=['tabs'], writes=['dbl1'])
                        P.op('dve', lambda: V.scalar_tensor_tensor(hi_c, lo_c, cm, a1, ALU.mult, ALU.subtract),
                             reads=['tabc', 'dbl1'], writes=['tabc'])
                        P.op('dve', lambda: V.tensor_scalar(a1, lo_c, sm, None, ALU.mult), reads=['tabc', 'tabs'], writes=['dbl1'])
                        P.op('dve', lambda: V.scalar_tensor_tensor(hi_s, lo_s, cm, a1, ALU.mult, ALU.add),
                             reads=['tabs', 'tabc', 'dbl1'], writes=['tabs'])
                    m_ *= 2
                P.barrier()
            print('SBUF remaining after weights/setup', nc.sbuf_bytes_remaining)
            L_ = {}
            hin = sb("hin", [128, D], F32)
            ss = sb("ss", [128, 4], F32)
            ub = sb("ub", [128, D], BF16)
            uT = sb("uT", [128, 8, NP], BF16)
            gated = sb("gated", [128, 12, NP], BF16)
            ua = sb("ua", [128, 4, NP], BF16)
            sza = sb("sza", [128, 4, NP], BF16)
            tA = sb("tA_", [128, NP], F32)
            tB = sb("tB_", [128, NP], F32)
            wre = sb("wre", [128, 4, NP], F32)
            wim = sb("wim", [128, 4, NP], F32)
            tA2 = sb("tA2_", [128, NP], F32)
            tB2 = sb("tB2_", [128, NP], F32)
            hsre = sb("hsre", [128, 4, NP], BF16)
            hsim = sb("hsim", [128, 4, NP], BF16)
            Hre = sb("Hre", [128, 16], F32)
            Him = sb("Him", [128, 16], F32)
            ch1 = sb("ch1", [128, 4], F32)
            ch2 = sb("ch2", [128, 4], F32)
            ch3 = sb("ch3", [128, 4], F32)
            ch4 = sb("ch4", [128, 4], F32)
            y1 = sb("y1", [128, NP], F32)
            y2 = sb("y2", [128, NP + 1], F32)
            y3 = sb("y3", [128, NP + 1], F32)
            yg = sb("yg", [128, 4, NP], BF16)
            xb = sb("xb", [128, 8, 3 + NP], BF16)
            szb = sb("szb", [128, 8, NP], BF16)
            acc = sb("acc", [128, NP], F32)
            xc = sb("xc", [128, 8, NP], BF16)
            qT = uT
            kT = sb("kT", [128, 8, NP], BF16)
            vtok = sb("vtok", [128, 4, 264], BF16)
            ktok = sb("ktok", [128, 4, 256], BF16)
            Cst = sb("Cst", [128, 4, 2, 264], F32)
            Cbf = sb("Cbf", [128, 4, 2, 264], BF16)
            r_ig = sb("r_ig", [4, NP], F32)
            r_lf = sb("r_lf", [4, NP], F32)
            Mfull = sb("Mfull", [4, NP + 1], F32)
            negM = sb("negM", [4, NP + 1], F32)
            r_wi = sb("r_wi", [4, NP], F32)
            r_g = sb("r_g", [4, NP], F32)
            r_emm = sb("r_emm", [4, NP], F32)
            r_wend = sb("r_wend", [4, NP], F32)
            r_e = r_wend
            mcarry = sb("mcarry", [4, 1], F32)
            dg4 = sb("dg4", [4, 4], F32)
            cols = sb("cols", [128, 20], F32)
            Esb = y1[:, 0:128]
            Dm = y1[:, 128:256]
            PT = sb("PT", [128, 128], BF16)
            kw = sb("kw", [128, 256], BF16)
            inter = y2
            tot = y3
            st6 = sb("st6", [128, 6], F32)
            st2 = sb("st2", [128, 8], F32)
            hn = ub
            ytmp = sb("ytmp", [128, 128], F32)
            xbo = sb("xbo", [128, 8, 3], F32)

            print('SBUF remaining after L0 activations', nc.sbuf_bytes_remaining)
            P.op('pool', lambda: G.memset(vtok[:], 1.0), writes=['vtok'])

            mmreg = [(pb[0], 'pb0'), (pb[1], 'pb1')]
            mmi = [0]

            def nextmm():
                r_ = mmreg[mmi[0] % 2]
                mmi[0] += 1
                return r_

            def inproj(f, n):
                ps, key = nextmm()
                for k in range(8):
                    mm(ps[:, 0:n], w_in[:, k, f * 128:(f + 1) * 128], uT[:, k, 0:n], k == 0, k == 7, ['w_in', 'uT'], [key])
                return ps, key

            for (g, si, pcs) in seqs:
                if g == 'p':
                    P.op('pool', lambda: G.memset(Hre[:], 0.0), writes=['Hre'])
                    P.op('pool', lambda: G.memset(Him[:], 0.0), writes=['Him'])
                    P.op('pool', lambda: G.memset(Cst[:], 0.0), writes=['Cst%d' % h_ for h_ in range(4)])
                    P.op('pool', lambda: G.memset(mcarry[:], 0.0), writes=['mcarry'])
                    P.op('pool', lambda: G.memset(xb[:, :, 0:3], 0.0), writes=['xb'])
                else:
                    P.dma('sp', Hre[:], I['s5re_in'][si, :].rearrange("(j q) -> q j", q=128), writes=['Hre'])
                    P.dma('sp', Him[:], I['s5im_in'][si, :].rearrange("(j q) -> q j", q=128), writes=['Him'])
                    for h in range(4):
                        P.dma('sp', Cst[:, h, :, 0:256], I['mlc_in'][si, h].rearrange("(k p) e -> p k e", p=128), writes=['Cst%d' % h])
                        P.dma('sp', Cst[:, h, :, 256], I['mln_in'][si, h].rearrange("(k p) -> p k", p=128), writes=['Cst%d' % h])
                    P.dma('sp', mcarry[:], I['mlm_in'][si, :].rearrange("(p o) -> p o", o=1), writes=['mcarry'])
                    for t_ in range(3):
                        P.dma('pool', xb[:, :, t_], I['mlconv_in'][si, t_].rearrange("(k p) -> p k", p=128), writes=['xb'])
                P.op('act', lambda: A.copy(Cbf[:], Cst[:]), reads=['Cst%d' % h_ for h_ in range(4)], writes=['Cbf%d' % h_ for h_ in range(4)])

                for (t0, n, src, _, _) in pcs:
                    tp = min(n, 128)
                    ntt = (n + 127) // 128
                    L = tp
                    nch = n // L
                    for tt in range(ntt):
                        P.dma('sp', hin[0:tp, :], src[tt * 128:tt * 128 + tp, :], writes=['hin'])
                        rmsnorm_T(hin, tp, tt, gcol, ub, uT, ss)
                    for f in range(4):
                        ps, key = inproj(f, n)
                        P.op('act', lambda f=f, ps=ps: A.copy(ua[:, f, 0:n], ps[:, 0:n]), reads=[key], writes=['ua'])
                    for f in range(4):
                        ps, key = inproj(4 + f, n)
                        P.op('act', lambda f=f, ps=ps: A.activation(out=sza[:, f, 0:n], in_=ps[:, 0:n], func=AF.Silu),
                             reads=[key], writes=['sza'])

                    def gen_s5():
                        Ls = min(n, LR)
                        ncs = n // Ls
                        def v3(ap2):
                            return ap2.rearrange("p (c l) -> p c l", c=ncs)

                        for kc in range(4):
                            tcs, tss = [], []
                            for jl in range(4):
                                j = 4 * kc + jl
                                sreg = (pb[4], 'pb4') if j % 2 == 0 else (pb[5], 'pb5')
                                pre, pim = sreg[0][:, 0:n], sreg[0][:, 256:256 + n]
                                mm(pre, s5B[:, 0, j, :], ua[:, kc, 0:n], True, True, ['s5B', 'ua'], [sreg[1]])
                                mm(pim, s5B[:, 1, j, :], ua[:, kc, 0:n], True, True, ['s5B', 'ua'], [sreg[1]])
                                tc_ = tabc[:, j, 0:Ls].unsqueeze(1).to_broadcast([128, ncs, Ls])
                                ts_ = tabs[:, j, 0:Ls].unsqueeze(1).to_broadcast([128, ncs, Ls])
                                tcs.append(tc_); tss.append(ts_)
                                P.op('dve', lambda pre=pre, tc_=tc_: V.tensor_tensor(v3(tA[:, 0:n]), v3(pre), tc_, ALU.mult),
                                     reads=[sreg[1], 'tabc'], writes=['tA'])
                                P.op('dve', lambda pim=pim, ts_=ts_: V.tensor_tensor(v3(tB[:, 0:n]), v3(pim), ts_, ALU.mult),
                                     reads=[sreg[1], 'tabs'], writes=['tB'])
                                P.op('pool', lambda jl=jl: G.tensor_tensor(wre[:, jl, 0:n], tA[:, 0:n], tB[:, 0:n], ALU.add),
                                     reads=['tA', 'tB'], writes=['wre%d' % jl])
                                P.op('dve', lambda pim=pim, tc_=tc_: V.tensor_tensor(v3(tA2[:, 0:n]), v3(pim), tc_, ALU.mult),
                                     reads=[sreg[1], 'tabc'], writes=['tA2'])
                                P.op('dve', lambda pre=pre, ts_=ts_: V.tensor_tensor(v3(tB2[:, 0:n]), v3(pre), ts_, ALU.mult),
                                     reads=[sreg[1], 'tabs'], writes=['tB2'])
                                P.op('pool', lambda jl=jl: G.tensor_tensor(wim[:, jl, 0:n], tA2[:, 0:n], tB2[:, 0:n], ALU.subtract),
                                     reads=['tA2', 'tB2'], writes=['wim%d' % jl])
                                if jl % 2 == 1:
                                    yield
                            j0 = 4 * kc
                            for c in range(ncs):
                                cs = slice(c * Ls, (c + 1) * Ls)
                                for jl in range(4):
                                    j = j0 + jl
                                    if c == 0:
                                        ire, iim, rk = Hre[:, j:j + 1], Him[:, j:j + 1], ['Hre', 'Him']
                                    else:
                                        ire, iim, rk = ch1[:, jl:jl + 1], ch2[:, jl:jl + 1], ['ch1', 'ch2']
                                    P.op('dve', lambda j=j, jl=jl, cs=cs, ire=ire: V.tensor_tensor_scan(
                                        wre[:, jl, cs], rtab[:, j, 0:Ls], wre[:, jl, cs], ire, ALU.mult, ALU.add),
                                        reads=['wre%d' % jl, 'rtab'] + rk, writes=['wre%d' % jl])
                                    P.op('dve', lambda j=j, jl=jl, cs=cs, iim=iim: V.tensor_tensor_scan(
                                        wim[:, jl, cs], rtab[:, j, 0:Ls], wim[:, jl, cs], iim, ALU.mult, ALU.add),
                                        reads=['wim%d' % jl, 'rtab'] + rk, writes=['wim%d' % jl])
                                e_ = (c + 1) * Ls - 1
                                last = (c == ncs - 1)
                                wlr, wli = wre[:, :, e_], wim[:, :, e_]
                                cL4, sL4 = tabc[:, j0:j0 + 4, Ls - 1], tabs[:, j0:j0 + 4, Ls - 1]
                                dre = Hre[:, j0:j0 + 4] if last else ch1[:, 0:4]
                                dim_ = Him[:, j0:j0 + 4] if last else ch2[:, 0:4]
                                kre_ = 'Hre' if last else 'ch1'
                                kim_ = 'Him' if last else 'ch2'
                                w4r = ['wre%d' % i_ for i_ in range(4)]
                                w4i = ['wim%d' % i_ for i_ in range(4)]
                                P.op('dve', lambda: V.tensor_tensor(ch3[:, 0:4], wli, sL4, ALU.mult), reads=w4i + ['tabs'], writes=['ch3'])
                                P.op('dve', lambda: V.tensor_tensor(ch4[:, 0:4], wlr, sL4, ALU.mult), reads=w4r + ['tabs'], writes=['ch4'])
                                P.op('dve', lambda: V.tensor_tensor(dre, wlr, cL4, ALU.mult), reads=w4r + ['tabc', kre_], writes=[kre_])
                                P.op('dve', lambda: V.tensor_tensor(dim_, wli, cL4, ALU.mult), reads=w4i + ['tabc', kim_], writes=[kim_])
                                P.op('dve', lambda: V.tensor_tensor(dre, dre, ch3[:, 0:4], ALU.subtract), reads=[kre_, 'ch3'], writes=[kre_])
                                P.op('dve', lambda: V.tensor_tensor(dim_, dim_, ch4[:, 0:4], ALU.add), reads=[kim_, 'ch4'], writes=[kim_])
                                if c % 2 == 1 or last:
                                    yield
                            for jl in range(4):
                                tc_, ts_ = tcs[jl], tss[jl]
                                P.op('dve', lambda jl=jl, tc_=tc_: V.tensor_tensor(v3(tA[:, 0:n]), v3(wre[:, jl, 0:n]), tc_, ALU.mult),
                                     reads=['wre%d' % jl, 'tabc'], writes=['tA'])
                                P.op('dve', lambda jl=jl, ts_=ts_: V.tensor_tensor(v3(tB[:, 0:n]), v3(wim[:, jl, 0:n]), ts_, ALU.mult),
                                     reads=['wim%d' % jl, 'tabs'], writes=['tB'])
                                P.op('pool', lambda jl=jl: G.tensor_tensor(hsre[:, jl, 0:n], tA[:, 0:n], tB[:, 0:n], ALU.subtract),
                                     reads=['tA', 'tB'], writes=['hsre%d' % jl])
                                P.op('dve', lambda jl=jl, tc_=tc_: V.tensor_tensor(v3(tA2[:, 0:n]), v3(wim[:, jl, 0:n]), tc_, ALU.mult),
                                     reads=['wim%d' % jl, 'tabc'], writes=['tA2'])
                                P.op('dve', lambda jl=jl, ts_=ts_: V.tensor_tensor(v3(tB2[:, 0:n]), v3(wre[:, jl, 0:n]), ts_, ALU.mult),
                                     reads=['wre%d' % jl, 'tabs'], writes=['tB2'])
                                P.op('pool', lambda jl=jl: G.tensor_tensor(hsim[:, jl, 0:n], tA2[:, 0:n], tB2[:, 0:n], ALU.add),
                                     reads=['tA2', 'tB2'], writes=['hsim%d' % jl])
                                if jl % 2 == 1:
                                    yield
                            ps, key = nextmm()
                            for jl in range(4):
                                j = 4 * kc + jl
                                mm(ps[:, 0:n], s5C[:, 0, j, :], hsre[:, jl, 0:n], jl == 0, False, ['s5C', 'hsre%d' % jl], [key])
                                mm(ps[:, 0:n], s5C[:, 1, j, :], hsim[:, jl, 0:n], False, jl == 3, ['s5C', 'hsim%d' % jl], [key])
                            P.op('dve', lambda kc=kc, ps=ps: V.scalar_tensor_tensor(y1[:, 0:n], ua[:, kc, 0:n], dcol[:, kc:kc + 1],
                                                                                   ps[:, 0:n], ALU.mult, ALU.add),
                                 reads=['ua', 'dcol', key], writes=['y1'])
                            P.op('pool', lambda: G.tensor_tensor(y2[:, 0:n], y1[:, 0:n], y1[:, 0:n], ALU.mult), reads=['y1'], writes=['y2'])
                            P.op('pool', lambda: G.tensor_scalar(y2[:, 0:n], y2[:, 0:n], 0.044715, 1.0, ALU.mult, ALU.add),
                                 reads=['y2'], writes=['y2'])
                            P.op('pool', lambda: G.tensor_tensor(y2[:, 0:n], y2[:, 0:n], y1[:, 0:n], ALU.mult), reads=['y2', 'y1'], writes=['y2'])
                            P.op('act', lambda: A.activation(out=y3[:, 0:n], in_=y2[:, 0:n], func=AF.Tanh, scale=math.sqrt(2.0 / PI)),
                                 reads=['y2'], writes=['y3'])
                            P.op('dve', lambda: V.tensor_scalar(y3[:, 0:n], y3[:, 0:n], 0.5, 0.5, ALU.mult, ALU.add), reads=['y3'], writes=['y3'])
                            P.op('dve', lambda kc=kc: V.tensor_tensor(yg[:, kc, 0:n], y3[:, 0:n], y1[:, 0:n], ALU.mult),
                                 reads=['y3', 'y1'], writes=['yg%d' % kc])
                            yield
                        for m_ in range(4):
                            ps, key = nextmm()
                            for kk in range(4):
                                mm(ps[:, 0:n], w_glu[:, kk, m_ * 128:(m_ + 1) * 128], yg[:, kk, 0:n], kk == 0, kk == 3,
                                   ['w_glu', 'yg%d' % kk], [key])
                            P.op('act', lambda m_=m_, ps=ps: A.activation(out=y3[:, 0:n], in_=ps[:, 0:n], func=AF.Tanh, scale=0.5,
                                                                          bias=bgluh[:, m_:m_ + 1]),
                                 reads=[key, 'bgluh'], writes=['y3'])
                            P.op('dve', lambda: V.tensor_scalar(y3[:, 0:n], y3[:, 0:n], 0.5, 0.5, ALU.mult, ALU.add), reads=['y3'], writes=['y3'])
                            P.op('dve', lambda m_=m_: V.tensor_tensor(y3[:, 0:n], y3[:, 0:n], yg[:, m_, 0:n], ALU.mult),
                                 reads=['y3', 'yg%d' % m_], writes=['y3'])
                            P.op('dve', lambda m_=m_: V.tensor_tensor(gated[:, m_, 0:n], y3[:, 0:n], sza[:, m_, 0:n], ALU.mult),
                                 reads=['y3', 'sza'], writes=['gated%d' % m_])
                            yield

                    def gen_ml():
                        for f in range(8):
                            ps, key = inproj(8 + f, n)
                            P.op('act', lambda f=f, ps=ps: A.copy(xb[:, f, 3:3 + n], ps[:, 0:n]), reads=[key], writes=['xb'])
                            if f % 2 == 1:
                                yield
                        for f in range(8):
                            ps, key = inproj(16 + f, n)
                            P.op('act', lambda f=f, ps=ps: A.activation(out=szb[:, f, 0:n], in_=ps[:, 0:n], func=AF.Silu),
                                 reads=[key], writes=['szb'])
                            if f % 2 == 1:
                                yield
                        for k in range(8):
                            P.op('dve', lambda k=k: V.tensor_scalar(acc[:, 0:n], xb[:, k, 0:n], convw[:, k, 0:1], convb[:, k:k + 1],
                                                                    ALU.mult, ALU.add),
                                 reads=['xb', 'convw', 'convb'], writes=['acc'])
                            for tap in range(1, 4):
                                P.op('dve', lambda k=k, tap=tap: V.scalar_tensor_tensor(
                                    acc[:, 0:n], xb[:, k, tap:tap + n], convw[:, k, tap:tap + 1], acc[:, 0:n], ALU.mult, ALU.add),
                                    reads=['xb', 'convw', 'acc'], writes=['acc'])
                            P.op('act', lambda k=k: A.activation(out=xc[:, k, 0:n], in_=acc[:, 0:n], func=AF.Silu),
                                 reads=['acc'], writes=['xc'])
                            yield
                        gip, gikey = nextmm()
                        gfp, gfkey = nextmm()
                        gi, gf = gip[0:4, 0:n], gfp[0:4, 0:n]
                        for gsel, o_, gkey_ in ((0, gi, gikey), (4, gf, gfkey)):
                            for k in range(8):
                                mm(o_, wcif[:, k, gsel:gsel + 4], xc[:, k, 0:n], k == 0, False, ['wcif', 'xc'], [gkey_])
                            for k in range(8):
                                mm(o_, wxif[:, k, gsel:gsel + 4], xb[:, k, 3:3 + n], False, k == 7, ['wxif', 'xb'], [gkey_])
                        P.op('act', lambda: A.activation(out=r_ig[:, 0:n], in_=gi, func=AF.Identity, bias=b_i[:, 0:1]),
                             reads=[gikey, 'b_i'], writes=['r_ig'])
                        P.op('act', lambda: A.activation(out=r_e[:, 0:n], in_=gf, func=AF.Exp, scale=-1.0, bias=b_f[:, 0:1]),
                             reads=[gfkey, 'b_f'], writes=['r_wend'])
                        P.op('act', lambda: A.activation(out=r_e[:, 0:n], in_=r_e[:, 0:n], func=AF.Ln, bias=1.0),
                             reads=['r_wend'], writes=['r_wend'])
                        P.op('dve', lambda: V.tensor_tensor_scan(r_lf[:, 0:n], onesrow[:, 0:n], r_e[:, 0:n], 0.0, ALU.mult, ALU.add),
                             reads=['r_wend', 'onesrow'], writes=['r_lf'])
                        P.op('dve', lambda: V.tensor_tensor(r_ig[:, 0:n], r_ig[:, 0:n], r_lf[:, 0:n], ALU.add),
                             reads=['r_ig', 'r_lf'], writes=['r_ig'])
                        P.op('dve', lambda: V.tensor_copy(Mfull[:, 0:1], mcarry[:, 0:1]), reads=['mcarry'], writes=['Mfull'])
                        P.op('dve', lambda: V.tensor_tensor_scan(Mfull[:, 1:n + 1], onesrow[:, 0:n], r_ig[:, 0:n], mcarry[:, 0:1],
                                                                 ALU.mult, ALU.max),
                             reads=['r_ig', 'onesrow', 'mcarry', 'Mfull'], writes=['Mfull'])
                        P.op('dve', lambda: V.tensor_scalar(negM[:, 0:n + 1], Mfull[:, 0:n + 1], -1.0, None, ALU.mult),
                             reads=['Mfull'], writes=['negM'])
                        P.op('dve', lambda: V.tensor_tensor(r_emm[:, 0:n], r_lf[:, 0:n], Mfull[:, 1:n + 1], ALU.subtract),
                             reads=['r_lf', 'Mfull'], writes=['r_emm'])
                        P.op('act', lambda: A.activation(out=r_emm[:, 0:n], in_=r_emm[:, 0:n], func=AF.Exp), reads=['r_emm'], writes=['r_emm'])
                        P.op('dve', lambda: V.tensor_tensor(mcarry[:, 0:1], Mfull[:, n:n + 1], r_lf[:, n - 1:n], ALU.subtract),
                             reads=['Mfull', 'r_lf'], writes=['mcarry'])
                        for c in range(nch):
                            cs = slice(c * L, (c + 1) * L)
                            P.op('act', lambda c=c, cs=cs: A.activation(out=r_wi[:, cs], in_=Mfull[:, 1 + c * L:1 + (c + 1) * L], func=AF.Exp,
                                                                        scale=-1.0, bias=Mfull[:, c * L:c * L + 1]),
                                 reads=['Mfull'], writes=['r_wi'])
                            P.op('act', lambda c=c, cs=cs: A.activation(out=r_g[:, cs], in_=Mfull[:, 1 + c * L:1 + (c + 1) * L], func=AF.Exp,
                                                                        scale=-1.0, bias=Mfull[:, (c + 1) * L:(c + 1) * L + 1]),
                                 reads=['Mfull'], writes=['r_g'])
                            P.op('act', lambda c=c, cs=cs: A.activation(out=r_wend[:, cs], in_=r_ig[:, cs], func=AF.Exp,
                                                                        bias=negM[:, (c + 1) * L:(c + 1) * L + 1]),
                                 reads=['r_ig', 'negM'], writes=['r_wend'])
                        yield
                        for h in range(4):
                            for e in range(2):
                                for (w_, dst, sc, wkey) in ((wq, qT, 1.0, 'ml_wq'), (wk, kT, 1.0 / 16.0, 'ml_wk')):
                                    ps, key = nextmm()
                                    for kd in range(2):
                                        mm(ps[:, 0:n], w_[:, h, kd, e * 128:(e + 1) * 128], xc[:, 2 * h + kd, 0:n], kd == 0, kd == 1,
                                           [wkey, 'xc'], [key])
                                    P.op('act', lambda ps=ps, dst=dst, h=h, e=e, sc=sc: A.mul(dst[:, 2 * h + e, 0:n], ps[:, 0:n], sc),
                                         reads=[key], writes=['uT'] if dst is qT else ['kT'])
                        for c in range(nch):
                            cs = slice(c * L, (c + 1) * L)
                            tt = c
                            ts0 = c * 128
                            for h in range(4):
                                ps, key = nextmm()
                                for kd in range(2):
                                    mm(ps[0:tp, 0:256], xb[:, 2 * h + kd, 3 + ts0:3 + ts0 + tp], wv[:, h, kd, :], kd == 0, kd == 1,
                                       ['xb', 'ml_wv'], [key])
                                P.op('act', lambda ps=ps, h=h: A.copy(vtok[0:tp, h, 0:256], ps[0:tp, 0:256]),
                                     reads=[key], writes=['vtok'])
                                ps, key = nextmm()
                                for kd in range(2):
                                    mm(ps[0:tp, 0:256], xc[:, 2 * h + kd, ts0:ts0 + tp], wk[:, h, kd, :], kd == 0, kd == 1,
                                       ['xc', 'ml_wk'], [key])
                                P.op('act', lambda ps=ps, h=h: A.mul(ktok[0:tp, h, :], ps[0:tp, 0:256], 1.0 / 16.0),
                                     reads=[key], writes=['ktok'])
                            pc = pb[6]
                            for i_, row in enumerate((r_wi, r_emm, r_wend, r_g)):
                                mm(pc[0:L, 384 + 4 * i_:388 + 4 * i_], row[0:4, cs], identf[0:4, 0:4], True, True,
                                   ['r_wi', 'r_emm', 'r_wend', 'r_g', 'identf'], ['pb6c'])
                            P.op('dve', lambda c=c: V.tensor_scalar(dg4[:, :], identf[0:4, 0:4], r_wi[0:4, (c + 1) * L - 1:(c + 1) * L], None, ALU.mult),
                                 reads=['r_wi', 'identf'], writes=['dg4'])
                            mm(pc[:, 400:404], ones4[0:4, :], dg4[0:4, :], True, True, ['ones4', 'dg4'], ['pb6c'])
                            P.op('dve', lambda pc=pc: V.tensor_copy(cols[:, 0:20], pc[:, 384:404]), reads=['pb6c'], writes=['cols'])
                            yield
                            for h in range(4):
                                stp, stkey = nextmm()
                                ST = stp[0:L, 0:L]
                                for ke in range(2):
                                    mm(ST, kT[:, 2 * h + ke, cs], qT[:, 2 * h + ke, cs], ke == 0, ke == 1, ['kT', 'uT'], [stkey])
                                P.op('dve', lambda ST=ST, h=h: V.scalar_tensor_tensor(PT[0:L, 0:L], ST, cols[0:L, 8 + h:9 + h], negmask[0:L, 0:L],
                                                                                     ALU.mult, ALU.mult),
                                     reads=[stkey, 'cols', 'negmask'], writes=['PT'])
                                mm(pb[7][0:L, 0:257], PT[0:L, 0:L], vtok[0:L, h, 0:257], True, True, ['PT', 'vtok'], ['pb7'])
                                for kd in range(2):
                                    mm(pb[3][0:L, 0:257], qT[:, 2 * h + kd, cs], Cbf[:, h, kd, 0:257], kd == 0, kd == 1, ['uT', 'Cbf%d' % h], ['pb3'])
                                P.op('act', lambda h=h: A.activation(out=inter[0:L, 0:257], in_=pb[3][0:L, 0:257], func=AF.Copy,
                                                                     scale=cols[0:L, h:h + 1]),
                                     reads=['pb3', 'cols'], writes=['y2'])
                                P.op('dve', lambda h=h: V.scalar_tensor_tensor(tot[0:L, 0:257], pb[7][0:L, 0:257], cols[0:L, 12 + h:13 + h],
                                                                            inter[0:L, 0:257], ALU.mult, ALU.add),
                                     reads=['pb7', 'y2', 'cols'], writes=['y3'])
                                P.op('dve', lambda: V.scalar_tensor_tensor(st2[0:L, 2:3], tot[0:L, 256:257], -1.0, tot[0:L, 256:257],
                                                                           ALU.mult, ALU.max), reads=['y3'], writes=['st2c'])
                                P.op('dve', lambda h=h: V.tensor_tensor(st2[0:L, 2:3], st2[0:L, 2:3], cols[0:L, 4 + h:5 + h], ALU.max),
                                     reads=['st2c', 'cols'], writes=['st2c'])
                                P.op('dve', lambda: V.reciprocal(st2[0:L, 3:4], st2[0:L, 2:3]), reads=['st2c'], writes=['st2d'])
                                P.op('dve', lambda: V.bn_stats(st6[0:L, :], tot[0:L, 0:256]), reads=['y3'], writes=['st6'])
                                P.op('dve', lambda: V.bn_aggr(st2[0:L, 4:6], st6[0:L, :]), reads=['st6'], writes=['st2e'])
                                P.op('dve', lambda: V.tensor_tensor(st2[0:L, 6:7], st2[0:L, 3:4], st2[0:L, 3:4], ALU.mult),
                                     reads=['st2d'], writes=['st2f'])
                                P.op('dve', lambda: V.tensor_scalar(st2[0:L, 6:7], st2[0:L, 6:7], st2[0:L, 5:6], EPS, ALU.mult, ALU.add),
                                     reads=['st2f', 'st2e'], writes=['st2f'])
                                P.op('pool', lambda: G.tensor_tensor(st2[0:L, 7:8], st2[0:L, 6:7], negh[0:L, :], ALU.pow),
                                     reads=['st2f', 'negh'], writes=['st2g'])
                                P.op('dve', lambda: V.tensor_tensor(st2[0:L, 7:8], st2[0:L, 7:8], st2[0:L, 3:4], ALU.mult),
                                     reads=['st2g', 'st2d'], writes=['st2g'])
                                P.op('dve', lambda tt=tt, h=h: V.tensor_scalar(hn[0:L, h * 256:(h + 1) * 256], tot[0:L, 0:256],
                                                                              st2[0:L, 4:5], st2[0:L, 7:8], ALU.subtract, ALU.mult),
                                     reads=['y3', 'st2e', 'st2g'], writes=['ub'])
                                P.op('act', lambda tt=tt, h=h: A.activation(out=kw[0:L, :], in_=ktok[0:L, h, :], func=AF.Copy, scale=cols[0:L, 8 + h:9 + h]),
                                     reads=['ktok', 'cols'], writes=['kw'])
                                for kd in range(2):
                                    cup = (pb[4], 'pb4') if kd == 0 else (pb[5], 'pb5')
                                    mm(cup[0][:, 0:257], kw[0:L, kd * 128:(kd + 1) * 128], vtok[0:L, h, 0:257], True, True, ['kw', 'vtok'], [cup[1]])
                                    P.op('dve', lambda h=h, kd=kd, cup=cup: V.scalar_tensor_tensor(
                                        Cst[:, h, kd, 0:257], Cst[:, h, kd, 0:257], cols[:, 16 + h:17 + h], cup[0][:, 0:257], ALU.mult, ALU.add),
                                        reads=['Cst%d' % h, 'cols', cup[1]], writes=['Cst%d' % h])
                                P.op('act', lambda h=h: A.copy(Cbf[:, h, :, :], Cst[:, h, :, :]), reads=['Cst%d' % h], writes=['Cbf%d' % h])
                                yield
                            pT = pb[2][:].bitcast(BF16)
                            for k in range(8):
                                P.op('pe', lambda k=k: PL.transpose(pT[:, k * 128:k * 128 + tp], hn[0:tp, k * 128:(k + 1) * 128],
                                                                    identb[0:tp, 0:tp]),
                                     reads=['ub', 'identb'], writes=['pb2'])
                            for k in range(8):
                                P.op('dve', lambda k=k: V.tensor_scalar(ytmp[:, 0:tp], pT[:, k * 128:k * 128 + tp], mlnw[:, k:k + 1], None, ALU.mult),
                                     reads=['pb2', 'mlnw'], writes=['ytmp'])
                                P.op('dve', lambda k=k: V.scalar_tensor_tensor(ytmp[:, 0:tp], xc[:, k, ts0:ts0 + tp], skipc[:, k:k + 1],
                                                                               ytmp[:, 0:tp], ALU.mult, ALU.add),
                                     reads=['xc', 'skipc', 'ytmp'], writes=['ytmp'])
                                P.op('dve', lambda k=k: V.tensor_tensor(gated[:, 4 + k, ts0:ts0 + tp], ytmp[:, 0:tp], szb[:, k, ts0:ts0 + tp], ALU.mult),
                                     reads=['ytmp', 'szb'], writes=['gatedB'])
                            yield
                        P.op('dve', lambda n=n: V.tensor_copy(xbo[:, :, :], xb[:, :, n:n + 3]), reads=['xb'], writes=['xbo'])
                        P.op('dve', lambda n=n: V.tensor_copy(xb[:, :, 0:3], xbo[:, :, :]), reads=['xbo'], writes=['xb'])
                    gens = [gen_s5(), gen_ml()]
                    while gens:
                        for g_ in list(gens):
                            try:
                                next(g_)
                            except StopIteration:
                                gens.remove(g_)
                    for tt in range(ntt):
                        ts0 = tt * 128
                        P.dma('sp', hin[0:tp, :], src[ts0:ts0 + tp, :], writes=['hin'])
                        for nh in range(2):
                            po = ((pb[7], 'pb7'), (pb[3], 'pb3'), (pb[4], 'pb4'), (pb[5], 'pb5'))[2 * (tt % 2) + nh]
                            for cch in range(12):
                                gk = ('gated%d' % cch) if cch < 4 else 'gatedB'
                                mm(po[0][0:tp, :], gated[:, cch, ts0:ts0 + tp], w_out[:, cch, nh * 512:(nh + 1) * 512], cch == 0, cch == 11,
                                   [gk, 'w_out'], [po[1]])
                            P.op('dve', lambda tt=tt, nh=nh, po=po: V.tensor_tensor(hin[0:tp, nh * 512:(nh + 1) * 512],
                                                                                   hin[0:tp, nh * 512:(nh + 1) * 512], po[0][0:tp, :], ALU.add),
                                 reads=['hin', po[1]], writes=['hin'])
                        P.dma('sp', h1rows(g, si, t0 + ts0, tp), hin[0:tp, :], reads=['hin'], writes=['h1'])
                P.enabled = True
                sfx = '_' + g
                P.dma('sp', O['o_s5re' + sfx][si, :].rearrange("(j q) -> q j", q=128), Hre[:], reads=['Hre'], writes=['o1'])
                P.dma('sp', O['o_s5im' + sfx][si, :].rearrange("(j q) -> q j", q=128), Him[:], reads=['Him'], writes=['o2'])
                for h in range(4):
                    P.dma('sp', O['o_mlc' + sfx][si, h].rearrange("(k p) e -> p k e", p=128), Cst[:, h, :, 0:256],
                          reads=['Cst%d' % h], writes=['o3'])
                    P.dma('sp', O['o_mln' + sfx][si, h].rearrange("(k p) -> p k", p=128), Cst[:, h, :, 256],
                          reads=['Cst%d' % h], writes=['o4'])
                P.dma('sp', O['o_mlm' + sfx][si, :].rearrange("(p o) -> p o", o=1), mcarry[:], reads=['mcarry'], writes=['o5'])
                for t_ in range(3):
                    P.dma('sp', O['o_mlconv' + sfx][si, t_].rearrange("(k p) -> p k", p=128), xbo[:, :, t_], reads=['xbo'], writes=['o6'])
            P.barrier()


        if cfg.layers >= 2:
          with ExitStack() as l1:
            sb = lambda n, s, d: l1.enter_context(nc.sbuf_tensor(n, s, d))
            NP = NPIECE
            w_in = sb("w_in1", [128, 8, 5120], BF16)
            w_sw = sb("w_sw1", [128, 8, 1024], BF16)
            w_out = sb("w_out1", [128, 16, D], BF16)
            lwa = sb("lwa", [128, 8, 128], BF16)
            lwx = sb("lwx", [128, 8, 128], BF16)
            for k in range(8):
                P.dma('pool', w_in[:, k, :], I['od_w_in'][k * 128:(k + 1) * 128, :], writes=['w_in'])
                P.dma('pool', w_sw[:, k, :], I['od_w_qksw'][k * 128:(k + 1) * 128, :], writes=['w_sw'])
            for k in range(16):
                P.dma('pool', w_out[:, k, :], I['od_w_out'][k * 128:(k + 1) * 128, :], writes=['w_out'])
            P.dma('pool', lwa[:], I['lru_w_a'].rearrange("h d e -> d h e"), writes=['lwa'])
            P.dma('pool', lwx[:], I['lru_w_x'].rearrange("h d e -> d h e"), writes=['lwx'])
            gcol = sb("gcol1", [128, 8], F32)
            rnw = sb("rnw", [128, 8], F32)
            cw = sb("cw1", [128, 8, 4], F32)
            cb = sb("cb1", [128, 8], F32)
            bah = sb("bah", [128, 8], F32)
            bxh = sb("bxh", [128, 8], F32)
            ccol = sb("ccol", [128, 8], F32)
            ccol2 = sb("ccol2", [128, 8], F32)
            fnw = sb("fnw", [128, D], F32)
            rmaskL = sb("rmaskL", [128, 4, 128], F32)
            rmaskS = sb("rmaskS", [128, 4, 128], F32)
            rxiL = sb("rxiL", [128, 4, 128], F32)
            rxiS = sb("rxiS", [128, 4, 16], F32)
            rzL = sb("rzL", [128, 4], F32)
            rzS = sb("rzS", [128, 4], F32)
            rglL = sb("rglL", [128, 4], F32)
            rglS = sb("rglS", [128, 4], F32)
            col = lambda nm: I[nm].rearrange("(k p) -> p k", p=128)
            P.dma('sp', gcol[:], I['norm_w'][1, :].rearrange("(k p) -> p k", p=128), writes=['gcol1'])
            P.dma('sp', rnw[:], col('ret_norm_w'), writes=['rnw'])
            for tap in range(4):
                P.dma('sp', cw[:, :, tap], I['lru_conv_w'][tap, :].rearrange("(k p) -> p k", p=128), writes=['cw1'])
            P.dma('sp', cb[:], col('lru_conv_b'), writes=['cb1'])
            P.dma('sp', bah[:], col('lru_b_a'), writes=['bah'])
            P.dma('sp', bxh[:], col('lru_b_x'), writes=['bxh'])
            P.dma('sp', ccol[:], col('lru_lambda'), writes=['ccol'])
            P.dma('sp', fnw[:], I['fnw_b'][:, :], writes=['fnw'])
            P.dma('sp', rmaskL[:], I['c_rmask128'][:, :, :], writes=['rmaskL'])
            P.dma('sp', rmaskS[:], I['c_rmask16'][:, :, :], writes=['rmaskS'])
            P.dma('sp', rxiL[:], I['c_rxi128'][:, :, :], writes=['rxiL'])
            P.dma('sp', rxiS[:], I['c_rxi16'][:, :, :], writes=['rxiS'])
            P.dma('sp', rzL[:], I['c_rzeta128'][:, :], writes=['rzL'])
            P.dma('sp', rzS[:], I['c_rzeta16'][:, :], writes=['rzS'])
            P.dma('sp', rglL[:], I['c_rgl128'][:, :], writes=['rglL'])
            P.dma('sp', rglS[:], I['c_rgl16'][:, :], writes=['rglS'])
            P.op('dve', lambda: V.tensor_scalar(bah[:], bah[:], 0.5, None, ALU.mult), reads=['bah'], writes=['bah'])
            P.op('dve', lambda: V.tensor_scalar(bxh[:], bxh[:], 0.5, None, ALU.mult), reads=['bxh'], writes=['bxh'])
            P.op('act', lambda: A.activation(out=ccol[:], in_=ccol[:], func=AF.Exp, scale=-1.0), reads=['ccol'], writes=['ccol'])
            P.op('act', lambda: A.activation(out=ccol[:], in_=ccol[:], func=AF.Ln, bias=1.0), reads=['ccol'], writes=['ccol'])
            P.op('dve', lambda: V.tensor_scalar(ccol2[:], ccol[:], -4.0, None, ALU.mult), reads=['ccol'], writes=['ccol2'])
            P.op('dve', lambda: V.tensor_scalar(ccol[:], ccol[:], -8.0, None, ALU.mult), reads=['ccol'], writes=['ccol'])

            hin = sb("hin1", [128, D], F32)
            ss = sb("ss1", [128, 4], F32)
            ub = sb("ub1", [128, D], BF16)
            hn = ub
            uT = sb("uT1", [128, 8, NP], BF16)
            gated = sb("gated1", [128, 16, NP], BF16)
            cosF = sb("cosF", [128, NP], F32)
            sinF = sb("sinF", [128, NP], F32)
            qT = sb("qT1", [128, 4, NP], BF16)
            kT = sb("kT1", [128, 4, NP], BF16)
            qxi = sb("qxi", [128, 128], BF16)
            kz = sb("kz", [128, 128], BF16)
            vtok = sb("vtok1", [128, 4, 256], BF16)
            szc = sb("szc", [128, 8, NP], BF16)
            xd = sb("xd", [128, 8, 3 + NP], BF16)
            szd = sb("szd", [128, 8, NP], BF16)
            Sst = sb("Sst", [128, 4, 256], F32)
            Sbf = sb("Sbf", [128, 4, 256], BF16)
            hst = sb("hst", [128, 8], F32)
            f1 = sb("f1", [128, NP], F32)
            f2 = sb("f2", [128, NP], F32)
            f3 = sb("f3", [128, NP], F32)
            f4 = sb("f4", [128, NP], F32)
            f5 = sb("f5", [128, NP], F32)
            xcb = sb("xcb", [128, NP], BF16)
            f1b = sb("f1b", [128, NP], F32)
            f2b = sb("f2b", [128, NP], F32)
            f3b = sb("f3b", [128, NP], F32)
            f4b = sb("f4b", [128, NP], F32)
            f5b = sb("f5b", [128, NP], F32)
            xcbb = sb("xcbb", [128, NP], BF16)
            ob = sb("ob", [128, 256], F32)
            halfT = sb("halfT", [128, NP], F32)
            P.op('pool', lambda: G.memset(halfT[:], 0.5), writes=['halfT'])
            PT = sb("PT1", [128, 128], BF16)
            st6 = sb("st6b", [128, 6], F32)
            st2 = sb("st2b", [128, 8], F32)
            xdo = sb("xdo", [128, 8, 3], F32)
            yout = sb("yout", [128, D], F32)

            print('SBUF remaining after L1 activations', nc.sbuf_bytes_remaining)
            mmreg = [(pb[0], 'pb0'), (pb[1], 'pb1'), (pb[4], 'pb4'), (pb[5], 'pb5')]
            mmi = [0]

            def nextmm():
                r_ = mmreg[mmi[0] % 4]
                mmi[0] += 1
                return r_

            def inproj(f, n, wt=None, wkey='w_in'):
                ps, key = nextmm()
                wt = w_in if wt is None else wt
                for k in range(8):
                    mm(ps[:, 0:n], wt[:, k, f * 128:(f + 1) * 128], uT[:, k, 0:n], k == 0, k == 7, [wkey, 'uT'], [key])
                return ps, key

            for (g, si, pcs) in seqs:
                if g == 'p':
                    P.op('pool', lambda: G.memset(Sst[:], 0.0), writes=['Sst%d' % h_ for h_ in range(4)])
                    P.op('pool', lambda: G.memset(hst[:], 0.0), writes=['hst'])
                    P.op('pool', lambda: G.memset(xd[:, :, 0:3], 0.0), writes=['xd'])
                else:
                    for h in range(4):
                        P.dma('sp', Sst[:, h, :], I['ret_in'][si, h], writes=['Sst%d' % h])
                    P.dma('sp', hst[:], I['lruh_in'][si, :].rearrange("(k p) -> p k", p=128), writes=['hst'])
                    for t_ in range(3):
                        P.dma('pool', xd[:, :, t_], I['lruconv_in'][si, t_].rearrange("(k p) -> p k", p=128), writes=['xd'])
                P.op('act', lambda: A.copy(Sbf[:], Sst[:]), reads=['Sst%d' % h_ for h_ in range(4)], writes=['Sbf%d' % h_ for h_ in range(4)])
                for (t0, n, src, dst, _) in pcs:
                    tp = min(n, 128)
                    ntt = (n + 127) // 128
                    L = tp
                    nch = n // L
                    rmask, rxi, rz, rgl = (rmaskL, rxiL, rzL, rglL) if L == 128 else (rmaskS, rxiS, rzS, rglS)
                    rp0 = t0 if g == 'p' else T
                    P.dma('sp', cosF[:, 0:n], I['ropeF_c'][:, rp0:rp0 + n], writes=['cosF'])
                    P.dma('sp', sinF[:, 0:n], I['ropeF_s'][:, rp0:rp0 + n], writes=['sinF'])
                    for tt in range(ntt):
                        P.dma('sp', hin[0:tp, :], h1rows(g, si, t0 + tt * 128, tp), reads=['h1'], writes=['hin1'])
                        rmsnorm_T(hin, tp, tt, gcol, ub, uT, ss, hkey='hin1', gkey='gcol1')
                    for f in range(8):
                        ps, key = inproj(24 + f, n)
                        P.op('act', lambda f=f, ps=ps: A.copy(xd[:, f, 3:3 + n], ps[:, 0:n]), reads=[key], writes=['xd'])
                    for f in range(8):
                        ps, key = inproj(32 + f, n)
                        P.op('act', lambda f=f, ps=ps: A.activation(out=szd[:, f, 0:n], in_=ps[:, 0:n], func=AF.Silu), reads=[key], writes=['szd'])
                    F1, F2, F3, F4, F5, XCB = f1, f2, f3, f4, f5, xcb
                    def gen_ret():
                        for (base, dstT, sc, dkey) in ((0, qT, 1.0, 'qT1'), (4, kT, 1.0 / math.sqrt(128.0), 'kT1')):
                            for h in range(4):
                                psA, keyA = inproj(base + h, n)
                                psB, keyB = inproj(base + h, n, wt=w_sw, wkey='w_sw')
                                P.op('dve', lambda psA=psA: V.tensor_tensor(yout[:, 0:n], psA[:, 0:n], cosF[:, 0:n], ALU.mult),
                                     reads=[keyA, 'cosF'], writes=['yout'])
                                P.op('dve', lambda psB=psB: V.tensor_tensor(yout[:, 256:256 + n], psB[:, 0:n], sinF[:, 0:n], ALU.mult),
                                     reads=[keyB, 'sinF'], writes=['yout'])
                                P.op('dve', lambda: V.tensor_tensor(yout[:, 0:n], yout[:, 0:n], yout[:, 256:256 + n], ALU.add), reads=['yout', 'yout'], writes=['yout'])
                                P.op('act', lambda h=h, dstT=dstT, sc=sc: A.mul(dstT[:, h, 0:n], yout[:, 0:n], sc), reads=['yout'], writes=[dkey])
                                yield
                        for f in range(8):
                            ps, key = inproj(16 + f, n)
                            P.op('act', lambda f=f, ps=ps: A.activation(out=szc[:, f, 0:n], in_=ps[:, 0:n], func=AF.Silu), reads=[key], writes=['szc'])
                            if f % 2 == 1:
                                yield
                        for c in range(nch):
                            cs = slice(c * L, (c + 1) * L)
                            ts0 = c * 128
                            for h in range(4):
                                ps, key = nextmm()
                                for k in range(8):
                                    mm(ps[0:tp, 0:256], uT[:, k, ts0:ts0 + tp], w_in[:, k, 1024 + h * 256:1024 + (h + 1) * 256], k == 0, k == 7,
                                       ['uT', 'w_in'], [key])
                                P.op('act', lambda ps=ps, h=h: A.copy(vtok[0:tp, h, :], ps[0:tp, 0:256]), reads=[key], writes=['vtok1'])
                            yield
                            for h in range(4):
                                stp, stkey = nextmm()
                                ST = stp[0:L, 0:L]
                                mm(ST, kT[:, h, cs], qT[:, h, cs], True, True, ['kT1', 'qT1'], [stkey])
                                P.op('dve', lambda ST=ST, h=h: V.tensor_tensor(PT[0:L, 0:L], ST, rmask[0:L, h, 0:L], ALU.mult),
                                     reads=[stkey, 'rmaskL', 'rmaskS'], writes=['PT1'])
                                P.op('dve', lambda h=h: V.tensor_tensor(qxi[:, 0:L], qT[:, h, cs], rxi[:, h, 0:L], ALU.mult),
                                     reads=['qT1', 'rxiL', 'rxiS'], writes=['qxi'])
                                mm(pb[7][0:L, 0:256], PT[0:L, 0:L], vtok[0:L, h, :], True, False, ['PT1', 'vtok1'], ['pb7'])
                                mm(pb[7][0:L, 0:256], qxi[:, 0:L], Sbf[:, h, :], False, True, ['qxi', 'Sbf%d' % h], ['pb7'])
                                P.op('act', lambda: A.copy(ob[0:L, 0:256], pb[7][0:L, 0:256]), reads=['pb7'], writes=['ob'])
                                P.op('dve', lambda: V.bn_stats(st6[0:L, :], ob[0:L, 0:256]), reads=['ob'], writes=['st6b'])
                                P.op('dve', lambda: V.bn_aggr(st2[0:L, 0:2], st6[0:L, :]), reads=['st6b'], writes=['st2x'])
                                P.op('dve', lambda: V.tensor_scalar(st2[0:L, 2:3], st2[0:L, 1:2], EPS, None, ALU.add), reads=['st2x'], writes=['st2y'])
                                P.op('pool', lambda: G.tensor_tensor(st2[0:L, 3:4], st2[0:L, 2:3], negh[0:L, :], ALU.pow),
                                     reads=['st2y', 'negh'], writes=['st2z'])
                                P.op('dve', lambda h=h: V.tensor_scalar(hn[0:L, h * 256:(h + 1) * 256], ob[0:L, 0:256], st2[0:L, 0:1], st2[0:L, 3:4],
                                                                        ALU.subtract, ALU.mult),
                                     reads=['ob', 'st2x', 'st2z'], writes=['ub'])
                                pT = pb[2][:].bitcast(BF16)
                                P.op('pe', lambda h=h: PL.transpose(pT[0:L, 0:128], kT[:, h, cs], identb[:, :]),
                                     reads=['kT1', 'identb'], writes=['pb2'])
                                P.op('dve', lambda h=h: V.tensor_scalar(kz[0:L, :], pT[0:L, 0:128], rz[0:L, h:h + 1], None, ALU.mult),
                                     reads=['pb2', 'rzL', 'rzS'], writes=['kz'])
                                mm(pb[3][:, 0:256], kz[0:L, :], vtok[0:L, h, :], True, True, ['kz', 'vtok1'], ['pb3'])
                                P.op('dve', lambda h=h: V.scalar_tensor_tensor(Sst[:, h, :], Sst[:, h, :], rgl[:, h:h + 1], pb[3][:, 0:256], ALU.mult, ALU.add),
                                     reads=['Sst%d' % h, 'rglL', 'rglS', 'pb3'], writes=['Sst%d' % h])
                                P.op('act', lambda h=h: A.copy(Sbf[:, h, :], Sst[:, h, :]), reads=['Sst%d' % h], writes=['Sbf%d' % h])
                                yield
                            pT = pb[2][:].bitcast(BF16)
                            for k in range(8):
                                P.op('pe', lambda k=k: PL.transpose(pT[:, k * 128:k * 128 + tp], hn[0:tp, k * 128:(k + 1) * 128], identb[0:tp, 0:tp]),
                                     reads=['ub', 'identb'], writes=['pb2'])
                            for k in range(8):
                                P.op('dve', lambda k=k: V.scalar_tensor_tensor(gated[:, k, ts0:ts0 + tp], pT[:, k * 128:k * 128 + tp], rnw[:, k:k + 1],
                                                                               szc[:, k, ts0:ts0 + tp], ALU.mult, ALU.mult),
                                     reads=['pb2', 'rnw', 'szc'], writes=['gatedA'])
                            yield
                    def gen_lru():
                        def bufs(k):
                            return ((F1, F2, F3, F4, F5, XCB), '') if k % 2 == 0 else ((f1b, f2b, f3b, f4b, f5b, xcbb), 'b')

                        def ph1(k):
                            (f1, f2, f3, f4, f5, xcb), sfx_ = bufs(k)
                            P.op('dve', lambda: V.tensor_scalar(f1[:, 0:n], xd[:, k, 0:n], cw[:, k, 0:1], cb[:, k:k + 1], ALU.mult, ALU.add),
                                 reads=['xd', 'cw1', 'cb1'], writes=['f1' + sfx_])
                            for tap in range(1, 4):
                                P.op('dve', lambda tap=tap: V.scalar_tensor_tensor(f1[:, 0:n], xd[:, k, tap:tap + n], cw[:, k, tap:tap + 1], f1[:, 0:n],
                                                                                   ALU.mult, ALU.add),
                                     reads=['xd', 'cw1', 'f1' + sfx_], writes=['f1' + sfx_])
                            P.op('act', lambda: A.copy(xcb[:, 0:n], f1[:, 0:n]), reads=['f1' + sfx_], writes=['xcb' + sfx_])
                            psr, keyr = nextmm()
                            mm(psr[:, 0:n], lwa[:, k, :], xcb[:, 0:n], True, True, ['lwa', 'xcb' + sfx_], [keyr])
                            psi, keyi = nextmm()
                            mm(psi[:, 0:n], lwx[:, k, :], xcb[:, 0:n], True, True, ['lwx', 'xcb' + sfx_], [keyi])
                            P.op('act', lambda: A.activation(out=f2[:, 0:n], in_=psr[:, 0:n], func=AF.Tanh, scale=0.5, bias=bah[:, k:k + 1]),
                                 reads=[keyr, 'bah'], writes=['f2' + sfx_])
                            P.op('act', lambda: A.activation(out=f3[:, 0:n], in_=psi[:, 0:n], func=AF.Tanh, scale=0.5, bias=bxh[:, k:k + 1]),
                                 reads=[keyi, 'bxh'], writes=['f3' + sfx_])
                            P.op('act', lambda: A.activation(out=f4[:, 0:n], in_=f2[:, 0:n], func=AF.Exp, scale=ccol2[:, k:k + 1], bias=ccol2[:, k:k + 1]),
                                 reads=['f2' + sfx_, 'ccol2'], writes=['f4' + sfx_])
                            P.op('pool', lambda: G.tensor_scalar(f3[:, 0:n], f3[:, 0:n], 0.5, 0.5, ALU.mult, ALU.add), reads=['f3' + sfx_], writes=['f3' + sfx_])
                            P.op('pool', lambda: G.tensor_tensor(f3[:, 0:n], f3[:, 0:n], f1[:, 0:n], ALU.mult), reads=['f3' + sfx_, 'f1' + sfx_], writes=['f3' + sfx_])
                            P.op('pool', lambda: G.tensor_tensor(f5[:, 0:n], f4[:, 0:n], f4[:, 0:n], ALU.mult), reads=['f4' + sfx_], writes=['f5' + sfx_])
                            P.op('pool', lambda: G.tensor_scalar(f5[:, 0:n], f5[:, 0:n], -1.0, 1.0, ALU.mult, ALU.add), reads=['f5' + sfx_], writes=['f5' + sfx_])

                        def ph2(k):
                            (f1, f2, f3, f4, f5, xcb), sfx_ = bufs(k)
                            P.op('act', lambda: A.activation(out=f5[:, 0:n], in_=f5[:, 0:n], func=AF.Ln), reads=['f5' + sfx_], writes=['f5' + sfx_])
                            P.op('act', lambda: A.activation(out=f5[:, 0:n], in_=f5[:, 0:n], func=AF.Exp, scale=0.5), reads=['f5' + sfx_], writes=['f5' + sfx_])

                        def ph3(k):
                            (f1, f2, f3, f4, f5, xcb), sfx_ = bufs(k)
                            P.op('dve', lambda: V.tensor_tensor(f3[:, 0:n], f3[:, 0:n], f5[:, 0:n], ALU.mult), reads=['f3' + sfx_, 'f5' + sfx_], writes=['f3' + sfx_])
                            P.op('dve', lambda: V.tensor_tensor_scan(f2[:, 0:n], f4[:, 0:n], f3[:, 0:n], hst[:, k:k + 1], ALU.mult, ALU.add),
                                 reads=['f4' + sfx_, 'f3' + sfx_, 'hst', 'f2' + sfx_], writes=['f2' + sfx_])
                            P.op('dve', lambda: V.tensor_copy(hst[:, k:k + 1], f2[:, n - 1:n]), reads=['f2' + sfx_], writes=['hst'])
                            P.op('dve', lambda: V.tensor_tensor(gated[:, 8 + k, 0:n], f2[:, 0:n], szd[:, k, 0:n], ALU.mult),
                                 reads=['f2' + sfx_, 'szd'], writes=['gatedB%d' % k])

                        for p_ in range(4):
                            ph1(2 * p_)
                            yield
                            ph1(2 * p_ + 1)
                            yield
                            ph2(2 * p_)
                            ph2(2 * p_ + 1)
                            yield
                            ph3(2 * p_)
                            ph3(2 * p_ + 1)
                            yield
                    gens = [gen_ret(), gen_lru()]
                    while gens:
                        for g_ in list(gens):
                            try:
                                next(g_)
                            except StopIteration:
                                gens.remove(g_)
                    P.op('dve', lambda n=n: V.tensor_copy(xdo[:, :, :], xd[:, :, n:n + 3]), reads=['xd'], writes=['xdo'])
                    P.op('dve', lambda: V.tensor_copy(xd[:, :, 0:3], xdo[:, :, :]), reads=['xdo'], writes=['xd'])
                    for tt in range(ntt):
                        ts0 = tt * 128
                        P.dma('sp', hin[0:tp, :], h1rows(g, si, t0 + ts0, tp), reads=['h1'], writes=['hin1'])
                        for nh in range(2):
                            po = ((pb[7], 'pb7'), (pb[3], 'pb3'), (pb[6], 'pb6'), (pb[2], 'pb2'))[2 * (tt % 2) + nh]
                            for cch in range(16):
                                gk = 'gatedA' if cch < 8 else 'gatedB%d' % (cch - 8)
                                mm(po[0][0:tp, :], gated[:, cch, ts0:ts0 + tp], w_out[:, cch, nh * 512:(nh + 1) * 512], cch == 0, cch == 15,
                                   [gk, 'w_out'], [po[1]])
                            P.op('dve', lambda nh=nh, po=po: V.tensor_tensor(hin[0:tp, nh * 512:(nh + 1) * 512],
                                                                             hin[0:tp, nh * 512:(nh + 1) * 512], po[0][0:tp, :], ALU.add),
                                 reads=['hin1', po[1]], writes=['hin1'])
                        if not (g == 'p' and t0 == 0):
                            P.op('pool', lambda: G.memset(ss[0:tp, 0:1], 0.0), writes=['ss'])
                            P.op('act', lambda: A.activation(out=ub[0:tp, :], in_=hin[0:tp, :], func=AF.Square, accum_out=ss[0:tp, 0:1]),
                                 reads=['hin1'], writes=['ub', 'ss'])
                            P.op('dve', lambda: V.tensor_scalar(ss[0:tp, 1:2], ss[0:tp, 0:1], 1.0 / D, EPS, ALU.mult, ALU.add), reads=['ss'], writes=['ss'])
                            P.op('pool', lambda: G.tensor_tensor(ss[0:tp, 2:3], ss[0:tp, 1:2], negh[0:tp, :], ALU.pow), reads=['ss', 'negh'], writes=['ss'])
                            P.op('dve', lambda: V.scalar_tensor_tensor(yout[0:tp, :], hin[0:tp, :], ss[0:tp, 2:3], fnw[0:tp, :], ALU.mult, ALU.mult),
                                 reads=['hin1', 'ss', 'fnw'], writes=['yout'])
                            P.dma('sp', dst[ts0:ts0 + tp, :], yout[0:tp, :], reads=['yout'], writes=['yo'])
                sfx = '_' + g
                for h in range(4):
                    P.dma('sp', O['o_ret' + sfx][si, h], Sst[:, h, :], reads=['Sst%d' % h], writes=['o7'])
                P.dma('sp', O['o_lruh' + sfx][si, :].rearrange("(k p) -> p k", p=128), hst[:], reads=['hst'], writes=['o8'])
                for t_ in range(3):
                    P.dma('sp', O['o_lruconv' + sfx][si, t_].rearrange("(k p) -> p k", p=128), xdo[:, :, t_], reads=['xdo'], writes=['o9'])
            P.barrier()
        P.barrier()
    return nc


def host_layout(inp):
    f = lambda a: np.ascontiguousarray(np.asarray(a, dtype=np.float32))
    d = {}
    d['norm_w'] = f(inp['norm_w'])
    d['final_norm_w'] = f(inp['final_norm_w'])
    d['ev_w_in'] = f(inp['ev_w_in'][0])
    d['ev_w_out'] = f(inp['ev_w_out'][0])
    lre, lim, ldt = inp['s5_lambda_re'][0], inp['s5_lambda_im'][0], inp['s5_log_dt'][0]
    def A_from_gp(x):
        x4 = np.asarray(x).reshape(4, 8, 64)
        o = np.broadcast_to(x4[:, :, None, :], (4, 8, 16, 64))
        return f(np.transpose(o, (1, 2, 0, 3)).reshape(128, 4, 64))
    d['lamre_A'] = A_from_gp(lre)
    d['lamim_A'] = A_from_gp(lim)
    d['logdt_A'] = A_from_gp(np.broadcast_to(np.asarray(ldt)[:, None], (32, 64)))
    def A_from_gpc(x):
        x5 = np.asarray(x).reshape(4, 8, 64, 16)
        return f(np.transpose(x5, (1, 3, 0, 2)).reshape(128, 4, 64))
    d['bre_A'] = A_from_gpc(inp['s5_b_re'][0])
    d['bim_A'] = A_from_gpc(inp['s5_b_im'][0])
    def S_from_gp(x):
        x3 = np.asarray(x).reshape(16, 2, 64)
        return f(np.transpose(x3, (1, 2, 0)).reshape(128, 16))
    d['lamre_S'] = S_from_gp(lre)
    d['lamim_S'] = S_from_gp(lim)
    d['logdt_S'] = S_from_gp(np.broadcast_to(np.asarray(ldt)[:, None], (32, 64)))
    def S_from_gcp(x):
        x4 = np.asarray(x).reshape(16, 2, 16, 64)
        return f(np.transpose(x4, (1, 3, 0, 2)).reshape(128, 16, 16))
    d['cre_S'] = S_from_gcp(inp['s5_c_re'][0])
    d['cim_S'] = S_from_gcp(inp['s5_c_im'][0])
    d['s5_d'] = f(inp['s5_d'][0]); d['s5_w_glu'] = f(inp['s5_w_glu'][0]); d['s5_b_glu'] = f(inp['s5_b_glu'][0])
    d['ml_conv_w'] = f(inp['ml_conv_w'][0]); d['ml_conv_b'] = f(inp['ml_conv_b'][0])
    for k in ('ml_wq', 'ml_wk', 'ml_wv'):
        d[k] = f(inp[k][0])
        d[k + 'T'] = f(np.transpose(np.asarray(inp[k][0]), (0, 2, 1)))
    d['ml_w_if'] = f(inp['ml_w_if'][0]); d['ml_b_if'] = f(inp['ml_b_if'][0])
    d['ml_norm_w'] = f(inp['ml_norm_w'][0]); d['ml_skip'] = f(inp['ml_skip'][0])
    w1 = np.asarray(inp['od_w_in'][0])
    d['od_w_in'] = f(w1)
    qk = w1[:, 0:1024].reshape(D, 8, 2, 64)
    d['od_w_qksw'] = f(qk[:, :, ::-1, :].reshape(D, 1024))
    d['fnw_b'] = f(np.broadcast_to(np.asarray(inp['final_norm_w'])[None, :], (128, D)))
    d['od_w_out'] = f(inp['od_w_out'][0]); d['ret_norm_w'] = f(inp['ret_norm_w'][0])
    d['lru_conv_w'] = f(inp['lru_conv_w'][0]); d['lru_conv_b'] = f(inp['lru_conv_b'][0])
    d['lru_w_a'] = f(inp['lru_w_a'][0]); d['lru_b_a'] = f(inp['lru_b_a'][0])
    d['lru_w_x'] = f(inp['lru_w_x'][0]); d['lru_b_x'] = f(inp['lru_b_x'][0]); d['lru_lambda'] = f(inp['lru_lambda'][0])
    d['meta'] = f(inp['meta'])
    return d


def make_in_maps(cfg, inputs, ncores):
    shared = host_layout(inputs)
    shared.update(host_consts())
    T = 16 + cfg.T_x
    pos = np.concatenate([np.arange(T, dtype=np.float32), (16 + cfg.past) + np.arange(16, dtype=np.float32)])
    cf, sf, ct, st = rope_tables(pos)
    shared['ropeF_c'], shared['ropeF_s'], shared['ropeT_c'], shared['ropeT_s'] = cf, sf, np.ascontiguousarray(ct), np.ascontiguousarray(st)
    f = lambda a: np.ascontiguousarray(np.asarray(a, dtype=np.float32))
    maps = []
    for c in range(ncores):
        ps = slice(c * cfg.n_p, (c + 1) * cfg.n_p)
        ss_ = slice(c * cfg.n_s, (c + 1) * cfg.n_s)
        m = dict(shared)
        m['xp'] = f(inputs['x_prompt'][ps])
        m['xs'] = f(inputs['x_sample'][ss_])
        m['s5re_in'] = f(inputs['state_s5_re'][0][ss_]).reshape(cfg.n_s, 2048)
        m['s5im_in'] = f(inputs['state_s5_im'][0][ss_]).reshape(cfg.n_s, 2048)
        m['mlc_in'] = f(inputs['state_ml_c'][0][ss_])
        m['mln_in'] = f(inputs['state_ml_n'][0][ss_])
        m['mlm_in'] = f(inputs['state_ml_m'][0][ss_])
        m['mlconv_in'] = f(inputs['state_ml_conv'][0][ss_])
        m['ret_in'] = f(inputs['state_ret'][0][ss_])
        m['lruh_in'] = f(inputs['state_lru_h'][0][ss_])
        m['lruconv_in'] = f(inputs['state_lru_conv'][0][ss_])
        maps.append(m)
    return maps


def gather(cfg, results):
    cat = lambda k: np.concatenate([np.asarray(r[k]) for r in results], axis=0)
    outs = [cat('yp'), cat('ys')]
    for g in ('p', 's'):
        n = cat('o_s5re_' + g).shape[0]
        outs += [cat('o_s5re_' + g).reshape(1, n, 32, 64), cat('o_s5im_' + g).reshape(1, n, 32, 64),
                 cat('o_mlc_' + g)[None], cat('o_mln_' + g)[None], cat('o_mlm_' + g)[None], cat('o_mlconv_' + g)[None],
                 cat('o_ret_' + g)[None], cat('o_lruh_' + g)[None], cat('o_lruconv_' + g)[None]]
    return tuple(np.ascontiguousarray(o.astype(np.float32)) for o in outs)


def kernel(**inputs):
    ncores = 8
    cfg = Cfg(n_p=2, T_x=4096, n_s=2, past=2048, layers=2)
    nc = build(cfg)
    maps = make_in_maps(cfg, inputs, ncores)
    res = run_bass_kernel_spmd(nc, maps, core_ids=list(range(ncores)))
    return gather(cfg, res.results)
```

```python
import math
from contextlib import ExitStack
import numpy as np
import concourse.bass as bass
import concourse.mybir as mybir
from concourse.bass_utils import run_bass_kernel_spmd

F32 = mybir.dt.float32
BF16 = mybir.dt.bfloat16
AF = mybir.ActivationFunctionType
ALU = mybir.AluOpType

D = 1024
NPIECE = 256
EPS = 1e-6
PI = math.pi


class Prog:
    def __init__(self, nc, ndma=6):
        self.nc = nc
        self.e = dict(pe=nc.tensor, dve=nc.vector, act=nc.scalar, pool=nc.gpsimd, sp=nc.sync)
        self.streams = {k: [] for k in self.e}
        self.cnt = {k: 0 for k in self.e}
        self.waited = {k: {} for k in self.e}
        self.lastw = {}
        self.reads = {}
        self.sems = {}
        self.ndma = ndma
        self.dmai = {q: 0 for q in ('sp', 'act', 'pool')}
        self.dma_last = {}
        self.enabled = True
        self.semnames = ['c_' + k for k in ('pe', 'dve', 'act', 'pool')] + \
            ['d_%s_%d' % (q, j) for q in ('sp', 'act', 'pool') for j in range(ndma)]

    def _wait(self, eng, tok):
        sem, val = tok
        if self.waited[eng].get(sem, 0) >= val:
            return
        self.waited[eng][sem] = val
        self.e[eng].wait_ge(self.sems[sem], val)

    def _deps(self, eng, reads, writes):
        deps = []
        own = 'c_' + eng
        for k in reads:
            t = self.lastw.get(k)
            if t is not None:
                if not (eng == 'pe' and t[0] == own):
                    deps.append(t)
        for k in writes:
            t = self.lastw.get(k)
            if t is not None and t[0] != own:
                deps.append(t)
            for t in self.reads.get(k, ()):
                if t[0] != own:
                    deps.append(t)
        for t in deps:
            self._wait(eng, t)

    def _commit(self, tok, reads, writes):
        for k in writes:
            self.lastw[k] = tok
            self.reads[k] = []
        for k in reads:
            if k not in writes:
                self.reads.setdefault(k, []).append(tok)

    def op(self, eng, fn, reads=(), writes=()):
        if not self.enabled:
            return None
        self._deps(eng, reads, writes)
        self.cnt[eng] += 1
        tok = ('c_' + eng, self.cnt[eng])
        fn().then_inc(self.sems['c_' + eng], 1)
        self._commit(tok, reads, writes)
        return tok

    def dma(self, q, out, in_, reads=(), writes=(), **kw):
        if not self.enabled:
            return None
        self._deps(q, reads, writes)
        i = self.dmai[q]
        self.dmai[q] += 1
        j, r = i % self.ndma, i // self.ndma
        sem = 'd_%s_%d' % (q, j)
        if r > 0:
            self._wait(q, (sem, 16 * r))
        e = self.e[q]
        e.dma_start(out=out, in_=in_, **kw).then_inc(self.sems[sem], 16)
        tok = (sem, 16 * (r + 1))
        self.dma_last[sem] = tok
        self._commit(tok, reads, writes)
        return tok

    def barrier(self):
        toks = set(self.lastw.values())
        for lst in self.reads.values():
            toks.update(lst)
        toks.update(self.dma_last.values())
        for k in ('pe', 'dve', 'act', 'pool'):
            if self.cnt[k] > 0:
                toks.add(('c_' + k, self.cnt[k]))
        for eng in self.e:
            for t in toks:
                self._wait(eng, t)
        self.lastw = {}
        self.reads = {}

    def emit(self, block):
        sems = self.sems

        def run(engname, engobj):
            for it in self.streams[engname]:
                if it[0] == 'w':
                    engobj.wait_ge(sems[it[1]], it[2])
                else:
                    inst = it[1]()
                    inst.then_inc(sems[it[2]], it[3])

        @block.tensor
        def _(x):
            run('pe', x)

        @block.vector
        def _(x):
            run('dve', x)

        @block.scalar
        def _(x):
            run('act', x)

        @block.gpsimd
        def _(x):
            run('pool', x)

        @block.sync
        def _(x):
            run('sp', x)


def host_consts():
    c = {}
    c['c_identb'] = np.eye(128, dtype=np.float32)
    c['c_identf'] = np.eye(128, dtype=np.float32)
    s = np.arange(128)
    c['c_negmask'] = np.where(s[:, None] <= s[None, :], 1.0, 0.0).astype(np.float32)
    sel = np.zeros((4, 4, 128), np.float32)
    for h in range(4):
        sel[h, h, :] = 1.0
    c['c_sel'] = sel
    c['c_ones4'] = np.ones((4, 128), np.float32)
    mA = np.zeros((128, 8), np.float32)
    for g8 in range(8):
        mA[g8 * 16:(g8 + 1) * 16, g8] = 1.0
    c['c_maskA'] = mA
    lg = np.log1p(-np.exp2(-5.0 - np.arange(4, dtype=np.float64)))
    for L in (16, 128):
        idx = np.arange(L, dtype=np.float64)
        diff = idx[None, :] - idx[:, None]
        dm = np.where(diff >= 0, np.exp(lg[:, None, None] * np.maximum(diff, 0.0)), 0.0)
        dmp = np.zeros((128, 4, 128), np.float32)
        dmp[:L, :, :L] = np.transpose(dm, (1, 0, 2))
        c['c_rmask%d' % L] = dmp
        xi = np.exp(lg[:, None] * (idx + 1.0))
        c['c_rxi%d' % L] = np.ascontiguousarray(np.broadcast_to(xi[None], (128, 4, L))).astype(np.float32)
        zeta = np.exp(lg[:, None] * (L - 1.0 - idx))
        zp = np.zeros((128, 4), np.float32)
        zp[:L] = zeta.T
        c['c_rzeta%d' % L] = zp
        c['c_rgl%d' % L] = np.ascontiguousarray(np.broadcast_to(np.exp(lg * L)[None], (128, 4))).astype(np.float32)
    return c


def rope_tables(pos):
    half = 64
    inv = (10000.0 ** (-np.arange(half, dtype=np.float32) / half)).astype(np.float32)
    ang = pos.astype(np.float32)[:, None] * inv[None, :]
    cos = np.cos(ang).astype(np.float32)
    sin = np.sin(ang).astype(np.float32)
    cf = np.concatenate([cos.T, cos.T], axis=0)
    sf = np.concatenate([-sin.T, sin.T], axis=0)
    return np.ascontiguousarray(cf), np.ascontiguousarray(sf), cos, sin


class Cfg:
    def __init__(self, n_p=2, T_x=4096, n_s=2, past=2048, layers=2, debug=False, stage=99):
        self.stage = stage
        self.n_p, self.T_x, self.n_s, self.past, self.layers, self.debug = n_p, T_x, n_s, past, layers, debug
        assert T_x % NPIECE == 0


IN_SPECS = None


def in_specs(cfg):
    n_p, T_x, n_s = cfg.n_p, cfg.T_x, cfg.n_s
    T = 16 + T_x
    sp = {
        'xp': [n_p, T_x, D], 'xs': [n_s, 16, D], 'meta': [16, D],
        's5re_in': [n_s, 2048], 's5im_in': [n_s, 2048],
        'mlc_in': [n_s, 4, 256, 256], 'mln_in': [n_s, 4, 256], 'mlm_in': [n_s, 4], 'mlconv_in': [n_s, 3, D],
        'ret_in': [n_s, 4, 128, 256], 'lruh_in': [n_s, D], 'lruconv_in': [n_s, 3, D],
        'norm_w': [2, D], 'final_norm_w': [D],
        'ev_w_in': [D, 3072], 'ev_w_out': [1536, D],
        'lamre_A': [128, 4, 64], 'lamim_A': [128, 4, 64], 'logdt_A': [128, 4, 64],
        'lamre_S': [128, 16], 'lamim_S': [128, 16], 'logdt_S': [128, 16],
        'bre_A': [128, 4, 64], 'bim_A': [128, 4, 64], 'cre_S': [128, 16, 16], 'cim_S': [128, 16, 16],
        's5_d': [512], 's5_w_glu': [512, 512], 's5_b_glu': [512],
        'ml_conv_w': [4, D], 'ml_conv_b': [D], 'ml_wq': [4, 256, 256], 'ml_wk': [4, 256, 256], 'ml_wv': [4, 256, 256],
        'ml_wqT': [4, 256, 256], 'ml_wkT': [4, 256, 256], 'ml_wvT': [4, 256, 256],
        'ml_w_if': [3072, 8], 'ml_b_if': [8], 'ml_norm_w': [D], 'ml_skip': [D],
        'od_w_in': [D, 5120], 'od_w_qksw': [D, 1024], 'fnw_b': [128, D], 'od_w_out': [2048, D], 'ret_norm_w': [D],
        'lru_conv_w': [4, D], 'lru_conv_b': [D], 'lru_w_a': [8, 128, 128], 'lru_b_a': [D],
        'lru_w_x': [8, 128, 128], 'lru_b_x': [D], 'lru_lambda': [D],
        'ropeF_c': [128, T + 16], 'ropeF_s': [128, T + 16], 'ropeT_c': [T + 16, 64], 'ropeT_s': [T + 16, 64],
    }
    for k, v in host_consts().items():
        sp[k] = list(v.shape)
    return sp


def out_specs(cfg):
    n_p, T_x, n_s = cfg.n_p, cfg.T_x, cfg.n_s
    o = {'yp': [n_p, T_x, D], 'ys': [n_s, 16, D]}
    for g, n in (('p', n_p), ('s', n_s)):
        o['o_s5re_' + g] = [n, 2048]
        o['o_s5im_' + g] = [n, 2048]
        o['o_mlc_' + g] = [n, 4, 256, 256]
        o['o_mln_' + g] = [n, 4, 256]
        o['o_mlm_' + g] = [n, 4]
        o['o_mlconv_' + g] = [n, 3, D]
        o['o_ret_' + g] = [n, 4, 128, 256]
        o['o_lruh_' + g] = [n, D]
        o['o_lruconv_' + g] = [n, 3, D]
    return o


def build(cfg):
    nc = bass.Bass("TRN2", target_bir_lowering=False)
    n_p, T_x, n_s = cfg.n_p, cfg.T_x, cfg.n_s
    T = 16 + T_x
    I = {k: nc.dram_tensor(k, v, F32, kind="ExternalInput").ap() for k, v in in_specs(cfg).items()}
    O = {k: nc.dram_tensor(k, v, F32, kind="ExternalOutput").ap() for k, v in out_specs(cfg).items()}
    hk = "ExternalOutput" if cfg.debug else "Internal"
    h1p = nc.dram_tensor("h1p", [n_p, T, D], F32, kind=hk).ap()
    h1s = nc.dram_tensor("h1s", [n_s, 16, D], F32, kind=hk).ap()
    dbg_o = nc.dram_tensor("dbg_o", [4, 128, 256], F32, kind=hk).ap()

    seqs = []
    for i in range(n_p):
        pcs = [(0, 16, I['meta'][:, :], O['yp'], None)]
        for t in range(0, T_x, NPIECE):
            pcs.append((16 + t, NPIECE, I['xp'][i, t:t + NPIECE, :], O['yp'][i, t:t + NPIECE, :], None))
        seqs.append(('p', i, pcs))
    for i in range(n_s):
        seqs.append(('s', i, [(0, 16, I['xs'][i, :, :], O['ys'][i, :, :], None)]))

    def h1rows(g, i, t0, n):
        return (h1p if g == 'p' else h1s)[i, t0:t0 + n, :]

    with ExitStack() as top:
        top.enter_context(nc.allow_non_contiguous_dma(reason='small strided parameter/state layouts'))
        P = Prog(nc)
        for s in P.semnames:
            P.sems[s] = top.enter_context(nc.semaphore(s))
        pb = [top.enter_context(nc.psum_tensor("pb%d" % i, [128, 512], F32)) for i in range(8)]

        V, G, A, PL = nc.vector, nc.gpsimd, nc.scalar, nc.tensor

        def mm(out, lhsT, rhs, start, stop, reads, writes):
            P.op('pe', lambda: PL.matmul(out, lhsT, rhs, start=start, stop=stop), reads=reads, writes=writes)

        sb0 = lambda n, s, d: top.enter_context(nc.sbuf_tensor(n, s, d))
        identb = sb0("identb", [128, 128], BF16)
        identf = sb0("identf", [128, 128], F32)
        negh = sb0("negh", [128, 1], F32)
        P.dma('pool', identb[:], I['c_identb'][:, :], writes=['identb'])
        P.dma('sp', identf[:], I['c_identf'][:, :], writes=['identf'])
        P.op('pool', lambda: G.memset(negh[:], -0.5), writes=['negh'])

        def rmsnorm_T(hin, tp, tt, gcol, ub, uT, ss, hkey='hin', gkey='gcol'):
            P.op('pool', lambda: G.memset(ss[0:tp, 0:1], 0.0), writes=['ss'])
            P.op('act', lambda: A.activation(out=ub[0:tp, :], in_=hin[0:tp, :], func=AF.Square, accum_out=ss[0:tp, 0:1]),
                 reads=[hkey], writes=['ub', 'ss'])
            P.op('dve', lambda: V.tensor_scalar(ss[0:tp, 1:2], ss[0:tp, 0:1], 1.0 / D, EPS, ALU.mult, ALU.add),
                 reads=['ss'], writes=['ss'])
            P.op('pool', lambda: G.tensor_tensor(ss[0:tp, 2:3], ss[0:tp, 1:2], negh[0:tp, :], ALU.pow),
                 reads=['ss', 'negh'], writes=['ss'])
            P.op('act', lambda: A.activation(out=ub[0:tp, :], in_=hin[0:tp, :], func=AF.Copy, scale=ss[0:tp, 2:3]),
                 reads=[hkey, 'ss'], writes=['ub'])
            pT = pb[2][:].bitcast(BF16)
            for k in range(8):
                P.op('pe', lambda k=k: PL.transpose(pT[:, k * 128:k * 128 + tp], ub[0:tp, k * 128:(k + 1) * 128],
                                                    identb[0:tp, 0:tp]),
                     reads=['ub', 'identb'], writes=['pb2'])
            for k in range(8):
                P.op('act', lambda k=k: A.activation(out=uT[:, k, tt * 128:tt * 128 + tp], in_=pT[:, k * 128:k * 128 + tp],
                                                     func=AF.Copy, scale=gcol[:, k:k + 1]),
                     reads=['pb2', gkey], writes=['uT'])

        with ExitStack() as l0:
            sb = lambda n, s, d: l0.enter_context(nc.sbuf_tensor(n, s, d))
            NP = NPIECE
            w_in = sb("w_in0", [128, 8, 3072], BF16)
            w_out = sb("w_out0", [128, 12, D], BF16)
            w_glu = sb("w_glu", [128, 4, 512], BF16)
            wq = sb("wq", [128, 4, 2, 256], BF16)
            wk = sb("wk", [128, 4, 2, 256], BF16)
            wv = sb("wv", [128, 4, 2, 256], BF16)
            wcif = sb("wcif", [128, 8, 8], BF16)
            wxif = sb("wxif", [128, 8, 8], BF16)
            s5B = sb("s5B", [128, 2, 16, 128], BF16)
            s5C = sb("s5C", [128, 2, 16, 128], BF16)
            LR = 64
            tabc = sb("tabc", [128, 16, LR], F32)
            tabs = sb("tabs", [128, 16, LR], F32)
            rtab = sb("rtab", [128, 16, 2], F32)
            rtab0 = sb("rtab0", [128, 16, LR], F32)
            gcol = sb("gcol0", [128, 8], F32)
            dcol = sb("dcol", [128, 4], F32)
            bgluh = sb("bgluh", [128, 4], F32)
            convw = sb("convw", [128, 8, 4], F32)
            convb = sb("convb", [128, 8], F32)
            mlnw = sb("mlnw", [128, 8], F32)
            skipc = sb("skipc", [128, 8], F32)
            b_i = sb("b_i", [4, 1], F32)
            b_f = sb("b_f", [4, 1], F32)
            negmask = sb("negmask", [128, 128], F32)
            sel = sb("sel", [4, 4, 128], F32)
            ones4 = sb("ones4", [4, 128], F32)
            onesrow = sb("onesrow", [4, NP], F32)

            for k in range(8):
                P.dma('pool', w_in[:, k, :], I['ev_w_in'][k * 128:(k + 1) * 128, :], writes=['w_in'])
            for k in range(12):
                P.dma('pool', w_out[:, k, :], I['ev_w_out'][k * 128:(k + 1) * 128, :], writes=['w_out'])
            P.dma('pool', w_glu[:], I['s5_w_glu'].rearrange("(k p) f -> p k f", p=128), writes=['w_glu'])
            for nm, t_ in (('ml_wq', wq), ('ml_wk', wk), ('ml_wv', wv)):
                for h in range(4):
                    P.dma('pool', t_[:, h, :, :], I[nm][h].rearrange("(k p) e -> p k e", p=128), writes=[nm])
            P.dma('sp', gcol[:], I['norm_w'][0, :].rearrange("(k p) -> p k", p=128), writes=['gcol'])
            P.dma('sp', dcol[:], I['s5_d'].rearrange("(k p) -> p k", p=128), writes=['dcol'])
            P.dma('sp', bgluh[:], I['s5_b_glu'].rearrange("(k p) -> p k", p=128), writes=['bgluh'])
            for tap in range(4):
                P.dma('sp', convw[:, :, tap], I['ml_conv_w'][tap, :].rearrange("(k p) -> p k", p=128), writes=['convw'])
            P.dma('sp', convb[:], I['ml_conv_b'].rearrange("(k p) -> p k", p=128), writes=['convb'])
            P.dma('sp', mlnw[:], I['ml_norm_w'].rearrange("(k p) -> p k", p=128), writes=['mlnw'])
            P.dma('sp', skipc[:], I['ml_skip'].rearrange("(k p) -> p k", p=128), writes=['skipc'])
            P.dma('sp', b_i[:], I['ml_b_if'][0:4].rearrange("(p o) -> p o", o=1), writes=['b_i'])
            P.dma('sp', b_f[:], I['ml_b_if'][4:8].rearrange("(p o) -> p o", o=1), writes=['b_f'])
            P.dma('sp', negmask[:], I['c_negmask'][:, :], writes=['negmask'])
            P.dma('sp', sel[:], I['c_sel'][:, :, :], writes=['sel'])
            P.dma('sp', ones4[:], I['c_ones4'][:, :], writes=['ones4'])
            P.op('pool', lambda: G.memset(onesrow[:], 1.0), writes=['onesrow'])
            P.op('dve', lambda: V.tensor_scalar(bgluh[:], bgluh[:], 0.5, None, ALU.mult), reads=['bgluh'], writes=['bgluh'])
            P.op('dve', lambda: V.tensor_scalar(b_f[:], b_f[:], -1.0, None, ALU.mult), reads=['b_f'], writes=['b_f'])

            print('SBUF remaining before setup', nc.sbuf_bytes_remaining)
            with ExitStack() as su:
                sbs = lambda n, s, d: su.enter_context(nc.sbuf_tensor(n, s, d))
                wqT = sbs("wqT", [128, 4, 2, 256], BF16)
                wkT = sbs("wkT", [128, 4, 2, 256], BF16)
                wvT = sbs("wvT", [128, 4, 2, 256], BF16)
                wif = sbs("wif", [128, 24, 8], BF16)
                for nm, t_ in (('ml_wqT', wqT), ('ml_wkT', wkT), ('ml_wvT', wvT)):
                    for h in range(4):
                        P.dma('pool', t_[:, h, :, :], I[nm][h].rearrange("(k p) e -> p k e", p=128), writes=[nm])
                P.dma('pool', wif[:], I['ml_w_if'].rearrange("(k p) g -> p k g", p=128), writes=['wif'])
                for h in range(4):
                    for dt_ in range(2):
                        o_ = pb[0][:, 0:8]
                        i_ = 0
                        for (wT, nm, off) in ((wqT, 'ml_wqT', 0), (wkT, 'ml_wkT', 8)):
                            for ke in range(2):
                                mm(o_, wT[:, h, ke, dt_ * 128:(dt_ + 1) * 128], wif[:, off + h * 2 + ke, :],
                                   i_ == 0, i_ == 3, [nm, 'wif'], ['pb0'])
                                i_ += 1
                        P.op('act', lambda h=h, dt_=dt_: A.copy(wcif[:, h * 2 + dt_, :], pb[0][:, 0:8]),
                             reads=['pb0'], writes=['wcif'])
                        o2 = pb[1][:, 0:8]
                        for ke in range(2):
                            mm(o2, wvT[:, h, ke, dt_ * 128:(dt_ + 1) * 128], wif[:, 16 + h * 2 + ke, :],
                               ke == 0, ke == 1, ['ml_wvT', 'wif'], ['pb1'])
                        P.op('act', lambda h=h, dt_=dt_: A.copy(wxif[:, h * 2 + dt_, :], pb[1][:, 0:8]),
                             reads=['pb1'], writes=['wxif'])

                def S(n, shp):
                    return sbs(n, shp, F32)

                def trig(name, ang_ap, shp, key):
                    cosT = S(name + "_cos", shp)
                    sinT = S(name + "_sin", shp)
                    sh = S(name + "_sh", shp)
                    acc = S(name + "_acc", shp)
                    tmp = S(name + "_tm", shp)
                    for (dst, shift) in ((cosT, PI / 2), (sinT, 0.0)):
                        P.op('dve', lambda shift=shift: V.tensor_scalar(sh[:], ang_ap, shift, None, ALU.add),
                             reads=[key], writes=[name + 'sh'])
                        P.op('dve', lambda: V.tensor_copy(acc[:], sh[:]), reads=[name + 'sh'], writes=[name + 'win_re'])
                        for m_ in range(1, 7):
                            thr = (2 * m_ - 1) * PI
                            P.op('dve', lambda thr=thr: V.tensor_scalar(tmp[:], sh[:], thr, -2 * PI, ALU.is_ge, ALU.mult),
                                 reads=[name + 'sh'], writes=[name + 'tm'])
                            P.op('dve', lambda: V.tensor_tensor(acc[:], acc[:], tmp[:], ALU.add),
                                 reads=[name + 'win_re', name + 'tm'], writes=[name + 'win_re'])
                        P.op('act', lambda dst=dst: A.activation(out=dst[:], in_=acc[:], func=AF.Sin),
                             reads=[name + 'win_re'], writes=[name + ('c' if dst is cosT else 's')])
                    return cosT, sinT

                shA = [128, 4, 64]
                lamreA = S("lamreA", shA); lamimA = S("lamimA", shA); dtA = S("dtA", shA)
                breA = S("breA", shA); bimA = S("bimA", shA)
                P.dma('sp', lamreA[:], I['lamre_A'][:, :, :], writes=['lamreA'])
                P.dma('sp', lamimA[:], I['lamim_A'][:, :, :], writes=['lamimA'])
                P.dma('sp', dtA[:], I['logdt_A'][:, :, :], writes=['dtA'])
                P.dma('sp', breA[:], I['bre_A'][:, :, :], writes=['breA'])
                P.dma('sp', bimA[:], I['bim_A'][:, :, :], writes=['bimA'])
                P.op('act', lambda: A.activation(out=dtA[:], in_=dtA[:], func=AF.Exp), reads=['dtA'], writes=['dtA'])
                magA = S("magA", shA); angA = S("angA", shA)
                P.op('dve', lambda: V.tensor_tensor(magA[:], lamreA[:], dtA[:], ALU.mult), reads=['lamreA', 'dtA'], writes=['magA'])
                P.op('act', lambda: A.activation(out=magA[:], in_=magA[:], func=AF.Exp), reads=['magA'], writes=['magA'])
                P.op('dve', lambda: V.tensor_tensor(angA[:], lamimA[:], dtA[:], ALU.mult), reads=['lamimA', 'dtA'], writes=['angA'])
                cosA, sinA = trig("tA", angA[:], shA, 'angA')
                abre = S("abre", shA); abim = S("abim", shA)
                P.op('dve', lambda: V.tensor_tensor(abre[:], magA[:], cosA[:], ALU.mult), reads=['magA', 'tAc'], writes=['abre'])
                P.op('dve', lambda: V.tensor_tensor(abim[:], magA[:], sinA[:], ALU.mult), reads=['magA', 'tAs'], writes=['abim'])
                den = S("denA", shA); t1 = S("t1A", shA); t2 = S("t2A", shA); kre = S("kreA", shA); kim = S("kimA", shA)
                P.op('dve', lambda: V.tensor_tensor(den[:], lamreA[:], lamreA[:], ALU.mult), reads=['lamreA'], writes=['denA'])
                P.op('dve', lambda: V.tensor_tensor(t1[:], lamimA[:], lamimA[:], ALU.mult), reads=['lamimA'], writes=['t1A'])
                P.op('dve', lambda: V.tensor_tensor(den[:], den[:], t1[:], ALU.add), reads=['denA', 't1A'], writes=['denA'])
                P.op('dve', lambda: V.reciprocal(den[:], den[:]), reads=['denA'], writes=['denA'])
                P.op('dve', lambda: V.tensor_scalar(abre[:], abre[:], -1.0, None, ALU.add), reads=['abre'], writes=['abre'])
                P.op('dve', lambda: V.tensor_tensor(t1[:], abre[:], lamreA[:], ALU.mult), reads=['abre', 'lamreA'], writes=['t1A'])
                P.op('dve', lambda: V.tensor_tensor(t2[:], abim[:], lamimA[:], ALU.mult), reads=['abim', 'lamimA'], writes=['t2A'])
                P.op('dve', lambda: V.tensor_tensor(t1[:], t1[:], t2[:], ALU.add), reads=['t1A', 't2A'], writes=['t1A'])
                P.op('dve', lambda: V.tensor_tensor(kre[:], t1[:], den[:], ALU.mult), reads=['t1A', 'denA'], writes=['kreA'])
                P.op('dve', lambda: V.tensor_tensor(t1[:], abim[:], lamreA[:], ALU.mult), reads=['abim', 'lamreA'], writes=['t1A'])
                P.op('dve', lambda: V.tensor_tensor(t2[:], abre[:], lamimA[:], ALU.mult), reads=['abre', 'lamimA'], writes=['t2A'])
                P.op('dve', lambda: V.tensor_tensor(t1[:], t1[:], t2[:], ALU.subtract), reads=['t1A', 't2A'], writes=['t1A'])
                P.op('dve', lambda: V.tensor_tensor(kim[:], t1[:], den[:], ALU.mult), reads=['t1A', 'denA'], writes=['kimA'])
                bbre = S("bbre", shA); bbim = S("bbim", shA)
                P.op('dve', lambda: V.tensor_tensor(t1[:], kre[:], breA[:], ALU.mult), reads=['kreA', 'breA'], writes=['t1A'])
                P.op('dve', lambda: V.tensor_tensor(t2[:], kim[:], bimA[:], ALU.mult), reads=['kimA', 'bimA'], writes=['t2A'])
                P.op('dve', lambda: V.tensor_tensor(bbre[:], t1[:], t2[:], ALU.subtract), reads=['t1A', 't2A'], writes=['bbre'])
                P.op('dve', lambda: V.tensor_tensor(t1[:], kre[:], bimA[:], ALU.mult), reads=['kreA', 'bimA'], writes=['t1A'])
                P.op('dve', lambda: V.tensor_tensor(t2[:], kim[:], breA[:], ALU.mult), reads=['kimA', 'breA'], writes=['t2A'])
                P.op('dve', lambda: V.tensor_tensor(bbim[:], t1[:], t2[:], ALU.add), reads=['t1A', 't2A'], writes=['bbim'])
                maskA = S("maskA", [128, 8])
                P.dma('sp', maskA[:], I['c_maskA'][:, :], writes=['maskA'])
                for j in range(16):
                    kc, jl = j // 4, j % 4
                    for h in range(2):
                        g8 = 2 * jl + h
                        for part, src, key in ((0, bbre, 'bbre'), (1, bbim, 'bbim')):
                            P.op('dve', lambda j=j, kc=kc, h=h, g8=g8, part=part, src=src: V.tensor_scalar(
                                s5B[:, part, j, h * 64:(h + 1) * 64], src[:, kc, :], maskA[:, g8:g8 + 1], None, ALU.mult),
                                reads=[key, 'maskA'], writes=['s5B'])
                creS = S("creS", [128, 16, 16]); cimS = S("cimS", [128, 16, 16])
                P.dma('sp', creS[:], I['cre_S'][:, :, :], writes=['creS'])
                P.dma('sp', cimS[:], I['cim_S'][:, :, :], writes=['cimS'])
                P.op('pool', lambda: G.memset(s5C[:], 0.0), writes=['s5C'])
                for j in range(16):
                    jl = j % 4
                    for h in range(2):
                        g8 = 2 * jl + h
                        P.op('dve', lambda j=j, h=h, g8=g8: V.tensor_copy(
                            s5C[h * 64:(h + 1) * 64, 0, j, g8 * 16:(g8 + 1) * 16], creS[h * 64:(h + 1) * 64, j, :]),
                            reads=['creS'], writes=['s5C'])
                        P.op('dve', lambda j=j, h=h, g8=g8: V.tensor_scalar(
                            s5C[h * 64:(h + 1) * 64, 1, j, g8 * 16:(g8 + 1) * 16], cimS[h * 64:(h + 1) * 64, j, :],
                            -1.0, None, ALU.mult), reads=['cimS'], writes=['s5C'])
                shS = [128, 16]
                lamreS = S("lamreS", shS); lamimS = S("lamimS", shS); dtS = S("dtS", shS)
                P.dma('sp', lamreS[:], I['lamre_S'][:, :], writes=['lamreS'])
                P.dma('sp', lamimS[:], I['lamim_S'][:, :], writes=['lamimS'])
                P.dma('sp', dtS[:], I['logdt_S'][:, :], writes=['dtS'])
                P.op('act', lambda: A.activation(out=dtS[:], in_=dtS[:], func=AF.Exp), reads=['dtS'], writes=['dtS'])
                rmag = S("rmag", shS); angS = S("angS", shS)
                P.op('dve', lambda: V.tensor_tensor(rmag[:], lamreS[:], dtS[:], ALU.mult), reads=['lamreS', 'dtS'], writes=['rmag'])
                P.op('act', lambda: A.activation(out=rmag[:], in_=rmag[:], func=AF.Exp), reads=['rmag'], writes=['rmag'])
                P.op('dve', lambda: V.tensor_tensor(angS[:], lamimS[:], dtS[:], ALU.mult), reads=['lamimS', 'dtS'], writes=['angS'])
                cosS, sinS = trig("tS", angS[:], shS, 'angS')
                onesL = S("onesL", [128, LR])
                P.op('pool', lambda: G.memset(onesL[:], 1.0), writes=['onesL'])
                for j in range(16):
                    P.op('dve', lambda j=j: V.tensor_scalar(rtab[:, j, :], onesL[:, 0:2], rmag[:, j:j + 1], None, ALU.mult),
                         reads=['rmag', 'onesL'], writes=['rtab'])
                for j in range(16):
                    P.op('dve', lambda j=j: V.tensor_scalar(rtab0[:, j, :], onesL[:], rmag[:, j:j + 1], None, ALU.mult),
                         reads=['rmag', 'onesL'], writes=['rtab0'])
                P.op('pool', lambda: G.memset(rtab0[:, :, 0:1], 0.0), reads=['rtab0'], writes=['rtab0'])
                P.op('dve', lambda: V.tensor_copy(tabc[:, :, 0:1], cosS[:].unsqueeze(2)), reads=['tSc'], writes=['tabc'])
                P.op('dve', lambda: V.tensor_copy(tabs[:, :, 0:1], sinS[:].unsqueeze(2)), reads=['tSs'], writes=['tabs'])
                d1 = S("dbl1", [128, LR // 2])
                m_ = 1
                while m_ < LR:
                    for j in range(16):
                        cm, sm = tabc[:, j, m_ - 1:m_], tabs[:, j, m_ - 1:m_]
                        lo_c, lo_s = tabc[:, j, 0:m_], tabs[:, j, 0:m_]
                        hi_c, hi_s = tabc[:, j, m_:2 * m_], tabs[:, j, m_:2 * m_]
                        a1 = d1[:, 0:m_]
                        P.op('dve', lambda: V.tensor_scalar(a1, lo_s, sm, None, ALU.mult), reads=['tabs'], writes=['dbl1'])
                        P.op('dve', lambda: V.scalar_tensor_tensor(hi_c, lo_c, cm, a1, ALU.mult, ALU.subtract),
                             reads=['tabc', 'dbl1'], writes=['tabc'])
                        P.op('dve', lambda: V.tensor_scalar(a1, lo_c, sm, None, ALU.mult), reads=['tabc', 'tabs'], writes=['dbl1'])
                        P.op('dve', lambda: V.scalar_tensor_tensor(hi_s, lo_s, cm, a1, ALU.mult, ALU.add),
                             reads=['tabs', 'tabc', 'dbl1'], writes=['tabs'])
                    m_ *= 2
                P.barrier()
            print('SBUF remaining after weights/setup', nc.sbuf_bytes_remaining)
            L_ = {}
            hin = sb("hin", [128, D], F32)
            ss = sb("ss", [128, 4], F32)
            ub = sb("ub", [128, D], BF16)
            uT = sb("uT", [128, 8, NP], BF16)
            gated = sb("gated", [128, 12, NP], BF16)
            ua = sb("ua", [128, 4, NP], BF16)
            sza = sb("sza", [128, 4, NP], BF16)
            tA = sb("tA_", [128, NP], F32)
            tB = sb("tB_", [128, NP], F32)
            wre = sb("wre", [128, NP // LR, 4, LR], F32)
            wim = sb("wim", [128, NP // LR, 4, LR], F32)
            tA2 = sb("tA2_", [128, NP], F32)
            tB2 = sb("tB2_", [128, NP], F32)
            hsre = sb("hsre", [128, 4, NP], BF16)
            hsim = sb("hsim", [128, 4, NP], BF16)
            Hre = sb("Hre", [128, 16], F32)
            Him = sb("Him", [128, 16], F32)
            ch1 = sb("ch1", [128, 4], F32)
            ch2 = sb("ch2", [128, 4], F32)
            ch3 = sb("ch3", [128, 4], F32)
            ch4 = sb("ch4", [128, 4], F32)
            y1 = sb("y1", [128, NP], F32)
            y2 = sb("y2", [128, NP + 1], F32)
            y3 = sb("y3", [128, NP + 1], F32)
            yg = sb("yg", [128, 4, NP], BF16)
            xb = sb("xb", [128, 8, 3 + NP], BF16)
            szb = sb("szb", [128, 8, NP], BF16)
            acc = sb("acc", [128, NP], F32)
            xc = sb("xc", [128, 8, NP], BF16)
            qT = uT
            kT = sb("kT", [128, 8, NP], BF16)
            vtok = sb("vtok", [128, 4, 264], BF16)
            ktok = sb("ktok", [128, 4, 256], BF16)
            Cst = sb("Cst", [128, 4, 2, 264], F32)
            Cbf = sb("Cbf", [128, 4, 2, 264], BF16)
            r_ig = sb("r_ig", [4, NP], F32)
            r_lf = sb("r_lf", [4, NP], F32)
            Mfull = sb("Mfull", [4, NP + 1], F32)
            negM = sb("negM", [4, NP + 1], F32)
            r_wi = sb("r_wi", [4, NP], F32)
            r_g = sb("r_g", [4, NP], F32)
            r_emm = sb("r_emm", [4, NP], F32)
            r_wend = sb("r_wend", [4, NP], F32)
            r_e = r_wend
            mcarry = sb("mcarry", [4, 1], F32)
            dg4 = sb("dg4", [4, 4], F32)
            cols = sb("cols", [128, 20], F32)
            Esb = y1[:, 0:128]
            Dm = y1[:, 128:256]
            PT = sb("PT", [128, 128], BF16)
            kw = sb("kw", [128, 256], BF16)
            inter = y2
            tot = y3
            st6 = sb("st6", [128, 6], F32)
            st2 = sb("st2", [128, 8], F32)
            hn = ub
            ytmp = sb("ytmp", [128, 128], F32)
            xbo = sb("xbo", [128, 8, 3], F32)

            print('SBUF remaining after L0 activations', nc.sbuf_bytes_remaining)
            P.op('pool', lambda: G.memset(vtok[:], 1.0), writes=['vtok'])

            mmreg = [(pb[0], 'pb0'), (pb[1], 'pb1')]
            mmi = [0]

            def nextmm():
                r_ = mmreg[mmi[0] % 2]
                mmi[0] += 1
                return r_

            def inproj(f, n):
                ps, key = nextmm()
                for k in range(8):
                    mm(ps[:, 0:n], w_in[:, k, f * 128:(f + 1) * 128], uT[:, k, 0:n], k == 0, k == 7, ['w_in', 'uT'], [key])
                return ps, key

            for (g, si, pcs) in seqs:
                if g == 'p':
                    P.op('pool', lambda: G.memset(Hre[:], 0.0), writes=['Hre'])
                    P.op('pool', lambda: G.memset(Him[:], 0.0), writes=['Him'])
                    P.op('pool', lambda: G.memset(Cst[:], 0.0), writes=['Cst%d' % h_ for h_ in range(4)])
                    P.op('pool', lambda: G.memset(mcarry[:], 0.0), writes=['mcarry'])
                    P.op('pool', lambda: G.memset(xb[:, :, 0:3], 0.0), writes=['xb'])
                else:
                    P.dma('sp', Hre[:], I['s5re_in'][si, :].rearrange("(j q) -> q j", q=128), writes=['Hre'])
                    P.dma('sp', Him[:], I['s5im_in'][si, :].rearrange("(j q) -> q j", q=128), writes=['Him'])
                    for h in range(4):
                        P.dma('sp', Cst[:, h, :, 0:256], I['mlc_in'][si, h].rearrange("(k p) e -> p k e", p=128), writes=['Cst%d' % h])
                        P.dma('sp', Cst[:, h, :, 256], I['mln_in'][si, h].rearrange("(k p) -> p k", p=128), writes=['Cst%d' % h])
                    P.dma('sp', mcarry[:], I['mlm_in'][si, :].rearrange("(p o) -> p o", o=1), writes=['mcarry'])
                    for t_ in range(3):
                        P.dma('pool', xb[:, :, t_], I['mlconv_in'][si, t_].rearrange("(k p) -> p k", p=128), writes=['xb'])
                P.op('act', lambda: A.copy(Cbf[:], Cst[:]), reads=['Cst%d' % h_ for h_ in range(4)], writes=['Cbf%d' % h_ for h_ in range(4)])

                for (t0, n, src, _, _) in pcs:
                    tp = min(n, 128)
                    ntt = (n + 127) // 128
                    L = tp
                    nch = n // L
                    for tt in range(ntt):
                        P.dma('sp', hin[0:tp, :], src[tt * 128:tt * 128 + tp, :], writes=['hin'])
                        rmsnorm_T(hin, tp, tt, gcol, ub, uT, ss)
                    for f in range(4):
                        ps, key = inproj(f, n)
                        P.op('act', lambda f=f, ps=ps: A.copy(ua[:, f, 0:n], ps[:, 0:n]), reads=[key], writes=['ua'])
                    for f in range(4):
                        ps, key = inproj(4 + f, n)
                        P.op('act', lambda f=f, ps=ps: A.activation(out=sza[:, f, 0:n], in_=ps[:, 0:n], func=AF.Silu),
                             reads=[key], writes=['sza'])

                    def gen_s5():
                        Ls = min(n, LR)
                        ncs = n // Ls
                        def v3(ap2):
                            return ap2.rearrange("p (c l) -> p c l", c=ncs)

                        for kc in range(4):
                            tcs, tss = [], []
                            for jl in range(4):
                                j = 4 * kc + jl
                                sreg = (pb[4], 'pb4') if j % 2 == 0 else (pb[5], 'pb5')
                                pre, pim = sreg[0][:, 0:n], sreg[0][:, 256:256 + n]
                                mm(pre, s5B[:, 0, j, :], ua[:, kc, 0:n], True, True, ['s5B', 'ua'], [sreg[1]])
                                mm(pim, s5B[:, 1, j, :], ua[:, kc, 0:n], True, True, ['s5B', 'ua'], [sreg[1]])
                                tc_ = tabc[:, j, 0:Ls].unsqueeze(1).to_broadcast([128, ncs, Ls])
                                ts_ = tabs[:, j, 0:Ls].unsqueeze(1).to_broadcast([128, ncs, Ls])
                                tcs.append(tc_); tss.append(ts_)
                                P.op('dve', lambda pre=pre, tc_=tc_: V.tensor_tensor(v3(tA[:, 0:n]), v3(pre), tc_, ALU.mult),
                                     reads=[sreg[1], 'tabc'], writes=['tA'])
                                P.op('dve', lambda pim=pim, ts_=ts_: V.tensor_tensor(v3(tB[:, 0:n]), v3(pim), ts_, ALU.mult),
                                     reads=[sreg[1], 'tabs'], writes=['tB'])
                                P.op('pool', lambda jl=jl: G.tensor_tensor(wre[:, 0:ncs, jl, 0:Ls], v3(tA[:, 0:n]), v3(tB[:, 0:n]), ALU.add),
                                     reads=['tA', 'tB'], writes=['wre%d' % jl])
                                P.op('dve', lambda pim=pim, tc_=tc_: V.tensor_tensor(v3(tA2[:, 0:n]), v3(pim), tc_, ALU.mult),
                                     reads=[sreg[1], 'tabc'], writes=['tA2'])
                                P.op('dve', lambda pre=pre, ts_=ts_: V.tensor_tensor(v3(tB2[:, 0:n]), v3(pre), ts_, ALU.mult),
                                     reads=[sreg[1], 'tabs'], writes=['tB2'])
                                P.op('pool', lambda jl=jl: G.tensor_tensor(wim[:, 0:ncs, jl, 0:Ls], v3(tA2[:, 0:n]), v3(tB2[:, 0:n]), ALU.subtract),
                                     reads=['tA2', 'tB2'], writes=['wim%d' % jl])
                                if jl % 2 == 1:
                                    yield
                            j0 = 4 * kc
                            for c in range(ncs):
                                cs = slice(c * Ls, (c + 1) * Ls)
                                if c == 0:
                                    i4r, i4i, rk = Hre[:, j0:j0 + 4], Him[:, j0:j0 + 4], ['Hre', 'Him']
                                else:
                                    i4r, i4i, rk = ch1[:, 0:4], ch2[:, 0:4], ['ch1', 'ch2']
                                w4r_ = ['wre%d' % i_ for i_ in range(4)]
                                w4i_ = ['wim%d' % i_ for i_ in range(4)]
                                r4 = rtab[:, j0:j0 + 4, 1]
                                c0 = c * Ls
                                P.op('dve', lambda: V.tensor_tensor(ch3[:, 0:4], i4r, r4, ALU.mult), reads=rk + ['rtab'], writes=['ch3'])
                                P.op('dve', lambda: V.tensor_tensor(ch4[:, 0:4], i4i, r4, ALU.mult), reads=rk + ['rtab'], writes=['ch4'])
                                P.op('dve', lambda: V.tensor_tensor(wre[:, c, :, 0], wre[:, c, :, 0], ch3[:, 0:4], ALU.add), reads=w4r_ + ['ch3'], writes=w4r_)
                                P.op('dve', lambda: V.tensor_tensor(wim[:, c, :, 0], wim[:, c, :, 0], ch4[:, 0:4], ALU.add), reads=w4i_ + ['ch4'], writes=w4i_)
                                if Ls == LR:
                                    d0 = rtab0[:, j0:j0 + 4, :].rearrange("p j l -> p (j l)")
                                    P.op('dve', lambda: V.tensor_tensor_scan(wre[:, c, :, :].rearrange("p j l -> p (j l)"), d0,
                                                                             wre[:, c, :, :].rearrange("p j l -> p (j l)"), 0.0, ALU.mult, ALU.add),
                                         reads=w4r_ + ['rtab0'], writes=w4r_)
                                    P.op('dve', lambda: V.tensor_tensor_scan(wim[:, c, :, :].rearrange("p j l -> p (j l)"), d0,
                                                                             wim[:, c, :, :].rearrange("p j l -> p (j l)"), 0.0, ALU.mult, ALU.add),
                                         reads=w4i_ + ['rtab0'], writes=w4i_)
                                else:
                                    for jl in range(4):
                                        P.op('dve', lambda jl=jl: V.tensor_tensor_scan(wre[:, c, jl, 0:Ls], rtab0[:, j0 + jl, 0:Ls], wre[:, c, jl, 0:Ls], 0.0,
                                                                                       ALU.mult, ALU.add), reads=w4r_ + ['rtab0'], writes=w4r_)
                                        P.op('dve', lambda jl=jl: V.tensor_tensor_scan(wim[:, c, jl, 0:Ls], rtab0[:, j0 + jl, 0:Ls], wim[:, c, jl, 0:Ls], 0.0,
                                                                                       ALU.mult, ALU.add), reads=w4i_ + ['rtab0'], writes=w4i_)
                                e_ = (c + 1) * Ls - 1
                                last = (c == ncs - 1)
                                wlr, wli = wre[:, c, :, Ls - 1], wim[:, c, :, Ls - 1]
                                cL4, sL4 = tabc[:, j0:j0 + 4, Ls - 1], tabs[:, j0:j0 + 4, Ls - 1]
                                dre = Hre[:, j0:j0 + 4] if last else ch1[:, 0:4]
                                dim_ = Him[:, j0:j0 + 4] if last else ch2[:, 0:4]
                                kre_ = 'Hre' if last else 'ch1'
                                kim_ = 'Him' if last else 'ch2'
                                w4r = ['wre%d' % i_ for i_ in range(4)]
                                w4i = ['wim%d' % i_ for i_ in range(4)]
                                P.op('dve', lambda: V.tensor_tensor(ch3[:, 0:4], wli, sL4, ALU.mult), reads=w4i + ['tabs'], writes=['ch3'])
                                P.op('dve', lambda: V.tensor_tensor(ch4[:, 0:4], wlr, sL4, ALU.mult), reads=w4r + ['tabs'], writes=['ch4'])
                                P.op('dve', lambda: V.tensor_tensor(dre, wlr, cL4, ALU.mult), reads=w4r + ['tabc', kre_], writes=[kre_])
                                P.op('dve', lambda: V.tensor_tensor(dim_, wli, cL4, ALU.mult), reads=w4i + ['tabc', kim_], writes=[kim_])
                                P.op('dve', lambda: V.tensor_tensor(dre, dre, ch3[:, 0:4], ALU.subtract), reads=[kre_, 'ch3'], writes=[kre_])
                                P.op('dve', lambda: V.tensor_tensor(dim_, dim_, ch4[:, 0:4], ALU.add), reads=[kim_, 'ch4'], writes=[kim_])
                                if c % 2 == 1 or last:
                                    yield
                            for jl in range(4):
                                tc_, ts_ = tcs[jl], tss[jl]
                                P.op('dve', lambda jl=jl, tc_=tc_: V.tensor_tensor(v3(tA[:, 0:n]), wre[:, 0:ncs, jl, 0:Ls], tc_, ALU.mult),
                                     reads=['wre%d' % jl, 'tabc'], writes=['tA'])
                                P.op('dve', lambda jl=jl, ts_=ts_: V.tensor_tensor(v3(tB[:, 0:n]), wim[:, 0:ncs, jl, 0:Ls], ts_, ALU.mult),
                                     reads=['wim%d' % jl, 'tabs'], writes=['tB'])
                                P.op('pool', lambda jl=jl: G.tensor_tensor(hsre[:, jl, 0:n], tA[:, 0:n], tB[:, 0:n], ALU.subtract),
                                     reads=['tA', 'tB'], writes=['hsre%d' % jl])
                                P.op('dve', lambda jl=jl, tc_=tc_: V.tensor_tensor(v3(tA2[:, 0:n]), wim[:, 0:ncs, jl, 0:Ls], tc_, ALU.mult),
                                     reads=['wim%d' % jl, 'tabc'], writes=['tA2'])
                                P.op('dve', lambda jl=jl, ts_=ts_: V.tensor_tensor(v3(tB2[:, 0:n]), wre[:, 0:ncs, jl, 0:Ls], ts_, ALU.mult),
                                     reads=['wre%d' % jl, 'tabs'], writes=['tB2'])
                                P.op('pool', lambda jl=jl: G.tensor_tensor(hsim[:, jl, 0:n], tA2[:, 0:n], tB2[:, 0:n], ALU.add),
                                     reads=['tA2', 'tB2'], writes=['hsim%d' % jl])
                                if jl % 2 == 1:
                                    yield
                            ps, key = nextmm()
                            for jl in range(4):
                                j = 4 * kc + jl
                                mm(ps[:, 0:n], s5C[:, 0, j, :], hsre[:, jl, 0:n], jl == 0, False, ['s5C', 'hsre%d' % jl], [key])
                                mm(ps[:, 0:n], s5C[:, 1, j, :], hsim[:, jl, 0:n], False, jl == 3, ['s5C', 'hsim%d' % jl], [key])
                            P.op('dve', lambda kc=kc, ps=ps: V.scalar_tensor_tensor(y1[:, 0:n], ua[:, kc, 0:n], dcol[:, kc:kc + 1],
                                                                                   ps[:, 0:n], ALU.mult, ALU.add),
                                 reads=['ua', 'dcol', key], writes=['y1'])
                            P.op('pool', lambda: G.tensor_tensor(y2[:, 0:n], y1[:, 0:n], y1[:, 0:n], ALU.mult), reads=['y1'], writes=['y2'])
                            P.op('pool', lambda: G.tensor_scalar(y2[:, 0:n], y2[:, 0:n], 0.044715, 1.0, ALU.mult, ALU.add),
                                 reads=['y2'], writes=['y2'])
                            P.op('pool', lambda: G.tensor_tensor(y2[:, 0:n], y2[:, 0:n], y1[:, 0:n], ALU.mult), reads=['y2', 'y1'], writes=['y2'])
                            P.op('act', lambda: A.activation(out=y3[:, 0:n], in_=y2[:, 0:n], func=AF.Tanh, scale=math.sqrt(2.0 / PI)),
                                 reads=['y2'], writes=['y3'])
                            P.op('dve', lambda: V.tensor_scalar(y3[:, 0:n], y3[:, 0:n], 0.5, 0.5, ALU.mult, ALU.add), reads=['y3'], writes=['y3'])
                            P.op('dve', lambda kc=kc: V.tensor_tensor(yg[:, kc, 0:n], y3[:, 0:n], y1[:, 0:n], ALU.mult),
                                 reads=['y3', 'y1'], writes=['yg%d' % kc])
                            yield
                        for m_ in range(4):
                            ps, key = nextmm()
                            for kk in range(4):
                                mm(ps[:, 0:n], w_glu[:, kk, m_ * 128:(m_ + 1) * 128], yg[:, kk, 0:n], kk == 0, kk == 3,
                                   ['w_glu', 'yg%d' % kk], [key])
                            P.op('act', lambda m_=m_, ps=ps: A.activation(out=y3[:, 0:n], in_=ps[:, 0:n], func=AF.Tanh, scale=0.5,
                                                                          bias=bgluh[:, m_:m_ + 1]),
                                 reads=[key, 'bgluh'], writes=['y3'])
                            P.op('dve', lambda: V.tensor_scalar(y3[:, 0:n], y3[:, 0:n], 0.5, 0.5, ALU.mult, ALU.add), reads=['y3'], writes=['y3'])
                            P.op('dve', lambda m_=m_: V.tensor_tensor(y3[:, 0:n], y3[:, 0:n], yg[:, m_, 0:n], ALU.mult),
                                 reads=['y3', 'yg%d' % m_], writes=['y3'])
                            P.op('dve', lambda m_=m_: V.tensor_tensor(gated[:, m_, 0:n], y3[:, 0:n], sza[:, m_, 0:n], ALU.mult),
                                 reads=['y3', 'sza'], writes=['gated%d' % m_])
                            yield

                    def gen_ml():
                        for f in range(8):
                            ps, key = inproj(8 + f, n)
                            P.op('act', lambda f=f, ps=ps: A.copy(xb[:, f, 3:3 + n], ps[:, 0:n]), reads=[key], writes=['xb'])
                            if f % 2 == 1:
                                yield
                        for f in range(8):
                            ps, key = inproj(16 + f, n)
                            P.op('act', lambda f=f, ps=ps: A.activation(out=szb[:, f, 0:n], in_=ps[:, 0:n], func=AF.Silu),
                                 reads=[key], writes=['szb'])
                            if f % 2 == 1:
                                yield
                        for k in range(8):
                            P.op('dve', lambda k=k: V.tensor_scalar(acc[:, 0:n], xb[:, k, 0:n], convw[:, k, 0:1], convb[:, k:k + 1],
                                                                    ALU.mult, ALU.add),
                                 reads=['xb', 'convw', 'convb'], writes=['acc'])
                            for tap in range(1, 4):
                                P.op('dve', lambda k=k, tap=tap: V.scalar_tensor_tensor(
                                    acc[:, 0:n], xb[:, k, tap:tap + n], convw[:, k, tap:tap + 1], acc[:, 0:n], ALU.mult, ALU.add),
                                    reads=['xb', 'convw', 'acc'], writes=['acc'])
                            P.op('act', lambda k=k: A.activation(out=xc[:, k, 0:n], in_=acc[:, 0:n], func=AF.Silu),
                                 reads=['acc'], writes=['xc'])
                            yield
                        gip, gikey = nextmm()
                        gfp, gfkey = nextmm()
                        gi, gf = gip[0:4, 0:n], gfp[0:4, 0:n]
                        for gsel, o_, gkey_ in ((0, gi, gikey), (4, gf, gfkey)):
                            for k in range(8):
                                mm(o_, wcif[:, k, gsel:gsel + 4], xc[:, k, 0:n], k == 0, False, ['wcif', 'xc'], [gkey_])
                            for k in range(8):
                                mm(o_, wxif[:, k, gsel:gsel + 4], xb[:, k, 3:3 + n], False, k == 7, ['wxif', 'xb'], [gkey_])
                        P.op('act', lambda: A.activation(out=r_ig[:, 0:n], in_=gi, func=AF.Identity, bias=b_i[:, 0:1]),
                             reads=[gikey, 'b_i'], writes=['r_ig'])
                        P.op('act', lambda: A.activation(out=r_e[:, 0:n], in_=gf, func=AF.Exp, scale=-1.0, bias=b_f[:, 0:1]),
                             reads=[gfkey, 'b_f'], writes=['r_wend'])
                        P.op('act', lambda: A.activation(out=r_e[:, 0:n], in_=r_e[:, 0:n], func=AF.Ln, bias=1.0),
                             reads=['r_wend'], writes=['r_wend'])
                        P.op('dve', lambda: V.tensor_tensor_scan(r_lf[:, 0:n], onesrow[:, 0:n], r_e[:, 0:n], 0.0, ALU.mult, ALU.add),
                             reads=['r_wend', 'onesrow'], writes=['r_lf'])
                        P.op('dve', lambda: V.tensor_tensor(r_ig[:, 0:n], r_ig[:, 0:n], r_lf[:, 0:n], ALU.add),
                             reads=['r_ig', 'r_lf'], writes=['r_ig'])
                        P.op('dve', lambda: V.tensor_copy(Mfull[:, 0:1], mcarry[:, 0:1]), reads=['mcarry'], writes=['Mfull'])
                        P.op('dve', lambda: V.tensor_tensor_scan(Mfull[:, 1:n + 1], onesrow[:, 0:n], r_ig[:, 0:n], mcarry[:, 0:1],
                                                                 ALU.mult, ALU.max),
                             reads=['r_ig', 'onesrow', 'mcarry', 'Mfull'], writes=['Mfull'])
                        P.op('dve', lambda: V.tensor_scalar(negM[:, 0:n + 1], Mfull[:, 0:n + 1], -1.0, None, ALU.mult),
                             reads=['Mfull'], writes=['negM'])
                        P.op('dve', lambda: V.tensor_tensor(r_emm[:, 0:n], r_lf[:, 0:n], Mfull[:, 1:n + 1], ALU.subtract),
                             reads=['r_lf', 'Mfull'], writes=['r_emm'])
                        P.op('act', lambda: A.activation(out=r_emm[:, 0:n], in_=r_emm[:, 0:n], func=AF.Exp), reads=['r_emm'], writes=['r_emm'])
                        P.op('dve', lambda: V.tensor_tensor(mcarry[:, 0:1], Mfull[:, n:n + 1], r_lf[:, n - 1:n], ALU.subtract),
                             reads=['Mfull', 'r_lf'], writes=['mcarry'])
                        for c in range(nch):
                            cs = slice(c * L, (c + 1) * L)
                            P.op('act', lambda c=c, cs=cs: A.activation(out=r_wi[:, cs], in_=Mfull[:, 1 + c * L:1 + (c + 1) * L], func=AF.Exp,
                                                                        scale=-1.0, bias=Mfull[:, c * L:c * L + 1]),
                                 reads=['Mfull'], writes=['r_wi'])
                            P.op('act', lambda c=c, cs=cs: A.activation(out=r_g[:, cs], in_=Mfull[:, 1 + c * L:1 + (c + 1) * L], func=AF.Exp,
                                                                        scale=-1.0, bias=Mfull[:, (c + 1) * L:(c + 1) * L + 1]),
                                 reads=['Mfull'], writes=['r_g'])
                            P.op('act', lambda c=c, cs=cs: A.activation(out=r_wend[:, cs], in_=r_ig[:, cs], func=AF.Exp,
                                                                        bias=negM[:, (c + 1) * L:(c + 1) * L + 1]),
                                 reads=['r_ig', 'negM'], writes=['r_wend'])
                        yield
                        for h in range(4):
                            for e in range(2):
                                for (w_, dst, sc, wkey) in ((wq, qT, 1.0, 'ml_wq'), (wk, kT, 1.0 / 16.0, 'ml_wk')):
                                    ps, key = nextmm()
                                    for kd in range(2):
                                        mm(ps[:, 0:n], w_[:, h, kd, e * 128:(e + 1) * 128], xc[:, 2 * h + kd, 0:n], kd == 0, kd == 1,
                                           [wkey, 'xc'], [key])
                                    P.op('act', lambda ps=ps, dst=dst, h=h, e=e, sc=sc: A.mul(dst[:, 2 * h + e, 0:n], ps[:, 0:n], sc),
                                         reads=[key], writes=['uT'] if dst is qT else ['kT'])
                        for c in range(nch):
                            cs = slice(c * L, (c + 1) * L)
                            tt = c
                            ts0 = c * 128
                            for h in range(4):
                                ps, key = nextmm()
                                for kd in range(2):
                                    mm(ps[0:tp, 0:256], xb[:, 2 * h + kd, 3 + ts0:3 + ts0 + tp], wv[:, h, kd, :], kd == 0, kd == 1,
                                       ['xb', 'ml_wv'], [key])
                                P.op('act', lambda ps=ps, h=h: A.copy(vtok[0:tp, h, 0:256], ps[0:tp, 0:256]),
                                     reads=[key], writes=['vtok'])
                                ps, key = nextmm()
                                for kd in range(2):
                                    mm(ps[0:tp, 0:256], xc[:, 2 * h + kd, ts0:ts0 + tp], wk[:, h, kd, :], kd == 0, kd == 1,
                                       ['xc', 'ml_wk'], [key])
                                P.op('act', lambda ps=ps, h=h: A.mul(ktok[0:tp, h, :], ps[0:tp, 0:256], 1.0 / 16.0),
                                     reads=[key], writes=['ktok'])
                            pc = pb[6]
                            for i_, row in enumerate((r_wi, r_emm, r_wend, r_g)):
                                mm(pc[0:L, 384 + 4 * i_:388 + 4 * i_], row[0:4, cs], identf[0:4, 0:4], True, True,
                                   ['r_wi', 'r_emm', 'r_wend', 'r_g', 'identf'], ['pb6c'])
                            P.op('dve', lambda c=c: V.tensor_scalar(dg4[:, :], identf[0:4, 0:4], r_wi[0:4, (c + 1) * L - 1:(c + 1) * L], None, ALU.mult),
                                 reads=['r_wi', 'identf'], writes=['dg4'])
                            mm(pc[:, 400:404], ones4[0:4, :], dg4[0:4, :], True, True, ['ones4', 'dg4'], ['pb6c'])
                            P.op('dve', lambda pc=pc: V.tensor_copy(cols[:, 0:20], pc[:, 384:404]), reads=['pb6c'], writes=['cols'])
                            yield
                            for h in range(4):
                                stp, stkey = nextmm()
                                ST = stp[0:L, 0:L]
                                for ke in range(2):
                                    mm(ST, kT[:, 2 * h + ke, cs], qT[:, 2 * h + ke, cs], ke == 0, ke == 1, ['kT', 'uT'], [stkey])
                                P.op('dve', lambda ST=ST, h=h: V.scalar_tensor_tensor(PT[0:L, 0:L], ST, cols[0:L, 8 + h:9 + h], negmask[0:L, 0:L],
                                                                                     ALU.mult, ALU.mult),
                                     reads=[stkey, 'cols', 'negmask'], writes=['PT'])
                                mm(pb[7][0:L, 0:257], PT[0:L, 0:L], vtok[0:L, h, 0:257], True, True, ['PT', 'vtok'], ['pb7'])
                                for kd in range(2):
                                    mm(pb[3][0:L, 0:257], qT[:, 2 * h + kd, cs], Cbf[:, h, kd, 0:257], kd == 0, kd == 1, ['uT', 'Cbf%d' % h], ['pb3'])
                                P.op('act', lambda h=h: A.activation(out=inter[0:L, 0:257], in_=pb[3][0:L, 0:257], func=AF.Copy,
                                                                     scale=cols[0:L, h:h + 1]),
                                     reads=['pb3', 'cols'], writes=['y2'])
                                P.op('dve', lambda h=h: V.scalar_tensor_tensor(tot[0:L, 0:257], pb[7][0:L, 0:257], cols[0:L, 12 + h:13 + h],
                                                                            inter[0:L, 0:257], ALU.mult, ALU.add),
                                     reads=['pb7', 'y2', 'cols'], writes=['y3'])
                                P.op('dve', lambda: V.scalar_tensor_tensor(st2[0:L, 2:3], tot[0:L, 256:257], -1.0, tot[0:L, 256:257],
                                                                           ALU.mult, ALU.max), reads=['y3'], writes=['st2c'])
                                P.op('dve', lambda h=h: V.tensor_tensor(st2[0:L, 2:3], st2[0:L, 2:3], cols[0:L, 4 + h:5 + h], ALU.max),
                                     reads=['st2c', 'cols'], writes=['st2c'])
                                P.op('dve', lambda: V.reciprocal(st2[0:L, 3:4], st2[0:L, 2:3]), reads=['st2c'], writes=['st2d'])
                                P.op('dve', lambda: V.bn_stats(st6[0:L, :], tot[0:L, 0:256]), reads=['y3'], writes=['st6'])
                                P.op('dve', lambda: V.bn_aggr(st2[0:L, 4:6], st6[0:L, :]), reads=['st6'], writes=['st2e'])
                                P.op('dve', lambda: V.tensor_tensor(st2[0:L, 6:7], st2[0:L, 3:4], st2[0:L, 3:4], ALU.mult),
                                     reads=['st2d'], writes=['st2f'])
                                P.op('dve', lambda: V.tensor_scalar(st2[0:L, 6:7], st2[0:L, 6:7], st2[0:L, 5:6], EPS, ALU.mult, ALU.add),
                                     reads=['st2f', 'st2e'], writes=['st2f'])
                                P.op('pool', lambda: G.tensor_tensor(st2[0:L, 7:8], st2[0:L, 6:7], negh[0:L, :], ALU.pow),
                                     reads=['st2f', 'negh'], writes=['st2g'])
                                P.op('dve', lambda: V.tensor_tensor(st2[0:L, 7:8], st2[0:L, 7:8], st2[0:L, 3:4], ALU.mult),
                                     reads=['st2g', 'st2d'], writes=['st2g'])
                                P.op('dve', lambda tt=tt, h=h: V.tensor_scalar(hn[0:L, h * 256:(h + 1) * 256], tot[0:L, 0:256],
                                                                              st2[0:L, 4:5], st2[0:L, 7:8], ALU.subtract, ALU.mult),
                                     reads=['y3', 'st2e', 'st2g'], writes=['ub'])
                                P.op('act', lambda tt=tt, h=h: A.activation(out=kw[0:L, :], in_=ktok[0:L, h, :], func=AF.Copy, scale=cols[0:L, 8 + h:9 + h]),
                                     reads=['ktok', 'cols'], writes=['kw'])
                                for kd in range(2):
                                    cup = (pb[4], 'pb4') if kd == 0 else (pb[5], 'pb5')
                                    mm(cup[0][:, 0:257], kw[0:L, kd * 128:(kd + 1) * 128], vtok[0:L, h, 0:257], True, True, ['kw', 'vtok'], [cup[1]])
                                    P.op('dve', lambda h=h, kd=kd, cup=cup: V.scalar_tensor_tensor(
                                        Cst[:, h, kd, 0:257], Cst[:, h, kd, 0:257], cols[:, 16 + h:17 + h], cup[0][:, 0:257], ALU.mult, ALU.add),
                                        reads=['Cst%d' % h, 'cols', cup[1]], writes=['Cst%d' % h])
                                P.op('act', lambda h=h: A.copy(Cbf[:, h, :, :], Cst[:, h, :, :]), reads=['Cst%d' % h], writes=['Cbf%d' % h])
                                yield
                            pT = pb[2][:].bitcast(BF16)
                            for k in range(8):
                                P.op('pe', lambda k=k: PL.transpose(pT[:, k * 128:k * 128 + tp], hn[0:tp, k * 128:(k + 1) * 128],
                                                                    identb[0:tp, 0:tp]),
                                     reads=['ub', 'identb'], writes=['pb2'])
                            for k in range(8):
                                P.op('dve', lambda k=k: V.tensor_scalar(ytmp[:, 0:tp], pT[:, k * 128:k * 128 + tp], mlnw[:, k:k + 1], None, ALU.mult),
                                     reads=['pb2', 'mlnw'], writes=['ytmp'])
                                P.op('dve', lambda k=k: V.scalar_tensor_tensor(ytmp[:, 0:tp], xc[:, k, ts0:ts0 + tp], skipc[:, k:k + 1],
                                                                               ytmp[:, 0:tp], ALU.mult, ALU.add),
                                     reads=['xc', 'skipc', 'ytmp'], writes=['ytmp'])
                                P.op('dve', lambda k=k: V.tensor_tensor(gated[:, 4 + k, ts0:ts0 + tp], ytmp[:, 0:tp], szb[:, k, ts0:ts0 + tp], ALU.mult),
                                     reads=['ytmp', 'szb'], writes=['gatedB'])
                            yield
                        P.op('dve', lambda n=n: V.tensor_copy(xbo[:, :, :], xb[:, :, n:n + 3]), reads=['xb'], writes=['xbo'])
                        P.op('dve', lambda n=n: V.tensor_copy(xb[:, :, 0:3], xbo[:, :, :]), reads=['xbo'], writes=['xb'])
                    gens = [gen_s5(), gen_ml()]
                    while gens:
                        for g_ in list(gens):
                            try:
                                next(g_)
                            except StopIteration:
                                gens.remove(g_)
                    for tt in range(ntt):
                        ts0 = tt * 128
                        P.dma('sp', hin[0:tp, :], src[ts0:ts0 + tp, :], writes=['hin'])
                        for nh in range(2):
                            po = ((pb[7], 'pb7'), (pb[3], 'pb3'), (pb[4], 'pb4'), (pb[5], 'pb5'))[2 * (tt % 2) + nh]
                            for cch in range(12):
                                gk = ('gated%d' % cch) if cch < 4 else 'gatedB'
                                mm(po[0][0:tp, :], gated[:, cch, ts0:ts0 + tp], w_out[:, cch, nh * 512:(nh + 1) * 512], cch == 0, cch == 11,
                                   [gk, 'w_out'], [po[1]])
                            P.op('dve', lambda tt=tt, nh=nh, po=po: V.tensor_tensor(hin[0:tp, nh * 512:(nh + 1) * 512],
                                                                                   hin[0:tp, nh * 512:(nh + 1) * 512], po[0][0:tp, :], ALU.add),
                                 reads=['hin', po[1]], writes=['hin'])
                        P.dma('sp', h1rows(g, si, t0 + ts0, tp), hin[0:tp, :], reads=['hin'], writes=['h1'])
                P.enabled = True
                sfx = '_' + g
                P.dma('sp', O['o_s5re' + sfx][si, :].rearrange("(j q) -> q j", q=128), Hre[:], reads=['Hre'], writes=['o1'])
                P.dma('sp', O['o_s5im' + sfx][si, :].rearrange("(j q) -> q j", q=128), Him[:], reads=['Him'], writes=['o2'])
                for h in range(4):
                    P.dma('sp', O['o_mlc' + sfx][si, h].rearrange("(k p) e -> p k e", p=128), Cst[:, h, :, 0:256],
                          reads=['Cst%d' % h], writes=['o3'])
                    P.dma('sp', O['o_mln' + sfx][si, h].rearrange("(k p) -> p k", p=128), Cst[:, h, :, 256],
                          reads=['Cst%d' % h], writes=['o4'])
                P.dma('sp', O['o_mlm' + sfx][si, :].rearrange("(p o) -> p o", o=1), mcarry[:], reads=['mcarry'], writes=['o5'])
                for t_ in range(3):
                    P.dma('sp', O['o_mlconv' + sfx][si, t_].rearrange("(k p) -> p k", p=128), xbo[:, :, t_], reads=['xbo'], writes=['o6'])
            P.barrier()


        if cfg.layers >= 2:
          with ExitStack() as l1:
            sb = lambda n, s, d: l1.enter_context(nc.sbuf_tensor(n, s, d))
            NP = NPIECE
            w_in = sb("w_in1", [128, 8, 5120], BF16)
            w_sw = sb("w_sw1", [128, 8, 1024], BF16)
            w_out = sb("w_out1", [128, 16, D], BF16)
            lwa = sb("lwa", [128, 8, 128], BF16)
            lwx = sb("lwx", [128, 8, 128], BF16)
            for k in range(8):
                P.dma('pool', w_in[:, k, :], I['od_w_in'][k * 128:(k + 1) * 128, :], writes=['w_in'])
                P.dma('pool', w_sw[:, k, :], I['od_w_qksw'][k * 128:(k + 1) * 128, :], writes=['w_sw'])
            for k in range(16):
                P.dma('pool', w_out[:, k, :], I['od_w_out'][k * 128:(k + 1) * 128, :], writes=['w_out'])
            P.dma('pool', lwa[:], I['lru_w_a'].rearrange("h d e -> d h e"), writes=['lwa'])
            P.dma('pool', lwx[:], I['lru_w_x'].rearrange("h d e -> d h e"), writes=['lwx'])
            gcol = sb("gcol1", [128, 8], F32)
            rnw = sb("rnw", [128, 8], F32)
            cw = sb("cw1", [128, 8, 4], F32)
            cb = sb("cb1", [128, 8], F32)
            bah = sb("bah", [128, 8], F32)
            bxh = sb("bxh", [128, 8], F32)
            ccol = sb("ccol", [128, 8], F32)
            ccol2 = sb("ccol2", [128, 8], F32)
            fnw = sb("fnw", [128, D], F32)
            rmaskL = sb("rmaskL", [128, 4, 128], F32)
            rmaskS = sb("rmaskS", [128, 4, 128], F32)
            rxiL = sb("rxiL", [128, 4, 128], F32)
            rxiS = sb("rxiS", [128, 4, 16], F32)
            rzL = sb("rzL", [128, 4], F32)
            rzS = sb("rzS", [128, 4], F32)
            rglL = sb("rglL", [128, 4], F32)
            rglS = sb("rglS", [128, 4], F32)
            col = lambda nm: I[nm].rearrange("(k p) -> p k", p=128)
            P.dma('sp', gcol[:], I['norm_w'][1, :].rearrange("(k p) -> p k", p=128), writes=['gcol1'])
            P.dma('sp', rnw[:], col('ret_norm_w'), writes=['rnw'])
            for tap in range(4):
                P.dma('sp', cw[:, :, tap], I['lru_conv_w'][tap, :].rearrange("(k p) -> p k", p=128), writes=['cw1'])
            P.dma('sp', cb[:], col('lru_conv_b'), writes=['cb1'])
            P.dma('sp', bah[:], col('lru_b_a'), writes=['bah'])
            P.dma('sp', bxh[:], col('lru_b_x'), writes=['bxh'])
            P.dma('sp', ccol[:], col('lru_lambda'), writes=['ccol'])
            P.dma('sp', fnw[:], I['fnw_b'][:, :], writes=['fnw'])
            P.dma('sp', rmaskL[:], I['c_rmask128'][:, :, :], writes=['rmaskL'])
            P.dma('sp', rmaskS[:], I['c_rmask16'][:, :, :], writes=['rmaskS'])
            P.dma('sp', rxiL[:], I['c_rxi128'][:, :, :], writes=['rxiL'])
            P.dma('sp', rxiS[:], I['c_rxi16'][:, :, :], writes=['rxiS'])
            P.dma('sp', rzL[:], I['c_rzeta128'][:, :], writes=['rzL'])
            P.dma('sp', rzS[:], I['c_rzeta16'][:, :], writes=['rzS'])
            P.dma('sp', rglL[:], I['c_rgl128'][:, :], writes=['rglL'])
            P.dma('sp', rglS[:], I['c_rgl16'][:, :], writes=['rglS'])
            P.op('dve', lambda: V.tensor_scalar(bah[:], bah[:], 0.5, None, ALU.mult), reads=['bah'], writes=['bah'])
            P.op('dve', lambda: V.tensor_scalar(bxh[:], bxh[:], 0.5, None, ALU.mult), reads=['bxh'], writes=['bxh'])
            P.op('act', lambda: A.activation(out=ccol[:], in_=ccol[:], func=AF.Exp, scale=-1.0), reads=['ccol'], writes=['ccol'])
            P.op('act', lambda: A.activation(out=ccol[:], in_=ccol[:], func=AF.Ln, bias=1.0), reads=['ccol'], writes=['ccol'])
            P.op('dve', lambda: V.tensor_scalar(ccol2[:], ccol[:], -4.0, None, ALU.mult), reads=['ccol'], writes=['ccol2'])
            P.op('dve', lambda: V.tensor_scalar(ccol[:], ccol[:], -8.0, None, ALU.mult), reads=['ccol'], writes=['ccol'])

            hin = sb("hin1", [128, D], F32)
            ss = sb("ss1", [128, 4], F32)
            ub = sb("ub1", [128, D], BF16)
            hn = ub
            uT = sb("uT1", [128, 8, NP], BF16)
            gated = sb("gated1", [128, 16, NP], BF16)
            cosF = sb("cosF", [128, NP], F32)
            sinF = sb("sinF", [128, NP], F32)
            qT = sb("qT1", [128, 4, NP], BF16)
            kT = sb("kT1", [128, 4, NP], BF16)
            qxi = sb("qxi", [128, 128], BF16)
            kz = sb("kz", [128, 128], BF16)
            vtok = sb("vtok1", [128, 4, 256], BF16)
            szc = sb("szc", [128, 8, NP], BF16)
            xd = sb("xd", [128, 8, 3 + NP], BF16)
            szd = sb("szd", [128, 8, NP], BF16)
            Sst = sb("Sst", [128, 4, 256], F32)
            Sbf = sb("Sbf", [128, 4, 256], BF16)
            hst = sb("hst", [128, 8], F32)
            f1 = sb("f1", [128, NP], F32)
            f2 = sb("f2", [128, NP], F32)
            f3 = sb("f3", [128, NP], F32)
            f4 = sb("f4", [128, NP], F32)
            f5 = sb("f5", [128, NP], F32)
            xcb = sb("xcb", [128, NP], BF16)
            f1b = sb("f1b", [128, NP], F32)
            f2b = sb("f2b", [128, NP], F32)
            f3b = sb("f3b", [128, NP], F32)
            f4b = sb("f4b", [128, NP], F32)
            f5b = sb("f5b", [128, NP], F32)
            xcbb = sb("xcbb", [128, NP], BF16)
            ob = sb("ob", [128, 256], F32)
            halfT = sb("halfT", [128, NP], F32)
            P.op('pool', lambda: G.memset(halfT[:], 0.5), writes=['halfT'])
            PT = sb("PT1", [128, 128], BF16)
            st6 = sb("st6b", [128, 6], F32)
            st2 = sb("st2b", [128, 8], F32)
            xdo = sb("xdo", [128, 8, 3], F32)
            yout = sb("yout", [128, D], F32)

            print('SBUF remaining after L1 activations', nc.sbuf_bytes_remaining)
            mmreg = [(pb[0], 'pb0'), (pb[1], 'pb1'), (pb[4], 'pb4'), (pb[5], 'pb5')]
            mmi = [0]

            def nextmm():
                r_ = mmreg[mmi[0] % 4]
                mmi[0] += 1
                return r_

            def inproj(f, n, wt=None, wkey='w_in'):
                ps, key = nextmm()
                wt = w_in if wt is None else wt
                for k in range(8):
                    mm(ps[:, 0:n], wt[:, k, f * 128:(f + 1) * 128], uT[:, k, 0:n], k == 0, k == 7, [wkey, 'uT'], [key])
                return ps, key

            for (g, si, pcs) in seqs:
                if g == 'p':
                    P.op('pool', lambda: G.memset(Sst[:], 0.0), writes=['Sst%d' % h_ for h_ in range(4)])
                    P.op('pool', lambda: G.memset(hst[:], 0.0), writes=['hst'])
                    P.op('pool', lambda: G.memset(xd[:, :, 0:3], 0.0), writes=['xd'])
                else:
                    for h in range(4):
                        P.dma('sp', Sst[:, h, :], I['ret_in'][si, h], writes=['Sst%d' % h])
                    P.dma('sp', hst[:], I['lruh_in'][si, :].rearrange("(k p) -> p k", p=128), writes=['hst'])
                    for t_ in range(3):
                        P.dma('pool', xd[:, :, t_], I['lruconv_in'][si, t_].rearrange("(k p) -> p k", p=128), writes=['xd'])
                P.op('act', lambda: A.copy(Sbf[:], Sst[:]), reads=['Sst%d' % h_ for h_ in range(4)], writes=['Sbf%d' % h_ for h_ in range(4)])
                for (t0, n, src, dst, _) in pcs:
                    tp = min(n, 128)
                    ntt = (n + 127) // 128
                    L = tp
                    nch = n // L
                    rmask, rxi, rz, rgl = (rmaskL, rxiL, rzL, rglL) if L == 128 else (rmaskS, rxiS, rzS, rglS)
                    rp0 = t0 if g == 'p' else T
                    P.dma('sp', cosF[:, 0:n], I['ropeF_c'][:, rp0:rp0 + n], writes=['cosF'])
                    P.dma('sp', sinF[:, 0:n], I['ropeF_s'][:, rp0:rp0 + n], writes=['sinF'])
                    for tt in range(ntt):
                        P.dma('sp', hin[0:tp, :], h1rows(g, si, t0 + tt * 128, tp), reads=['h1'], writes=['hin1'])
                        rmsnorm_T(hin, tp, tt, gcol, ub, uT, ss, hkey='hin1', gkey='gcol1')
                    for f in range(8):
                        ps, key = inproj(24 + f, n)
                        P.op('act', lambda f=f, ps=ps: A.copy(xd[:, f, 3:3 + n], ps[:, 0:n]), reads=[key], writes=['xd'])
                    for f in range(8):
                        ps, key = inproj(32 + f, n)
                        P.op('act', lambda f=f, ps=ps: A.activation(out=szd[:, f, 0:n], in_=ps[:, 0:n], func=AF.Silu), reads=[key], writes=['szd'])
                    F1, F2, F3, F4, F5, XCB = f1, f2, f3, f4, f5, xcb
                    def gen_ret():
                        for (base, dstT, sc, dkey) in ((0, qT, 1.0, 'qT1'), (4, kT, 1.0 / math.sqrt(128.0), 'kT1')):
                            for h in range(4):
                                psA, keyA = inproj(base + h, n)
                                psB, keyB = inproj(base + h, n, wt=w_sw, wkey='w_sw')
                                P.op('dve', lambda psA=psA: V.tensor_tensor(yout[:, 0:n], psA[:, 0:n], cosF[:, 0:n], ALU.mult),
                                     reads=[keyA, 'cosF'], writes=['yout'])
                                P.op('dve', lambda psB=psB: V.tensor_tensor(yout[:, 256:256 + n], psB[:, 0:n], sinF[:, 0:n], ALU.mult),
                                     reads=[keyB, 'sinF'], writes=['yout'])
                                P.op('dve', lambda: V.tensor_tensor(yout[:, 0:n], yout[:, 0:n], yout[:, 256:256 + n], ALU.add), reads=['yout', 'yout'], writes=['yout'])
                                P.op('act', lambda h=h, dstT=dstT, sc=sc: A.mul(dstT[:, h, 0:n], yout[:, 0:n], sc), reads=['yout'], writes=[dkey])
                                yield
                        for f in range(8):
                            ps, key = inproj(16 + f, n)
                            P.op('act', lambda f=f, ps=ps: A.activation(out=szc[:, f, 0:n], in_=ps[:, 0:n], func=AF.Silu), reads=[key], writes=['szc'])
                            if f % 2 == 1:
                                yield
                        for c in range(nch):
                            cs = slice(c * L, (c + 1) * L)
                            ts0 = c * 128
                            for h in range(4):
                                ps, key = nextmm()
                                for k in range(8):
                                    mm(ps[0:tp, 0:256], uT[:, k, ts0:ts0 + tp], w_in[:, k, 1024 + h * 256:1024 + (h + 1) * 256], k == 0, k == 7,
                                       ['uT', 'w_in'], [key])
                                P.op('act', lambda ps=ps, h=h: A.copy(vtok[0:tp, h, :], ps[0:tp, 0:256]), reads=[key], writes=['vtok1'])
                            yield
                            for h in range(4):
                                stp, stkey = nextmm()
                                ST = stp[0:L, 0:L]
                                mm(ST, kT[:, h, cs], qT[:, h, cs], True, True, ['kT1', 'qT1'], [stkey])
                                P.op('dve', lambda ST=ST, h=h: V.tensor_tensor(PT[0:L, 0:L], ST, rmask[0:L, h, 0:L], ALU.mult),
                                     reads=[stkey, 'rmaskL', 'rmaskS'], writes=['PT1'])
                                P.op('dve', lambda h=h: V.tensor_tensor(qxi[:, 0:L], qT[:, h, cs], rxi[:, h, 0:L], ALU.mult),
                                     reads=['qT1', 'rxiL', 'rxiS'], writes=['qxi'])
                                mm(pb[7][0:L, 0:256], PT[0:L, 0:L], vtok[0:L, h, :], True, False, ['PT1', 'vtok1'], ['pb7'])
                                mm(pb[7][0:L, 0:256], qxi[:, 0:L], Sbf[:, h, :], False, True, ['qxi', 'Sbf%d' % h], ['pb7'])
                                P.op('act', lambda: A.copy(ob[0:L, 0:256], pb[7][0:L, 0:256]), reads=['pb7'], writes=['ob'])
                                P.op('dve', lambda: V.bn_stats(st6[0:L, :], ob[0:L, 0:256]), reads=['ob'], writes=['st6b'])
                                P.op('dve', lambda: V.bn_aggr(st2[0:L, 0:2], st6[0:L, :]), reads=['st6b'], writes=['st2x'])
                                P.op('dve', lambda: V.tensor_scalar(st2[0:L, 2:3], st2[0:L, 1:2], EPS, None, ALU.add), reads=['st2x'], writes=['st2y'])
                                P.op('pool', lambda: G.tensor_tensor(st2[0:L, 3:4], st2[0:L, 2:3], negh[0:L, :], ALU.pow),
                                     reads=['st2y', 'negh'], writes=['st2z'])
                                P.op('dve', lambda h=h: V.tensor_scalar(hn[0:L, h * 256:(h + 1) * 256], ob[0:L, 0:256], st2[0:L, 0:1], st2[0:L, 3:4],
                                                                        ALU.subtract, ALU.mult),
                                     reads=['ob', 'st2x', 'st2z'], writes=['ub'])
                                pT = pb[2][:].bitcast(BF16)
                                P.op('pe', lambda h=h: PL.transpose(pT[0:L, 0:128], kT[:, h, cs], identb[:, :]),
                                     reads=['kT1', 'identb'], writes=['pb2'])
                                P.op('dve', lambda h=h: V.tensor_scalar(kz[0:L, :], pT[0:L, 0:128], rz[0:L, h:h + 1], None, ALU.mult),
                                     reads=['pb2', 'rzL', 'rzS'], writes=['kz'])
                                mm(pb[3][:, 0:256], kz[0:L, :], vtok[0:L, h, :], True, True, ['kz', 'vtok1'], ['pb3'])
                                P.op('dve', lambda h=h: V.scalar_tensor_tensor(Sst[:, h, :], Sst[:, h, :], rgl[:, h:h + 1], pb[3][:, 0:256], ALU.mult, ALU.add),
                                     reads=['Sst%d' % h, 'rglL', 'rglS', 'pb3'], writes=['Sst%d' % h])
                                P.op('act', lambda h=h: A.copy(Sbf[:, h, :], Sst[:, h, :]), reads=['Sst%d' % h], writes=['Sbf%d' % h])
                                yield
                            pT = pb[2][:].bitcast(BF16)
                            for k in range(8):
                                P.op('pe', lambda k=k: PL.transpose(pT[:, k * 128:k * 128 + tp], hn[0:tp, k * 128:(k + 1) * 128], identb[0:tp, 0:tp]),
                                     reads=['ub', 'identb'], writes=['pb2'])
                            for k in range(8):
                                P.op('dve', lambda k=k: V.scalar_tensor_tensor(gated[:, k, ts0:ts0 + tp], pT[:, k * 128:k * 128 + tp], rnw[:, k:k + 1],
                                                                               szc[:, k, ts0:ts0 + tp], ALU.mult, ALU.mult),
                                     reads=['pb2', 'rnw', 'szc'], writes=['gatedA'])
                            yield
                    def gen_lru():
                        def bufs(k):
                            return ((F1, F2, F3, F4, F5, XCB), '') if k % 2 == 0 else ((f1b, f2b, f3b, f4b, f5b, xcbb), 'b')

                        def ph1(k):
                            (f1, f2, f3, f4, f5, xcb), sfx_ = bufs(k)
                            P.op('dve', lambda: V.tensor_scalar(f1[:, 0:n], xd[:, k, 0:n], cw[:, k, 0:1], cb[:, k:k + 1], ALU.mult, ALU.add),
                                 reads=['xd', 'cw1', 'cb1'], writes=['f1' + sfx_])
                            for tap in range(1, 4):
                                P.op('dve', lambda tap=tap: V.scalar_tensor_tensor(f1[:, 0:n], xd[:, k, tap:tap + n], cw[:, k, tap:tap + 1], f1[:, 0:n],
                                                                                   ALU.mult, ALU.add),
                                     reads=['xd', 'cw1', 'f1' + sfx_], writes=['f1' + sfx_])
                            P.op('act', lambda: A.copy(xcb[:, 0:n], f1[:, 0:n]), reads=['f1' + sfx_], writes=['xcb' + sfx_])
                            psr, keyr = nextmm()
                            mm(psr[:, 0:n], lwa[:, k, :], xcb[:, 0:n], True, True, ['lwa', 'xcb' + sfx_], [keyr])
                            psi, keyi = nextmm()
                            mm(psi[:, 0:n], lwx[:, k, :], xcb[:, 0:n], True, True, ['lwx', 'xcb' + sfx_], [keyi])
                            P.op('act', lambda: A.activation(out=f2[:, 0:n], in_=psr[:, 0:n], func=AF.Tanh, scale=0.5, bias=bah[:, k:k + 1]),
                                 reads=[keyr, 'bah'], writes=['f2' + sfx_])
                            P.op('act', lambda: A.activation(out=f3[:, 0:n], in_=psi[:, 0:n], func=AF.Tanh, scale=0.5, bias=bxh[:, k:k + 1]),
                                 reads=[keyi, 'bxh'], writes=['f3' + sfx_])
                            P.op('act', lambda: A.activation(out=f4[:, 0:n], in_=f2[:, 0:n], func=AF.Exp, scale=ccol2[:, k:k + 1], bias=ccol2[:, k:k + 1]),
                                 reads=['f2' + sfx_, 'ccol2'], writes=['f4' + sfx_])
                            P.op('pool', lambda: G.tensor_scalar(f3[:, 0:n], f3[:, 0:n], 0.5, 0.5, ALU.mult, ALU.add), reads=['f3' + sfx_], writes=['f3' + sfx_])
                            P.op('pool', lambda: G.tensor_tensor(f3[:, 0:n], f3[:, 0:n], f1[:, 0:n], ALU.mult), reads=['f3' + sfx_, 'f1' + sfx_], writes=['f3' + sfx_])
                            P.op('pool', lambda: G.tensor_tensor(f5[:, 0:n], f4[:, 0:n], f4[:, 0:n], ALU.mult), reads=['f4' + sfx_], writes=['f5' + sfx_])
                            P.op('pool', lambda: G.tensor_scalar(f5[:, 0:n], f5[:, 0:n], -1.0, 1.0, ALU.mult, ALU.add), reads=['f5' + sfx_], writes=['f5' + sfx_])

                        def ph2(k):
                            (f1, f2, f3, f4, f5, xcb), sfx_ = bufs(k)
                            P.op('act', lambda: A.activation(out=f5[:, 0:n], in_=f5[:, 0:n], func=AF.Ln), reads=['f5' + sfx_], writes=['f5' + sfx_])
                            P.op('act', lambda: A.activation(out=f5[:, 0:n], in_=f5[:, 0:n], func=AF.Exp, scale=0.5), reads=['f5' + sfx_], writes=['f5' + sfx_])

                        def ph3(k):
                            (f1, f2, f3, f4, f5, xcb), sfx_ = bufs(k)
                            P.op('dve', lambda: V.tensor_tensor(f3[:, 0:n], f3[:, 0:n], f5[:, 0:n], ALU.mult), reads=['f3' + sfx_, 'f5' + sfx_], writes=['f3' + sfx_])
                            P.op('dve', lambda: V.tensor_tensor_scan(f2[:, 0:n], f4[:, 0:n], f3[:, 0:n], hst[:, k:k + 1], ALU.mult, ALU.add),
                                 reads=['f4' + sfx_, 'f3' + sfx_, 'hst', 'f2' + sfx_], writes=['f2' + sfx_])
                            P.op('dve', lambda: V.tensor_copy(hst[:, k:k + 1], f2[:, n - 1:n]), reads=['f2' + sfx_], writes=['hst'])
                            P.op('dve', lambda: V.tensor_tensor(gated[:, 8 + k, 0:n], f2[:, 0:n], szd[:, k, 0:n], ALU.mult),
                                 reads=['f2' + sfx_, 'szd'], writes=['gatedB%d' % k])

                        for p_ in range(4):
                            ph1(2 * p_)
                            yield
                            ph1(2 * p_ + 1)
                            yield
                            ph2(2 * p_)
                            ph2(2 * p_ + 1)
                            yield
                            ph3(2 * p_)
                            ph3(2 * p_ + 1)
                            yield
                    gens = [gen_ret(), gen_lru()]
                    while gens:
                        for g_ in list(gens):
                            try:
                                next(g_)
                            except StopIteration:
                                gens.remove(g_)
                    P.op('dve', lambda n=n: V.tensor_copy(xdo[:, :, :], xd[:, :, n:n + 3]), reads=['xd'], writes=['xdo'])
                    P.op('dve', lambda: V.tensor_copy(xd[:, :, 0:3], xdo[:, :, :]), reads=['xdo'], writes=['xd'])
                    for tt in range(ntt):
                        ts0 = tt * 128
                        P.dma('sp', hin[0:tp, :], h1rows(g, si, t0 + ts0, tp), reads=['h1'], writes=['hin1'])
                        for nh in range(2):
                            po = ((pb[7], 'pb7'), (pb[3], 'pb3'), (pb[6], 'pb6'), (pb[2], 'pb2'))[2 * (tt % 2) + nh]
                            for cch in range(16):
                                gk = 'gatedA' if cch < 8 else 'gatedB%d' % (cch - 8)
                                mm(po[0][0:tp, :], gated[:, cch, ts0:ts0 + tp], w_out[:, cch, nh * 512:(nh + 1) * 512], cch == 0, cch == 15,
                                   [gk, 'w_out'], [po[1]])
                            P.op('dve', lambda nh=nh, po=po: V.tensor_tensor(hin[0:tp, nh * 512:(nh + 1) * 512],
                                                                             hin[0:tp, nh * 512:(nh + 1) * 512], po[0][0:tp, :], ALU.add),
                                 reads=['hin1', po[1]], writes=['hin1'])
                        if not (g == 'p' and t0 == 0):
                            P.op('pool', lambda: G.memset(ss[0:tp, 0:1], 0.0), writes=['ss'])
                            P.op('act', lambda: A.activation(out=ub[0:tp, :], in_=hin[0:tp, :], func=AF.Square, accum_out=ss[0:tp, 0:1]),
                                 reads=['hin1'], writes=['ub', 'ss'])
                            P.op('dve', lambda: V.tensor_scalar(ss[0:tp, 1:2], ss[0:tp, 0:1], 1.0 / D, EPS, ALU.mult, ALU.add), reads=['ss'], writes=['ss'])
                            P.op('pool', lambda: G.tensor_tensor(ss[0:tp, 2:3], ss[0:tp, 1:2], negh[0:tp, :], ALU.pow), reads=['ss', 'negh'], writes=['ss'])
                            P.op('dve', lambda: V.scalar_tensor_tensor(yout[0:tp, :], hin[0:tp, :], ss[0:tp, 2:3], fnw[0:tp, :], ALU.mult, ALU.mult),
                                 reads=['hin1', 'ss', 'fnw'], writes=['yout'])
                            P.dma('sp', dst[ts0:ts0 + tp, :], yout[0:tp, :], reads=['yout'], writes=['yo'])
                sfx = '_' + g
                for h in range(4):
                    P.dma('sp', O['o_ret' + sfx][si, h], Sst[:, h, :], reads=['Sst%d' % h], writes=['o7'])
                P.dma('sp', O['o_lruh' + sfx][si, :].rearrange("(k p) -> p k", p=128), hst[:], reads=['hst'], writes=['o8'])
                for t_ in range(3):
                    P.dma('sp', O['o_lruconv' + sfx][si, t_].rearrange("(k p) -> p k", p=128), xdo[:, :, t_], reads=['xdo'], writes=['o9'])
            P.barrier()
        P.barrier()
    return nc


def host_layout(inp):
    f = lambda a: np.ascontiguousarray(np.asarray(a, dtype=np.float32))
    d = {}
    d['norm_w'] = f(inp['norm_w'])
    d['final_norm_w'] = f(inp['final_norm_w'])
    d['ev_w_in'] = f(inp['ev_w_in'][0])
    d['ev_w_out'] = f(inp['ev_w_out'][0])
    lre, lim, ldt = inp['s5_lambda_re'][0], inp['s5_lambda_im'][0], inp['s5_log_dt'][0]
    def A_from_gp(x):
        x4 = np.asarray(x).reshape(4, 8, 64)
        o = np.broadcast_to(x4[:, :, None, :], (4, 8, 16, 64))
        return f(np.transpose(o, (1, 2, 0, 3)).reshape(128, 4, 64))
    d['lamre_A'] = A_from_gp(lre)
    d['lamim_A'] = A_from_gp(lim)
    d['logdt_A'] = A_from_gp(np.broadcast_to(np.asarray(ldt)[:, None], (32, 64)))
    def A_from_gpc(x):
        x5 = np.asarray(x).reshape(4, 8, 64, 16)
        return f(np.transpose(x5, (1, 3, 0, 2)).reshape(128, 4, 64))
    d['bre_A'] = A_from_gpc(inp['s5_b_re'][0])
    d['bim_A'] = A_from_gpc(inp['s5_b_im'][0])
    def S_from_gp(x):
        x3 = np.asarray(x).reshape(16, 2, 64)
        return f(np.transpose(x3, (1, 2, 0)).reshape(128, 16))
    d['lamre_S'] = S_from_gp(lre)
    d['lamim_S'] = S_from_gp(lim)
    d['logdt_S'] = S_from_gp(np.broadcast_to(np.asarray(ldt)[:, None], (32, 64)))
    def S_from_gcp(x):
        x4 = np.asarray(x).reshape(16, 2, 16, 64)
        return f(np.transpose(x4, (1, 3, 0, 2)).reshape(128, 16, 16))
    d['cre_S'] = S_from_gcp(inp['s5_c_re'][0])
    d['cim_S'] = S_from_gcp(inp['s5_c_im'][0])
    d['s5_d'] = f(inp['s5_d'][0]); d['s5_w_glu'] = f(inp['s5_w_glu'][0]); d['s5_b_glu'] = f(inp['s5_b_glu'][0])
    d['ml_conv_w'] = f(inp['ml_conv_w'][0]); d['ml_conv_b'] = f(inp['ml_conv_b'][0])
    for k in ('ml_wq', 'ml_wk', 'ml_wv'):
        d[k] = f(inp[k][0])
        d[k + 'T'] = f(np.transpose(np.asarray(inp[k][0]), (0, 2, 1)))
    d['ml_w_if'] = f(inp['ml_w_if'][0]); d['ml_b_if'] = f(inp['ml_b_if'][0])
    d['ml_norm_w'] = f(inp['ml_norm_w'][0]); d['ml_skip'] = f(inp['ml_skip'][0])
    w1 = np.asarray(inp['od_w_in'][0])
    d['od_w_in'] = f(w1)
    qk = w1[:, 0:1024].reshape(D, 8, 2, 64)
    d['od_w_qksw'] = f(qk[:, :, ::-1, :].reshape(D, 1024))
    d['fnw_b'] = f(np.broadcast_to(np.asarray(inp['final_norm_w'])[None, :], (128, D)))
    d['od_w_out'] = f(inp['od_w_out'][0]); d['ret_norm_w'] = f(inp['ret_norm_w'][0])
    d['lru_conv_w'] = f(inp['lru_conv_w'][0]); d['lru_conv_b'] = f(inp['lru_conv_b'][0])
    d['lru_w_a'] = f(inp['lru_w_a'][0]); d['lru_b_a'] = f(inp['lru_b_a'][0])
    d['lru_w_x'] = f(inp['lru_w_x'][0]); d['lru_b_x'] = f(inp['lru_b_x'][0]); d['lru_lambda'] = f(inp['lru_lambda'][0])
    d['meta'] = f(inp['meta'])
    return d


def make_in_maps(cfg, inputs, ncores):
    shared = host_layout(inputs)
    shared.update(host_consts())
    T = 16 + cfg.T_x
    pos = np.concatenate([np.arange(T, dtype=np.float32), (16 + cfg.past) + np.arange(16, dtype=np.float32)])
    cf, sf, ct, st = rope_tables(pos)
    shared['ropeF_c'], shared['ropeF_s'], shared['ropeT_c'], shared['ropeT_s'] = cf, sf, np.ascontiguousarray(ct), np.ascontiguousarray(st)
    f = lambda a: np.ascontiguousarray(np.asarray(a, dtype=np.float32))
    maps = []
    for c in range(ncores):
        ps = slice(c * cfg.n_p, (c + 1) * cfg.n_p)
        ss_ = slice(c * cfg.n_s, (c + 1) * cfg.n_s)
        m = dict(shared)
        m['xp'] = f(inputs['x_prompt'][ps])
        m['xs'] = f(inputs['x_sample'][ss_])
        m['s5re_in'] = f(inputs['state_s5_re'][0][ss_]).reshape(cfg.n_s, 2048)
        m['s5im_in'] = f(inputs['state_s5_im'][0][ss_]).reshape(cfg.n_s, 2048)
        m['mlc_in'] = f(inputs['state_ml_c'][0][ss_])
        m['mln_in'] = f(inputs['state_ml_n'][0][ss_])
        m['mlm_in'] = f(inputs['state_ml_m'][0][ss_])
        m['mlconv_in'] = f(inputs['state_ml_conv'][0][ss_])
        m['ret_in'] = f(inputs['state_ret'][0][ss_])
        m['lruh_in'] = f(inputs['state_lru_h'][0][ss_])
        m['lruconv_in'] = f(inputs['state_lru_conv'][0][ss_])
        maps.append(m)
    return maps


def gather(cfg, results):
    cat = lambda k: np.concatenate([np.asarray(r[k]) for r in results], axis=0)
    outs = [cat('yp'), cat('ys')]
    for g in ('p', 's'):
        n = cat('o_s5re_' + g).shape[0]
        outs += [cat('o_s5re_' + g).reshape(1, n, 32, 64), cat('o_s5im_' + g).reshape(1, n, 32, 64),
                 cat('o_mlc_' + g)[None], cat('o_mln_' + g)[None], cat('o_mlm_' + g)[None], cat('o_mlconv_' + g)[None],
                 cat('o_ret_' + g)[None], cat('o_lruh_' + g)[None], cat('o_lruconv_' + g)[None]]
    return tuple(np.ascontiguousarray(o.astype(np.float32)) for o in outs)


def kernel(**inputs):
    ncores = 8
    cfg = Cfg(n_p=2, T_x=4096, n_s=2, past=2048, layers=2)
    nc = build(cfg)
    maps = make_in_maps(cfg, inputs, ncores)
    res = run_bass_kernel_spmd(nc, maps, core_ids=list(range(ncores)))
    return gather(cfg, res.results)
```

```python
import math
from contextlib import ExitStack
import numpy as np
import concourse.bass as bass
import concourse.mybir as mybir
from concourse.bass_utils import run_bass_kernel_spmd

F32 = mybir.dt.float32
BF16 = mybir.dt.bfloat16
AF = mybir.ActivationFunctionType
ALU = mybir.AluOpType

D = 1024
NPIECE = 256
EPS = 1e-6
PI = math.pi


class Prog:
    def __init__(self, nc, ndma=6):
        self.nc = nc
        self.e = dict(pe=nc.tensor, dve=nc.vector, act=nc.scalar, pool=nc.gpsimd, sp=nc.sync)
        self.streams = {k: [] for k in self.e}
        self.cnt = {k: 0 for k in self.e}
        self.waited = {k: {} for k in self.e}
        self.lastw = {}
        self.reads = {}
        self.sems = {}
        self.ndma = ndma
        self.dmai = {q: 0 for q in ('sp', 'act', 'pool')}
        self.dma_last = {}
        self.enabled = True
        self.semnames = ['c_' + k for k in ('pe', 'dve', 'act', 'pool')] + \
            ['d_%s_%d' % (q, j) for q in ('sp', 'act', 'pool') for j in range(ndma)]

    def _wait(self, eng, tok):
        sem, val = tok
        if self.waited[eng].get(sem, 0) >= val:
            return
        self.waited[eng][sem] = val
        self.e[eng].wait_ge(self.sems[sem], val)

    def _deps(self, eng, reads, writes):
        deps = []
        own = 'c_' + eng
        for k in reads:
            t = self.lastw.get(k)
            if t is not None:
                if not (eng == 'pe' and t[0] == own):
                    deps.append(t)
        for k in writes:
            t = self.lastw.get(k)
            if t is not None and t[0] != own:
                deps.append(t)
            for t in self.reads.get(k, ()):
                if t[0] != own:
                    deps.append(t)
        for t in deps:
            self._wait(eng, t)

    def _commit(self, tok, reads, writes):
        for k in writes:
            self.lastw[k] = tok
            self.reads[k] = []
        for k in reads:
            if k not in writes:
                self.reads.setdefault(k, []).append(tok)

    def op(self, eng, fn, reads=(), writes=()):
        if not self.enabled:
            return None
        self._deps(eng, reads, writes)
        self.cnt[eng] += 1
        tok = ('c_' + eng, self.cnt[eng])
        fn().then_inc(self.sems['c_' + eng], 1)
        self._commit(tok, reads, writes)
        return tok

    def dma(self, q, out, in_, reads=(), writes=(), **kw):
        if not self.enabled:
            return None
        self._deps(q, reads, writes)
        i = self.dmai[q]
        self.dmai[q] += 1
        j, r = i % self.ndma, i // self.ndma
        sem = 'd_%s_%d' % (q, j)
        if r > 0:
            self._wait(q, (sem, 16 * r))
        e = self.e[q]
        e.dma_start(out=out, in_=in_, **kw).then_inc(self.sems[sem], 16)
        tok = (sem, 16 * (r + 1))
        self.dma_last[sem] = tok
        self._commit(tok, reads, writes)
        return tok

    def barrier(self):
        toks = set(self.lastw.values())
        for lst in self.reads.values():
            toks.update(lst)
        toks.update(self.dma_last.values())
        for k in ('pe', 'dve', 'act', 'pool'):
            if self.cnt[k] > 0:
                toks.add(('c_' + k, self.cnt[k]))
        for eng in self.e:
            for t in toks:
                self._wait(eng, t)
        self.lastw = {}
        self.reads = {}

    def emit(self, block):
        sems = self.sems

        def run(engname, engobj):
            for it in self.streams[engname]:
                if it[0] == 'w':
                    engobj.wait_ge(sems[it[1]], it[2])
                else:
                    inst = it[1]()
                    inst.then_inc(sems[it[2]], it[3])

        @block.tensor
        def _(x):
            run('pe', x)

        @block.vector
        def _(x):
            run('dve', x)

        @block.scalar
        def _(x):
            run('act', x)

        @block.gpsimd
        def _(x):
            run('pool', x)

        @block.sync
        def _(x):
            run('sp', x)


def host_consts():
    c = {}
    c['c_identb'] = np.eye(128, dtype=np.float32)
    c['c_identf'] = np.eye(128, dtype=np.float32)
    s = np.arange(128)
    c['c_negmask'] = np.where(s[:, None] <= s[None, :], 1.0, 0.0).astype(np.float32)
    sel = np.zeros((4, 4, 128), np.float32)
    for h in range(4):
        sel[h, h, :] = 1.0
    c['c_sel'] = sel
    c['c_ones4'] = np.ones((4, 128), np.float32)
    mA = np.zeros((128, 8), np.float32)
    for g8 in range(8):
        mA[g8 * 16:(g8 + 1) * 16, g8] = 1.0
    c['c_maskA'] = mA
    lg = np.log1p(-np.exp2(-5.0 - np.arange(4, dtype=np.float64)))
    for L in (16, 128):
        idx = np.arange(L, dtype=np.float64)
        diff = idx[None, :] - idx[:, None]
        dm = np.where(diff >= 0, np.exp(lg[:, None, None] * np.maximum(diff, 0.0)), 0.0)
        dmp = np.zeros((128, 4, 128), np.float32)
        dmp[:L, :, :L] = np.transpose(dm, (1, 0, 2))
        c['c_rmask%d' % L] = dmp
        xi = np.exp(lg[:, None] * (idx + 1.0))
        c['c_rxi%d' % L] = np.ascontiguousarray(np.broadcast_to(xi[None], (128, 4, L))).astype(np.float32)
        zeta = np.exp(lg[:, None] * (L - 1.0 - idx))
        zp = np.zeros((128, 4), np.float32)
        zp[:L] = zeta.T
        c['c_rzeta%d' % L] = zp
        c['c_rgl%d' % L] = np.ascontiguousarray(np.broadcast_to(np.exp(lg * L)[None], (128, 4))).astype(np.float32)
    return c


def rope_tables(pos):
    half = 64
    inv = (10000.0 ** (-np.arange(half, dtype=np.float32) / half)).astype(np.float32)
    ang = pos.astype(np.float32)[:, None] * inv[None, :]
    cos = np.cos(ang).astype(np.float32)
    sin = np.sin(ang).astype(np.float32)
    cf = np.concatenate([cos.T, cos.T], axis=0)
    sf = np.concatenate([-sin.T, sin.T], axis=0)
    return np.ascontiguousarray(cf), np.ascontiguousarray(sf), cos, sin


class Cfg:
    def __init__(self, n_p=2, T_x=4096, n_s=2, past=2048, layers=2, debug=False, stage=99):
        self.stage = stage
        self.n_p, self.T_x, self.n_s, self.past, self.layers, self.debug = n_p, T_x, n_s, past, layers, debug
        assert T_x % NPIECE == 0


IN_SPECS = None


def in_specs(cfg):
    n_p, T_x, n_s = cfg.n_p, cfg.T_x, cfg.n_s
    T = 16 + T_x
    sp = {
        'xp': [n_p, T_x, D], 'xs': [n_s, 16, D], 'meta': [16, D],
        's5re_in': [n_s, 2048], 's5im_in': [n_s, 2048],
        'mlc_in': [n_s, 4, 256, 256], 'mln_in': [n_s, 4, 256], 'mlm_in': [n_s, 4], 'mlconv_in': [n_s, 3, D],
        'ret_in': [n_s, 4, 128, 256], 'lruh_in': [n_s, D], 'lruconv_in': [n_s, 3, D],
        'norm_w': [2, D], 'final_norm_w': [D],
        'ev_w_in': [D, 3072], 'ev_w_out': [1536, D],
        'lamre_A': [128, 4, 64], 'lamim_A': [128, 4, 64], 'logdt_A': [128, 4, 64],
        'lamre_S': [128, 16], 'lamim_S': [128, 16], 'logdt_S': [128, 16],
        'bre_A': [128, 4, 64], 'bim_A': [128, 4, 64], 'cre_S': [128, 16, 16], 'cim_S': [128, 16, 16],
        's5_d': [512], 's5_w_glu': [512, 512], 's5_b_glu': [512],
        'ml_conv_w': [4, D], 'ml_conv_b': [D], 'ml_wq': [4, 256, 256], 'ml_wk': [4, 256, 256], 'ml_wv': [4, 256, 256],
        'ml_wqT': [4, 256, 256], 'ml_wkT': [4, 256, 256], 'ml_wvT': [4, 256, 256],
        'ml_w_if': [3072, 8], 'ml_b_if': [8], 'ml_norm_w': [D], 'ml_skip': [D],
        'od_w_in': [D, 5120], 'od_w_qksw': [D, 1024], 'fnw_b': [128, D], 'od_w_out': [2048, D], 'ret_norm_w': [D],
        'lru_conv_w': [4, D], 'lru_conv_b': [D], 'lru_w_a': [8, 128, 128], 'lru_b_a': [D],
        'lru_w_x': [8, 128, 128], 'lru_b_x': [D], 'lru_lambda': [D],
        'ropeF_c': [128, T + 16], 'ropeF_s': [128, T + 16], 'ropeT_c': [T + 16, 64], 'ropeT_s': [T + 16, 64],
    }
    for k, v in host_consts().items():
        sp[k] = list(v.shape)
    return sp


def out_specs(cfg):
    n_p, T_x, n_s = cfg.n_p, cfg.T_x, cfg.n_s
    o = {'yp': [n_p, T_x, D], 'ys': [n_s, 16, D]}
    for g, n in (('p', n_p), ('s', n_s)):
        o['o_s5re_' + g] = [n, 2048]
        o['o_s5im_' + g] = [n, 2048]
        o['o_mlc_' + g] = [n, 4, 256, 256]
        o['o_mln_' + g] = [n, 4, 256]
        o['o_mlm_' + g] = [n, 4]
        o['o_mlconv_' + g] = [n, 3, D]
        o['o_ret_' + g] = [n, 4, 128, 256]
        o['o_lruh_' + g] = [n, D]
        o['o_lruconv_' + g] = [n, 3, D]
    return o


def build(cfg):
    nc = bass.Bass("TRN2", target_bir_lowering=False)
    n_p, T_x, n_s = cfg.n_p, cfg.T_x, cfg.n_s
    T = 16 + T_x
    I = {k: nc.dram_tensor(k, v, F32, kind="ExternalInput").ap() for k, v in in_specs(cfg).items()}
    O = {k: nc.dram_tensor(k, v, F32, kind="ExternalOutput").ap() for k, v in out_specs(cfg).items()}
    hk = "ExternalOutput" if cfg.debug else "Internal"
    h1p = nc.dram_tensor("h1p", [n_p, T, D], F32, kind=hk).ap()
    h1s = nc.dram_tensor("h1s", [n_s, 16, D], F32, kind=hk).ap()
    dbg_o = nc.dram_tensor("dbg_o", [4, 128, 256], F32, kind=hk).ap()

    seqs = []
    for i in range(n_p):
        pcs = [(0, 16, I['meta'][:, :], O['yp'], None)]
        for t in range(0, T_x, NPIECE):
            pcs.append((16 + t, NPIECE, I['xp'][i, t:t + NPIECE, :], O['yp'][i, t:t + NPIECE, :], None))
        seqs.append(('p', i, pcs))
    for i in range(n_s):
        seqs.append(('s', i, [(0, 16, I['xs'][i, :, :], O['ys'][i, :, :], None)]))

    def h1rows(g, i, t0, n):
        return (h1p if g == 'p' else h1s)[i, t0:t0 + n, :]

    with ExitStack() as top:
        top.enter_context(nc.allow_non_contiguous_dma(reason='small strided parameter/state layouts'))
        P = Prog(nc)
        for s in P.semnames:
            P.sems[s] = top.enter_context(nc.semaphore(s))
        pb = [top.enter_context(nc.psum_tensor("pb%d" % i, [128, 512], F32)) for i in range(8)]

        V, G, A, PL = nc.vector, nc.gpsimd, nc.scalar, nc.tensor

        def mm(out, lhsT, rhs, start, stop, reads, writes):
            P.op('pe', lambda: PL.matmul(out, lhsT, rhs, start=start, stop=stop), reads=reads, writes=writes)

        sb0 = lambda n, s, d: top.enter_context(nc.sbuf_tensor(n, s, d))
        identb = sb0("identb", [128, 128], BF16)
        identf = sb0("identf", [128, 128], F32)
        negh = sb0("negh", [128, 1], F32)
        P.dma('pool', identb[:], I['c_identb'][:, :], writes=['identb'])
        P.dma('sp', identf[:], I['c_identf'][:, :], writes=['identf'])
        P.op('pool', lambda: G.memset(negh[:], -0.5), writes=['negh'])

        def rmsnorm_T(hin, tp, tt, gcol, ub, uT, ss, hkey='hin', gkey='gcol'):
            P.op('pool', lambda: G.memset(ss[0:tp, 0:1], 0.0), writes=['ss'])
            P.op('act', lambda: A.activation(out=ub[0:tp, :], in_=hin[0:tp, :], func=AF.Square, accum_out=ss[0:tp, 0:1]),
                 reads=[hkey], writes=['ub', 'ss'])
            P.op('dve', lambda: V.tensor_scalar(ss[0:tp, 1:2], ss[0:tp, 0:1], 1.0 / D, EPS, ALU.mult, ALU.add),
                 reads=['ss'], writes=['ss'])
            P.op('pool', lambda: G.tensor_tensor(ss[0:tp, 2:3], ss[0:tp, 1:2], negh[0:tp, :], ALU.pow),
                 reads=['ss', 'negh'], writes=['ss'])
            P.op('act', lambda: A.activation(out=ub[0:tp, :], in_=hin[0:tp, :], func=AF.Copy, scale=ss[0:tp, 2:3]),
                 reads=[hkey, 'ss'], writes=['ub'])
            pT = pb[2][:].bitcast(BF16)
            for k in range(8):
                P.op('pe', lambda k=k: PL.transpose(pT[:, k * 128:k * 128 + tp], ub[0:tp, k * 128:(k + 1) * 128],
                                                    identb[0:tp, 0:tp]),
                     reads=['ub', 'identb'], writes=['pb2'])
            for k in range(8):
                P.op('act', lambda k=k: A.activation(out=uT[:, k, tt * 128:tt * 128 + tp], in_=pT[:, k * 128:k * 128 + tp],
                                                     func=AF.Copy, scale=gcol[:, k:k + 1]),
                     reads=['pb2', gkey], writes=['uT'])

        with ExitStack() as l0:
            sb = lambda n, s, d: l0.enter_context(nc.sbuf_tensor(n, s, d))
            NP = NPIECE
            w_in = sb("w_in0", [128, 8, 3072], BF16)
            w_out = sb("w_out0", [128, 12, D], BF16)
            w_glu = sb("w_glu", [128, 4, 512], BF16)
            wq = sb("wq", [128, 4, 2, 256], BF16)
            wk = sb("wk", [128, 4, 2, 256], BF16)
            wv = sb("wv", [128, 4, 2, 256], BF16)
            wcif = sb("wcif", [128, 8, 8], BF16)
            wxif = sb("wxif", [128, 8, 8], BF16)
            s5B = sb("s5B", [128, 2, 16, 128], BF16)
            s5C = sb("s5C", [128, 2, 16, 128], BF16)
            LR = 64
            tabc = sb("tabc", [128, 16, LR], F32)
            tabs = sb("tabs", [128, 16, LR], F32)
            rtab = sb("rtab", [128, 16, 2], F32)
            rtab0 = sb("rtab0", [128, 16, LR], F32)
            gcol = sb("gcol0", [128, 8], F32)
            dcol = sb("dcol", [128, 4], F32)
            bgluh = sb("bgluh", [128, 4], F32)
            convw = sb("convw", [128, 8, 4], F32)
            convb = sb("convb", [128, 8], F32)
            mlnw = sb("mlnw", [128, 8], F32)
            skipc = sb("skipc", [128, 8], F32)
            b_i = sb("b_i", [4, 1], F32)
            b_f = sb("b_f", [4, 1], F32)
            negmask = sb("negmask", [128, 128], F32)
            sel = sb("sel", [4, 4, 128], F32)
            ones4 = sb("ones4", [4, 128], F32)
            onesrow = sb("onesrow", [4, NP], F32)

            for k in range(8):
                P.dma('pool', w_in[:, k, :], I['ev_w_in'][k * 128:(k + 1) * 128, :], writes=['w_in'])
            for k in range(12):
                P.dma('pool', w_out[:, k, :], I['ev_w_out'][k * 128:(k + 1) * 128, :], writes=['w_out'])
            P.dma('pool', w_glu[:], I['s5_w_glu'].rearrange("(k p) f -> p k f", p=128), writes=['w_glu'])
            for nm, t_ in (('ml_wq', wq), ('ml_wk', wk), ('ml_wv', wv)):
                for h in range(4):
                    P.dma('pool', t_[:, h, :, :], I[nm][h].rearrange("(k p) e -> p k e", p=128), writes=[nm])
            P.dma('sp', gcol[:], I['norm_w'][0, :].rearrange("(k p) -> p k", p=128), writes=['gcol'])
            P.dma('sp', dcol[:], I['s5_d'].rearrange("(k p) -> p k", p=128), writes=['dcol'])
            P.dma('sp', bgluh[:], I['s5_b_glu'].rearrange("(k p) -> p k", p=128), writes=['bgluh'])
            for tap in range(4):
                P.dma('sp', convw[:, :, tap], I['ml_conv_w'][tap, :].rearrange("(k p) -> p k", p=128), writes=['convw'])
            P.dma('sp', convb[:], I['ml_conv_b'].rearrange("(k p) -> p k", p=128), writes=['convb'])
            P.dma('sp', mlnw[:], I['ml_norm_w'].rearrange("(k p) -> p k", p=128), writes=['mlnw'])
            P.dma('sp', skipc[:], I['ml_skip'].rearrange("(k p) -> p k", p=128), writes=['skipc'])
            P.dma('sp', b_i[:], I['ml_b_if'][0:4].rearrange("(p o) -> p o", o=1), writes=['b_i'])
            P.dma('sp', b_f[:], I['ml_b_if'][4:8].rearrange("(p o) -> p o", o=1), writes=['b_f'])
            P.dma('sp', negmask[:], I['c_negmask'][:, :], writes=['negmask'])
            P.dma('sp', sel[:], I['c_sel'][:, :, :], writes=['sel'])
            P.dma('sp', ones4[:], I['c_ones4'][:, :], writes=['ones4'])
            P.op('pool', lambda: G.memset(onesrow[:], 1.0), writes=['onesrow'])
            P.op('dve', lambda: V.tensor_scalar(bgluh[:], bgluh[:], 0.5, None, ALU.mult), reads=['bgluh'], writes=['bgluh'])
            P.op('dve', lambda: V.tensor_scalar(b_f[:], b_f[:], -1.0, None, ALU.mult), reads=['b_f'], writes=['b_f'])

            print('SBUF remaining before setup', nc.sbuf_bytes_remaining)
            with ExitStack() as su:
                sbs = lambda n, s, d: su.enter_context(nc.sbuf_tensor(n, s, d))
                wqT = sbs("wqT", [128, 4, 2, 256], BF16)
                wkT = sbs("wkT", [128, 4, 2, 256], BF16)
                wvT = sbs("wvT", [128, 4, 2, 256], BF16)
                wif = sbs("wif", [128, 24, 8], BF16)
                for nm, t_ in (('ml_wqT', wqT), ('ml_wkT', wkT), ('ml_wvT', wvT)):
                    for h in range(4):
                        P.dma('pool', t_[:, h, :, :], I[nm][h].rearrange("(k p) e -> p k e", p=128), writes=[nm])
                P.dma('pool', wif[:], I['ml_w_if'].rearrange("(k p) g -> p k g", p=128), writes=['wif'])
                for h in range(4):
                    for dt_ in range(2):
                        o_ = pb[0][:, 0:8]
                        i_ = 0
                        for (wT, nm, off) in ((wqT, 'ml_wqT', 0), (wkT, 'ml_wkT', 8)):
                            for ke in range(2):
                                mm(o_, wT[:, h, ke, dt_ * 128:(dt_ + 1) * 128], wif[:, off + h * 2 + ke, :],
                                   i_ == 0, i_ == 3, [nm, 'wif'], ['pb0'])
                                i_ += 1
                        P.op('act', lambda h=h, dt_=dt_: A.copy(wcif[:, h * 2 + dt_, :], pb[0][:, 0:8]),
                             reads=['pb0'], writes=['wcif'])
                        o2 = pb[1][:, 0:8]
                        for ke in range(2):
                            mm(o2, wvT[:, h, ke, dt_ * 128:(dt_ + 1) * 128], wif[:, 16 + h * 2 + ke, :],
                               ke == 0, ke == 1, ['ml_wvT', 'wif'], ['pb1'])
                        P.op('act', lambda h=h, dt_=dt_: A.copy(wxif[:, h * 2 + dt_, :], pb[1][:, 0:8]),
                             reads=['pb1'], writes=['wxif'])

                def S(n, shp):
                    return sbs(n, shp, F32)

                def trig(name, ang_ap, shp, key):
                    cosT = S(name + "_cos", shp)
                    sinT = S(name + "_sin", shp)
                    sh = S(name + "_sh", shp)
                    acc = S(name + "_acc", shp)
                    tmp = S(name + "_tm", shp)
                    for (dst, shift) in ((cosT, PI / 2), (sinT, 0.0)):
                        P.op('dve', lambda shift=shift: V.tensor_scalar(sh[:], ang_ap, shift, None, ALU.add),
                             reads=[key], writes=[name + 'sh'])
                        P.op('dve', lambda: V.tensor_copy(acc[:], sh[:]), reads=[name + 'sh'], writes=[name + 'win_re'])
                        for m_ in range(1, 7):
                            thr = (2 * m_ - 1) * PI
                            P.op('dve', lambda thr=thr: V.tensor_scalar(tmp[:], sh[:], thr, -2 * PI, ALU.is_ge, ALU.mult),
                                 reads=[name + 'sh'], writes=[name + 'tm'])
                            P.op('dve', lambda: V.tensor_tensor(acc[:], acc[:], tmp[:], ALU.add),
                                 reads=[name + 'win_re', name + 'tm'], writes=[name + 'win_re'])
                        P.op('act', lambda dst=dst: A.activation(out=dst[:], in_=acc[:], func=AF.Sin),
                             reads=[name + 'win_re'], writes=[name + ('c' if dst is cosT else 's')])
                    return cosT, sinT

                shA = [128, 4, 64]
                lamreA = S("lamreA", shA); lamimA = S("lamimA", shA); dtA = S("dtA", shA)
                breA = S("breA", shA); bimA = S("bimA", shA)
                P.dma('sp', lamreA[:], I['lamre_A'][:, :, :], writes=['lamreA'])
                P.dma('sp', lamimA[:], I['lamim_A'][:, :, :], writes=['lamimA'])
                P.dma('sp', dtA[:], I['logdt_A'][:, :, :], writes=['dtA'])
                P.dma('sp', breA[:], I['bre_A'][:, :, :], writes=['breA'])
                P.dma('sp', bimA[:], I['bim_A'][:, :, :], writes=['bimA'])
                P.op('act', lambda: A.activation(out=dtA[:], in_=dtA[:], func=AF.Exp), reads=['dtA'], writes=['dtA'])
                magA = S("magA", shA); angA = S("angA", shA)
                P.op('dve', lambda: V.tensor_tensor(magA[:], lamreA[:], dtA[:], ALU.mult), reads=['lamreA', 'dtA'], writes=['magA'])
                P.op('act', lambda: A.activation(out=magA[:], in_=magA[:], func=AF.Exp), reads=['magA'], writes=['magA'])
                P.op('dve', lambda: V.tensor_tensor(angA[:], lamimA[:], dtA[:], ALU.mult), reads=['lamimA', 'dtA'], writes=['angA'])
                cosA, sinA = trig("tA", angA[:], shA, 'angA')
                abre = S("abre", shA); abim = S("abim", shA)
                P.op('dve', lambda: V.tensor_tensor(abre[:], magA[:], cosA[:], ALU.mult), reads=['magA', 'tAc'], writes=['abre'])
                P.op('dve', lambda: V.tensor_tensor(abim[:], magA[:], sinA[:], ALU.mult), reads=['magA', 'tAs'], writes=['abim'])
                den = S("denA", shA); t1 = S("t1A", shA); t2 = S("t2A", shA); kre = S("kreA", shA); kim = S("kimA", shA)
                P.op('dve', lambda: V.tensor_tensor(den[:], lamreA[:], lamreA[:], ALU.mult), reads=['lamreA'], writes=['denA'])
                P.op('dve', lambda: V.tensor_tensor(t1[:], lamimA[:], lamimA[:], ALU.mult), reads=['lamimA'], writes=['t1A'])
                P.op('dve', lambda: V.tensor_tensor(den[:], den[:], t1[:], ALU.add), reads=['denA', 't1A'], writes=['denA'])
                P.op('dve', lambda: V.reciprocal(den[:], den[:]), reads=['denA'], writes=['denA'])
                P.op('dve', lambda: V.tensor_scalar(abre[:], abre[:], -1.0, None, ALU.add), reads=['abre'], writes=['abre'])
                P.op('dve', lambda: V.tensor_tensor(t1[:], abre[:], lamreA[:], ALU.mult), reads=['abre', 'lamreA'], writes=['t1A'])
                P.op('dve', lambda: V.tensor_tensor(t2[:], abim[:], lamimA[:], ALU.mult), reads=['abim', 'lamimA'], writes=['t2A'])
                P.op('dve', lambda: V.tensor_tensor(t1[:], t1[:], t2[:], ALU.add), reads=['t1A', 't2A'], writes=['t1A'])
                P.op('dve', lambda: V.tensor_tensor(kre[:], t1[:], den[:], ALU.mult), reads=['t1A', 'denA'], writes=['kreA'])
                P.op('dve', lambda: V.tensor_tensor(t1[:], abim[:], lamreA[:], ALU.mult), reads=['abim', 'lamreA'], writes=['t1A'])
                P.op('dve', lambda: V.tensor_tensor(t2[:], abre[:], lamimA[:], ALU.mult), reads=['abre', 'lamimA'], writes=['t2A'])
                P.op('dve', lambda: V.tensor_tensor(t1[:], t1[:], t2[:], ALU.subtract), reads=['t1A', 't2A'], writes=['t1A'])
                P.op('dve', lambda: V.tensor_tensor(kim[:], t1[:], den[:], ALU.mult), reads=['t1A', 'denA'], writes=['kimA'])
                bbre = S("bbre", shA); bbim = S("bbim", shA)
                P.op('dve', lambda: V.tensor_tensor(t1[:], kre[:], breA[:], ALU.mult), reads=['kreA', 'breA'], writes=['t1A'])
                P.op('dve', lambda: V.tensor_tensor(t2[:], kim[:], bimA[:], ALU.mult), reads=['kimA', 'bimA'], writes=['t2A'])
                P.op('dve', lambda: V.tensor_tensor(bbre[:], t1[:], t2[:], ALU.subtract), reads=['t1A', 't2A'], writes=['bbre'])
                P.op('dve', lambda: V.tensor_tensor(t1[:], kre[:], bimA[:], ALU.mult), reads=['kreA', 'bimA'], writes=['t1A'])
                P.op('dve', lambda: V.tensor_tensor(t2[:], kim[:], breA[:], ALU.mult), reads=['kimA', 'breA'], writes=['t2A'])
                P.op('dve', lambda: V.tensor_tensor(bbim[:], t1[:], t2[:], ALU.add), reads=['t1A', 't2A'], writes=['bbim'])
                maskA = S("maskA", [128, 8])
                P.dma('sp', maskA[:], I['c_maskA'][:, :], writes=['maskA'])
                for j in range(16):
                    kc, jl = j // 4, j % 4
                    for h in range(2):
                        g8 = 2 * jl + h
                        for part, src, key in ((0, bbre, 'bbre'), (1, bbim, 'bbim')):
                            P.op('dve', lambda j=j, kc=kc, h=h, g8=g8, part=part, src=src: V.tensor_scalar(
                                s5B[:, part, j, h * 64:(h + 1) * 64], src[:, kc, :], maskA[:, g8:g8 + 1], None, ALU.mult),
                                reads=[key, 'maskA'], writes=['s5B'])
                creS = S("creS", [128, 16, 16]); cimS = S("cimS", [128, 16, 16])
                P.dma('sp', creS[:], I['cre_S'][:, :, :], writes=['creS'])
                P.dma('sp', cimS[:], I['cim_S'][:, :, :], writes=['cimS'])
                P.op('pool', lambda: G.memset(s5C[:], 0.0), writes=['s5C'])
                for j in range(16):
                    jl = j % 4
                    for h in range(2):
                        g8 = 2 * jl + h
                        P.op('dve', lambda j=j, h=h, g8=g8: V.tensor_copy(
                            s5C[h * 64:(h + 1) * 64, 0, j, g8 * 16:(g8 + 1) * 16], creS[h * 64:(h + 1) * 64, j, :]),
                            reads=['creS'], writes=['s5C'])
                        P.op('dve', lambda j=j, h=h, g8=g8: V.tensor_scalar(
                            s5C[h * 64:(h + 1) * 64, 1, j, g8 * 16:(g8 + 1) * 16], cimS[h * 64:(h + 1) * 64, j, :],
                            -1.0, None, ALU.mult), reads=['cimS'], writes=['s5C'])
                shS = [128, 16]
                lamreS = S("lamreS", shS); lamimS = S("lamimS", shS); dtS = S("dtS", shS)
                P.dma('sp', lamreS[:], I['lamre_S'][:, :], writes=['lamreS'])
                P.dma('sp', lamimS[:], I['lamim_S'][:, :], writes=['lamimS'])
                P.dma('sp', dtS[:], I['logdt_S'][:, :], writes=['dtS'])
                P.op('act', lambda: A.activation(out=dtS[:], in_=dtS[:], func=AF.Exp), reads=['dtS'], writes=['dtS'])
                rmag = S("rmag", shS); angS = S("angS", shS)
                P.op('dve', lambda: V.tensor_tensor(rmag[:], lamreS[:], dtS[:], ALU.mult), reads=['lamreS', 'dtS'], writes=['rmag'])
                P.op('act', lambda: A.activation(out=rmag[:], in_=rmag[:], func=AF.Exp), reads=['rmag'], writes=['rmag'])
                P.op('dve', lambda: V.tensor_tensor(angS[:], lamimS[:], dtS[:], ALU.mult), reads=['lamimS', 'dtS'], writes=['angS'])
                cosS, sinS = trig("tS", angS[:], shS, 'angS')
                onesL = S("onesL", [128, LR])
                P.op('pool', lambda: G.memset(onesL[:], 1.0), writes=['onesL'])
                for j in range(16):
                    P.op('dve', lambda j=j: V.tensor_scalar(rtab[:, j, :], onesL[:, 0:2], rmag[:, j:j + 1], None, ALU.mult),
                         reads=['rmag', 'onesL'], writes=['rtab'])
                for j in range(16):
                    P.op('dve', lambda j=j: V.tensor_scalar(rtab0[:, j, :], onesL[:], rmag[:, j:j + 1], None, ALU.mult),
                         reads=['rmag', 'onesL'], writes=['rtab0'])
                P.op('pool', lambda: G.memset(rtab0[:, :, 0:1], 0.0), reads=['rtab0'], writes=['rtab0'])
                P.op('dve', lambda: V.tensor_copy(tabc[:, :, 0:1], cosS[:].unsqueeze(2)), reads=['tSc'], writes=['tabc'])
                P.op('dve', lambda: V.tensor_copy(tabs[:, :, 0:1], sinS[:].unsqueeze(2)), reads=['tSs'], writes=['tabs'])
                d1 = S("dbl1", [128, LR // 2])
                m_ = 1
                while m_ < LR:
                    for j in range(16):
                        cm, sm = tabc[:, j, m_ - 1:m_], tabs[:, j, m_ - 1:m_]
                        lo_c, lo_s = tabc[:, j, 0:m_], tabs[:, j, 0:m_]
                        hi_c, hi_s = tabc[:, j, m_:2 * m_], tabs[:, j, m_:2 * m_]
                        a1 = d1[:, 0:m_]
                        P.op('dve', lambda: V.tensor_scalar(a1, lo_s, sm, None, ALU.mult), reads=['tabs'], writes=['dbl1'])
                        P.op('dve', lambda: V.scalar_tensor_tensor(hi_c, lo_c, cm, a1, ALU.mult, ALU.subtract),
                             reads=['tabc', 'dbl1'], writes=['tabc'])
                        P.op('dve', lambda: V.tensor_scalar(a1, lo_c, sm, None, ALU.mult), reads=['tabc', 'tabs'], writes=['dbl1'])
                        P.op('dve', lambda: V.scalar_tensor_tensor(hi_s, lo_s, cm, a1, ALU.mult, ALU.add),
                             reads=['tabs', 'tabc', 'dbl1'], writes=['tabs'])
                    m_ *= 2
                P.barrier()
            print('SBUF remaining after weights/setup', nc.sbuf_bytes_remaining)
            L_ = {}
            hin = sb("hin", [128, D], F32)
            ss = sb("ss", [128, 4], F32)
            ub = sb("ub", [128, D], BF16)
            uT = sb("uT", [128, 8, NP], BF16)
            gated = sb("gated", [128, 12, NP], BF16)
            ua = sb("ua", [128, 4, NP], BF16)
            sza = sb("sza", [128, 4, NP], BF16)
            tA = sb("tA_", [128, NP], F32)
            tB = sb("tB_", [128, NP], F32)
            wre = sb("wre", [128, NP // LR, 4, LR], F32)
            wim = sb("wim", [128, NP // LR, 4, LR], F32)
            tA2 = sb("tA2_", [128, NP], F32)
            tB2 = sb("tB2_", [128, NP], F32)
            hsre = sb("hsre", [128, 4, NP], BF16)
            hsim = sb("hsim", [128, 4, NP], BF16)
            Hre = sb("Hre", [128, 16], F32)
            Him = sb("Him", [128, 16], F32)
            ch1 = sb("ch1", [128, 4], F32)
            ch2 = sb("ch2", [128, 4], F32)
            ch3 = sb("ch3", [128, 4], F32)
            ch4 = sb("ch4", [128, 4], F32)
            y1 = sb("y1", [128, NP], F32)
            y2 = sb("y2", [128, NP + 1], F32)
            y3 = sb("y3", [128, NP + 1], F32)
            yg = sb("yg", [128, 4, NP], BF16)
            xb = sb("xb", [128, 8, 3 + NP], BF16)
            szb = sb("szb", [128, 8, NP], BF16)
            acc = sb("acc", [128, NP], F32)
            xc = sb("xc", [128, 8, NP], BF16)
            qT = uT
            kT = sb("kT", [128, 8, NP], BF16)
            vtok = sb("vtok", [128, 4, 264], BF16)
            ktok = sb("ktok", [128, 4, 256], BF16)
            Cst = sb("Cst", [128, 4, 2, 264], F32)
            Cbf = sb("Cbf", [128, 4, 2, 264], BF16)
            r_ig = sb("r_ig", [4, NP], F32)
            r_lf = sb("r_lf", [4, NP], F32)
            Mfull = sb("Mfull", [4, NP + 1], F32)
            negM = sb("negM", [4, NP + 1], F32)
            r_wi = sb("r_wi", [4, NP], F32)
            r_g = sb("r_g", [4, NP], F32)
            r_emm = sb("r_emm", [4, NP], F32)
            r_wend = sb("r_wend", [4, NP], F32)
            r_e = r_wend
            mcarry = sb("mcarry", [4, 1], F32)
            dg4 = sb("dg4", [4, 4], F32)
            cols = sb("cols", [128, 20], F32)
            Esb = y1[:, 0:128]
            Dm = y1[:, 128:256]
            PT = sb("PT", [128, 128], BF16)
            kw = sb("kw", [128, 256], BF16)
            inter = y2
            tot = y3
            st6 = sb("st6", [128, 6], F32)
            st2 = sb("st2", [128, 8], F32)
            hn = ub
            ytmp = sb("ytmp", [128, 128], F32)
            xbo = sb("xbo", [128, 8, 3], F32)

            print('SBUF remaining after L0 activations', nc.sbuf_bytes_remaining)
            P.op('pool', lambda: G.memset(vtok[:], 1.0), writes=['vtok'])

            mmreg = [(pb[0], 'pb0'), (pb[1], 'pb1')]
            mmi = [0]

            def nextmm():
                r_ = mmreg[mmi[0] % 2]
                mmi[0] += 1
                return r_

            def inproj(f, n):
                ps, key = nextmm()
                for k in range(8):
                    mm(ps[:, 0:n], w_in[:, k, f * 128:(f + 1) * 128], uT[:, k, 0:n], k == 0, k == 7, ['w_in', 'uT'], [key])
                return ps, key

            for (g, si, pcs) in seqs:
                if g == 'p':
                    P.op('pool', lambda: G.memset(Hre[:], 0.0), writes=['Hre'])
                    P.op('pool', lambda: G.memset(Him[:], 0.0), writes=['Him'])
                    P.op('pool', lambda: G.memset(Cst[:], 0.0), writes=['Cst%d' % h_ for h_ in range(4)])
                    P.op('pool', lambda: G.memset(mcarry[:], 0.0), writes=['mcarry'])
                    P.op('pool', lambda: G.memset(xb[:, :, 0:3], 0.0), writes=['xb'])
                else:
                    P.dma('sp', Hre[:], I['s5re_in'][si, :].rearrange("(j q) -> q j", q=128), writes=['Hre'])
                    P.dma('sp', Him[:], I['s5im_in'][si, :].rearrange("(j q) -> q j", q=128), writes=['Him'])
                    for h in range(4):
                        P.dma('sp', Cst[:, h, :, 0:256], I['mlc_in'][si, h].rearrange("(k p) e -> p k e", p=128), writes=['Cst%d' % h])
                        P.dma('sp', Cst[:, h, :, 256], I['mln_in'][si, h].rearrange("(k p) -> p k", p=128), writes=['Cst%d' % h])
                    P.dma('sp', mcarry[:], I['mlm_in'][si, :].rearrange("(p o) -> p o", o=1), writes=['mcarry'])
                    for t_ in range(3):
                        P.dma('pool', xb[:, :, t_], I['mlconv_in'][si, t_].rearrange("(k p) -> p k", p=128), writes=['xb'])
                P.op('act', lambda: A.copy(Cbf[:], Cst[:]), reads=['Cst%d' % h_ for h_ in range(4)], writes=['Cbf%d' % h_ for h_ in range(4)])

                for (t0, n, src, _, _) in pcs:
                    tp = min(n, 128)
                    ntt = (n + 127) // 128
                    L = tp
                    nch = n // L
                    for tt in range(ntt):
                        P.dma('sp', hin[0:tp, :], src[tt * 128:tt * 128 + tp, :], writes=['hin'])
                        rmsnorm_T(hin, tp, tt, gcol, ub, uT, ss)
                    for f in range(4):
                        ps, key = inproj(f, n)
                        P.op('act', lambda f=f, ps=ps: A.copy(ua[:, f, 0:n], ps[:, 0:n]), reads=[key], writes=['ua'])
                    for f in range(4):
                        ps, key = inproj(4 + f, n)
                        P.op('act', lambda f=f, ps=ps: A.activation(out=sza[:, f, 0:n], in_=ps[:, 0:n], func=AF.Silu),
                             reads=[key], writes=['sza'])

                    def gen_s5():
                        Ls = min(n, LR)
                        ncs = n // Ls
                        def v3(ap2):
                            return ap2.rearrange("p (c l) -> p c l", c=ncs)

                        for kc in range(4):
                            tcs, tss = [], []
                            for jl in range(4):
                                j = 4 * kc + jl
                                sreg = (pb[4], 'pb4') if j % 2 == 0 else (pb[5], 'pb5')
                                pre, pim = sreg[0][:, 0:n], sreg[0][:, 256:256 + n]
                                mm(pre, s5B[:, 0, j, :], ua[:, kc, 0:n], True, True, ['s5B', 'ua'], [sreg[1]])
                                mm(pim, s5B[:, 1, j, :], ua[:, kc, 0:n], True, True, ['s5B', 'ua'], [sreg[1]])
                                tc_ = tabc[:, j, 0:Ls].unsqueeze(1).to_broadcast([128, ncs, Ls])
                                ts_ = tabs[:, j, 0:Ls].unsqueeze(1).to_broadcast([128, ncs, Ls])
                                tcs.append(tc_); tss.append(ts_)
                                P.op('dve', lambda pre=pre, tc_=tc_: V.tensor_tensor(v3(tA[:, 0:n]), v3(pre), tc_, ALU.mult),
                                     reads=[sreg[1], 'tabc'], writes=['tA'])
                                P.op('dve', lambda pim=pim, ts_=ts_: V.tensor_tensor(v3(tB[:, 0:n]), v3(pim), ts_, ALU.mult),
                                     reads=[sreg[1], 'tabs'], writes=['tB'])
                                P.op('pool', lambda jl=jl: G.tensor_tensor(wre[:, 0:ncs, jl, 0:Ls], v3(tA[:, 0:n]), v3(tB[:, 0:n]), ALU.add),
                                     reads=['tA', 'tB'], writes=['wre%d' % jl])
                                P.op('dve', lambda pim=pim, tc_=tc_: V.tensor_tensor(v3(tA2[:, 0:n]), v3(pim), tc_, ALU.mult),
                                     reads=[sreg[1], 'tabc'], writes=['tA2'])
                                P.op('dve', lambda pre=pre, ts_=ts_: V.tensor_tensor(v3(tB2[:, 0:n]), v3(pre), ts_, ALU.mult),
                                     reads=[sreg[1], 'tabs'], writes=['tB2'])
                                P.op('pool', lambda jl=jl: G.tensor_tensor(wim[:, 0:ncs, jl, 0:Ls], v3(tA2[:, 0:n]), v3(tB2[:, 0:n]), ALU.subtract),
                                     reads=['tA2', 'tB2'], writes=['wim%d' % jl])
                                if jl % 2 == 1:
                                    yield
                            j0 = 4 * kc
                            for c in range(ncs):
                                cs = slice(c * Ls, (c + 1) * Ls)
                                if c == 0:
                                    i4r, i4i, rk = Hre[:, j0:j0 + 4], Him[:, j0:j0 + 4], ['Hre', 'Him']
                                else:
                                    i4r, i4i, rk = ch1[:, 0:4], ch2[:, 0:4], ['ch1', 'ch2']
                                w4r_ = ['wre%d' % i_ for i_ in range(4)]
                                w4i_ = ['wim%d' % i_ for i_ in range(4)]
                                r4 = rtab[:, j0:j0 + 4, 1]
                                c0 = c * Ls
                                P.op('dve', lambda: V.tensor_tensor(ch3[:, 0:4], i4r, r4, ALU.mult), reads=rk + ['rtab'], writes=['ch3'])
                                P.op('dve', lambda: V.tensor_tensor(ch4[:, 0:4], i4i, r4, ALU.mult), reads=rk + ['rtab'], writes=['ch4'])
                                P.op('dve', lambda: V.tensor_tensor(wre[:, c, :, 0], wre[:, c, :, 0], ch3[:, 0:4], ALU.add), reads=w4r_ + ['ch3'], writes=w4r_)
                                P.op('dve', lambda: V.tensor_tensor(wim[:, c, :, 0], wim[:, c, :, 0], ch4[:, 0:4], ALU.add), reads=w4i_ + ['ch4'], writes=w4i_)
                                if Ls == LR:
                                    d0 = rtab0[:, j0:j0 + 4, :].rearrange("p j l -> p (j l)")
                                    P.op('dve', lambda: V.tensor_tensor_scan(wre[:, c, :, :].rearrange("p j l -> p (j l)"), d0,
                                                                             wre[:, c, :, :].rearrange("p j l -> p (j l)"), 0.0, ALU.mult, ALU.add),
                                         reads=w4r_ + ['rtab0'], writes=w4r_)
                                    P.op('dve', lambda: V.tensor_tensor_scan(wim[:, c, :, :].rearrange("p j l -> p (j l)"), d0,
                                                                             wim[:, c, :, :].rearrange("p j l -> p (j l)"), 0.0, ALU.mult, ALU.add),
                                         reads=w4i_ + ['rtab0'], writes=w4i_)
                                else:
                                    for jl in range(4):
                                        P.op('dve', lambda jl=jl: V.tensor_tensor_scan(wre[:, c, jl, 0:Ls], rtab0[:, j0 + jl, 0:Ls], wre[:, c, jl, 0:Ls], 0.0,
                                                                                       ALU.mult, ALU.add), reads=w4r_ + ['rtab0'], writes=w4r_)
                                        P.op('dve', lambda jl=jl: V.tensor_tensor_scan(wim[:, c, jl, 0:Ls], rtab0[:, j0 + jl, 0:Ls], wim[:, c, jl, 0:Ls], 0.0,
                                                                                       ALU.mult, ALU.add), reads=w4i_ + ['rtab0'], writes=w4i_)
                                e_ = (c + 1) * Ls - 1
                                last = (c == ncs - 1)
                                wlr, wli = wre[:, c, :, Ls - 1], wim[:, c, :, Ls - 1]
                                cL4, sL4 = tabc[:, j0:j0 + 4, Ls - 1], tabs[:, j0:j0 + 4, Ls - 1]
                                dre = Hre[:, j0:j0 + 4] if last else ch1[:, 0:4]
                                dim_ = Him[:, j0:j0 + 4] if last else ch2[:, 0:4]
                                kre_ = 'Hre' if last else 'ch1'
                                kim_ = 'Him' if last else 'ch2'
                                w4r = ['wre%d' % i_ for i_ in range(4)]
                                w4i = ['wim%d' % i_ for i_ in range(4)]
                                P.op('dve', lambda: V.tensor_tensor(ch3[:, 0:4], wli, sL4, ALU.mult), reads=w4i + ['tabs'], writes=['ch3'])
                                P.op('dve', lambda: V.tensor_tensor(ch4[:, 0:4], wlr, sL4, ALU.mult), reads=w4r + ['tabs'], writes=['ch4'])
                                P.op('dve', lambda: V.tensor_tensor(dre, wlr, cL4, ALU.mult), reads=w4r + ['tabc', kre_], writes=[kre_])
                                P.op('dve', lambda: V.tensor_tensor(dim_, wli, cL4, ALU.mult), reads=w4i + ['tabc', kim_], writes=[kim_])
                                P.op('dve', lambda: V.tensor_tensor(dre, dre, ch3[:, 0:4], ALU.subtract), reads=[kre_, 'ch3'], writes=[kre_])
                                P.op('dve', lambda: V.tensor_tensor(dim_, dim_, ch4[:, 0:4], ALU.add), reads=[kim_, 'ch4'], writes=[kim_])
                                if c % 2 == 1 or last:
                                    yield
                            for jl in range(4):
                                tc_, ts_ = tcs[jl], tss[jl]
                                P.op('dve', lambda jl=jl, tc_=tc_: V.tensor_tensor(v3(tA[:, 0:n]), wre[:, 0:ncs, jl, 0:Ls], tc_, ALU.mult),
                                     reads=['wre%d' % jl, 'tabc'], writes=['tA'])
                                P.op('dve', lambda jl=jl, ts_=ts_: V.tensor_tensor(v3(tB[:, 0:n]), wim[:, 0:ncs, jl, 0:Ls], ts_, ALU.mult),
                                     reads=['wim%d' % jl, 'tabs'], writes=['tB'])
                                P.op('pool', lambda jl=jl: G.tensor_tensor(hsre[:, jl, 0:n], tA[:, 0:n], tB[:, 0:n], ALU.subtract),
                                     reads=['tA', 'tB'], writes=['hsre%d' % jl])
                                P.op('dve', lambda jl=jl, tc_=tc_: V.tensor_tensor(v3(tA2[:, 0:n]), wim[:, 0:ncs, jl, 0:Ls], tc_, ALU.mult),
                                     reads=['wim%d' % jl, 'tabc'], writes=['tA2'])
                                P.op('dve', lambda jl=jl, ts_=ts_: V.tensor_tensor(v3(tB2[:, 0:n]), wre[:, 0:ncs, jl, 0:Ls], ts_, ALU.mult),
                                     reads=['wre%d' % jl, 'tabs'], writes=['tB2'])
                                P.op('pool', lambda jl=jl: G.tensor_tensor(hsim[:, jl, 0:n], tA2[:, 0:n], tB2[:, 0:n], ALU.add),
                                     reads=['tA2', 'tB2'], writes=['hsim%d' % jl])
                                if jl % 2 == 1:
                                    yield
                            ps, key = nextmm()
                            for jl in range(4):
                                j = 4 * kc + jl
                                mm(ps[:, 0:n], s5C[:, 0, j, :], hsre[:, jl, 0:n], jl == 0, False, ['s5C', 'hsre%d' % jl], [key])
                                mm(ps[:, 0:n], s5C[:, 1, j, :], hsim[:, jl, 0:n], False, jl == 3, ['s5C', 'hsim%d' % jl], [key])
                            P.op('dve', lambda kc=kc, ps=ps: V.scalar_tensor_tensor(y1[:, 0:n], ua[:, kc, 0:n], dcol[:, kc:kc + 1],
                                                                                   ps[:, 0:n], ALU.mult, ALU.add),
                                 reads=['ua', 'dcol', key], writes=['y1'])
                            P.op('pool', lambda: G.tensor_tensor(y2[:, 0:n], y1[:, 0:n], y1[:, 0:n], ALU.mult), reads=['y1'], writes=['y2'])
                            P.op('pool', lambda: G.tensor_scalar(y2[:, 0:n], y2[:, 0:n], 0.044715, 1.0, ALU.mult, ALU.add),
                                 reads=['y2'], writes=['y2'])
                            P.op('pool', lambda: G.tensor_tensor(y2[:, 0:n], y2[:, 0:n], y1[:, 0:n], ALU.mult), reads=['y2', 'y1'], writes=['y2'])
                            P.op('act', lambda: A.activation(out=y3[:, 0:n], in_=y2[:, 0:n], func=AF.Tanh, scale=math.sqrt(2.0 / PI)),
                                 reads=['y2'], writes=['y3'])
                            P.op('dve', lambda: V.tensor_scalar(y3[:, 0:n], y3[:, 0:n], 0.5, 0.5, ALU.mult, ALU.add), reads=['y3'], writes=['y3'])
                            P.op('dve', lambda kc=kc: V.tensor_tensor(yg[:, kc, 0:n], y3[:, 0:n], y1[:, 0:n], ALU.mult),
                                 reads=['y3', 'y1'], writes=['yg%d' % kc])
                            yield
                        for m_ in range(4):
                            ps, key = nextmm()
                            for kk in range(4):
                                mm(ps[:, 0:n], w_glu[:, kk, m_ * 128:(m_ + 1) * 128], yg[:, kk, 0:n], kk == 0, kk == 3,
                                   ['w_glu', 'yg%d' % kk], [key])
                            P.op('act', lambda m_=m_, ps=ps: A.activation(out=y3[:, 0:n], in_=ps[:, 0:n], func=AF.Tanh, scale=0.5,
                                                                          bias=bgluh[:, m_:m_ + 1]),
                                 reads=[key, 'bgluh'], writes=['y3'])
                            P.op('dve', lambda: V.tensor_scalar(y3[:, 0:n], y3[:, 0:n], 0.5, 0.5, ALU.mult, ALU.add), reads=['y3'], writes=['y3'])
                            P.op('dve', lambda m_=m_: V.tensor_tensor(y3[:, 0:n], y3[:, 0:n], yg[:, m_, 0:n], ALU.mult),
                                 reads=['y3', 'yg%d' % m_], writes=['y3'])
                            P.op('dve', lambda m_=m_: V.tensor_tensor(gated[:, m_, 0:n], y3[:, 0:n], sza[:, m_, 0:n], ALU.mult),
                                 reads=['y3', 'sza'], writes=['gated%d' % m_])
                            yield

                    def gen_ml():
                        for f in range(8):
                            ps, key = inproj(8 + f, n)
                            P.op('act', lambda f=f, ps=ps: A.copy(xb[:, f, 3:3 + n], ps[:, 0:n]), reads=[key], writes=['xb'])
                            if f % 2 == 1:
                                yield
                        for f in range(8):
                            ps, key = inproj(16 + f, n)
                            P.op('act', lambda f=f, ps=ps: A.activation(out=szb[:, f, 0:n], in_=ps[:, 0:n], func=AF.Silu),
                                 reads=[key], writes=['szb'])
                            if f % 2 == 1:
                                yield
                        for k in range(8):
                            P.op('dve', lambda k=k: V.tensor_scalar(acc[:, 0:n], xb[:, k, 0:n], convw[:, k, 0:1], convb[:, k:k + 1],
                                                                    ALU.mult, ALU.add),
                                 reads=['xb', 'convw', 'convb'], writes=['acc'])
                            for tap in range(1, 4):
                                P.op('dve', lambda k=k, tap=tap: V.scalar_tensor_tensor(
                                    acc[:, 0:n], xb[:, k, tap:tap + n], convw[:, k, tap:tap + 1], acc[:, 0:n], ALU.mult, ALU.add),
                                    reads=['xb', 'convw', 'acc'], writes=['acc'])
                            P.op('act', lambda k=k: A.activation(out=xc[:, k, 0:n], in_=acc[:, 0:n], func=AF.Silu),
                                 reads=['acc'], writes=['xc'])
                            yield
                        gip, gikey = nextmm()
                        gfp, gfkey = nextmm()
                        gi, gf = gip[0:4, 0:n], gfp[0:4, 0:n]
                        for gsel, o_, gkey_ in ((0, gi, gikey), (4, gf, gfkey)):
                            for k in range(8):
                                mm(o_, wcif[:, k, gsel:gsel + 4], xc[:, k, 0:n], k == 0, False, ['wcif', 'xc'], [gkey_])
                            for k in range(8):
                                mm(o_, wxif[:, k, gsel:gsel + 4], xb[:, k, 3:3 + n], False, k == 7, ['wxif', 'xb'], [gkey_])
                        P.op('act', lambda: A.activation(out=r_ig[:, 0:n], in_=gi, func=AF.Identity, bias=b_i[:, 0:1]),
                             reads=[gikey, 'b_i'], writes=['r_ig'])
                        P.op('act', lambda: A.activation(out=r_e[:, 0:n], in_=gf, func=AF.Exp, scale=-1.0, bias=b_f[:, 0:1]),
                             reads=[gfkey, 'b_f'], writes=['r_wend'])
                        P.op('act', lambda: A.activation(out=r_e[:, 0:n], in_=r_e[:, 0:n], func=AF.Ln, bias=1.0),
                             reads=['r_wend'], writes=['r_wend'])
                        P.op('dve', lambda: V.tensor_tensor_scan(r_lf[:, 0:n], onesrow[:, 0:n], r_e[:, 0:n], 0.0, ALU.mult, ALU.add),
                             reads=['r_wend', 'onesrow'], writes=['r_lf'])
                        P.op('dve', lambda: V.tensor_tensor(r_ig[:, 0:n], r_ig[:, 0:n], r_lf[:, 0:n], ALU.add),
                             reads=['r_ig', 'r_lf'], writes=['r_ig'])
                        P.op('dve', lambda: V.tensor_copy(Mfull[:, 0:1], mcarry[:, 0:1]), reads=['mcarry'], writes=['Mfull'])
                        P.op('dve', lambda: V.tensor_tensor_scan(Mfull[:, 1:n + 1], onesrow[:, 0:n], r_ig[:, 0:n], mcarry[:, 0:1],
                                                                 ALU.mult, ALU.max),
                             reads=['r_ig', 'onesrow', 'mcarry', 'Mfull'], writes=['Mfull'])
                        P.op('dve', lambda: V.tensor_scalar(negM[:, 0:n + 1], Mfull[:, 0:n + 1], -1.0, None, ALU.mult),
                             reads=['Mfull'], writes=['negM'])
                        P.op('dve', lambda: V.tensor_tensor(r_emm[:, 0:n], r_lf[:, 0:n], Mfull[:, 1:n + 1], ALU.subtract),
                             reads=['r_lf', 'Mfull'], writes=['r_emm'])
                        P.op('act', lambda: A.activation(out=r_emm[:, 0:n], in_=r_emm[:, 0:n], func=AF.Exp), reads=['r_emm'], writes=['r_emm'])
                        P.op('dve', lambda: V.tensor_tensor(mcarry[:, 0:1], Mfull[:, n:n + 1], r_lf[:, n - 1:n], ALU.subtract),
                             reads=['Mfull', 'r_lf'], writes=['mcarry'])
                        for c in range(nch):
                            cs = slice(c * L, (c + 1) * L)
                            P.op('act', lambda c=c, cs=cs: A.activation(out=r_wi[:, cs], in_=Mfull[:, 1 + c * L:1 + (c + 1) * L], func=AF.Exp,
                                                                        scale=-1.0, bias=Mfull[:, c * L:c * L + 1]),
                                 reads=['Mfull'], writes=['r_wi'])
                            P.op('act', lambda c=c, cs=cs: A.activation(out=r_g[:, cs], in_=Mfull[:, 1 + c * L:1 + (c + 1) * L], func=AF.Exp,
                                                                        scale=-1.0, bias=Mfull[:, (c + 1) * L:(c + 1) * L + 1]),
                                 reads=['Mfull'], writes=['r_g'])
                            P.op('act', lambda c=c, cs=cs: A.activation(out=r_wend[:, cs], in_=r_ig[:, cs], func=AF.Exp,
                                                                        bias=negM[:, (c + 1) * L:(c + 1) * L + 1]),
                                 reads=['r_ig', 'negM'], writes=['r_wend'])
                        yield
                        for h in range(4):
                            for e in range(2):
                                for (w_, dst, sc, wkey) in ((wq, qT, 1.0, 'ml_wq'), (wk, kT, 1.0 / 16.0, 'ml_wk')):
                                    ps, key = nextmm()
                                    for kd in range(2):
                                        mm(ps[:, 0:n], w_[:, h, kd, e * 128:(e + 1) * 128], xc[:, 2 * h + kd, 0:n], kd == 0, kd == 1,
                                           [wkey, 'xc'], [key])
                                    P.op('act', lambda ps=ps, dst=dst, h=h, e=e, sc=sc: A.mul(dst[:, 2 * h + e, 0:n], ps[:, 0:n], sc),
                                         reads=[key], writes=['uT'] if dst is qT else ['kT'])
                        for c in range(nch):
                            cs = slice(c * L, (c + 1) * L)
                            tt = c
                            ts0 = c * 128
                            for h in range(4):
                                ps, key = nextmm()
                                for kd in range(2):
                                    mm(ps[0:tp, 0:256], xb[:, 2 * h + kd, 3 + ts0:3 + ts0 + tp], wv[:, h, kd, :], kd == 0, kd == 1,
                                       ['xb', 'ml_wv'], [key])
                                P.op('act', lambda ps=ps, h=h: A.copy(vtok[0:tp, h, 0:256], ps[0:tp, 0:256]),
                                     reads=[key], writes=['vtok'])
                                ps, key = nextmm()
                                for kd in range(2):
                                    mm(ps[0:tp, 0:256], xc[:, 2 * h + kd, ts0:ts0 + tp], wk[:, h, kd, :], kd == 0, kd == 1,
                                       ['xc', 'ml_wk'], [key])
                                P.op('act', lambda ps=ps, h=h: A.mul(ktok[0:tp, h, :], ps[0:tp, 0:256], 1.0 / 16.0),
                                     reads=[key], writes=['ktok'])
                            pc = pb[6]
                            for i_, row in enumerate((r_wi, r_emm, r_wend, r_g)):
                                mm(pc[0:L, 384 + 4 * i_:388 + 4 * i_], row[0:4, cs], identf[0:4, 0:4], True, True,
                                   ['r_wi', 'r_emm', 'r_wend', 'r_g', 'identf'], ['pb6c'])
                            P.op('dve', lambda c=c: V.tensor_scalar(dg4[:, :], identf[0:4, 0:4], r_wi[0:4, (c + 1) * L - 1:(c + 1) * L], None, ALU.mult),
                                 reads=['r_wi', 'identf'], writes=['dg4'])
                            mm(pc[:, 400:404], ones4[0:4, :], dg4[0:4, :], True, True, ['ones4', 'dg4'], ['pb6c'])
                            P.op('dve', lambda pc=pc: V.tensor_copy(cols[:, 0:20], pc[:, 384:404]), reads=['pb6c'], writes=['cols'])
                            yield
                            for h in range(4):
                                stp, stkey = nextmm()
                                ST = stp[0:L, 0:L]
                                for ke in range(2):
                                    mm(ST, kT[:, 2 * h + ke, cs], qT[:, 2 * h + ke, cs], ke == 0, ke == 1, ['kT', 'uT'], [stkey])
                                P.op('dve', lambda ST=ST, h=h: V.scalar_tensor_tensor(PT[0:L, 0:L], ST, cols[0:L, 8 + h:9 + h], negmask[0:L, 0:L],
                                                                                     ALU.mult, ALU.mult),
                                     reads=[stkey, 'cols', 'negmask'], writes=['PT'])
                                mm(pb[7][0:L, 0:257], PT[0:L, 0:L], vtok[0:L, h, 0:257], True, True, ['PT', 'vtok'], ['pb7'])
                                for kd in range(2):
                                    mm(pb[3][0:L, 0:257], qT[:, 2 * h + kd, cs], Cbf[:, h, kd, 0:257], kd == 0, kd == 1, ['uT', 'Cbf%d' % h], ['pb3'])
                                P.op('act', lambda h=h: A.activation(out=inter[0:L, 0:257], in_=pb[3][0:L, 0:257], func=AF.Copy,
                                                                     scale=cols[0:L, h:h + 1]),
                                     reads=['pb3', 'cols'], writes=['y2'])
                                P.op('dve', lambda h=h: V.scalar_tensor_tensor(tot[0:L, 0:257], pb[7][0:L, 0:257], cols[0:L, 12 + h:13 + h],
                                                                            inter[0:L, 0:257], ALU.mult, ALU.add),
                                     reads=['pb7', 'y2', 'cols'], writes=['y3'])
                                P.op('dve', lambda: V.scalar_tensor_tensor(st2[0:L, 2:3], tot[0:L, 256:257], -1.0, tot[0:L, 256:257],
                                                                           ALU.mult, ALU.max), reads=['y3'], writes=['st2c'])
                                P.op('dve', lambda h=h: V.tensor_tensor(st2[0:L, 2:3], st2[0:L, 2:3], cols[0:L, 4 + h:5 + h], ALU.max),
                                     reads=['st2c', 'cols'], writes=['st2c'])
                                P.op('dve', lambda: V.reciprocal(st2[0:L, 3:4], st2[0:L, 2:3]), reads=['st2c'], writes=['st2d'])
                                P.op('dve', lambda: V.bn_stats(st6[0:L, :], tot[0:L, 0:256]), reads=['y3'], writes=['st6'])
                                P.op('dve', lambda: V.bn_aggr(st2[0:L, 4:6], st6[0:L, :]), reads=['st6'], writes=['st2e'])
                                P.op('dve', lambda: V.tensor_tensor(st2[0:L, 6:7], st2[0:L, 3:4], st2[0:L, 3:4], ALU.mult),
                                     reads=['st2d'], writes=['st2f'])
                                P.op('dve', lambda: V.tensor_scalar(st2[0:L, 6:7], st2[0:L, 6:7], st2[0:L, 5:6], EPS, ALU.mult, ALU.add),
                                     reads=['st2f', 'st2e'], writes=['st2f'])
                                P.op('pool', lambda: G.tensor_tensor(st2[0:L, 7:8], st2[0:L, 6:7], negh[0:L, :], ALU.pow),
                                     reads=['st2f', 'negh'], writes=['st2g'])
                                P.op('dve', lambda: V.tensor_tensor(st2[0:L, 7:8], st2[0:L, 7:8], st2[0:L, 3:4], ALU.mult),
                                     reads=['st2g', 'st2d'], writes=['st2g'])
                                P.op('dve', lambda tt=tt, h=h: V.tensor_scalar(hn[0:L, h * 256:(h + 1) * 256], tot[0:L, 0:256],
                                                                              st2[0:L, 4:5], st2[0:L, 7:8], ALU.subtract, ALU.mult),
                                     reads=['y3', 'st2e', 'st2g'], writes=['ub'])
                                P.op('act', lambda tt=tt, h=h: A.activation(out=kw[0:L, :], in_=ktok[0:L, h, :], func=AF.Copy, scale=cols[0:L, 8 + h:9 + h]),
                                     reads=['ktok', 'cols'], writes=['kw'])
                                for kd in range(2):
                                    cup = (pb[4], 'pb4') if kd == 0 else (pb[5], 'pb5')
                                    mm(cup[0][:, 0:257], kw[0:L, kd * 128:(kd + 1) * 128], vtok[0:L, h, 0:257], True, True, ['kw', 'vtok'], [cup[1]])
                                    P.op('dve', lambda h=h, kd=kd, cup=cup: V.scalar_tensor_tensor(
                                        Cst[:, h, kd, 0:257], Cst[:, h, kd, 0:257], cols[:, 16 + h:17 + h], cup[0][:, 0:257], ALU.mult, ALU.add),
                                        reads=['Cst%d' % h, 'cols', cup[1]], writes=['Cst%d' % h])
                                P.op('act', lambda h=h: A.copy(Cbf[:, h, :, :], Cst[:, h, :, :]), reads=['Cst%d' % h], writes=['Cbf%d' % h])
                                yield
                            pT = pb[2][:].bitcast(BF16)
                            for k in range(8):
                                P.op('pe', lambda k=k: PL.transpose(pT[:, k * 128:k * 128 + tp], hn[0:tp, k * 128:(k + 1) * 128],
                                                                    identb[0:tp, 0:tp]),
                                     reads=['ub', 'identb'], writes=['pb2'])
                            for k in range(8):
                                P.op('dve', lambda k=k: V.tensor_scalar(ytmp[:, 0:tp], pT[:, k * 128:k * 128 + tp], mlnw[:, k:k + 1], None, ALU.mult),
                                     reads=['pb2', 'mlnw'], writes=['ytmp'])
                                P.op('dve', lambda k=k: V.scalar_tensor_tensor(ytmp[:, 0:tp], xc[:, k, ts0:ts0 + tp], skipc[:, k:k + 1],
                                                                               ytmp[:, 0:tp], ALU.mult, ALU.add),
                                     reads=['xc', 'skipc', 'ytmp'], writes=['ytmp'])
                                P.op('dve', lambda k=k: V.tensor_tensor(gated[:, 4 + k, ts0:ts0 + tp], ytmp[:, 0:tp], szb[:, k, ts0:ts0 + tp], ALU.mult),
                                     reads=['ytmp', 'szb'], writes=['gatedB'])
                            yield
                        P.op('dve', lambda n=n: V.tensor_copy(xbo[:, :, :], xb[:, :, n:n + 3]), reads=['xb'], writes=['xbo'])
                        P.op('dve', lambda n=n: V.tensor_copy(xb[:, :, 0:3], xbo[:, :, :]), reads=['xbo'], writes=['xb'])
                    gens = [gen_s5(), gen_ml()]
                    while gens:
                        for g_ in list(gens):
                            try:
                                next(g_)
                            except StopIteration:
                                gens.remove(g_)
                    for tt in range(ntt):
                        ts0 = tt * 128
                        P.dma('sp', hin[0:tp, :], src[ts0:ts0 + tp, :], writes=['hin'])
                        for nh in range(2):
                            po = ((pb[7], 'pb7'), (pb[3], 'pb3'), (pb[4], 'pb4'), (pb[5], 'pb5'))[2 * (tt % 2) + nh]
                            for cch in range(12):
                                gk = ('gated%d' % cch) if cch < 4 else 'gatedB'
                                mm(po[0][0:tp, :], gated[:, cch, ts0:ts0 + tp], w_out[:, cch, nh * 512:(nh + 1) * 512], cch == 0, cch == 11,
                                   [gk, 'w_out'], [po[1]])
                            P.op('dve', lambda tt=tt, nh=nh, po=po: V.tensor_tensor(hin[0:tp, nh * 512:(nh + 1) * 512],
                                                                                   hin[0:tp, nh * 512:(nh + 1) * 512], po[0][0:tp, :], ALU.add),
                                 reads=['hin', po[1]], writes=['hin'])
                        P.dma('sp', h1rows(g, si, t0 + ts0, tp), hin[0:tp, :], reads=['hin'], writes=['h1'])
                P.enabled = True
                sfx = '_' + g
                P.dma('sp', O['o_s5re' + sfx][si, :].rearrange("(j q) -> q j", q=128), Hre[:], reads=['Hre'], writes=['o1'])
                P.dma('sp', O['o_s5im' + sfx][si, :].rearrange("(j q) -> q j", q=128), Him[:], reads=['Him'], writes=['o2'])
                for h in range(4):
                    P.dma('sp', O['o_mlc' + sfx][si, h].rearrange("(k p) e -> p k e", p=128), Cst[:, h, :, 0:256],
                          reads=['Cst%d' % h], writes=['o3'])
                    P.dma('sp', O['o_mln' + sfx][si, h].rearrange("(k p) -> p k", p=128), Cst[:, h, :, 256],
                          reads=['Cst%d' % h], writes=['o4'])
                P.dma('sp', O['o_mlm' + sfx][si, :].rearrange("(p o) -> p o", o=1), mcarry[:], reads=['mcarry'], writes=['o5'])
                for t_ in range(3):
                    P.dma('sp', O['o_mlconv' + sfx][si, t_].rearrange("(k p) -> p k", p=128), xbo[:, :, t_], reads=['xbo'], writes=['o6'])
            P.barrier()


        if cfg.layers >= 2:
          with ExitStack() as l1:
            sb = lambda n, s, d: l1.enter_context(nc.sbuf_tensor(n, s, d))
            NP = NPIECE
            w_in = sb("w_in1", [128, 8, 5120], BF16)
            w_sw = sb("w_sw1", [128, 8, 1024], BF16)
            w_out = sb("w_out1", [128, 16, D], BF16)
            lwa = sb("lwa", [128, 8, 128], BF16)
            lwx = sb("lwx", [128, 8, 128], BF16)
            for k in range(8):
                P.dma('pool', w_in[:, k, :], I['od_w_in'][k * 128:(k + 1) * 128, :], writes=['w_in'])
                P.dma('pool', w_sw[:, k, :], I['od_w_qksw'][k * 128:(k + 1) * 128, :], writes=['w_sw'])
            for k in range(16):
                P.dma('pool', w_out[:, k, :], I['od_w_out'][k * 128:(k + 1) * 128, :], writes=['w_out'])
            P.dma('pool', lwa[:], I['lru_w_a'].rearrange("h d e -> d h e"), writes=['lwa'])
            P.dma('pool', lwx[:], I['lru_w_x'].rearrange("h d e -> d h e"), writes=['lwx'])
            gcol = sb("gcol1", [128, 8], F32)
            rnw = sb("rnw", [128, 8], F32)
            cw = sb("cw1", [128, 8, 4], F32)
            cb = sb("cb1", [128, 8], F32)
            bah = sb("bah", [128, 8], F32)
            bxh = sb("bxh", [128, 8], F32)
            ccol = sb("ccol", [128, 8], F32)
            ccol2 = sb("ccol2", [128, 8], F32)
            fnw = sb("fnw", [128, D], F32)
            rmaskL = sb("rmaskL", [128, 4, 128], F32)
            rmaskS = sb("rmaskS", [128, 4, 128], F32)
            rxiL = sb("rxiL", [128, 4, 128], F32)
            rxiS = sb("rxiS", [128, 4, 16], F32)
            rzL = sb("rzL", [128, 4], F32)
            rzS = sb("rzS", [128, 4], F32)
            rglL = sb("rglL", [128, 4], F32)
            rglS = sb("rglS", [128, 4], F32)
            col = lambda nm: I[nm].rearrange("(k p) -> p k", p=128)
            P.dma('sp', gcol[:], I['norm_w'][1, :].rearrange("(k p) -> p k", p=128), writes=['gcol1'])
            P.dma('sp', rnw[:], col('ret_norm_w'), writes=['rnw'])
            for tap in range(4):
                P.dma('sp', cw[:, :, tap], I['lru_conv_w'][tap, :].rearrange("(k p) -> p k", p=128), writes=['cw1'])
            P.dma('sp', cb[:], col('lru_conv_b'), writes=['cb1'])
            P.dma('sp', bah[:], col('lru_b_a'), writes=['bah'])
            P.dma('sp', bxh[:], col('lru_b_x'), writes=['bxh'])
            P.dma('sp', ccol[:], col('lru_lambda'), writes=['ccol'])
            P.dma('sp', fnw[:], I['fnw_b'][:, :], writes=['fnw'])
            P.dma('sp', rmaskL[:], I['c_rmask128'][:, :, :], writes=['rmaskL'])
            P.dma('sp', rmaskS[:], I['c_rmask16'][:, :, :], writes=['rmaskS'])
            P.dma('sp', rxiL[:], I['c_rxi128'][:, :, :], writes=['rxiL'])
            P.dma('sp', rxiS[:], I['c_rxi16'][:, :, :], writes=['rxiS'])
            P.dma('sp', rzL[:], I['c_rzeta128'][:, :], writes=['rzL'])
            P.dma('sp', rzS[:], I['c_rzeta16'][:, :], writes=['rzS'])
            P.dma('sp', rglL[:], I['c_rgl128'][:, :], writes=['rglL'])
            P.dma('sp', rglS[:], I['c_rgl16'][:, :], writes=['rglS'])
            P.op('dve', lambda: V.tensor_scalar(bah[:], bah[:], 0.5, None, ALU.mult), reads=['bah'], writes=['bah'])
            P.op('dve', lambda: V.tensor_scalar(bxh[:], bxh[:], 0.5, None, ALU.mult), reads=['bxh'], writes=['bxh'])
            P.op('act', lambda: A.activation(out=ccol[:], in_=ccol[:], func=AF.Exp, scale=-1.0), reads=['ccol'], writes=['ccol'])
            P.op('act', lambda: A.activation(out=ccol[:], in_=ccol[:], func=AF.Ln, bias=1.0), reads=['ccol'], writes=['ccol'])
            P.op('dve', lambda: V.tensor_scalar(ccol2[:], ccol[:], -4.0, None, ALU.mult), reads=['ccol'], writes=['ccol2'])
            P.op('dve', lambda: V.tensor_scalar(ccol[:], ccol[:], -8.0, None, ALU.mult), reads=['ccol'], writes=['ccol'])

            hin = sb("hin1", [128, D], F32)
            ss = sb("ss1", [128, 4], F32)
            ub = sb("ub1", [128, D], BF16)
            hn = ub
            uT = sb("uT1", [128, 8, NP], BF16)
            gated = sb("gated1", [128, 16, NP], BF16)
            cosF = sb("cosF", [128, NP], F32)
            sinF = sb("sinF", [128, NP], F32)
            qT = sb("qT1", [128, 4, NP], BF16)
            kT = sb("kT1", [128, 4, NP], BF16)
            qxi = sb("qxi", [128, 128], BF16)
            kz = sb("kz", [128, 128], BF16)
            vtok = sb("vtok1", [128, 4, 256], BF16)
            szc = sb("szc", [128, 8, NP], BF16)
            xd = sb("xd", [128, 8, 3 + NP], BF16)
            szd = sb("szd", [128, 8, NP], BF16)
            Sst = sb("Sst", [128, 4, 256], F32)
            Sbf = sb("Sbf", [128, 4, 256], BF16)
            hst = sb("hst", [128, 8], F32)
            f1 = sb("f1", [128, NP], F32)
            LS = [dict(xcb=sb("lx%d" % i_, [128, NP], BF16), f2=sb("lf2_%d" % i_, [128, NP], F32), f3=sb("lf3_%d" % i_, [128, NP], F32),
                       f4=sb("lf4_%d" % i_, [128, NP], F32)) for i_ in range(4)]
            PT = sb("PT1", [128, 128], BF16)
            st6 = sb("st6b", [128, 6], F32)
            st2 = sb("st2b", [128, 8], F32)
            xdo = sb("xdo", [128, 8, 3], F32)
            yout = sb("yout", [128, D], F32)

            print('SBUF remaining after L1 activations', nc.sbuf_bytes_remaining)
            mmreg = [(pb[0], 'pb0'), (pb[1], 'pb1'), (pb[4], 'pb4'), (pb[5], 'pb5')]
            mmi = [0]

            def nextmm():
                r_ = mmreg[mmi[0] % 4]
                mmi[0] += 1
                return r_

            def inproj(f, n, wt=None, wkey='w_in'):
                ps, key = nextmm()
                wt = w_in if wt is None else wt
                for k in range(8):
                    mm(ps[:, 0:n], wt[:, k, f * 128:(f + 1) * 128], uT[:, k, 0:n], k == 0, k == 7, [wkey, 'uT'], [key])
                return ps, key

            for (g, si, pcs) in seqs:
                if g == 'p':
                    P.op('pool', lambda: G.memset(Sst[:], 0.0), writes=['Sst%d' % h_ for h_ in range(4)])
                    P.op('pool', lambda: G.memset(hst[:], 0.0), writes=['hst'])
                    P.op('pool', lambda: G.memset(xd[:, :, 0:3], 0.0), writes=['xd'])
                else:
                    for h in range(4):
                        P.dma('sp', Sst[:, h, :], I['ret_in'][si, h], writes=['Sst%d' % h])
                    P.dma('sp', hst[:], I['lruh_in'][si, :].rearrange("(k p) -> p k", p=128), writes=['hst'])
                    for t_ in range(3):
                        P.dma('pool', xd[:, :, t_], I['lruconv_in'][si, t_].rearrange("(k p) -> p k", p=128), writes=['xd'])
                P.op('act', lambda: A.copy(Sbf[:], Sst[:]), reads=['Sst%d' % h_ for h_ in range(4)], writes=['Sbf%d' % h_ for h_ in range(4)])
                for (t0, n, src, dst, _) in pcs:
                    tp = min(n, 128)
                    ntt = (n + 127) // 128
                    L = tp
                    nch = n // L
                    rmask, rxi, rz, rgl = (rmaskL, rxiL, rzL, rglL) if L == 128 else (rmaskS, rxiS, rzS, rglS)
                    rp0 = t0 if g == 'p' else T
                    P.dma('sp', cosF[:, 0:n], I['ropeF_c'][:, rp0:rp0 + n], writes=['cosF'])
                    P.dma('sp', sinF[:, 0:n], I['ropeF_s'][:, rp0:rp0 + n], writes=['sinF'])
                    for tt in range(ntt):
                        P.dma('sp', hin[0:tp, :], h1rows(g, si, t0 + tt * 128, tp), reads=['h1'], writes=['hin1'])
                        rmsnorm_T(hin, tp, tt, gcol, ub, uT, ss, hkey='hin1', gkey='gcol1')
                    for f in range(8):
                        ps, key = inproj(24 + f, n)
                        P.op('act', lambda f=f, ps=ps: A.copy(xd[:, f, 3:3 + n], ps[:, 0:n]), reads=[key], writes=['xd'])
                    for f in range(8):
                        ps, key = inproj(32 + f, n)
                        P.op('act', lambda f=f, ps=ps: A.activation(out=szd[:, f, 0:n], in_=ps[:, 0:n], func=AF.Silu), reads=[key], writes=['szd'])
                    def gen_ret():
                        for (base, dstT, sc, dkey) in ((0, qT, 1.0, 'qT1'), (4, kT, 1.0 / math.sqrt(128.0), 'kT1')):
                            for h in range(4):
                                psA, keyA = inproj(base + h, n)
                                psB, keyB = inproj(base + h, n, wt=w_sw, wkey='w_sw')
                                P.op('dve', lambda psA=psA: V.tensor_tensor(yout[:, 0:n], psA[:, 0:n], cosF[:, 0:n], ALU.mult),
                                     reads=[keyA, 'cosF'], writes=['yout'])
                                P.op('dve', lambda psB=psB: V.tensor_tensor(yout[:, 256:256 + n], psB[:, 0:n], sinF[:, 0:n], ALU.mult),
                                     reads=[keyB, 'sinF'], writes=['yout'])
                                P.op('dve', lambda: V.tensor_tensor(yout[:, 0:n], yout[:, 0:n], yout[:, 256:256 + n], ALU.add), reads=['yout', 'yout'], writes=['yout'])
                                P.op('act', lambda h=h, dstT=dstT, sc=sc: A.mul(dstT[:, h, 0:n], yout[:, 0:n], sc), reads=['yout'], writes=[dkey])
                                yield
                        for f in range(8):
                            ps, key = inproj(16 + f, n)
                            P.op('act', lambda f=f, ps=ps: A.activation(out=szc[:, f, 0:n], in_=ps[:, 0:n], func=AF.Silu), reads=[key], writes=['szc'])
                            if f % 2 == 1:
                                yield
                        for c in range(nch):
                            cs = slice(c * L, (c + 1) * L)
                            ts0 = c * 128
                            for h in range(4):
                                ps, key = nextmm()
                                for k in range(8):
                                    mm(ps[0:tp, 0:256], uT[:, k, ts0:ts0 + tp], w_in[:, k, 1024 + h * 256:1024 + (h + 1) * 256], k == 0, k == 7,
                                       ['uT', 'w_in'], [key])
                                P.op('act', lambda ps=ps, h=h: A.copy(vtok[0:tp, h, :], ps[0:tp, 0:256]), reads=[key], writes=['vtok1'])
                            yield
                            for h in range(4):
                                stp, stkey = nextmm()
                                ST = stp[0:L, 0:L]
                                mm(ST, kT[:, h, cs], qT[:, h, cs], True, True, ['kT1', 'qT1'], [stkey])
                                P.op('dve', lambda ST=ST, h=h: V.tensor_tensor(PT[0:L, 0:L], ST, rmask[0:L, h, 0:L], ALU.mult),
                                     reads=[stkey, 'rmaskL', 'rmaskS'], writes=['PT1'])
                                P.op('dve', lambda h=h: V.tensor_tensor(qxi[:, 0:L], qT[:, h, cs], rxi[:, h, 0:L], ALU.mult),
                                     reads=['qT1', 'rxiL', 'rxiS'], writes=['qxi'])
                                mm(pb[7][0:L, 0:256], PT[0:L, 0:L], vtok[0:L, h, :], True, False, ['PT1', 'vtok1'], ['pb7'])
                                mm(pb[7][0:L, 0:256], qxi[:, 0:L], Sbf[:, h, :], False, True, ['qxi', 'Sbf%d' % h], ['pb7'])
                                P.op('act', lambda: A.copy(f1[0:L, 0:256], pb[7][0:L, 0:256]), reads=['pb7'], writes=['f1'])
                                P.op('dve', lambda: V.bn_stats(st6[0:L, :], f1[0:L, 0:256]), reads=['f1'], writes=['st6b'])
                                P.op('dve', lambda: V.bn_aggr(st2[0:L, 0:2], st6[0:L, :]), reads=['st6b'], writes=['st2x'])
                                P.op('dve', lambda: V.tensor_scalar(st2[0:L, 2:3], st2[0:L, 1:2], EPS, None, ALU.add), reads=['st2x'], writes=['st2y'])
                                P.op('pool', lambda: G.tensor_tensor(st2[0:L, 3:4], st2[0:L, 2:3], negh[0:L, :], ALU.pow),
                                     reads=['st2y', 'negh'], writes=['st2z'])
                                P.op('dve', lambda h=h: V.tensor_scalar(hn[0:L, h * 256:(h + 1) * 256], f1[0:L, 0:256], st2[0:L, 0:1], st2[0:L, 3:4],
                                                                        ALU.subtract, ALU.mult),
                                     reads=['f1', 'st2x', 'st2z'], writes=['ub'])
                                pT = pb[2][:].bitcast(BF16)
                                P.op('pe', lambda h=h: PL.transpose(pT[0:L, 0:128], kT[:, h, cs], identb[:, :]),
                                     reads=['kT1', 'identb'], writes=['pb2'])
                                P.op('dve', lambda h=h: V.tensor_scalar(kz[0:L, :], pT[0:L, 0:128], rz[0:L, h:h + 1], None, ALU.mult),
                                     reads=['pb2', 'rzL', 'rzS'], writes=['kz'])
                                mm(pb[3][:, 0:256], kz[0:L, :], vtok[0:L, h, :], True, True, ['kz', 'vtok1'], ['pb3'])
                                P.op('dve', lambda h=h: V.scalar_tensor_tensor(Sst[:, h, :], Sst[:, h, :], rgl[:, h:h + 1], pb[3][:, 0:256], ALU.mult, ALU.add),
                                     reads=['Sst%d' % h, 'rglL', 'rglS', 'pb3'], writes=['Sst%d' % h])
                                P.op('act', lambda h=h: A.copy(Sbf[:, h, :], Sst[:, h, :]), reads=['Sst%d' % h], writes=['Sbf%d' % h])
                                yield
                            pT = pb[2][:].bitcast(BF16)
                            for k in range(8):
                                P.op('pe', lambda k=k: PL.transpose(pT[:, k * 128:k * 128 + tp], hn[0:tp, k * 128:(k + 1) * 128], identb[0:tp, 0:tp]),
                                     reads=['ub', 'identb'], writes=['pb2'])
                            for k in range(8):
                                P.op('dve', lambda k=k: V.scalar_tensor_tensor(gated[:, k, ts0:ts0 + tp], pT[:, k * 128:k * 128 + tp], rnw[:, k:k + 1],
                                                                               szc[:, k, ts0:ts0 + tp], ALU.mult, ALU.mult),
                                     reads=['pb2', 'rnw', 'szc'], writes=['gatedA'])
                            yield
                    def gen_lru():
                        def ph1(k):
                            B_ = LS[k % 4]; sfx_ = '_%d' % (k % 4)
                            xcb, f2, f3, f4 = B_['xcb'], B_['f2'], B_['f3'], B_['f4']
                            P.op('dve', lambda: V.tensor_scalar(f1[:, 0:n], xd[:, k, 0:n], cw[:, k, 0:1], cb[:, k:k + 1], ALU.mult, ALU.add),
                                 reads=['xd', 'cw1', 'cb1'], writes=['f1'])
                            for tap in range(1, 4):
                                P.op('dve', lambda tap=tap: V.scalar_tensor_tensor(f1[:, 0:n], xd[:, k, tap:tap + n], cw[:, k, tap:tap + 1], f1[:, 0:n],
                                                                                   ALU.mult, ALU.add),
                                     reads=['xd', 'cw1', 'f1'], writes=['f1'])
                            P.op('act', lambda: A.copy(xcb[:, 0:n], f1[:, 0:n]), reads=['f1'], writes=['xcb' + sfx_])
                            psr, keyr = nextmm()
                            mm(psr[:, 0:n], lwa[:, k, :], xcb[:, 0:n], True, True, ['lwa', 'xcb' + sfx_], [keyr])
                            psi, keyi = nextmm()
                            mm(psi[:, 0:n], lwx[:, k, :], xcb[:, 0:n], True, True, ['lwx', 'xcb' + sfx_], [keyi])
                            P.op('act', lambda: A.activation(out=f2[:, 0:n], in_=psr[:, 0:n], func=AF.Tanh, scale=0.5, bias=bah[:, k:k + 1]),
                                 reads=[keyr, 'bah'], writes=['f2' + sfx_])
                            P.op('act', lambda: A.activation(out=f3[:, 0:n], in_=psi[:, 0:n], func=AF.Tanh, scale=0.5, bias=bxh[:, k:k + 1]),
                                 reads=[keyi, 'bxh'], writes=['f3' + sfx_])
                            P.op('act', lambda: A.activation(out=f4[:, 0:n], in_=f2[:, 0:n], func=AF.Exp, scale=ccol2[:, k:k + 1], bias=ccol2[:, k:k + 1]),
                                 reads=['f2' + sfx_, 'ccol2'], writes=['f4' + sfx_])
                            P.op('pool', lambda: G.tensor_scalar(f3[:, 0:n], f3[:, 0:n], 0.5, 0.5, ALU.mult, ALU.add), reads=['f3' + sfx_], writes=['f3' + sfx_])
                            P.op('pool', lambda: G.tensor_tensor(f3[:, 0:n], f3[:, 0:n], xcb[:, 0:n], ALU.mult), reads=['f3' + sfx_, 'xcb' + sfx_], writes=['f3' + sfx_])
                            P.op('pool', lambda: G.tensor_tensor(f2[:, 0:n], f4[:, 0:n], f4[:, 0:n], ALU.mult), reads=['f4' + sfx_, 'f2' + sfx_], writes=['f2' + sfx_])
                            P.op('pool', lambda: G.tensor_scalar(f2[:, 0:n], f2[:, 0:n], -1.0, 1.0, ALU.mult, ALU.add), reads=['f2' + sfx_], writes=['f2' + sfx_])

                        def ph2(k):
                            B_ = LS[k % 4]; sfx_ = '_%d' % (k % 4)
                            f2 = B_['f2']
                            P.op('act', lambda: A.activation(out=f2[:, 0:n], in_=f2[:, 0:n], func=AF.Ln), reads=['f2' + sfx_], writes=['f2' + sfx_])
                            P.op('act', lambda: A.activation(out=f2[:, 0:n], in_=f2[:, 0:n], func=AF.Exp, scale=0.5), reads=['f2' + sfx_], writes=['f2' + sfx_])

                        def ph3(k):
                            B_ = LS[k % 4]; sfx_ = '_%d' % (k % 4)
                            xcb, f2, f3, f4 = B_['xcb'], B_['f2'], B_['f3'], B_['f4']
                            P.op('dve', lambda: V.tensor_tensor(f3[:, 0:n], f3[:, 0:n], f2[:, 0:n], ALU.mult), reads=['f3' + sfx_, 'f2' + sfx_], writes=['f3' + sfx_])
                            P.op('dve', lambda: V.tensor_tensor_scan(f2[:, 0:n], f4[:, 0:n], f3[:, 0:n], hst[:, k:k + 1], ALU.mult, ALU.add),
                                 reads=['f4' + sfx_, 'f3' + sfx_, 'hst', 'f2' + sfx_], writes=['f2' + sfx_])
                            P.op('dve', lambda: V.tensor_copy(hst[:, k:k + 1], f2[:, n - 1:n]), reads=['f2' + sfx_], writes=['hst'])
                            P.op('dve', lambda: V.tensor_tensor(gated[:, 8 + k, 0:n], f2[:, 0:n], szd[:, k, 0:n], ALU.mult),
                                 reads=['f2' + sfx_, 'szd'], writes=['gatedB%d' % k])

                        for q_ in range(2):
                            for i_ in range(4):
                                ph1(4 * q_ + i_)
                                yield
                            for i_ in range(4):
                                ph2(4 * q_ + i_)
                            yield
                            for i_ in range(4):
                                ph3(4 * q_ + i_)
                                if i_ % 2 == 1:
                                    yield
                    gens = [gen_ret(), gen_lru()]
                    while gens:
                        for g_ in list(gens):
                            try:
                                next(g_)
                            except StopIteration:
                                gens.remove(g_)
                    P.op('dve', lambda n=n: V.tensor_copy(xdo[:, :, :], xd[:, :, n:n + 3]), reads=['xd'], writes=['xdo'])
                    P.op('dve', lambda: V.tensor_copy(xd[:, :, 0:3], xdo[:, :, :]), reads=['xdo'], writes=['xd'])
                    for tt in range(ntt):
                        ts0 = tt * 128
                        P.dma('sp', hin[0:tp, :], h1rows(g, si, t0 + ts0, tp), reads=['h1'], writes=['hin1'])
                        for nh in range(2):
                            po = ((pb[7], 'pb7'), (pb[3], 'pb3'), (pb[6], 'pb6'), (pb[2], 'pb2'))[2 * (tt % 2) + nh]
                            for cch in range(16):
                                gk = 'gatedA' if cch < 8 else 'gatedB%d' % (cch - 8)
                                mm(po[0][0:tp, :], gated[:, cch, ts0:ts0 + tp], w_out[:, cch, nh * 512:(nh + 1) * 512], cch == 0, cch == 15,
                                   [gk, 'w_out'], [po[1]])
                            P.op('dve', lambda nh=nh, po=po: V.tensor_tensor(hin[0:tp, nh * 512:(nh + 1) * 512],
                                                                             hin[0:tp, nh * 512:(nh + 1) * 512], po[0][0:tp, :], ALU.add),
                                 reads=['hin1', po[1]], writes=['hin1'])
                        if not (g == 'p' and t0 == 0):
                            P.op('pool', lambda: G.memset(ss[0:tp, 0:1], 0.0), writes=['ss'])
                            P.op('act', lambda: A.activation(out=ub[0:tp, :], in_=hin[0:tp, :], func=AF.Square, accum_out=ss[0:tp, 0:1]),
                                 reads=['hin1'], writes=['ub', 'ss'])
                            P.op('dve', lambda: V.tensor_scalar(ss[0:tp, 1:2], ss[0:tp, 0:1], 1.0 / D, EPS, ALU.mult, ALU.add), reads=['ss'], writes=['ss'])
                            P.op('pool', lambda: G.tensor_tensor(ss[0:tp, 2:3], ss[0:tp, 1:2], negh[0:tp, :], ALU.pow), reads=['ss', 'negh'], writes=['ss'])
                            P.op('dve', lambda: V.scalar_tensor_tensor(yout[0:tp, :], hin[0:tp, :], ss[0:tp, 2:3], fnw[0:tp, :], ALU.mult, ALU.mult),
                                 reads=['hin1', 'ss', 'fnw'], writes=['yout'])
                            P.dma('sp', dst[ts0:ts0 + tp, :], yout[0:tp, :], reads=['yout'], writes=['yo'])
                sfx = '_' + g
                for h in range(4):
                    P.dma('sp', O['o_ret' + sfx][si, h], Sst[:, h, :], reads=['Sst%d' % h], writes=['o7'])
                P.dma('sp', O['o_lruh' + sfx][si, :].rearrange("(k p) -> p k", p=128), hst[:], reads=['hst'], writes=['o8'])
                for t_ in range(3):
                    P.dma('sp', O['o_lruconv' + sfx][si, t_].rearrange("(k p) -> p k", p=128), xdo[:, :, t_], reads=['xdo'], writes=['o9'])
            P.barrier()
        P.barrier()
    return nc


def host_layout(inp):
    f = lambda a: np.ascontiguousarray(np.asarray(a, dtype=np.float32))
    d = {}
    d['norm_w'] = f(inp['norm_w'])
    d['final_norm_w'] = f(inp['final_norm_w'])
    d['ev_w_in'] = f(inp['ev_w_in'][0])
    d['ev_w_out'] = f(inp['ev_w_out'][0])
    lre, lim, ldt = inp['s5_lambda_re'][0], inp['s5_lambda_im'][0], inp['s5_log_dt'][0]
    def A_from_gp(x):
        x4 = np.asarray(x).reshape(4, 8, 64)
        o = np.broadcast_to(x4[:, :, None, :], (4, 8, 16, 64))
        return f(np.transpose(o, (1, 2, 0, 3)).reshape(128, 4, 64))
    d['lamre_A'] = A_from_gp(lre)
    d['lamim_A'] = A_from_gp(lim)
    d['logdt_A'] = A_from_gp(np.broadcast_to(np.asarray(ldt)[:, None], (32, 64)))
    def A_from_gpc(x):
        x5 = np.asarray(x).reshape(4, 8, 64, 16)
        return f(np.transpose(x5, (1, 3, 0, 2)).reshape(128, 4, 64))
    d['bre_A'] = A_from_gpc(inp['s5_b_re'][0])
    d['bim_A'] = A_from_gpc(inp['s5_b_im'][0])
    def S_from_gp(x):
        x3 = np.asarray(x).reshape(16, 2, 64)
        return f(np.transpose(x3, (1, 2, 0)).reshape(128, 16))
    d['lamre_S'] = S_from_gp(lre)
    d['lamim_S'] = S_from_gp(lim)
    d['logdt_S'] = S_from_gp(np.broadcast_to(np.asarray(ldt)[:, None], (32, 64)))
    def S_from_gcp(x):
        x4 = np.asarray(x).reshape(16, 2, 16, 64)
        return f(np.transpose(x4, (1, 3, 0, 2)).reshape(128, 16, 16))
    d['cre_S'] = S_from_gcp(inp['s5_c_re'][0])
    d['cim_S'] = S_from_gcp(inp['s5_c_im'][0])
    d['s5_d'] = f(inp['s5_d'][0]); d['s5_w_glu'] = f(inp['s5_w_glu'][0]); d['s5_b_glu'] = f(inp['s5_b_glu'][0])
    d['ml_conv_w'] = f(inp['ml_conv_w'][0]); d['ml_conv_b'] = f(inp['ml_conv_b'][0])
    for k in ('ml_wq', 'ml_wk', 'ml_wv'):
        d[k] = f(inp[k][0])
        d[k + 'T'] = f(np.transpose(np.asarray(inp[k][0]), (0, 2, 1)))
    d['ml_w_if'] = f(inp['ml_w_if'][0]); d['ml_b_if'] = f(inp['ml_b_if'][0])
    d['ml_norm_w'] = f(inp['ml_norm_w'][0]); d['ml_skip'] = f(inp['ml_skip'][0])
    w1 = np.asarray(inp['od_w_in'][0])
    d['od_w_in'] = f(w1)
    qk = w1[:, 0:1024].reshape(D, 8, 2, 64)
    d['od_w_qksw'] = f(qk[:, :, ::-1, :].reshape(D, 1024))
    d['fnw_b'] = f(np.broadcast_to(np.asarray(inp['final_norm_w'])[None, :], (128, D)))
    d['od_w_out'] = f(inp['od_w_out'][0]); d['ret_norm_w'] = f(inp['ret_norm_w'][0])
    d['lru_conv_w'] = f(inp['lru_conv_w'][0]); d['lru_conv_b'] = f(inp['lru_conv_b'][0])
    d['lru_w_a'] = f(inp['lru_w_a'][0]); d['lru_b_a'] = f(inp['lru_b_a'][0])
    d['lru_w_x'] = f(inp['lru_w_x'][0]); d['lru_b_x'] = f(inp['lru_b_x'][0]); d['lru_lambda'] = f(inp['lru_lambda'][0])
    d['meta'] = f(inp['meta'])
    return d


def make_in_maps(cfg, inputs, ncores):
    shared = host_layout(inputs)
    shared.update(host_consts())
    T = 16 + cfg.T_x
    pos = np.concatenate([np.arange(T, dtype=np.float32), (16 + cfg.past) + np.arange(16, dtype=np.float32)])
    cf, sf, ct, st = rope_tables(pos)
    shared['ropeF_c'], shared['ropeF_s'], shared['ropeT_c'], shared['ropeT_s'] = cf, sf, np.ascontiguousarray(ct), np.ascontiguousarray(st)
    f = lambda a: np.ascontiguousarray(np.asarray(a, dtype=np.float32))
    maps = []
    for c in range(ncores):
        ps = slice(c * cfg.n_p, (c + 1) * cfg.n_p)
        ss_ = slice(c * cfg.n_s, (c + 1) * cfg.n_s)
        m = dict(shared)
        m['xp'] = f(inputs['x_prompt'][ps])
        m['xs'] = f(inputs['x_sample'][ss_])
        m['s5re_in'] = f(inputs['state_s5_re'][0][ss_]).reshape(cfg.n_s, 2048)
        m['s5im_in'] = f(inputs['state_s5_im'][0][ss_]).reshape(cfg.n_s, 2048)
        m['mlc_in'] = f(inputs['state_ml_c'][0][ss_])
        m['mln_in'] = f(inputs['state_ml_n'][0][ss_])
        m['mlm_in'] = f(inputs['state_ml_m'][0][ss_])
        m['mlconv_in'] = f(inputs['state_ml_conv'][0][ss_])
        m['ret_in'] = f(inputs['state_ret'][0][ss_])
        m['lruh_in'] = f(inputs['state_lru_h'][0][ss_])
        m['lruconv_in'] = f(inputs['state_lru_conv'][0][ss_])
        maps.append(m)
    return maps


def gather(cfg, results):
    cat = lambda k: np.concatenate([np.asarray(r[k]) for r in results], axis=0)
    outs = [cat('yp'), cat('ys')]
    for g in ('p', 's'):
        n = cat('o_s5re_' + g).shape[0]
        outs += [cat('o_s5re_' + g).reshape(1, n, 32, 64), cat('o_s5im_' + g).reshape(1, n, 32, 64),
                 cat('o_mlc_' + g)[None], cat('o_mln_' + g)[None], cat('o_mlm_' + g)[None], cat('o_mlconv_' + g)[None],
                 cat('o_ret_' + g)[None], cat('o_lruh_' + g)[None], cat('o_lruconv_' + g)[None]]
    return tuple(np.ascontiguousarray(o.astype(np.float32)) for o in outs)


def kernel(**inputs):
    ncores = 8
    cfg = Cfg(n_p=2, T_x=4096, n_s=2, past=2048, layers=2)
    nc = build(cfg)
    maps = make_in_maps(cfg, inputs, ncores)
    res = run_bass_kernel_spmd(nc, maps, core_ids=list(range(ncores)))
    return gather(cfg, res.results)
```

```python
import math
from contextlib import ExitStack
import numpy as np
import concourse.bass as bass
import concourse.mybir as mybir
from concourse.bass_utils import run_bass_kernel_spmd

F32 = mybir.dt.float32
BF16 = mybir.dt.bfloat16
AF = mybir.ActivationFunctionType
ALU = mybir.AluOpType

D = 1024
NPIECE = 256
EPS = 1e-6
PI = math.pi


class Prog:
    def __init__(self, nc, ndma=6):
        self.nc = nc
        self.e = dict(pe=nc.tensor, dve=nc.vector, act=nc.scalar, pool=nc.gpsimd, sp=nc.sync)
        self.streams = {k: [] for k in self.e}
        self.cnt = {k: 0 for k in self.e}
        self.waited = {k: {} for k in self.e}
        self.lastw = {}
        self.reads = {}
        self.sems = {}
        self.ndma = ndma
        self.dmai = {q: 0 for q in ('sp', 'act', 'pool')}
        self.dma_last = {}
        self.enabled = True
        self.semnames = ['c_' + k for k in ('pe', 'dve', 'act', 'pool')] + \
            ['d_%s_%d' % (q, j) for q in ('sp', 'act', 'pool') for j in range(ndma)]

    def _wait(self, eng, tok):
        sem, val = tok
        if self.waited[eng].get(sem, 0) >= val:
            return
        self.waited[eng][sem] = val
        self.e[eng].wait_ge(self.sems[sem], val)

    def _deps(self, eng, reads, writes):
        deps = []
        own = 'c_' + eng
        for k in reads:
            t = self.lastw.get(k)
            if t is not None:
                if not (eng == 'pe' and t[0] == own):
                    deps.append(t)
        for k in writes:
            t = self.lastw.get(k)
            if t is not None and t[0] != own:
                deps.append(t)
            for t in self.reads.get(k, ()):
                if t[0] != own:
                    deps.append(t)
        for t in deps:
            self._wait(eng, t)

    def _commit(self, tok, reads, writes):
        for k in writes:
            self.lastw[k] = tok
            self.reads[k] = []
        for k in reads:
            if k not in writes:
                self.reads.setdefault(k, []).append(tok)

    def op(self, eng, fn, reads=(), writes=()):
        if not self.enabled:
            return None
        self._deps(eng, reads, writes)
        self.cnt[eng] += 1
        tok = ('c_' + eng, self.cnt[eng])
        fn().then_inc(self.sems['c_' + eng], 1)
        self._commit(tok, reads, writes)
        return tok

    def dma(self, q, out, in_, reads=(), writes=(), **kw):
        if not self.enabled:
            return None
        self._deps(q, reads, writes)
        i = self.dmai[q]
        self.dmai[q] += 1
        j, r = i % self.ndma, i // self.ndma
        sem = 'd_%s_%d' % (q, j)
        if r > 0:
            self._wait(q, (sem, 16 * r))
        e = self.e[q]
        e.dma_start(out=out, in_=in_, **kw).then_inc(self.sems[sem], 16)
        tok = (sem, 16 * (r + 1))
        self.dma_last[sem] = tok
        self._commit(tok, reads, writes)
        return tok

    def barrier(self):
        toks = set(self.lastw.values())
        for lst in self.reads.values():
            toks.update(lst)
        toks.update(self.dma_last.values())
        for k in ('pe', 'dve', 'act', 'pool'):
            if self.cnt[k] > 0:
                toks.add(('c_' + k, self.cnt[k]))
        for eng in self.e:
            for t in toks:
                self._wait(eng, t)
        self.lastw = {}
        self.reads = {}

    def emit(self, block):
        sems = self.sems

        def run(engname, engobj):
            for it in self.streams[engname]:
                if it[0] == 'w':
                    engobj.wait_ge(sems[it[1]], it[2])
                else:
                    inst = it[1]()
                    inst.then_inc(sems[it[2]], it[3])

        @block.tensor
        def _(x):
            run('pe', x)

        @block.vector
        def _(x):
            run('dve', x)

        @block.scalar
        def _(x):
            run('act', x)

        @block.gpsimd
        def _(x):
            run('pool', x)

        @block.sync
        def _(x):
            run('sp', x)


def host_consts():
    c = {}
    c['c_identb'] = np.eye(128, dtype=np.float32)
    c['c_identf'] = np.eye(128, dtype=np.float32)
    s = np.arange(128)
    c['c_negmask'] = np.where(s[:, None] <= s[None, :], 1.0, 0.0).astype(np.float32)
    sel = np.zeros((4, 4, 128), np.float32)
    for h in range(4):
        sel[h, h, :] = 1.0
    c['c_sel'] = sel
    c['c_ones4'] = np.ones((4, 128), np.float32)
    mA = np.zeros((128, 8), np.float32)
    for g8 in range(8):
        mA[g8 * 16:(g8 + 1) * 16, g8] = 1.0
    c['c_maskA'] = mA
    lg = np.log1p(-np.exp2(-5.0 - np.arange(4, dtype=np.float64)))
    for L in (16, 128):
        idx = np.arange(L, dtype=np.float64)
        diff = idx[None, :] - idx[:, None]
        dm = np.where(diff >= 0, np.exp(lg[:, None, None] * np.maximum(diff, 0.0)), 0.0)
        dmp = np.zeros((128, 4, 128), np.float32)
        dmp[:L, :, :L] = np.transpose(dm, (1, 0, 2))
        c['c_rmask%d' % L] = dmp
        xi = np.exp(lg[:, None] * (idx + 1.0))
        c['c_rxi%d' % L] = np.ascontiguousarray(np.broadcast_to(xi[None], (128, 4, L))).astype(np.float32)
        zeta = np.exp(lg[:, None] * (L - 1.0 - idx))
        zp = np.zeros((128, 4), np.float32)
        zp[:L] = zeta.T
        c['c_rzeta%d' % L] = zp
        c['c_rgl%d' % L] = np.ascontiguousarray(np.broadcast_to(np.exp(lg * L)[None], (128, 4))).astype(np.float32)
    return c


def rope_tables(pos):
    half = 64
    inv = (10000.0 ** (-np.arange(half, dtype=np.float32) / half)).astype(np.float32)
    ang = pos.astype(np.float32)[:, None] * inv[None, :]
    cos = np.cos(ang).astype(np.float32)
    sin = np.sin(ang).astype(np.float32)
    cf = np.concatenate([cos.T, cos.T], axis=0)
    sf = np.concatenate([-sin.T, sin.T], axis=0)
    return np.ascontiguousarray(cf), np.ascontiguousarray(sf), cos, sin


class Cfg:
    def __init__(self, n_p=2, T_x=4096, n_s=2, past=2048, layers=2, debug=False, stage=99):
        self.stage = stage
        self.n_p, self.T_x, self.n_s, self.past, self.layers, self.debug = n_p, T_x, n_s, past, layers, debug
        assert T_x % NPIECE == 0


IN_SPECS = None


def in_specs(cfg):
    n_p, T_x, n_s = cfg.n_p, cfg.T_x, cfg.n_s
    T = 16 + T_x
    sp = {
        'xp': [n_p, T_x, D], 'xs': [n_s, 16, D], 'meta': [16, D],
        's5re_in': [n_s, 2048], 's5im_in': [n_s, 2048],
        'mlc_in': [n_s, 4, 256, 256], 'mln_in': [n_s, 4, 256], 'mlm_in': [n_s, 4], 'mlconv_in': [n_s, 3, D],
        'ret_in': [n_s, 4, 128, 256], 'lruh_in': [n_s, D], 'lruconv_in': [n_s, 3, D],
        'norm_w': [2, D], 'final_norm_w': [D],
        'ev_w_in': [D, 3072], 'ev_w_out': [1536, D],
        'lamre_A': [128, 4, 64], 'lamim_A': [128, 4, 64], 'logdt_A': [128, 4, 64],
        'lamre_S': [128, 16], 'lamim_S': [128, 16], 'logdt_S': [128, 16],
        'bre_A': [128, 4, 64], 'bim_A': [128, 4, 64], 'cre_S': [128, 16, 16], 'cim_S': [128, 16, 16],
        's5_d': [512], 's5_w_glu': [512, 512], 's5_b_glu': [512],
        'ml_conv_w': [4, D], 'ml_conv_b': [D], 'ml_wq': [4, 256, 256], 'ml_wk': [4, 256, 256], 'ml_wv': [4, 256, 256],
        'ml_wqT': [4, 256, 256], 'ml_wkT': [4, 256, 256], 'ml_wvT': [4, 256, 256],
        'ml_w_if': [3072, 8], 'ml_b_if': [8], 'ml_norm_w': [D], 'ml_skip': [D],
        'od_w_in': [D, 5120], 'od_w_qksw': [D, 1024], 'fnw_b': [128, D], 'od_w_out': [2048, D], 'ret_norm_w': [D],
        'lru_conv_w': [4, D], 'lru_conv_b': [D], 'lru_w_a': [8, 128, 128], 'lru_b_a': [D],
        'lru_w_x': [8, 128, 128], 'lru_b_x': [D], 'lru_lambda': [D],
        'ropeF_c': [128, T + 16], 'ropeF_s': [128, T + 16], 'ropeT_c': [T + 16, 64], 'ropeT_s': [T + 16, 64],
    }
    for k, v in host_consts().items():
        sp[k] = list(v.shape)
    return sp


def out_specs(cfg):
    n_p, T_x, n_s = cfg.n_p, cfg.T_x, cfg.n_s
    o = {'yp': [n_p, T_x, D], 'ys': [n_s, 16, D]}
    for g, n in (('p', n_p), ('s', n_s)):
        o['o_s5re_' + g] = [n, 2048]
        o['o_s5im_' + g] = [n, 2048]
        o['o_mlc_' + g] = [n, 4, 256, 256]
        o['o_mln_' + g] = [n, 4, 256]
        o['o_mlm_' + g] = [n, 4]
        o['o_mlconv_' + g] = [n, 3, D]
        o['o_ret_' + g] = [n, 4, 128, 256]
        o['o_lruh_' + g] = [n, D]
        o['o_lruconv_' + g] = [n, 3, D]
    return o


def build(cfg):
    nc = bass.Bass("TRN2", target_bir_lowering=False)
    n_p, T_x, n_s = cfg.n_p, cfg.T_x, cfg.n_s
    T = 16 + T_x
    I = {k: nc.dram_tensor(k, v, F32, kind="ExternalInput").ap() for k, v in in_specs(cfg).items()}
    O = {k: nc.dram_tensor(k, v, F32, kind="ExternalOutput").ap() for k, v in out_specs(cfg).items()}
    hk = "ExternalOutput" if cfg.debug else "Internal"
    h1p = nc.dram_tensor("h1p", [n_p, T, D], F32, kind=hk).ap()
    h1s = nc.dram_tensor("h1s", [n_s, 16, D], F32, kind=hk).ap()
    dbg_o = nc.dram_tensor("dbg_o", [4, 128, 256], F32, kind=hk).ap()

    seqs = []
    for i in range(n_p):
        pcs = [(0, 16, I['meta'][:, :], O['yp'], None)]
        for t in range(0, T_x, NPIECE):
            pcs.append((16 + t, NPIECE, I['xp'][i, t:t + NPIECE, :], O['yp'][i, t:t + NPIECE, :], None))
        seqs.append(('p', i, pcs))
    for i in range(n_s):
        seqs.append(('s', i, [(0, 16, I['xs'][i, :, :], O['ys'][i, :, :], None)]))

    def h1rows(g, i, t0, n):
        return (h1p if g == 'p' else h1s)[i, t0:t0 + n, :]

    with ExitStack() as top:
        top.enter_context(nc.allow_non_contiguous_dma(reason='small strided parameter/state layouts'))
        P = Prog(nc)
        for s in P.semnames:
            P.sems[s] = top.enter_context(nc.semaphore(s))
        pb = [top.enter_context(nc.psum_tensor("pb%d" % i, [128, 512], F32)) for i in range(8)]

        V, G, A, PL = nc.vector, nc.gpsimd, nc.scalar, nc.tensor

        def mm(out, lhsT, rhs, start, stop, reads, writes):
            P.op('pe', lambda: PL.matmul(out, lhsT, rhs, start=start, stop=stop), reads=reads, writes=writes)

        sb0 = lambda n, s, d: top.enter_context(nc.sbuf_tensor(n, s, d))
        identb = sb0("identb", [128, 128], BF16)
        identf = sb0("identf", [128, 128], F32)
        negh = sb0("negh", [128, 1], F32)
        P.dma('pool', identb[:], I['c_identb'][:, :], writes=['identb'])
        P.dma('sp', identf[:], I['c_identf'][:, :], writes=['identf'])
        P.op('pool', lambda: G.memset(negh[:], -0.5), writes=['negh'])

        def rmsnorm_T(hin, tp, tt, gcol, ub, uT, ss, hkey='hin', gkey='gcol'):
            P.op('pool', lambda: G.memset(ss[0:tp, 0:1], 0.0), writes=['ss'])
            P.op('act', lambda: A.activation(out=ub[0:tp, :], in_=hin[0:tp, :], func=AF.Square, accum_out=ss[0:tp, 0:1]),
                 reads=[hkey], writes=['ub', 'ss'])
            P.op('dve', lambda: V.tensor_scalar(ss[0:tp, 1:2], ss[0:tp, 0:1], 1.0 / D, EPS, ALU.mult, ALU.add),
                 reads=['ss'], writes=['ss'])
            P.op('pool', lambda: G.tensor_tensor(ss[0:tp, 2:3], ss[0:tp, 1:2], negh[0:tp, :], ALU.pow),
                 reads=['ss', 'negh'], writes=['ss'])
            P.op('act', lambda: A.activation(out=ub[0:tp, :], in_=hin[0:tp, :], func=AF.Copy, scale=ss[0:tp, 2:3]),
                 reads=[hkey, 'ss'], writes=['ub'])
            pT = pb[2][:].bitcast(BF16)
            for k in range(8):
                P.op('pe', lambda k=k: PL.transpose(pT[:, k * 128:k * 128 + tp], ub[0:tp, k * 128:(k + 1) * 128],
                                                    identb[0:tp, 0:tp]),
                     reads=['ub', 'identb'], writes=['pb2'])
            for k in range(8):
                P.op('act', lambda k=k: A.activation(out=uT[:, k, tt * 128:tt * 128 + tp], in_=pT[:, k * 128:k * 128 + tp],
                                                     func=AF.Copy, scale=gcol[:, k:k + 1]),
                     reads=['pb2', gkey], writes=['uT'])

        with ExitStack() as l0:
            sb = lambda n, s, d: l0.enter_context(nc.sbuf_tensor(n, s, d))
            NP = NPIECE
            w_in = sb("w_in0", [128, 8, 3072], BF16)
            w_out = sb("w_out0", [128, 12, D], BF16)
            w_glu = sb("w_glu", [128, 4, 512], BF16)
            wq = sb("wq", [128, 4, 2, 256], BF16)
            wk = sb("wk", [128, 4, 2, 256], BF16)
            wv = sb("wv", [128, 4, 2, 256], BF16)
            wcif = sb("wcif", [128, 8, 8], BF16)
            wxif = sb("wxif", [128, 8, 8], BF16)
            s5B = sb("s5B", [128, 2, 16, 128], BF16)
            s5C = sb("s5C", [128, 2, 16, 128], BF16)
            LR = 64
            tabc = sb("tabc", [128, 16, LR], F32)
            tabs = sb("tabs", [128, 16, LR], F32)
            rtab = sb("rtab", [128, 16, 2], F32)
            rtab0 = sb("rtab0", [128, 16, LR], F32)
            gcol = sb("gcol0", [128, 8], F32)
            dcol = sb("dcol", [128, 4], F32)
            bgluh = sb("bgluh", [128, 4], F32)
            convw = sb("convw", [128, 8, 4], F32)
            convb = sb("convb", [128, 8], F32)
            mlnw = sb("mlnw", [128, 8], F32)
            skipc = sb("skipc", [128, 8], F32)
            b_i = sb("b_i", [4, 1], F32)
            b_f = sb("b_f", [4, 1], F32)
            negmask = sb("negmask", [128, 128], F32)
            sel = sb("sel", [4, 4, 128], F32)
            ones4 = sb("ones4", [4, 128], F32)
            onesrow = sb("onesrow", [4, NP], F32)

            for k in range(8):
                P.dma('pool', w_in[:, k, :], I['ev_w_in'][k * 128:(k + 1) * 128, :], writes=['w_in'])
            for k in range(12):
                P.dma('pool', w_out[:, k, :], I['ev_w_out'][k * 128:(k + 1) * 128, :], writes=['w_out'])
            P.dma('pool', w_glu[:], I['s5_w_glu'].rearrange("(k p) f -> p k f", p=128), writes=['w_glu'])
            for nm, t_ in (('ml_wq', wq), ('ml_wk', wk), ('ml_wv', wv)):
                for h in range(4):
                    P.dma('pool', t_[:, h, :, :], I[nm][h].rearrange("(k p) e -> p k e", p=128), writes=[nm])
            P.dma('sp', gcol[:], I['norm_w'][0, :].rearrange("(k p) -> p k", p=128), writes=['gcol'])
            P.dma('sp', dcol[:], I['s5_d'].rearrange("(k p) -> p k", p=128), writes=['dcol'])
            P.dma('sp', bgluh[:], I['s5_b_glu'].rearrange("(k p) -> p k", p=128), writes=['bgluh'])
            for tap in range(4):
                P.dma('sp', convw[:, :, tap], I['ml_conv_w'][tap, :].rearrange("(k p) -> p k", p=128), writes=['convw'])
            P.dma('sp', convb[:], I['ml_conv_b'].rearrange("(k p) -> p k", p=128), writes=['convb'])
            P.dma('sp', mlnw[:], I['ml_norm_w'].rearrange("(k p) -> p k", p=128), writes=['mlnw'])
            P.dma('sp', skipc[:], I['ml_skip'].rearrange("(k p) -> p k", p=128), writes=['skipc'])
            P.dma('sp', b_i[:], I['ml_b_if'][0:4].rearrange("(p o) -> p o", o=1), writes=['b_i'])
            P.dma('sp', b_f[:], I['ml_b_if'][4:8].rearrange("(p o) -> p o", o=1), writes=['b_f'])
            P.dma('sp', negmask[:], I['c_negmask'][:, :], writes=['negmask'])
            P.dma('sp', sel[:], I['c_sel'][:, :, :], writes=['sel'])
            P.dma('sp', ones4[:], I['c_ones4'][:, :], writes=['ones4'])
            P.op('pool', lambda: G.memset(onesrow[:], 1.0), writes=['onesrow'])
            P.op('dve', lambda: V.tensor_scalar(bgluh[:], bgluh[:], 0.5, None, ALU.mult), reads=['bgluh'], writes=['bgluh'])
            P.op('dve', lambda: V.tensor_scalar(b_f[:], b_f[:], -1.0, None, ALU.mult), reads=['b_f'], writes=['b_f'])

            print('SBUF remaining before setup', nc.sbuf_bytes_remaining)
            with ExitStack() as su:
                sbs = lambda n, s, d: su.enter_context(nc.sbuf_tensor(n, s, d))
                wqT = sbs("wqT", [128, 4, 2, 256], BF16)
                wkT = sbs("wkT", [128, 4, 2, 256], BF16)
                wvT = sbs("wvT", [128, 4, 2, 256], BF16)
                wif = sbs("wif", [128, 24, 8], BF16)
                for nm, t_ in (('ml_wqT', wqT), ('ml_wkT', wkT), ('ml_wvT', wvT)):
                    for h in range(4):
                        P.dma('pool', t_[:, h, :, :], I[nm][h].rearrange("(k p) e -> p k e", p=128), writes=[nm])
                P.dma('pool', wif[:], I['ml_w_if'].rearrange("(k p) g -> p k g", p=128), writes=['wif'])
                for h in range(4):
                    for dt_ in range(2):
                        o_ = pb[0][:, 0:8]
                        i_ = 0
                        for (wT, nm, off) in ((wqT, 'ml_wqT', 0), (wkT, 'ml_wkT', 8)):
                            for ke in range(2):
                                mm(o_, wT[:, h, ke, dt_ * 128:(dt_ + 1) * 128], wif[:, off + h * 2 + ke, :],
                                   i_ == 0, i_ == 3, [nm, 'wif'], ['pb0'])
                                i_ += 1
                        P.op('act', lambda h=h, dt_=dt_: A.copy(wcif[:, h * 2 + dt_, :], pb[0][:, 0:8]),
                             reads=['pb0'], writes=['wcif'])
                        o2 = pb[1][:, 0:8]
                        for ke in range(2):
                            mm(o2, wvT[:, h, ke, dt_ * 128:(dt_ + 1) * 128], wif[:, 16 + h * 2 + ke, :],
                               ke == 0, ke == 1, ['ml_wvT', 'wif'], ['pb1'])
                        P.op('act', lambda h=h, dt_=dt_: A.copy(wxif[:, h * 2 + dt_, :], pb[1][:, 0:8]),
                             reads=['pb1'], writes=['wxif'])

                def S(n, shp):
                    return sbs(n, shp, F32)

                def trig(name, ang_ap, shp, key):
                    cosT = S(name + "_cos", shp)
                    sinT = S(name + "_sin", shp)
                    sh = S(name + "_sh", shp)
                    acc = S(name + "_acc", shp)
                    tmp = S(name + "_tm", shp)
                    for (dst, shift) in ((cosT, PI / 2), (sinT, 0.0)):
                        P.op('dve', lambda shift=shift: V.tensor_scalar(sh[:], ang_ap, shift, None, ALU.add),
                             reads=[key], writes=[name + 'sh'])
                        P.op('dve', lambda: V.tensor_copy(acc[:], sh[:]), reads=[name + 'sh'], writes=[name + 'win_re'])
                        for m_ in range(1, 7):
                            thr = (2 * m_ - 1) * PI
                            P.op('dve', lambda thr=thr: V.tensor_scalar(tmp[:], sh[:], thr, -2 * PI, ALU.is_ge, ALU.mult),
                                 reads=[name + 'sh'], writes=[name + 'tm'])
                            P.op('dve', lambda: V.tensor_tensor(acc[:], acc[:], tmp[:], ALU.add),
                                 reads=[name + 'win_re', name + 'tm'], writes=[name + 'win_re'])
                        P.op('act', lambda dst=dst: A.activation(out=dst[:], in_=acc[:], func=AF.Sin),
                             reads=[name + 'win_re'], writes=[name + ('c' if dst is cosT else 's')])
                    return cosT, sinT

                shA = [128, 4, 64]
                lamreA = S("lamreA", shA); lamimA = S("lamimA", shA); dtA = S("dtA", shA)
                breA = S("breA", shA); bimA = S("bimA", shA)
                P.dma('sp', lamreA[:], I['lamre_A'][:, :, :], writes=['lamreA'])
                P.dma('sp', lamimA[:], I['lamim_A'][:, :, :], writes=['lamimA'])
                P.dma('sp', dtA[:], I['logdt_A'][:, :, :], writes=['dtA'])
                P.dma('sp', breA[:], I['bre_A'][:, :, :], writes=['breA'])
                P.dma('sp', bimA[:], I['bim_A'][:, :, :], writes=['bimA'])
                P.op('act', lambda: A.activation(out=dtA[:], in_=dtA[:], func=AF.Exp), reads=['dtA'], writes=['dtA'])
                magA = S("magA", shA); angA = S("angA", shA)
                P.op('dve', lambda: V.tensor_tensor(magA[:], lamreA[:], dtA[:], ALU.mult), reads=['lamreA', 'dtA'], writes=['magA'])
                P.op('act', lambda: A.activation(out=magA[:], in_=magA[:], func=AF.Exp), reads=['magA'], writes=['magA'])
                P.op('dve', lambda: V.tensor_tensor(angA[:], lamimA[:], dtA[:], ALU.mult), reads=['lamimA', 'dtA'], writes=['angA'])
                cosA, sinA = trig("tA", angA[:], shA, 'angA')
                abre = S("abre", shA); abim = S("abim", shA)
                P.op('dve', lambda: V.tensor_tensor(abre[:], magA[:], cosA[:], ALU.mult), reads=['magA', 'tAc'], writes=['abre'])
                P.op('dve', lambda: V.tensor_tensor(abim[:], magA[:], sinA[:], ALU.mult), reads=['magA', 'tAs'], writes=['abim'])
                den = S("denA", shA); t1 = S("t1A", shA); t2 = S("t2A", shA); kre = S("kreA", shA); kim = S("kimA", shA)
                P.op('dve', lambda: V.tensor_tensor(den[:], lamreA[:], lamreA[:], ALU.mult), reads=['lamreA'], writes=['denA'])
                P.op('dve', lambda: V.tensor_tensor(t1[:], lamimA[:], lamimA[:], ALU.mult), reads=['lamimA'], writes=['t1A'])
                P.op('dve', lambda: V.tensor_tensor(den[:], den[:], t1[:], ALU.add), reads=['denA', 't1A'], writes=['denA'])
                P.op('dve', lambda: V.reciprocal(den[:], den[:]), reads=['denA'], writes=['denA'])
                P.op('dve', lambda: V.tensor_scalar(abre[:], abre[:], -1.0, None, ALU.add), reads=['abre'], writes=['abre'])
                P.op('dve', lambda: V.tensor_tensor(t1[:], abre[:], lamreA[:], ALU.mult), reads=['abre', 'lamreA'], writes=['t1A'])
                P.op('dve', lambda: V.tensor_tensor(t2[:], abim[:], lamimA[:], ALU.mult), reads=['abim', 'lamimA'], writes=['t2A'])
                P.op('dve', lambda: V.tensor_tensor(t1[:], t1[:], t2[:], ALU.add), reads=['t1A', 't2A'], writes=['t1A'])
                P.op('dve', lambda: V.tensor_tensor(kre[:], t1[:], den[:], ALU.mult), reads=['t1A', 'denA'], writes=['kreA'])
                P.op('dve', lambda: V.tensor_tensor(t1[:], abim[:], lamreA[:], ALU.mult), reads=['abim', 'lamreA'], writes=['t1A'])
                P.op('dve', lambda: V.tensor_tensor(t2[:], abre[:], lamimA[:], ALU.mult), reads=['abre', 'lamimA'], writes=['t2A'])
                P.op('dve', lambda: V.tensor_tensor(t1[:], t1[:], t2[:], ALU.subtract), reads=['t1A', 't2A'], writes=['t1A'])
                P.op('dve', lambda: V.tensor_tensor(kim[:], t1[:], den[:], ALU.mult), reads=['t1A', 'denA'], writes=['kimA'])
                bbre = S("bbre", shA); bbim = S("bbim", shA)
                P.op('dve', lambda: V.tensor_tensor(t1[:], kre[:], breA[:], ALU.mult), reads=['kreA', 'breA'], writes=['t1A'])
                P.op('dve', lambda: V.tensor_tensor(t2[:], kim[:], bimA[:], ALU.mult), reads=['kimA', 'bimA'], writes=['t2A'])
                P.op('dve', lambda: V.tensor_tensor(bbre[:], t1[:], t2[:], ALU.subtract), reads=['t1A', 't2A'], writes=['bbre'])
                P.op('dve', lambda: V.tensor_tensor(t1[:], kre[:], bimA[:], ALU.mult), reads=['kreA', 'bimA'], writes=['t1A'])
                P.op('dve', lambda: V.tensor_tensor(t2[:], kim[:], breA[:], ALU.mult), reads=['kimA', 'breA'], writes=['t2A'])
                P.op('dve', lambda: V.tensor_tensor(bbim[:], t1[:], t2[:], ALU.add), reads=['t1A', 't2A'], writes=['bbim'])
                maskA = S("maskA", [128, 8])
                P.dma('sp', maskA[:], I['c_maskA'][:, :], writes=['maskA'])
                for j in range(16):
                    kc, jl = j // 4, j % 4
                    for h in range(2):
                        g8 = 2 * jl + h
                        for part, src, key in ((0, bbre, 'bbre'), (1, bbim, 'bbim')):
                            P.op('dve', lambda j=j, kc=kc, h=h, g8=g8, part=part, src=src: V.tensor_scalar(
                                s5B[:, part, j, h * 64:(h + 1) * 64], src[:, kc, :], maskA[:, g8:g8 + 1], None, ALU.mult),
                                reads=[key, 'maskA'], writes=['s5B'])
                creS = S("creS", [128, 16, 16]); cimS = S("cimS", [128, 16, 16])
                P.dma('sp', creS[:], I['cre_S'][:, :, :], writes=['creS'])
                P.dma('sp', cimS[:], I['cim_S'][:, :, :], writes=['cimS'])
                P.op('pool', lambda: G.memset(s5C[:], 0.0), writes=['s5C'])
                for j in range(16):
                    jl = j % 4
                    for h in range(2):
                        g8 = 2 * jl + h
                        P.op('dve', lambda j=j, h=h, g8=g8: V.tensor_copy(
                            s5C[h * 64:(h + 1) * 64, 0, j, g8 * 16:(g8 + 1) * 16], creS[h * 64:(h + 1) * 64, j, :]),
                            reads=['creS'], writes=['s5C'])
                        P.op('dve', lambda j=j, h=h, g8=g8: V.tensor_scalar(
                            s5C[h * 64:(h + 1) * 64, 1, j, g8 * 16:(g8 + 1) * 16], cimS[h * 64:(h + 1) * 64, j, :],
                            -1.0, None, ALU.mult), reads=['cimS'], writes=['s5C'])
                shS = [128, 16]
                lamreS = S("lamreS", shS); lamimS = S("lamimS", shS); dtS = S("dtS", shS)
                P.dma('sp', lamreS[:], I['lamre_S'][:, :], writes=['lamreS'])
                P.dma('sp', lamimS[:], I['lamim_S'][:, :], writes=['lamimS'])
                P.dma('sp', dtS[:], I['logdt_S'][:, :], writes=['dtS'])
                P.op('act', lambda: A.activation(out=dtS[:], in_=dtS[:], func=AF.Exp), reads=['dtS'], writes=['dtS'])
                rmag = S("rmag", shS); angS = S("angS", shS)
                P.op('dve', lambda: V.tensor_tensor(rmag[:], lamreS[:], dtS[:], ALU.mult), reads=['lamreS', 'dtS'], writes=['rmag'])
                P.op('act', lambda: A.activation(out=rmag[:], in_=rmag[:], func=AF.Exp), reads=['rmag'], writes=['rmag'])
                P.op('dve', lambda: V.tensor_tensor(angS[:], lamimS[:], dtS[:], ALU.mult), reads=['lamimS', 'dtS'], writes=['angS'])
                cosS, sinS = trig("tS", angS[:], shS, 'angS')
                onesL = S("onesL", [128, LR])
                P.op('pool', lambda: G.memset(onesL[:], 1.0), writes=['onesL'])
                for j in range(16):
                    P.op('dve', lambda j=j: V.tensor_scalar(rtab[:, j, :], onesL[:, 0:2], rmag[:, j:j + 1], None, ALU.mult),
                         reads=['rmag', 'onesL'], writes=['rtab'])
                for j in range(16):
                    P.op('dve', lambda j=j: V.tensor_scalar(rtab0[:, j, :], onesL[:], rmag[:, j:j + 1], None, ALU.mult),
                         reads=['rmag', 'onesL'], writes=['rtab0'])
                P.op('pool', lambda: G.memset(rtab0[:, :, 0:1], 0.0), reads=['rtab0'], writes=['rtab0'])
                P.op('dve', lambda: V.tensor_copy(tabc[:, :, 0:1], cosS[:].unsqueeze(2)), reads=['tSc'], writes=['tabc'])
                P.op('dve', lambda: V.tensor_copy(tabs[:, :, 0:1], sinS[:].unsqueeze(2)), reads=['tSs'], writes=['tabs'])
                d1 = S("dbl1", [128, LR // 2])
                m_ = 1
                while m_ < LR:
                    for j in range(16):
                        cm, sm = tabc[:, j, m_ - 1:m_], tabs[:, j, m_ - 1:m_]
                        lo_c, lo_s = tabc[:, j, 0:m_], tabs[:, j, 0:m_]
                        hi_c, hi_s = tabc[:, j, m_:2 * m_], tabs[:, j, m_:2 * m_]
                        a1 = d1[:, 0:m_]
                        P.op('dve', lambda: V.tensor_scalar(a1, lo_s, sm, None, ALU.mult), reads=['tabs'], writes=['dbl1'])
                        P.op('dve', lambda: V.scalar_tensor_tensor(hi_c, lo_c, cm, a1, ALU.mult, ALU.subtract),
                             reads=['tabc', 'dbl1'], writes=['tabc'])
                        P.op('dve', lambda: V.tensor_scalar(a1, lo_c, sm, None, ALU.mult), reads=['tabc', 'tabs'], writes=['dbl1'])
                        P.op('dve', lambda: V.scalar_tensor_tensor(hi_s, lo_s, cm, a1, ALU.mult, ALU.add),
                             reads=['tabs', 'tabc', 'dbl1'], writes=['tabs'])
                    m_ *= 2
                P.barrier()
            print('SBUF remaining after weights/setup', nc.sbuf_bytes_remaining)
            L_ = {}
            hin = sb("hin", [128, D], F32)
            ss = sb("ss", [128, 4], F32)
            ub = sb("ub", [128, D], BF16)
            uT = sb("uT", [128, 8, NP], BF16)
            gated = sb("gated", [128, 12, NP], BF16)
            ua = sb("ua", [128, 4, NP], BF16)
            sza = sb("sza", [128, 4, NP], BF16)
            tA = sb("tA_", [128, NP], F32)
            tB = sb("tB_", [128, NP], F32)
            wre = sb("wre", [128, NP // LR, 4, LR], F32)
            wim = sb("wim", [128, NP // LR, 4, LR], F32)
            tA2 = sb("tA2_", [128, NP], F32)
            tB2 = sb("tB2_", [128, NP], F32)
            hsre = sb("hsre", [128, 4, NP], BF16)
            hsim = sb("hsim", [128, 4, NP], BF16)
            Hre = sb("Hre", [128, 16], F32)
            Him = sb("Him", [128, 16], F32)
            ch1 = sb("ch1", [128, 4], F32)
            ch2 = sb("ch2", [128, 4], F32)
            ch3 = sb("ch3", [128, 4], F32)
            ch4 = sb("ch4", [128, 4], F32)
            y1 = sb("y1", [128, NP], F32)
            y2 = sb("y2", [128, NP + 1], F32)
            y3 = sb("y3", [128, NP + 1], F32)
            yg = sb("yg", [128, 4, NP], BF16)
            xb = sb("xb", [128, 8, 3 + NP], BF16)
            szb = sb("szb", [128, 8, NP], BF16)
            acc = sb("acc", [128, NP], F32)
            xc = sb("xc", [128, 8, NP], BF16)
            qT = uT
            kT = sb("kT", [128, 8, NP], BF16)
            vtok = sb("vtok", [128, 4, 264], BF16)
            ktok = sb("ktok", [128, 4, 256], BF16)
            Cst = sb("Cst", [128, 4, 2, 264], F32)
            Cbf = sb("Cbf", [128, 4, 2, 264], BF16)
            r_ig = sb("r_ig", [4, NP], F32)
            r_lf = sb("r_lf", [4, NP], F32)
            Mfull = sb("Mfull", [4, NP + 1], F32)
            negM = sb("negM", [4, NP + 1], F32)
            r_wi = sb("r_wi", [4, NP], F32)
            r_g = sb("r_g", [4, NP], F32)
            r_emm = sb("r_emm", [4, NP], F32)
            r_wend = sb("r_wend", [4, NP], F32)
            r_e = r_wend
            mcarry = sb("mcarry", [4, 1], F32)
            dg4 = sb("dg4", [4, 4], F32)
            cols = sb("cols", [128, 20], F32)
            Esb = y1[:, 0:128]
            Dm = y1[:, 128:256]
            PT = sb("PT", [128, 128], BF16)
            kw = sb("kw", [128, 256], BF16)
            inter = y2
            tot = y3
            st6 = sb("st6", [128, 6], F32)
            st2 = sb("st2", [128, 8], F32)
            hn = ub
            ytmp = sb("ytmp", [128, 128], F32)
            xbo = sb("xbo", [128, 8, 3], F32)

            print('SBUF remaining after L0 activations', nc.sbuf_bytes_remaining)
            P.op('pool', lambda: G.memset(vtok[:], 1.0), writes=['vtok'])

            mmreg = [(pb[0], 'pb0'), (pb[1], 'pb1')]
            mmi = [0]

            def nextmm():
                r_ = mmreg[mmi[0] % 2]
                mmi[0] += 1
                return r_

            def inproj(f, n):
                ps, key = nextmm()
                for k in range(8):
                    mm(ps[:, 0:n], w_in[:, k, f * 128:(f + 1) * 128], uT[:, k, 0:n], k == 0, k == 7, ['w_in', 'uT'], [key])
                return ps, key

            for (g, si, pcs) in seqs:
                if g == 'p':
                    P.op('pool', lambda: G.memset(Hre[:], 0.0), writes=['Hre'])
                    P.op('pool', lambda: G.memset(Him[:], 0.0), writes=['Him'])
                    P.op('pool', lambda: G.memset(Cst[:], 0.0), writes=['Cst%d' % h_ for h_ in range(4)])
                    P.op('pool', lambda: G.memset(mcarry[:], 0.0), writes=['mcarry'])
                    P.op('pool', lambda: G.memset(xb[:, :, 0:3], 0.0), writes=['xb'])
                else:
                    P.dma('sp', Hre[:], I['s5re_in'][si, :].rearrange("(j q) -> q j", q=128), writes=['Hre'])
                    P.dma('sp', Him[:], I['s5im_in'][si, :].rearrange("(j q) -> q j", q=128), writes=['Him'])
                    for h in range(4):
                        P.dma('sp', Cst[:, h, :, 0:256], I['mlc_in'][si, h].rearrange("(k p) e -> p k e", p=128), writes=['Cst%d' % h])
                        P.dma('sp', Cst[:, h, :, 256], I['mln_in'][si, h].rearrange("(k p) -> p k", p=128), writes=['Cst%d' % h])
                    P.dma('sp', mcarry[:], I['mlm_in'][si, :].rearrange("(p o) -> p o", o=1), writes=['mcarry'])
                    for t_ in range(3):
                        P.dma('pool', xb[:, :, t_], I['mlconv_in'][si, t_].rearrange("(k p) -> p k", p=128), writes=['xb'])
                P.op('act', lambda: A.copy(Cbf[:], Cst[:]), reads=['Cst%d' % h_ for h_ in range(4)], writes=['Cbf%d' % h_ for h_ in range(4)])

                for (t0, n, src, _, _) in pcs:
                    tp = min(n, 128)
                    ntt = (n + 127) // 128
                    L = tp
                    nch = n // L
                    for tt in range(ntt):
                        P.dma('sp', hin[0:tp, :], src[tt * 128:tt * 128 + tp, :], writes=['hin'])
                        rmsnorm_T(hin, tp, tt, gcol, ub, uT, ss)
                    for f in range(4):
                        ps, key = inproj(f, n)
                        P.op('act', lambda f=f, ps=ps: A.copy(ua[:, f, 0:n], ps[:, 0:n]), reads=[key], writes=['ua'])
                    for f in range(4):
                        ps, key = inproj(4 + f, n)
                        P.op('act', lambda f=f, ps=ps: A.activation(out=sza[:, f, 0:n], in_=ps[:, 0:n], func=AF.Silu),
                             reads=[key], writes=['sza'])

                    def gen_s5():
                        Ls = min(n, LR)
                        ncs = n // Ls
                        def v3(ap2):
                            return ap2.rearrange("p (c l) -> p c l", c=ncs)

                        for kc in range(4):
                            tcs, tss = [], []
                            for jl in range(4):
                                j = 4 * kc + jl
                                sreg = (pb[4], 'pb4') if j % 2 == 0 else (pb[5], 'pb5')
                                pre, pim = sreg[0][:, 0:n], sreg[0][:, 256:256 + n]
                                mm(pre, s5B[:, 0, j, :], ua[:, kc, 0:n], True, True, ['s5B', 'ua'], [sreg[1]])
                                mm(pim, s5B[:, 1, j, :], ua[:, kc, 0:n], True, True, ['s5B', 'ua'], [sreg[1]])
                                tc_ = tabc[:, j, 0:Ls].unsqueeze(1).to_broadcast([128, ncs, Ls])
                                ts_ = tabs[:, j, 0:Ls].unsqueeze(1).to_broadcast([128, ncs, Ls])
                                tcs.append(tc_); tss.append(ts_)
                                P.op('dve', lambda pre=pre, tc_=tc_: V.tensor_tensor(v3(tA[:, 0:n]), v3(pre), tc_, ALU.mult),
                                     reads=[sreg[1], 'tabc'], writes=['tA'])
                                P.op('dve', lambda pim=pim, ts_=ts_: V.tensor_tensor(v3(tB[:, 0:n]), v3(pim), ts_, ALU.mult),
                                     reads=[sreg[1], 'tabs'], writes=['tB'])
                                P.op('pool', lambda jl=jl: G.tensor_tensor(wre[:, 0:ncs, jl, 0:Ls], v3(tA[:, 0:n]), v3(tB[:, 0:n]), ALU.add),
                                     reads=['tA', 'tB'], writes=['wre%d' % jl])
                                P.op('dve', lambda pim=pim, tc_=tc_: V.tensor_tensor(v3(tA2[:, 0:n]), v3(pim), tc_, ALU.mult),
                                     reads=[sreg[1], 'tabc'], writes=['tA2'])
                                P.op('dve', lambda pre=pre, ts_=ts_: V.tensor_tensor(v3(tB2[:, 0:n]), v3(pre), ts_, ALU.mult),
                                     reads=[sreg[1], 'tabs'], writes=['tB2'])
                                P.op('pool', lambda jl=jl: G.tensor_tensor(wim[:, 0:ncs, jl, 0:Ls], v3(tA2[:, 0:n]), v3(tB2[:, 0:n]), ALU.subtract),
                                     reads=['tA2', 'tB2'], writes=['wim%d' % jl])
                                if jl % 2 == 1:
                                    yield
                            j0 = 4 * kc
                            for c in range(ncs):
                                cs = slice(c * Ls, (c + 1) * Ls)
                                if c == 0:
                                    i4r, i4i, rk = Hre[:, j0:j0 + 4], Him[:, j0:j0 + 4], ['Hre', 'Him']
                                else:
                                    i4r, i4i, rk = ch1[:, 0:4], ch2[:, 0:4], ['ch1', 'ch2']
                                w4r_ = ['wre%d' % i_ for i_ in range(4)]
                                w4i_ = ['wim%d' % i_ for i_ in range(4)]
                                r4 = rtab[:, j0:j0 + 4, 1]
                                c0 = c * Ls
                                P.op('dve', lambda: V.tensor_tensor(ch3[:, 0:4], i4r, r4, ALU.mult), reads=rk + ['rtab'], writes=['ch3'])
                                P.op('dve', lambda: V.tensor_tensor(ch4[:, 0:4], i4i, r4, ALU.mult), reads=rk + ['rtab'], writes=['ch4'])
                                P.op('dve', lambda: V.tensor_tensor(wre[:, c, :, 0], wre[:, c, :, 0], ch3[:, 0:4], ALU.add), reads=w4r_ + ['ch3'], writes=w4r_)
                                P.op('dve', lambda: V.tensor_tensor(wim[:, c, :, 0], wim[:, c, :, 0], ch4[:, 0:4], ALU.add), reads=w4i_ + ['ch4'], writes=w4i_)
                                if Ls == LR:
                                    d0 = rtab0[:, j0:j0 + 4, :].rearrange("p j l -> p (j l)")
                                    P.op('dve', lambda: V.tensor_tensor_scan(wre[:, c, :, :].rearrange("p j l -> p (j l)"), d0,
                                                                             wre[:, c, :, :].rearrange("p j l -> p (j l)"), 0.0, ALU.mult, ALU.add),
                                         reads=w4r_ + ['rtab0'], writes=w4r_)
                                    P.op('dve', lambda: V.tensor_tensor_scan(wim[:, c, :, :].rearrange("p j l -> p (j l)"), d0,
                                                                             wim[:, c, :, :].rearrange("p j l -> p (j l)"), 0.0, ALU.mult, ALU.add),
                                         reads=w4i_ + ['rtab0'], writes=w4i_)
                                else:
                                    for jl in range(4):
                                        P.op('dve', lambda jl=jl: V.tensor_tensor_scan(wre[:, c, jl, 0:Ls], rtab0[:, j0 + jl, 0:Ls], wre[:, c, jl, 0:Ls], 0.0,
                                                                                       ALU.mult, ALU.add), reads=w4r_ + ['rtab0'], writes=w4r_)
                                        P.op('dve', lambda jl=jl: V.tensor_tensor_scan(wim[:, c, jl, 0:Ls], rtab0[:, j0 + jl, 0:Ls], wim[:, c, jl, 0:Ls], 0.0,
                                                                                       ALU.mult, ALU.add), reads=w4i_ + ['rtab0'], writes=w4i_)
                                e_ = (c + 1) * Ls - 1
                                last = (c == ncs - 1)
                                wlr, wli = wre[:, c, :, Ls - 1], wim[:, c, :, Ls - 1]
                                cL4, sL4 = tabc[:, j0:j0 + 4, Ls - 1], tabs[:, j0:j0 + 4, Ls - 1]
                                dre = Hre[:, j0:j0 + 4] if last else ch1[:, 0:4]
                                dim_ = Him[:, j0:j0 + 4] if last else ch2[:, 0:4]
                                kre_ = 'Hre' if last else 'ch1'
                                kim_ = 'Him' if last else 'ch2'
                                w4r = ['wre%d' % i_ for i_ in range(4)]
                                w4i = ['wim%d' % i_ for i_ in range(4)]
                                P.op('dve', lambda: V.tensor_tensor(ch3[:, 0:4], wli, sL4, ALU.mult), reads=w4i + ['tabs'], writes=['ch3'])
                                P.op('dve', lambda: V.tensor_tensor(ch4[:, 0:4], wlr, sL4, ALU.mult), reads=w4r + ['tabs'], writes=['ch4'])
                                P.op('dve', lambda: V.tensor_tensor(dre, wlr, cL4, ALU.mult), reads=w4r + ['tabc', kre_], writes=[kre_])
                                P.op('dve', lambda: V.tensor_tensor(dim_, wli, cL4, ALU.mult), reads=w4i + ['tabc', kim_], writes=[kim_])
                                P.op('dve', lambda: V.tensor_tensor(dre, dre, ch3[:, 0:4], ALU.subtract), reads=[kre_, 'ch3'], writes=[kre_])
                                P.op('dve', lambda: V.tensor_tensor(dim_, dim_, ch4[:, 0:4], ALU.add), reads=[kim_, 'ch4'], writes=[kim_])
                                if c % 2 == 1 or last:
                                    yield
                            for jl in range(4):
                                tc_, ts_ = tcs[jl], tss[jl]
                                P.op('dve', lambda jl=jl, tc_=tc_: V.tensor_tensor(v3(tA[:, 0:n]), wre[:, 0:ncs, jl, 0:Ls], tc_, ALU.mult),
                                     reads=['wre%d' % jl, 'tabc'], writes=['tA'])
                                P.op('dve', lambda jl=jl, ts_=ts_: V.tensor_tensor(v3(tB[:, 0:n]), wim[:, 0:ncs, jl, 0:Ls], ts_, ALU.mult),
                                     reads=['wim%d' % jl, 'tabs'], writes=['tB'])
                                P.op('pool', lambda jl=jl: G.tensor_tensor(hsre[:, jl, 0:n], tA[:, 0:n], tB[:, 0:n], ALU.subtract),
                                     reads=['tA', 'tB'], writes=['hsre%d' % jl])
                                P.op('dve', lambda jl=jl, tc_=tc_: V.tensor_tensor(v3(tA2[:, 0:n]), wim[:, 0:ncs, jl, 0:Ls], tc_, ALU.mult),
                                     reads=['wim%d' % jl, 'tabc'], writes=['tA2'])
                                P.op('dve', lambda jl=jl, ts_=ts_: V.tensor_tensor(v3(tB2[:, 0:n]), wre[:, 0:ncs, jl, 0:Ls], ts_, ALU.mult),
                                     reads=['wre%d' % jl, 'tabs'], writes=['tB2'])
                                P.op('pool', lambda jl=jl: G.tensor_tensor(hsim[:, jl, 0:n], tA2[:, 0:n], tB2[:, 0:n], ALU.add),
                                     reads=['tA2', 'tB2'], writes=['hsim%d' % jl])
                                if jl % 2 == 1:
                                    yield
                            ps, key = nextmm()
                            for jl in range(4):
                                j = 4 * kc + jl
                                mm(ps[:, 0:n], s5C[:, 0, j, :], hsre[:, jl, 0:n], jl == 0, False, ['s5C', 'hsre%d' % jl], [key])
                                mm(ps[:, 0:n], s5C[:, 1, j, :], hsim[:, jl, 0:n], False, jl == 3, ['s5C', 'hsim%d' % jl], [key])
                            P.op('dve', lambda kc=kc, ps=ps: V.scalar_tensor_tensor(y1[:, 0:n], ua[:, kc, 0:n], dcol[:, kc:kc + 1],
                                                                                   ps[:, 0:n], ALU.mult, ALU.add),
                                 reads=['ua', 'dcol', key], writes=['y1'])
                            P.op('pool', lambda: G.tensor_tensor(y2[:, 0:n], y1[:, 0:n], y1[:, 0:n], ALU.mult), reads=['y1'], writes=['y2'])
                            P.op('pool', lambda: G.tensor_scalar(y2[:, 0:n], y2[:, 0:n], 0.044715, 1.0, ALU.mult, ALU.add),
                                 reads=['y2'], writes=['y2'])
                            P.op('pool', lambda: G.tensor_tensor(y2[:, 0:n], y2[:, 0:n], y1[:, 0:n], ALU.mult), reads=['y2', 'y1'], writes=['y2'])
                            P.op('act', lambda: A.activation(out=y3[:, 0:n], in_=y2[:, 0:n], func=AF.Tanh, scale=math.sqrt(2.0 / PI)),
                                 reads=['y2'], writes=['y3'])
                            P.op('dve', lambda: V.tensor_scalar(y3[:, 0:n], y3[:, 0:n], 0.5, 0.5, ALU.mult, ALU.add), reads=['y3'], writes=['y3'])
                            P.op('dve', lambda kc=kc: V.tensor_tensor(yg[:, kc, 0:n], y3[:, 0:n], y1[:, 0:n], ALU.mult),
                                 reads=['y3', 'y1'], writes=['yg%d' % kc])
                            yield
                        for m_ in range(4):
                            ps, key = nextmm()
                            for kk in range(4):
                                mm(ps[:, 0:n], w_glu[:, kk, m_ * 128:(m_ + 1) * 128], yg[:, kk, 0:n], kk == 0, kk == 3,
                                   ['w_glu', 'yg%d' % kk], [key])
                            P.op('act', lambda m_=m_, ps=ps: A.activation(out=y3[:, 0:n], in_=ps[:, 0:n], func=AF.Tanh, scale=0.5,
                                                                          bias=bgluh[:, m_:m_ + 1]),
                                 reads=[key, 'bgluh'], writes=['y3'])
                            P.op('dve', lambda: V.tensor_scalar(y3[:, 0:n], y3[:, 0:n], 0.5, 0.5, ALU.mult, ALU.add), reads=['y3'], writes=['y3'])
                            P.op('dve', lambda m_=m_: V.tensor_tensor(y3[:, 0:n], y3[:, 0:n], yg[:, m_, 0:n], ALU.mult),
                                 reads=['y3', 'yg%d' % m_], writes=['y3'])
                            P.op('dve', lambda m_=m_: V.tensor_tensor(gated[:, m_, 0:n], y3[:, 0:n], sza[:, m_, 0:n], ALU.mult),
                                 reads=['y3', 'sza'], writes=['gated%d' % m_])
                            yield

                    def gen_ml():
                        for f in range(8):
                            ps, key = inproj(8 + f, n)
                            P.op('act', lambda f=f, ps=ps: A.copy(xb[:, f, 3:3 + n], ps[:, 0:n]), reads=[key], writes=['xb'])
                            if f % 2 == 1:
                                yield
                        for f in range(8):
                            ps, key = inproj(16 + f, n)
                            P.op('act', lambda f=f, ps=ps: A.activation(out=szb[:, f, 0:n], in_=ps[:, 0:n], func=AF.Silu),
                                 reads=[key], writes=['szb'])
                            if f % 2 == 1:
                                yield
                        for k in range(8):
                            P.op('dve', lambda k=k: V.tensor_scalar(acc[:, 0:n], xb[:, k, 0:n], convw[:, k, 0:1], convb[:, k:k + 1],
                                                                    ALU.mult, ALU.add),
                                 reads=['xb', 'convw', 'convb'], writes=['acc'])
                            for tap in range(1, 4):
                                P.op('dve', lambda k=k, tap=tap: V.scalar_tensor_tensor(
                                    acc[:, 0:n], xb[:, k, tap:tap + n], convw[:, k, tap:tap + 1], acc[:, 0:n], ALU.mult, ALU.add),
                                    reads=['xb', 'convw', 'acc'], writes=['acc'])
                            P.op('act', lambda k=k: A.activation(out=xc[:, k, 0:n], in_=acc[:, 0:n], func=AF.Silu),
                                 reads=['acc'], writes=['xc'])
                            yield
                        gip, gikey = nextmm()
                        gfp, gfkey = nextmm()
                        gi, gf = gip[0:4, 0:n], gfp[0:4, 0:n]
                        for gsel, o_, gkey_ in ((0, gi, gikey), (4, gf, gfkey)):
                            for k in range(8):
                                mm(o_, wcif[:, k, gsel:gsel + 4], xc[:, k, 0:n], k == 0, False, ['wcif', 'xc'], [gkey_])
                            for k in range(8):
                                mm(o_, wxif[:, k, gsel:gsel + 4], xb[:, k, 3:3 + n], False, k == 7, ['wxif', 'xb'], [gkey_])
                        P.op('act', lambda: A.activation(out=r_ig[:, 0:n], in_=gi, func=AF.Identity, bias=b_i[:, 0:1]),
                             reads=[gikey, 'b_i'], writes=['r_ig'])
                        P.op('act', lambda: A.activation(out=r_e[:, 0:n], in_=gf, func=AF.Exp, scale=-1.0, bias=b_f[:, 0:1]),
                             reads=[gfkey, 'b_f'], writes=['r_wend'])
                        P.op('act', lambda: A.activation(out=r_e[:, 0:n], in_=r_e[:, 0:n], func=AF.Ln, bias=1.0),
                             reads=['r_wend'], writes=['r_wend'])
                        P.op('dve', lambda: V.tensor_tensor_scan(r_lf[:, 0:n], onesrow[:, 0:n], r_e[:, 0:n], 0.0, ALU.mult, ALU.add),
                             reads=['r_wend', 'onesrow'], writes=['r_lf'])
                        P.op('dve', lambda: V.tensor_tensor(r_ig[:, 0:n], r_ig[:, 0:n], r_lf[:, 0:n], ALU.add),
                             reads=['r_ig', 'r_lf'], writes=['r_ig'])
                        P.op('dve', lambda: V.tensor_copy(Mfull[:, 0:1], mcarry[:, 0:1]), reads=['mcarry'], writes=['Mfull'])
                        P.op('dve', lambda: V.tensor_tensor_scan(Mfull[:, 1:n + 1], onesrow[:, 0:n], r_ig[:, 0:n], mcarry[:, 0:1],
                                                                 ALU.mult, ALU.max),
                             reads=['r_ig', 'onesrow', 'mcarry', 'Mfull'], writes=['Mfull'])
                        P.op('dve', lambda: V.tensor_scalar(negM[:, 0:n + 1], Mfull[:, 0:n + 1], -1.0, None, ALU.mult),
                             reads=['Mfull'], writes=['negM'])
                        P.op('dve', lambda: V.tensor_tensor(r_emm[:, 0:n], r_lf[:, 0:n], Mfull[:, 1:n + 1], ALU.subtract),
                             reads=['r_lf', 'Mfull'], writes=['r_emm'])
                        P.op('act', lambda: A.activation(out=r_emm[:, 0:n], in_=r_emm[:, 0:n], func=AF.Exp), reads=['r_emm'], writes=['r_emm'])
                        P.op('dve', lambda: V.tensor_tensor(mcarry[:, 0:1], Mfull[:, n:n + 1], r_lf[:, n - 1:n], ALU.subtract),
                             reads=['Mfull', 'r_lf'], writes=['mcarry'])
                        for c in range(nch):
                            cs = slice(c * L, (c + 1) * L)
                            P.op('act', lambda c=c, cs=cs: A.activation(out=r_wi[:, cs], in_=Mfull[:, 1 + c * L:1 + (c + 1) * L], func=AF.Exp,
                                                                        scale=-1.0, bias=Mfull[:, c * L:c * L + 1]),
                                 reads=['Mfull'], writes=['r_wi'])
                            P.op('act', lambda c=c, cs=cs: A.activation(out=r_g[:, cs], in_=Mfull[:, 1 + c * L:1 + (c + 1) * L], func=AF.Exp,
                                                                        scale=-1.0, bias=Mfull[:, (c + 1) * L:(c + 1) * L + 1]),
                                 reads=['Mfull'], writes=['r_g'])
                            P.op('act', lambda c=c, cs=cs: A.activation(out=r_wend[:, cs], in_=r_ig[:, cs], func=AF.Exp,
                                                                        bias=negM[:, (c + 1) * L:(c + 1) * L + 1]),
                                 reads=['r_ig', 'negM'], writes=['r_wend'])
                        yield
                        for h in range(4):
                            for e in range(2):
                                for (w_, dst, sc, wkey) in ((wq, qT, 1.0, 'ml_wq'), (wk, kT, 1.0 / 16.0, 'ml_wk')):
                                    ps, key = nextmm()
                                    for kd in range(2):
                                        mm(ps[:, 0:n], w_[:, h, kd, e * 128:(e + 1) * 128], xc[:, 2 * h + kd, 0:n], kd == 0, kd == 1,
                                           [wkey, 'xc'], [key])
                                    P.op('act', lambda ps=ps, dst=dst, h=h, e=e, sc=sc: A.mul(dst[:, 2 * h + e, 0:n], ps[:, 0:n], sc),
                                         reads=[key], writes=['uT'] if dst is qT else ['kT'])
                        for c in range(nch):
                            cs = slice(c * L, (c + 1) * L)
                            tt = c
                            ts0 = c * 128
                            for h in range(4):
                                ps, key = nextmm()
                                for kd in range(2):
                                    mm(ps[0:tp, 0:256], xb[:, 2 * h + kd, 3 + ts0:3 + ts0 + tp], wv[:, h, kd, :], kd == 0, kd == 1,
                                       ['xb', 'ml_wv'], [key])
                                P.op('act', lambda ps=ps, h=h: A.copy(vtok[0:tp, h, 0:256], ps[0:tp, 0:256]),
                                     reads=[key], writes=['vtok'])
                                ps, key = nextmm()
                                for kd in range(2):
                                    mm(ps[0:tp, 0:256], xc[:, 2 * h + kd, ts0:ts0 + tp], wk[:, h, kd, :], kd == 0, kd == 1,
                                       ['xc', 'ml_wk'], [key])
                                P.op('act', lambda ps=ps, h=h: A.mul(ktok[0:tp, h, :], ps[0:tp, 0:256], 1.0 / 16.0),
                                     reads=[key], writes=['ktok'])
                            pc = pb[6]
                            for i_, row in enumerate((r_wi, r_emm, r_wend, r_g)):
                                mm(pc[0:L, 384 + 4 * i_:388 + 4 * i_], row[0:4, cs], identf[0:4, 0:4], True, True,
                                   ['r_wi', 'r_emm', 'r_wend', 'r_g', 'identf'], ['pb6c'])
                            P.op('dve', lambda c=c: V.tensor_scalar(dg4[:, :], identf[0:4, 0:4], r_wi[0:4, (c + 1) * L - 1:(c + 1) * L], None, ALU.mult),
                                 reads=['r_wi', 'identf'], writes=['dg4'])
                            mm(pc[:, 400:404], ones4[0:4, :], dg4[0:4, :], True, True, ['ones4', 'dg4'], ['pb6c'])
                            P.op('dve', lambda pc=pc: V.tensor_copy(cols[:, 0:20], pc[:, 384:404]), reads=['pb6c'], writes=['cols'])
                            yield
                            for h in range(4):
                                stp, stkey = nextmm()
                                ST = stp[0:L, 0:L]
                                for ke in range(2):
                                    mm(ST, kT[:, 2 * h + ke, cs], qT[:, 2 * h + ke, cs], ke == 0, ke == 1, ['kT', 'uT'], [stkey])
                                P.op('dve', lambda ST=ST, h=h: V.scalar_tensor_tensor(PT[0:L, 0:L], ST, cols[0:L, 8 + h:9 + h], negmask[0:L, 0:L],
                                                                                     ALU.mult, ALU.mult),
                                     reads=[stkey, 'cols', 'negmask'], writes=['PT'])
                                mm(pb[7][0:L, 0:257], PT[0:L, 0:L], vtok[0:L, h, 0:257], True, True, ['PT', 'vtok'], ['pb7'])
                                for kd in range(2):
                                    mm(pb[3][0:L, 0:257], qT[:, 2 * h + kd, cs], Cbf[:, h, kd, 0:257], kd == 0, kd == 1, ['uT', 'Cbf%d' % h], ['pb3'])
                                P.op('act', lambda h=h: A.activation(out=inter[0:L, 0:257], in_=pb[3][0:L, 0:257], func=AF.Copy,
                                                                     scale=cols[0:L, h:h + 1]),
                                     reads=['pb3', 'cols'], writes=['y2'])
                                P.op('dve', lambda h=h: V.scalar_tensor_tensor(tot[0:L, 0:257], pb[7][0:L, 0:257], cols[0:L, 12 + h:13 + h],
                                                                            inter[0:L, 0:257], ALU.mult, ALU.add),
                                     reads=['pb7', 'y2', 'cols'], writes=['y3'])
                                P.op('dve', lambda: V.scalar_tensor_tensor(st2[0:L, 2:3], tot[0:L, 256:257], -1.0, tot[0:L, 256:257],
                                                                           ALU.mult, ALU.max), reads=['y3'], writes=['st2c'])
                                P.op('dve', lambda h=h: V.tensor_tensor(st2[0:L, 2:3], st2[0:L, 2:3], cols[0:L, 4 + h:5 + h], ALU.max),
                                     reads=['st2c', 'cols'], writes=['st2c'])
                                P.op('dve', lambda: V.reciprocal(st2[0:L, 3:4], st2[0:L, 2:3]), reads=['st2c'], writes=['st2d'])
                                P.op('dve', lambda: V.bn_stats(st6[0:L, :], tot[0:L, 0:256]), reads=['y3'], writes=['st6'])
                                P.op('dve', lambda: V.bn_aggr(st2[0:L, 4:6], st6[0:L, :]), reads=['st6'], writes=['st2e'])
                                P.op('dve', lambda: V.tensor_tensor(st2[0:L, 6:7], st2[0:L, 3:4], st2[0:L, 3:4], ALU.mult),
                                     reads=['st2d'], writes=['st2f'])
                                P.op('dve', lambda: V.tensor_scalar(st2[0:L, 6:7], st2[0:L, 6:7], st2[0:L, 5:6], EPS, ALU.mult, ALU.add),
                                     reads=['st2f', 'st2e'], writes=['st2f'])
                                P.op('pool', lambda: G.tensor_tensor(st2[0:L, 7:8], st2[0:L, 6:7], negh[0:L, :], ALU.pow),
                                     reads=['st2f', 'negh'], writes=['st2g'])
                                P.op('dve', lambda: V.tensor_tensor(st2[0:L, 7:8], st2[0:L, 7:8], st2[0:L, 3:4], ALU.mult),
                                     reads=['st2g', 'st2d'], writes=['st2g'])
                                P.op('dve', lambda tt=tt, h=h: V.tensor_scalar(hn[0:L, h * 256:(h + 1) * 256], tot[0:L, 0:256],
                                                                              st2[0:L, 4:5], st2[0:L, 7:8], ALU.subtract, ALU.mult),
                                     reads=['y3', 'st2e', 'st2g'], writes=['ub'])
                                P.op('act', lambda tt=tt, h=h: A.activation(out=kw[0:L, :], in_=ktok[0:L, h, :], func=AF.Copy, scale=cols[0:L, 8 + h:9 + h]),
                                     reads=['ktok', 'cols'], writes=['kw'])
                                for kd in range(2):
                                    cup = (pb[4], 'pb4') if kd == 0 else (pb[5], 'pb5')
                                    mm(cup[0][:, 0:257], kw[0:L, kd * 128:(kd + 1) * 128], vtok[0:L, h, 0:257], True, True, ['kw', 'vtok'], [cup[1]])
                                    P.op('dve', lambda h=h, kd=kd, cup=cup: V.scalar_tensor_tensor(
                                        Cst[:, h, kd, 0:257], Cst[:, h, kd, 0:257], cols[:, 16 + h:17 + h], cup[0][:, 0:257], ALU.mult, ALU.add),
                                        reads=['Cst%d' % h, 'cols', cup[1]], writes=['Cst%d' % h])
                                P.op('act', lambda h=h: A.copy(Cbf[:, h, :, :], Cst[:, h, :, :]), reads=['Cst%d' % h], writes=['Cbf%d' % h])
                                yield
                            pT = pb[2][:].bitcast(BF16)
                            for k in range(8):
                                P.op('pe', lambda k=k: PL.transpose(pT[:, k * 128:k * 128 + tp], hn[0:tp, k * 128:(k + 1) * 128],
                                                                    identb[0:tp, 0:tp]),
                                     reads=['ub', 'identb'], writes=['pb2'])
                            for k in range(8):
                                P.op('dve', lambda k=k: V.tensor_scalar(ytmp[:, 0:tp], pT[:, k * 128:k * 128 + tp], mlnw[:, k:k + 1], None, ALU.mult),
                                     reads=['pb2', 'mlnw'], writes=['ytmp'])
                                P.op('dve', lambda k=k: V.scalar_tensor_tensor(ytmp[:, 0:tp], xc[:, k, ts0:ts0 + tp], skipc[:, k:k + 1],
                                                                               ytmp[:, 0:tp], ALU.mult, ALU.add),
                                     reads=['xc', 'skipc', 'ytmp'], writes=['ytmp'])
                                P.op('dve', lambda k=k: V.tensor_tensor(gated[:, 4 + k, ts0:ts0 + tp], ytmp[:, 0:tp], szb[:, k, ts0:ts0 + tp], ALU.mult),
                                     reads=['ytmp', 'szb'], writes=['gatedB'])
                            yield
                        P.op('dve', lambda n=n: V.tensor_copy(xbo[:, :, :], xb[:, :, n:n + 3]), reads=['xb'], writes=['xbo'])
                        P.op('dve', lambda n=n: V.tensor_copy(xb[:, :, 0:3], xbo[:, :, :]), reads=['xbo'], writes=['xb'])
                    gens = [gen_s5(), gen_ml()]
                    while gens:
                        for g_ in list(gens):
                            try:
                                next(g_)
                            except StopIteration:
                                gens.remove(g_)
                    for tt in range(ntt):
                        ts0 = tt * 128
                        P.dma('sp', hin[0:tp, :], src[ts0:ts0 + tp, :], writes=['hin'])
                        for nh in range(2):
                            po = ((pb[7], 'pb7'), (pb[3], 'pb3'), (pb[4], 'pb4'), (pb[5], 'pb5'))[2 * (tt % 2) + nh]
                            for cch in range(12):
                                gk = ('gated%d' % cch) if cch < 4 else 'gatedB'
                                mm(po[0][0:tp, :], gated[:, cch, ts0:ts0 + tp], w_out[:, cch, nh * 512:(nh + 1) * 512], cch == 0, cch == 11,
                                   [gk, 'w_out'], [po[1]])
                            P.op('dve', lambda tt=tt, nh=nh, po=po: V.tensor_tensor(hin[0:tp, nh * 512:(nh + 1) * 512],
                                                                                   hin[0:tp, nh * 512:(nh + 1) * 512], po[0][0:tp, :], ALU.add),
                                 reads=['hin', po[1]], writes=['hin'])
                        P.dma('sp', h1rows(g, si, t0 + ts0, tp), hin[0:tp, :], reads=['hin'], writes=['h1'])
                P.enabled = True
                sfx = '_' + g
                P.dma('sp', O['o_s5re' + sfx][si, :].rearrange("(j q) -> q j", q=128), Hre[:], reads=['Hre'], writes=['o1'])
                P.dma('sp', O['o_s5im' + sfx][si, :].rearrange("(j q) -> q j", q=128), Him[:], reads=['Him'], writes=['o2'])
                for h in range(4):
                    P.dma('sp', O['o_mlc' + sfx][si, h].rearrange("(k p) e -> p k e", p=128), Cst[:, h, :, 0:256],
                          reads=['Cst%d' % h], writes=['o3'])
                    P.dma('sp', O['o_mln' + sfx][si, h].rearrange("(k p) -> p k", p=128), Cst[:, h, :, 256],
                          reads=['Cst%d' % h], writes=['o4'])
                P.dma('sp', O['o_mlm' + sfx][si, :].rearrange("(p o) -> p o", o=1), mcarry[:], reads=['mcarry'], writes=['o5'])
                for t_ in range(3):
                    P.dma('sp', O['o_mlconv' + sfx][si, t_].rearrange("(k p) -> p k", p=128), xbo[:, :, t_], reads=['xbo'], writes=['o6'])
            P.barrier()


        if cfg.layers >= 2:
          with ExitStack() as l1:
            sb = lambda n, s, d: l1.enter_context(nc.sbuf_tensor(n, s, d))
            NP = NPIECE
            w_in = sb("w_in1", [128, 8, 5120], BF16)
            w_sw = sb("w_sw1", [128, 8, 1024], BF16)
            w_out = sb("w_out1", [128, 16, D], BF16)
            lwa = sb("lwa", [128, 8, 128], BF16)
            lwx = sb("lwx", [128, 8, 128], BF16)
            for k in range(8):
                P.dma('pool', w_in[:, k, :], I['od_w_in'][k * 128:(k + 1) * 128, :], writes=['w_in'])
                P.dma('pool', w_sw[:, k, :], I['od_w_qksw'][k * 128:(k + 1) * 128, :], writes=['w_sw'])
            for k in range(16):
                P.dma('pool', w_out[:, k, :], I['od_w_out'][k * 128:(k + 1) * 128, :], writes=['w_out'])
            P.dma('pool', lwa[:], I['lru_w_a'].rearrange("h d e -> d h e"), writes=['lwa'])
            P.dma('pool', lwx[:], I['lru_w_x'].rearrange("h d e -> d h e"), writes=['lwx'])
            gcol = sb("gcol1", [128, 8], F32)
            rnw = sb("rnw", [128, 8], F32)
            cw = sb("cw1", [128, 8, 4], F32)
            cb = sb("cb1", [128, 8], F32)
            bah = sb("bah", [128, 8], F32)
            bxh = sb("bxh", [128, 8], F32)
            ccol = sb("ccol", [128, 8], F32)
            ccol2 = sb("ccol2", [128, 8], F32)
            fnw = sb("fnw", [128, D], F32)
            rmaskL = sb("rmaskL", [128, 4, 128], F32)
            rmaskS = sb("rmaskS", [128, 4, 128], F32)
            rxiL = sb("rxiL", [128, 4, 128], F32)
            rxiS = sb("rxiS", [128, 4, 16], F32)
            rzL = sb("rzL", [128, 4], F32)
            rzS = sb("rzS", [128, 4], F32)
            rglL = sb("rglL", [128, 4], F32)
            rglS = sb("rglS", [128, 4], F32)
            col = lambda nm: I[nm].rearrange("(k p) -> p k", p=128)
            P.dma('sp', gcol[:], I['norm_w'][1, :].rearrange("(k p) -> p k", p=128), writes=['gcol1'])
            P.dma('sp', rnw[:], col('ret_norm_w'), writes=['rnw'])
            for tap in range(4):
                P.dma('sp', cw[:, :, tap], I['lru_conv_w'][tap, :].rearrange("(k p) -> p k", p=128), writes=['cw1'])
            P.dma('sp', cb[:], col('lru_conv_b'), writes=['cb1'])
            P.dma('sp', bah[:], col('lru_b_a'), writes=['bah'])
            P.dma('sp', bxh[:], col('lru_b_x'), writes=['bxh'])
            P.dma('sp', ccol[:], col('lru_lambda'), writes=['ccol'])
            P.dma('sp', fnw[:], I['fnw_b'][:, :], writes=['fnw'])
            P.dma('sp', rmaskL[:], I['c_rmask128'][:, :, :], writes=['rmaskL'])
            P.dma('sp', rmaskS[:], I['c_rmask16'][:, :, :], writes=['rmaskS'])
            P.dma('sp', rxiL[:], I['c_rxi128'][:, :, :], writes=['rxiL'])
            P.dma('sp', rxiS[:], I['c_rxi16'][:, :, :], writes=['rxiS'])
            P.dma('sp', rzL[:], I['c_rzeta128'][:, :], writes=['rzL'])
            P.dma('sp', rzS[:], I['c_rzeta16'][:, :], writes=['rzS'])
            P.dma('sp', rglL[:], I['c_rgl128'][:, :], writes=['rglL'])
            P.dma('sp', rglS[:], I['c_rgl16'][:, :], writes=['rglS'])
            P.op('dve', lambda: V.tensor_scalar(bah[:], bah[:], 0.5, None, ALU.mult), reads=['bah'], writes=['bah'])
            P.op('dve', lambda: V.tensor_scalar(bxh[:], bxh[:], 0.5, None, ALU.mult), reads=['bxh'], writes=['bxh'])
            P.op('act', lambda: A.activation(out=ccol[:], in_=ccol[:], func=AF.Exp, scale=-1.0), reads=['ccol'], writes=['ccol'])
            P.op('act', lambda: A.activation(out=ccol[:], in_=ccol[:], func=AF.Ln, bias=1.0), reads=['ccol'], writes=['ccol'])
            P.op('dve', lambda: V.tensor_scalar(ccol2[:], ccol[:], -4.0, None, ALU.mult), reads=['ccol'], writes=['ccol2'])
            P.op('dve', lambda: V.tensor_scalar(ccol[:], ccol[:], -8.0, None, ALU.mult), reads=['ccol'], writes=['ccol'])

            hin = sb("hin1", [128, D], F32)
            ss = sb("ss1", [128, 4], F32)
            ub = sb("ub1", [128, D], BF16)
            hn = ub
            uT = sb("uT1", [128, 8, NP], BF16)
            gated = sb("gated1", [128, 16, NP], BF16)
            cosF = sb("cosF", [128, NP], F32)
            sinF = sb("sinF", [128, NP], F32)
            qT = sb("qT1", [128, 4, NP], BF16)
            kT = sb("kT1", [128, 4, NP], BF16)
            qxi = sb("qxi", [128, 128], BF16)
            kz = sb("kz", [128, 128], BF16)
            vtok = sb("vtok1", [128, 4, 256], BF16)
            szc = sb("szc", [128, 8, NP], BF16)
            xd = sb("xd", [128, 8, 3 + NP], BF16)
            szd = sb("szd", [128, 8, NP], BF16)
            Sst = sb("Sst", [128, 4, 256], F32)
            Sbf = sb("Sbf", [128, 4, 256], BF16)
            hst = sb("hst", [128, 8], F32)
            f1 = sb("f1", [128, NP], F32)
            LS = [dict(xcb=sb("lx%d" % i_, [128, NP], BF16), f2=sb("lf2_%d" % i_, [128, NP], F32), f3=sb("lf3_%d" % i_, [128, NP], F32),
                       f4=sb("lf4_%d" % i_, [128, NP], F32)) for i_ in range(4)]
            PT = sb("PT1", [128, 128], BF16)
            st6 = sb("st6b", [128, 6], F32)
            st2 = sb("st2b", [128, 8], F32)
            xdo = sb("xdo", [128, 8, 3], F32)
            yout = sb("yout", [128, D], F32)

            print('SBUF remaining after L1 activations', nc.sbuf_bytes_remaining)
            mmreg = [(pb[0], 'pb0'), (pb[1], 'pb1'), (pb[4], 'pb4'), (pb[5], 'pb5')]
            mmi = [0]

            def nextmm():
                r_ = mmreg[mmi[0] % 4]
                mmi[0] += 1
                return r_

            def inproj(f, n, wt=None, wkey='w_in'):
                ps, key = nextmm()
                wt = w_in if wt is None else wt
                for k in range(8):
                    mm(ps[:, 0:n], wt[:, k, f * 128:(f + 1) * 128], uT[:, k, 0:n], k == 0, k == 7, [wkey, 'uT'], [key])
                return ps, key

            for (g, si, pcs) in seqs:
                if g == 'p':
                    P.op('pool', lambda: G.memset(Sst[:], 0.0), writes=['Sst%d' % h_ for h_ in range(4)])
                    P.op('pool', lambda: G.memset(hst[:], 0.0), writes=['hst'])
                    P.op('pool', lambda: G.memset(xd[:, :, 0:3], 0.0), writes=['xd'])
                else:
                    for h in range(4):
                        P.dma('sp', Sst[:, h, :], I['ret_in'][si, h], writes=['Sst%d' % h])
                    P.dma('sp', hst[:], I['lruh_in'][si, :].rearrange("(k p) -> p k", p=128), writes=['hst'])
                    for t_ in range(3):
                        P.dma('pool', xd[:, :, t_], I['lruconv_in'][si, t_].rearrange("(k p) -> p k", p=128), writes=['xd'])
                P.op('act', lambda: A.copy(Sbf[:], Sst[:]), reads=['Sst%d' % h_ for h_ in range(4)], writes=['Sbf%d' % h_ for h_ in range(4)])
                for (t0, n, src, dst, _) in pcs:
                    tp = min(n, 128)
                    ntt = (n + 127) // 128
                    L = tp
                    nch = n // L
                    rmask, rxi, rz, rgl = (rmaskL, rxiL, rzL, rglL) if L == 128 else (rmaskS, rxiS, rzS, rglS)
                    rp0 = t0 if g == 'p' else T
                    P.dma('sp', cosF[:, 0:n], I['ropeF_c'][:, rp0:rp0 + n], writes=['cosF'])
                    P.dma('sp', sinF[:, 0:n], I['ropeF_s'][:, rp0:rp0 + n], writes=['sinF'])
                    for tt in range(ntt):
                        P.dma('sp', hin[0:tp, :], h1rows(g, si, t0 + tt * 128, tp), reads=['h1'], writes=['hin1'])
                        rmsnorm_T(hin, tp, tt, gcol, ub, uT, ss, hkey='hin1', gkey='gcol1')
                    for (base, dstT, sc, dkey) in ((0, qT, 1.0, 'qT1'), (4, kT, 1.0 / math.sqrt(128.0), 'kT1')):
                        for h in range(4):
                            psA, keyA = inproj(base + h, n)
                            psB, keyB = inproj(base + h, n, wt=w_sw, wkey='w_sw')
                            P.op('dve', lambda psA=psA: V.tensor_tensor(yout[:, 0:n], psA[:, 0:n], cosF[:, 0:n], ALU.mult),
                                 reads=[keyA, 'cosF'], writes=['yout'])
                            P.op('dve', lambda psB=psB: V.tensor_tensor(yout[:, 256:256 + n], psB[:, 0:n], sinF[:, 0:n], ALU.mult),
                                 reads=[keyB, 'sinF'], writes=['yout'])
                            P.op('dve', lambda: V.tensor_tensor(yout[:, 0:n], yout[:, 0:n], yout[:, 256:256 + n], ALU.add), reads=['yout', 'yout'], writes=['yout'])
                            P.op('act', lambda h=h, dstT=dstT, sc=sc: A.mul(dstT[:, h, 0:n], yout[:, 0:n], sc), reads=['yout'], writes=[dkey])
                    for f in range(8):
                        ps, key = inproj(16 + f, n)
                        P.op('act', lambda f=f, ps=ps: A.activation(out=szc[:, f, 0:n], in_=ps[:, 0:n], func=AF.Silu), reads=[key], writes=['szc'])
                    for f in range(8):
                        ps, key = inproj(24 + f, n)
                        P.op('act', lambda f=f, ps=ps: A.copy(xd[:, f, 3:3 + n], ps[:, 0:n]), reads=[key], writes=['xd'])
                    for f in range(8):
                        ps, key = inproj(32 + f, n)
                        P.op('act', lambda f=f, ps=ps: A.activation(out=szd[:, f, 0:n], in_=ps[:, 0:n], func=AF.Silu), reads=[key], writes=['szd'])
                    def gen_ret():
                        for c in range(nch):
                            cs = slice(c * L, (c + 1) * L)
                            ts0 = c * 128
                            for h in range(4):
                                ps, key = nextmm()
                                for k in range(8):
                                    mm(ps[0:tp, 0:256], uT[:, k, ts0:ts0 + tp], w_in[:, k, 1024 + h * 256:1024 + (h + 1) * 256], k == 0, k == 7,
                                       ['uT', 'w_in'], [key])
                                P.op('act', lambda ps=ps, h=h: A.copy(vtok[0:tp, h, :], ps[0:tp, 0:256]), reads=[key], writes=['vtok1'])
                            yield
                            for h in range(4):
                                stp, stkey = nextmm()
                                ST = stp[0:L, 0:L]
                                mm(ST, kT[:, h, cs], qT[:, h, cs], True, True, ['kT1', 'qT1'], [stkey])
                                P.op('dve', lambda ST=ST, h=h: V.tensor_tensor(PT[0:L, 0:L], ST, rmask[0:L, h, 0:L], ALU.mult),
                                     reads=[stkey, 'rmaskL', 'rmaskS'], writes=['PT1'])
                                P.op('dve', lambda h=h: V.tensor_tensor(qxi[:, 0:L], qT[:, h, cs], rxi[:, h, 0:L], ALU.mult),
                                     reads=['qT1', 'rxiL', 'rxiS'], writes=['qxi'])
                                mm(pb[7][0:L, 0:256], PT[0:L, 0:L], vtok[0:L, h, :], True, False, ['PT1', 'vtok1'], ['pb7'])
                                mm(pb[7][0:L, 0:256], qxi[:, 0:L], Sbf[:, h, :], False, True, ['qxi', 'Sbf%d' % h], ['pb7'])
                                P.op('act', lambda: A.copy(f1[0:L, 0:256], pb[7][0:L, 0:256]), reads=['pb7'], writes=['f1'])
                                P.op('dve', lambda: V.bn_stats(st6[0:L, :], f1[0:L, 0:256]), reads=['f1'], writes=['st6b'])
                                P.op('dve', lambda: V.bn_aggr(st2[0:L, 0:2], st6[0:L, :]), reads=['st6b'], writes=['st2x'])
                                P.op('dve', lambda: V.tensor_scalar(st2[0:L, 2:3], st2[0:L, 1:2], EPS, None, ALU.add), reads=['st2x'], writes=['st2y'])
                                P.op('pool', lambda: G.tensor_tensor(st2[0:L, 3:4], st2[0:L, 2:3], negh[0:L, :], ALU.pow),
                                     reads=['st2y', 'negh'], writes=['st2z'])
                                P.op('dve', lambda h=h: V.tensor_scalar(hn[0:L, h * 256:(h + 1) * 256], f1[0:L, 0:256], st2[0:L, 0:1], st2[0:L, 3:4],
                                                                        ALU.subtract, ALU.mult),
                                     reads=['f1', 'st2x', 'st2z'], writes=['ub'])
                                pT = pb[2][:].bitcast(BF16)
                                P.op('pe', lambda h=h: PL.transpose(pT[0:L, 0:128], kT[:, h, cs], identb[:, :]),
                                     reads=['kT1', 'identb'], writes=['pb2'])
                                P.op('dve', lambda h=h: V.tensor_scalar(kz[0:L, :], pT[0:L, 0:128], rz[0:L, h:h + 1], None, ALU.mult),
                                     reads=['pb2', 'rzL', 'rzS'], writes=['kz'])
                                mm(pb[3][:, 0:256], kz[0:L, :], vtok[0:L, h, :], True, True, ['kz', 'vtok1'], ['pb3'])
                                P.op('dve', lambda h=h: V.scalar_tensor_tensor(Sst[:, h, :], Sst[:, h, :], rgl[:, h:h + 1], pb[3][:, 0:256], ALU.mult, ALU.add),
                                     reads=['Sst%d' % h, 'rglL', 'rglS', 'pb3'], writes=['Sst%d' % h])
                                P.op('act', lambda h=h: A.copy(Sbf[:, h, :], Sst[:, h, :]), reads=['Sst%d' % h], writes=['Sbf%d' % h])
                                yield
                            pT = pb[2][:].bitcast(BF16)
                            for k in range(8):
                                P.op('pe', lambda k=k: PL.transpose(pT[:, k * 128:k * 128 + tp], hn[0:tp, k * 128:(k + 1) * 128], identb[0:tp, 0:tp]),
                                     reads=['ub', 'identb'], writes=['pb2'])
                            for k in range(8):
                                P.op('dve', lambda k=k: V.scalar_tensor_tensor(gated[:, k, ts0:ts0 + tp], pT[:, k * 128:k * 128 + tp], rnw[:, k:k + 1],
                                                                               szc[:, k, ts0:ts0 + tp], ALU.mult, ALU.mult),
                                     reads=['pb2', 'rnw', 'szc'], writes=['gatedA'])
                            yield
                    def gen_lru():
                        def ph1(k):
                            B_ = LS[k % 4]; sfx_ = '_%d' % (k % 4)
                            xcb, f2, f3, f4 = B_['xcb'], B_['f2'], B_['f3'], B_['f4']
                            P.op('dve', lambda: V.tensor_scalar(f1[:, 0:n], xd[:, k, 0:n], cw[:, k, 0:1], cb[:, k:k + 1], ALU.mult, ALU.add),
                                 reads=['xd', 'cw1', 'cb1'], writes=['f1'])
                            for tap in range(1, 4):
                                P.op('dve', lambda tap=tap: V.scalar_tensor_tensor(f1[:, 0:n], xd[:, k, tap:tap + n], cw[:, k, tap:tap + 1], f1[:, 0:n],
                                                                                   ALU.mult, ALU.add),
                                     reads=['xd', 'cw1', 'f1'], writes=['f1'])
                            P.op('act', lambda: A.copy(xcb[:, 0:n], f1[:, 0:n]), reads=['f1'], writes=['xcb' + sfx_])
                            psr, keyr = nextmm()
                            mm(psr[:, 0:n], lwa[:, k, :], xcb[:, 0:n], True, True, ['lwa', 'xcb' + sfx_], [keyr])
                            psi, keyi = nextmm()
                            mm(psi[:, 0:n], lwx[:, k, :], xcb[:, 0:n], True, True, ['lwx', 'xcb' + sfx_], [keyi])
                            P.op('act', lambda: A.activation(out=f2[:, 0:n], in_=psr[:, 0:n], func=AF.Tanh, scale=0.5, bias=bah[:, k:k + 1]),
                                 reads=[keyr, 'bah'], writes=['f2' + sfx_])
                            P.op('act', lambda: A.activation(out=f3[:, 0:n], in_=psi[:, 0:n], func=AF.Tanh, scale=0.5, bias=bxh[:, k:k + 1]),
                                 reads=[keyi, 'bxh'], writes=['f3' + sfx_])
                            P.op('act', lambda: A.activation(out=f4[:, 0:n], in_=f2[:, 0:n], func=AF.Exp, scale=ccol2[:, k:k + 1], bias=ccol2[:, k:k + 1]),
                                 reads=['f2' + sfx_, 'ccol2'], writes=['f4' + sfx_])
                            P.op('pool', lambda: G.tensor_scalar(f3[:, 0:n], f3[:, 0:n], 0.5, 0.5, ALU.mult, ALU.add), reads=['f3' + sfx_], writes=['f3' + sfx_])
                            P.op('pool', lambda: G.tensor_tensor(f3[:, 0:n], f3[:, 0:n], xcb[:, 0:n], ALU.mult), reads=['f3' + sfx_, 'xcb' + sfx_], writes=['f3' + sfx_])
                            P.op('pool', lambda: G.tensor_tensor(f2[:, 0:n], f4[:, 0:n], f4[:, 0:n], ALU.mult), reads=['f4' + sfx_, 'f2' + sfx_], writes=['f2' + sfx_])
                            P.op('pool', lambda: G.tensor_scalar(f2[:, 0:n], f2[:, 0:n], -1.0, 1.0, ALU.mult, ALU.add), reads=['f2' + sfx_], writes=['f2' + sfx_])

                        def ph2(k):
                            B_ = LS[k % 4]; sfx_ = '_%d' % (k % 4)
                            f2 = B_['f2']
                            P.op('act', lambda: A.activation(out=f2[:, 0:n], in_=f2[:, 0:n], func=AF.Ln), reads=['f2' + sfx_], writes=['f2' + sfx_])
                            P.op('act', lambda: A.activation(out=f2[:, 0:n], in_=f2[:, 0:n], func=AF.Exp, scale=0.5), reads=['f2' + sfx_], writes=['f2' + sfx_])

                        def ph3(k):
                            B_ = LS[k % 4]; sfx_ = '_%d' % (k % 4)
                            xcb, f2, f3, f4 = B_['xcb'], B_['f2'], B_['f3'], B_['f4']
                            P.op('dve', lambda: V.tensor_tensor(f3[:, 0:n], f3[:, 0:n], f2[:, 0:n], ALU.mult), reads=['f3' + sfx_, 'f2' + sfx_], writes=['f3' + sfx_])
                            P.op('dve', lambda: V.tensor_tensor_scan(f2[:, 0:n], f4[:, 0:n], f3[:, 0:n], hst[:, k:k + 1], ALU.mult, ALU.add),
                                 reads=['f4' + sfx_, 'f3' + sfx_, 'hst', 'f2' + sfx_], writes=['f2' + sfx_])
                            P.op('dve', lambda: V.tensor_copy(hst[:, k:k + 1], f2[:, n - 1:n]), reads=['f2' + sfx_], writes=['hst'])
                            P.op('dve', lambda: V.tensor_tensor(gated[:, 8 + k, 0:n], f2[:, 0:n], szd[:, k, 0:n], ALU.mult),
                                 reads=['f2' + sfx_, 'szd'], writes=['gatedB%d' % k])

                        for q_ in range(2):
                            for i_ in range(4):
                                ph1(4 * q_ + i_)
                                yield
                            for i_ in range(4):
                                ph2(4 * q_ + i_)
                            yield
                            for i_ in range(4):
                                ph3(4 * q_ + i_)
                                if i_ % 2 == 1:
                                    yield
                    gens = [gen_ret(), gen_lru()]
                    while gens:
                        for g_ in list(gens):
                            try:
                                next(g_)
                            except StopIteration:
                                gens.remove(g_)
                    P.op('dve', lambda n=n: V.tensor_copy(xdo[:, :, :], xd[:, :, n:n + 3]), reads=['xd'], writes=['xdo'])
                    P.op('dve', lambda: V.tensor_copy(xd[:, :, 0:3], xdo[:, :, :]), reads=['xdo'], writes=['xd'])
                    for tt in range(ntt):
                        ts0 = tt * 128
                        P.dma('sp', hin[0:tp, :], h1rows(g, si, t0 + ts0, tp), reads=['h1'], writes=['hin1'])
                        for nh in range(2):
                            po = ((pb[7], 'pb7'), (pb[3], 'pb3'), (pb[6], 'pb6'), (pb[2], 'pb2'))[2 * (tt % 2) + nh]
                            for cch in range(16):
                                gk = 'gatedA' if cch < 8 else 'gatedB%d' % (cch - 8)
                                mm(po[0][0:tp, :], gated[:, cch, ts0:ts0 + tp], w_out[:, cch, nh * 512:(nh + 1) * 512], cch == 0, cch == 15,
                                   [gk, 'w_out'], [po[1]])
                            P.op('dve', lambda nh=nh, po=po: V.tensor_tensor(hin[0:tp, nh * 512:(nh + 1) * 512],
                                                                             hin[0:tp, nh * 512:(nh + 1) * 512], po[0][0:tp, :], ALU.add),
                                 reads=['hin1', po[1]], writes=['hin1'])
                        if not (g == 'p' and t0 == 0):
                            P.op('pool', lambda: G.memset(ss[0:tp, 0:1], 0.0), writes=['ss'])
                            P.op('act', lambda: A.activation(out=ub[0:tp, :], in_=hin[0:tp, :], func=AF.Square, accum_out=ss[0:tp, 0:1]),
                                 reads=['hin1'], writes=['ub', 'ss'])
                            P.op('dve', lambda: V.tensor_scalar(ss[0:tp, 1:2], ss[0:tp, 0:1], 1.0 / D, EPS, ALU.mult, ALU.add), reads=['ss'], writes=['ss'])
                            P.op('pool', lambda: G.tensor_tensor(ss[0:tp, 2:3], ss[0:tp, 1:2], negh[0:tp, :], ALU.pow), reads=['ss', 'negh'], writes=['ss'])
                            P.op('dve', lambda: V.scalar_tensor_tensor(yout[0:tp, :], hin[0:tp, :], ss[0:tp, 2:3], fnw[0:tp, :], ALU.mult, ALU.mult),
                                 reads=['hin1', 'ss', 'fnw'], writes=['yout'])
                            P.dma('sp', dst[ts0:ts0 + tp, :], yout[0:tp, :], reads=['yout'], writes=['yo'])
                sfx = '_' + g
                for h in range(4):
                    P.dma('sp', O['o_ret' + sfx][si, h], Sst[:, h, :], reads=['Sst%d' % h], writes=['o7'])
                P.dma('sp', O['o_lruh' + sfx][si, :].rearrange("(k p) -> p k", p=128), hst[:], reads=['hst'], writes=['o8'])
                for t_ in range(3):
                    P.dma('sp', O['o_lruconv' + sfx][si, t_].rearrange("(k p) -> p k", p=128), xdo[:, :, t_], reads=['xdo'], writes=['o9'])
            P.barrier()
        P.barrier()
    return nc


def host_layout(inp):
    f = lambda a: np.ascontiguousarray(np.asarray(a, dtype=np.float32))
    d = {}
    d['norm_w'] = f(inp['norm_w'])
    d['final_norm_w'] = f(inp['final_norm_w'])
    d['ev_w_in'] = f(inp['ev_w_in'][0])
    d['ev_w_out'] = f(inp['ev_w_out'][0])
    lre, lim, ldt = inp['s5_lambda_re'][0], inp['s5_lambda_im'][0], inp['s5_log_dt'][0]
    def A_from_gp(x):
        x4 = np.asarray(x).reshape(4, 8, 64)
        o = np.broadcast_to(x4[:, :, None, :], (4, 8, 16, 64))
        return f(np.transpose(o, (1, 2, 0, 3)).reshape(128, 4, 64))
    d['lamre_A'] = A_from_gp(lre)
    d['lamim_A'] = A_from_gp(lim)
    d['logdt_A'] = A_from_gp(np.broadcast_to(np.asarray(ldt)[:, None], (32, 64)))
    def A_from_gpc(x):
        x5 = np.asarray(x).reshape(4, 8, 64, 16)
        return f(np.transpose(x5, (1, 3, 0, 2)).reshape(128, 4, 64))
    d['bre_A'] = A_from_gpc(inp['s5_b_re'][0])
    d['bim_A'] = A_from_gpc(inp['s5_b_im'][0])
    def S_from_gp(x):
        x3 = np.asarray(x).reshape(16, 2, 64)
        return f(np.transpose(x3, (1, 2, 0)).reshape(128, 16))
    d['lamre_S'] = S_from_gp(lre)
    d['lamim_S'] = S_from_gp(lim)
    d['logdt_S'] = S_from_gp(np.broadcast_to(np.asarray(ldt)[:, None], (32, 64)))
    def S_from_gcp(x):
        x4 = np.asarray(x).reshape(16, 2, 16, 64)
        return f(np.transpose(x4, (1, 3, 0, 2)).reshape(128, 16, 16))
    d['cre_S'] = S_from_gcp(inp['s5_c_re'][0])
    d['cim_S'] = S_from_gcp(inp['s5_c_im'][0])
    d['s5_d'] = f(inp['s5_d'][0]); d['s5_w_glu'] = f(inp['s5_w_glu'][0]); d['s5_b_glu'] = f(inp['s5_b_glu'][0])
    d['ml_conv_w'] = f(inp['ml_conv_w'][0]); d['ml_conv_b'] = f(inp['ml_conv_b'][0])
    for k in ('ml_wq', 'ml_wk', 'ml_wv'):
        d[k] = f(inp[k][0])
        d[k + 'T'] = f(np.transpose(np.asarray(inp[k][0]), (0, 2, 1)))
    d['ml_w_if'] = f(inp['ml_w_if'][0]); d['ml_b_if'] = f(inp['ml_b_if'][0])
    d['ml_norm_w'] = f(inp['ml_norm_w'][0]); d['ml_skip'] = f(inp['ml_skip'][0])
    w1 = np.asarray(inp['od_w_in'][0])
    d['od_w_in'] = f(w1)
    qk = w1[:, 0:1024].reshape(D, 8, 2, 64)
    d['od_w_qksw'] = f(qk[:, :, ::-1, :].reshape(D, 1024))
    d['fnw_b'] = f(np.broadcast_to(np.asarray(inp['final_norm_w'])[None, :], (128, D)))
    d['od_w_out'] = f(inp['od_w_out'][0]); d['ret_norm_w'] = f(inp['ret_norm_w'][0])
    d['lru_conv_w'] = f(inp['lru_conv_w'][0]); d['lru_conv_b'] = f(inp['lru_conv_b'][0])
    d['lru_w_a'] = f(inp['lru_w_a'][0]); d['lru_b_a'] = f(inp['lru_b_a'][0])
    d['lru_w_x'] = f(inp['lru_w_x'][0]); d['lru_b_x'] = f(inp['lru_b_x'][0]); d['lru_lambda'] = f(inp['lru_lambda'][0])
    d['meta'] = f(inp['meta'])
    return d


def make_in_maps(cfg, inputs, ncores):
    shared = host_layout(inputs)
    shared.update(host_consts())
    T = 16 + cfg.T_x
    pos = np.concatenate([np.arange(T, dtype=np.float32), (16 + cfg.past) + np.arange(16, dtype=np.float32)])
    cf, sf, ct, st = rope_tables(pos)
    shared['ropeF_c'], shared['ropeF_s'], shared['ropeT_c'], shared['ropeT_s'] = cf, sf, np.ascontiguousarray(ct), np.ascontiguousarray(st)
    f = lambda a: np.ascontiguousarray(np.asarray(a, dtype=np.float32))
    maps = []
    for c in range(ncores):
        ps = slice(c * cfg.n_p, (c + 1) * cfg.n_p)
        ss_ = slice(c * cfg.n_s, (c + 1) * cfg.n_s)
        m = dict(shared)
        m['xp'] = f(inputs['x_prompt'][ps])
        m['xs'] = f(inputs['x_sample'][ss_])
        m['s5re_in'] = f(inputs['state_s5_re'][0][ss_]).reshape(cfg.n_s, 2048)
        m['s5im_in'] = f(inputs['state_s5_im'][0][ss_]).reshape(cfg.n_s, 2048)
        m['mlc_in'] = f(inputs['state_ml_c'][0][ss_])
        m['mln_in'] = f(inputs['state_ml_n'][0][ss_])
        m['mlm_in'] = f(inputs['state_ml_m'][0][ss_])
        m['mlconv_in'] = f(inputs['state_ml_conv'][0][ss_])
        m['ret_in'] = f(inputs['state_ret'][0][ss_])
        m['lruh_in'] = f(inputs['state_lru_h'][0][ss_])
        m['lruconv_in'] = f(inputs['state_lru_conv'][0][ss_])
        maps.append(m)
    return maps


def gather(cfg, results):
    cat = lambda k: np.concatenate([np.asarray(r[k]) for r in results], axis=0)
    outs = [cat('yp'), cat('ys')]
    for g in ('p', 's'):
        n = cat('o_s5re_' + g).shape[0]
        outs += [cat('o_s5re_' + g).reshape(1, n, 32, 64), cat('o_s5im_' + g).reshape(1, n, 32, 64),
                 cat('o_mlc_' + g)[None], cat('o_mln_' + g)[None], cat('o_mlm_' + g)[None], cat('o_mlconv_' + g)[None],
                 cat('o_ret_' + g)[None], cat('o_lruh_' + g)[None], cat('o_lruconv_' + g)[None]]
    return tuple(np.ascontiguousarray(o.astype(np.float32)) for o in outs)


def kernel(**inputs):
    ncores = 8
    cfg = Cfg(n_p=2, T_x=4096, n_s=2, past=2048, layers=2)
    nc = build(cfg)
    maps = make_in_maps(cfg, inputs, ncores)
    res = run_bass_kernel_spmd(nc, maps, core_ids=list(range(ncores)))
    return gather(cfg, res.results)
```

```python
import math
from contextlib import ExitStack
import numpy as np
import concourse.bass as bass
import concourse.mybir as mybir
from concourse.bass_utils import run_bass_kernel_spmd

F32 = mybir.dt.float32
BF16 = mybir.dt.bfloat16
AF = mybir.ActivationFunctionType
ALU = mybir.AluOpType

D = 1024
NPIECE = 256
EPS = 1e-6
PI = math.pi


class Prog:
    def __init__(self, nc, ndma=6):
        self.nc = nc
        self.e = dict(pe=nc.tensor, dve=nc.vector, act=nc.scalar, pool=nc.gpsimd, sp=nc.sync)
        self.streams = {k: [] for k in self.e}
        self.cnt = {k: 0 for k in self.e}
        self.waited = {k: {} for k in self.e}
        self.lastw = {}
        self.reads = {}
        self.sems = {}
        self.ndma = ndma
        self.dmai = {q: 0 for q in ('sp', 'act', 'pool')}
        self.dma_last = {}
        self.enabled = True
        self.semnames = ['c_' + k for k in ('pe', 'dve', 'act', 'pool')] + \
            ['d_%s_%d' % (q, j) for q in ('sp', 'act', 'pool') for j in range(ndma)]

    def _wait(self, eng, tok):
        sem, val = tok
        if self.waited[eng].get(sem, 0) >= val:
            return
        self.waited[eng][sem] = val
        self.e[eng].wait_ge(self.sems[sem], val)

    def _deps(self, eng, reads, writes):
        deps = []
        own = 'c_' + eng
        for k in reads:
            t = self.lastw.get(k)
            if t is not None:
                if not (eng == 'pe' and t[0] == own):
                    deps.append(t)
        for k in writes:
            t = self.lastw.get(k)
            if t is not None and t[0] != own:
                deps.append(t)
            for t in self.reads.get(k, ()):
                if t[0] != own:
                    deps.append(t)
        for t in deps:
            self._wait(eng, t)

    def _commit(self, tok, reads, writes):
        for k in writes:
            self.lastw[k] = tok
            self.reads[k] = []
        for k in reads:
            if k not in writes:
                self.reads.setdefault(k, []).append(tok)

    def op(self, eng, fn, reads=(), writes=()):
        if not self.enabled:
            return None
        self._deps(eng, reads, writes)
        self.cnt[eng] += 1
        tok = ('c_' + eng, self.cnt[eng])
        fn().then_inc(self.sems['c_' + eng], 1)
        self._commit(tok, reads, writes)
        return tok

    def dma(self, q, out, in_, reads=(), writes=(), **kw):
        if not self.enabled:
            return None
        self._deps(q, reads, writes)
        i = self.dmai[q]
        self.dmai[q] += 1
        j, r = i % self.ndma, i // self.ndma
        sem = 'd_%s_%d' % (q, j)
        if r > 0:
            self._wait(q, (sem, 16 * r))
        e = self.e[q]
        e.dma_start(out=out, in_=in_, **kw).then_inc(self.sems[sem], 16)
        tok = (sem, 16 * (r + 1))
        self.dma_last[sem] = tok
        self._commit(tok, reads, writes)
        return tok

    def barrier(self):
        toks = set(self.lastw.values())
        for lst in self.reads.values():
            toks.update(lst)
        toks.update(self.dma_last.values())
        for k in ('pe', 'dve', 'act', 'pool'):
            if self.cnt[k] > 0:
                toks.add(('c_' + k, self.cnt[k]))
        for eng in self.e:
            for t in toks:
                self._wait(eng, t)
        self.lastw = {}
        self.reads = {}

    def emit(self, block):
        sems = self.sems

        def run(engname, engobj):
            for it in self.streams[engname]:
                if it[0] == 'w':
                    engobj.wait_ge(sems[it[1]], it[2])
                else:
                    inst = it[1]()
                    inst.then_inc(sems[it[2]], it[3])

        @block.tensor
        def _(x):
            run('pe', x)

        @block.vector
        def _(x):
            run('dve', x)

        @block.scalar
        def _(x):
            run('act', x)

        @block.gpsimd
        def _(x):
            run('pool', x)

        @block.sync
        def _(x):
            run('sp', x)


def host_consts():
    c = {}
    c['c_identb'] = np.eye(128, dtype=np.float32)
    c['c_identf'] = np.eye(128, dtype=np.float32)
    s = np.arange(128)
    c['c_negmask'] = np.where(s[:, None] <= s[None, :], 1.0, 0.0).astype(np.float32)
    sel = np.zeros((4, 4, 128), np.float32)
    for h in range(4):
        sel[h, h, :] = 1.0
    c['c_sel'] = sel
    c['c_ones4'] = np.ones((4, 128), np.float32)
    mA = np.zeros((128, 8), np.float32)
    for g8 in range(8):
        mA[g8 * 16:(g8 + 1) * 16, g8] = 1.0
    c['c_maskA'] = mA
    lg = np.log1p(-np.exp2(-5.0 - np.arange(4, dtype=np.float64)))
    for L in (16, 128):
        idx = np.arange(L, dtype=np.float64)
        diff = idx[None, :] - idx[:, None]
        dm = np.where(diff >= 0, np.exp(lg[:, None, None] * np.maximum(diff, 0.0)), 0.0)
        dmp = np.zeros((128, 4, 128), np.float32)
        dmp[:L, :, :L] = np.transpose(dm, (1, 0, 2))
        c['c_rmask%d' % L] = dmp
        xi = np.exp(lg[:, None] * (idx + 1.0))
        c['c_rxi%d' % L] = np.ascontiguousarray(np.broadcast_to(xi[None], (128, 4, L))).astype(np.float32)
        zeta = np.exp(lg[:, None] * (L - 1.0 - idx))
        zp = np.zeros((128, 4), np.float32)
        zp[:L] = zeta.T
        c['c_rzeta%d' % L] = zp
        c['c_rgl%d' % L] = np.ascontiguousarray(np.broadcast_to(np.exp(lg * L)[None], (128, 4))).astype(np.float32)
    return c


def rope_tables(pos):
    half = 64
    inv = (10000.0 ** (-np.arange(half, dtype=np.float32) / half)).astype(np.float32)
    ang = pos.astype(np.float32)[:, None] * inv[None, :]
    cos = np.cos(ang).astype(np.float32)
    sin = np.sin(ang).astype(np.float32)
    cf = np.concatenate([cos.T, cos.T], axis=0)
    sf = np.concatenate([-sin.T, sin.T], axis=0)
    return np.ascontiguousarray(cf), np.ascontiguousarray(sf), cos, sin


class Cfg:
    def __init__(self, n_p=2, T_x=4096, n_s=2, past=2048, layers=2, debug=False, stage=99):
        self.stage = stage
        self.n_p, self.T_x, self.n_s, self.past, self.layers, self.debug = n_p, T_x, n_s, past, layers, debug
        assert T_x % NPIECE == 0


IN_SPECS = None


def in_specs(cfg):
    n_p, T_x, n_s = cfg.n_p, cfg.T_x, cfg.n_s
    T = 16 + T_x
    sp = {
        'xp': [n_p, T_x, D], 'xs': [n_s, 16, D], 'meta': [16, D],
        's5re_in': [n_s, 2048], 's5im_in': [n_s, 2048],
        'mlc_in': [n_s, 4, 256, 256], 'mln_in': [n_s, 4, 256], 'mlm_in': [n_s, 4], 'mlconv_in': [n_s, 3, D],
        'ret_in': [n_s, 4, 128, 256], 'lruh_in': [n_s, D], 'lruconv_in': [n_s, 3, D],
        'norm_w': [2, D], 'final_norm_w': [D],
        'ev_w_in': [D, 3072], 'ev_w_out': [1536, D],
        'lamre_A': [128, 4, 64], 'lamim_A': [128, 4, 64], 'logdt_A': [128, 4, 64],
        'lamre_S': [128, 16], 'lamim_S': [128, 16], 'logdt_S': [128, 16],
        'bre_A': [128, 4, 64], 'bim_A': [128, 4, 64], 'cre_S': [128, 16, 16], 'cim_S': [128, 16, 16],
        's5_d': [512], 's5_w_glu': [512, 512], 's5_b_glu': [512],
        'ml_conv_w': [4, D], 'ml_conv_b': [D], 'ml_wq': [4, 256, 256], 'ml_wk': [4, 256, 256], 'ml_wv': [4, 256, 256],
        'ml_wqT': [4, 256, 256], 'ml_wkT': [4, 256, 256], 'ml_wvT': [4, 256, 256],
        'ml_w_if': [3072, 8], 'ml_b_if': [8], 'ml_norm_w': [D], 'ml_skip': [D],
        'od_w_in': [D, 5120], 'od_w_qksw': [D, 1024], 'fnw_b': [128, D], 'od_w_out': [2048, D], 'ret_norm_w': [D],
        'lru_conv_w': [4, D], 'lru_conv_b': [D], 'lru_w_a': [8, 128, 128], 'lru_b_a': [D],
        'lru_w_x': [8, 128, 128], 'lru_b_x': [D], 'lru_lambda': [D],
        'ropeF_c': [128, T + 16], 'ropeF_s': [128, T + 16], 'ropeT_c': [T + 16, 64], 'ropeT_s': [T + 16, 64],
    }
    for k, v in host_consts().items():
        sp[k] = list(v.shape)
    return sp


def out_specs(cfg):
    n_p, T_x, n_s = cfg.n_p, cfg.T_x, cfg.n_s
    o = {'yp': [n_p, T_x, D], 'ys': [n_s, 16, D]}
    for g, n in (('p', n_p), ('s', n_s)):
        o['o_s5re_' + g] = [n, 2048]
        o['o_s5im_' + g] = [n, 2048]
        o['o_mlc_' + g] = [n, 4, 256, 256]
        o['o_mln_' + g] = [n, 4, 256]
        o['o_mlm_' + g] = [n, 4]
        o['o_mlconv_' + g] = [n, 3, D]
        o['o_ret_' + g] = [n, 4, 128, 256]
        o['o_lruh_' + g] = [n, D]
        o['o_lruconv_' + g] = [n, 3, D]
    return o


def build(cfg):
    nc = bass.Bass("TRN2", target_bir_lowering=False)
    n_p, T_x, n_s = cfg.n_p, cfg.T_x, cfg.n_s
    T = 16 + T_x
    I = {k: nc.dram_tensor(k, v, F32, kind="ExternalInput").ap() for k, v in in_specs(cfg).items()}
    O = {k: nc.dram_tensor(k, v, F32, kind="ExternalOutput").ap() for k, v in out_specs(cfg).items()}
    hk = "ExternalOutput" if cfg.debug else "Internal"
    h1p = nc.dram_tensor("h1p", [n_p, T, D], F32, kind=hk).ap()
    h1s = nc.dram_tensor("h1s", [n_s, 16, D], F32, kind=hk).ap()
    dbg_o = nc.dram_tensor("dbg_o", [4, 128, 256], F32, kind=hk).ap()

    seqs = []
    for i in range(n_p):
        pcs = [(0, 16, I['meta'][:, :], O['yp'], None)]
        for t in range(0, T_x, NPIECE):
            pcs.append((16 + t, NPIECE, I['xp'][i, t:t + NPIECE, :], O['yp'][i, t:t + NPIECE, :], None))
        seqs.append(('p', i, pcs))
    for i in range(n_s):
        seqs.append(('s', i, [(0, 16, I['xs'][i, :, :], O['ys'][i, :, :], None)]))

    def h1rows(g, i, t0, n):
        return (h1p if g == 'p' else h1s)[i, t0:t0 + n, :]

    with ExitStack() as top:
        top.enter_context(nc.allow_non_contiguous_dma(reason='small strided parameter/state layouts'))
        P = Prog(nc)
        for s in P.semnames:
            P.sems[s] = top.enter_context(nc.semaphore(s))
        pb = [top.enter_context(nc.psum_tensor("pb%d" % i, [128, 512], F32)) for i in range(8)]

        V, G, A, PL = nc.vector, nc.gpsimd, nc.scalar, nc.tensor

        def mm(out, lhsT, rhs, start, stop, reads, writes):
            P.op('pe', lambda: PL.matmul(out, lhsT, rhs, start=start, stop=stop), reads=reads, writes=writes)

        sb0 = lambda n, s, d: top.enter_context(nc.sbuf_tensor(n, s, d))
        identb = sb0("identb", [128, 128], BF16)
        identf = sb0("identf", [128, 128], F32)
        negh = sb0("negh", [128, 1], F32)
        P.dma('pool', identb[:], I['c_identb'][:, :], writes=['identb'])
        P.dma('sp', identf[:], I['c_identf'][:, :], writes=['identf'])
        P.op('pool', lambda: G.memset(negh[:], -0.5), writes=['negh'])

        def rmsnorm_T(hin, tp, tt, gcol, ub, uT, ss, hkey='hin', gkey='gcol'):
            P.op('pool', lambda: G.memset(ss[0:tp, 0:1], 0.0), writes=['ss'])
            P.op('act', lambda: A.activation(out=ub[0:tp, :], in_=hin[0:tp, :], func=AF.Square, accum_out=ss[0:tp, 0:1]),
                 reads=[hkey], writes=['ub', 'ss'])
            P.op('dve', lambda: V.tensor_scalar(ss[0:tp, 1:2], ss[0:tp, 0:1], 1.0 / D, EPS, ALU.mult, ALU.add),
                 reads=['ss'], writes=['ss'])
            P.op('pool', lambda: G.tensor_tensor(ss[0:tp, 2:3], ss[0:tp, 1:2], negh[0:tp, :], ALU.pow),
                 reads=['ss', 'negh'], writes=['ss'])
            P.op('act', lambda: A.activation(out=ub[0:tp, :], in_=hin[0:tp, :], func=AF.Copy, scale=ss[0:tp, 2:3]),
                 reads=[hkey, 'ss'], writes=['ub'])
            pT = pb[2][:].bitcast(BF16)
            for k in range(8):
                P.op('pe', lambda k=k: PL.transpose(pT[:, k * 128:k * 128 + tp], ub[0:tp, k * 128:(k + 1) * 128],
                                                    identb[0:tp, 0:tp]),
                     reads=['ub', 'identb'], writes=['pb2'])
            for k in range(8):
                P.op('act', lambda k=k: A.activation(out=uT[:, k, tt * 128:tt * 128 + tp], in_=pT[:, k * 128:k * 128 + tp],
                                                     func=AF.Copy, scale=gcol[:, k:k + 1]),
                     reads=['pb2', gkey], writes=['uT'])

        with ExitStack() as l0:
            sb = lambda n, s, d: l0.enter_context(nc.sbuf_tensor(n, s, d))
            NP = NPIECE
            w_in = sb("w_in0", [128, 8, 3072], BF16)
            w_out = sb("w_out0", [128, 12, D], BF16)
            w_glu = sb("w_glu", [128, 4, 512], BF16)
            wq = sb("wq", [128, 4, 2, 256], BF16)
            wk = sb("wk", [128, 4, 2, 256], BF16)
            wv = sb("wv", [128, 4, 2, 256], BF16)
            wcif = sb("wcif", [128, 8, 8], BF16)
            wxif = sb("wxif", [128, 8, 8], BF16)
            s5B = sb("s5B", [128, 2, 16, 128], BF16)
            s5C = sb("s5C", [128, 2, 16, 128], BF16)
            LR = 64
            tabc = sb("tabc", [128, 16, LR], F32)
            tabs = sb("tabs", [128, 16, LR], F32)
            rtab = sb("rtab", [128, 16, 2], F32)
            rtab0 = sb("rtab0", [128, 16, LR], F32)
            gcol = sb("gcol0", [128, 8], F32)
            dcol = sb("dcol", [128, 4], F32)
            bgluh = sb("bgluh", [128, 4], F32)
            convw = sb("convw", [128, 8, 4], F32)
            convb = sb("convb", [128, 8], F32)
            mlnw = sb("mlnw", [128, 8], F32)
            skipc = sb("skipc", [128, 8], F32)
            b_i = sb("b_i", [4, 1], F32)
            b_f = sb("b_f", [4, 1], F32)
            negmask = sb("negmask", [128, 128], F32)
            sel = sb("sel", [4, 4, 128], F32)
            ones4 = sb("ones4", [4, 128], F32)
            onesrow = sb("onesrow", [4, NP], F32)

            for k in range(8):
                P.dma('pool', w_in[:, k, :], I['ev_w_in'][k * 128:(k + 1) * 128, :], writes=['w_in'])
            for k in range(12):
                P.dma('pool', w_out[:, k, :], I['ev_w_out'][k * 128:(k + 1) * 128, :], writes=['w_out'])
            P.dma('pool', w_glu[:], I['s5_w_glu'].rearrange("(k p) f -> p k f", p=128), writes=['w_glu'])
            for nm, t_ in (('ml_wq', wq), ('ml_wk', wk), ('ml_wv', wv)):
                for h in range(4):
                    P.dma('pool', t_[:, h, :, :], I[nm][h].rearrange("(k p) e -> p k e", p=128), writes=[nm])
            P.dma('sp', gcol[:], I['norm_w'][0, :].rearrange("(k p) -> p k", p=128), writes=['gcol'])
            P.dma('sp', dcol[:], I['s5_d'].rearrange("(k p) -> p k", p=128), writes=['dcol'])
            P.dma('sp', bgluh[:], I['s5_b_glu'].rearrange("(k p) -> p k", p=128), writes=['bgluh'])
            for tap in range(4):
                P.dma('sp', convw[:, :, tap], I['ml_conv_w'][tap, :].rearrange("(k p) -> p k", p=128), writes=['convw'])
            P.dma('sp', convb[:], I['ml_conv_b'].rearrange("(k p) -> p k", p=128), writes=['convb'])
            P.dma('sp', mlnw[:], I['ml_norm_w'].rearrange("(k p) -> p k", p=128), writes=['mlnw'])
            P.dma('sp', skipc[:], I['ml_skip'].rearrange("(k p) -> p k", p=128), writes=['skipc'])
            P.dma('sp', b_i[:], I['ml_b_if'][0:4].rearrange("(p o) -> p o", o=1), writes=['b_i'])
            P.dma('sp', b_f[:], I['ml_b_if'][4:8].rearrange("(p o) -> p o", o=1), writes=['b_f'])
            P.dma('sp', negmask[:], I['c_negmask'][:, :], writes=['negmask'])
            P.dma('sp', sel[:], I['c_sel'][:, :, :], writes=['sel'])
            P.dma('sp', ones4[:], I['c_ones4'][:, :], writes=['ones4'])
            P.op('pool', lambda: G.memset(onesrow[:], 1.0), writes=['onesrow'])
            P.op('dve', lambda: V.tensor_scalar(bgluh[:], bgluh[:], 0.5, None, ALU.mult), reads=['bgluh'], writes=['bgluh'])
            P.op('dve', lambda: V.tensor_scalar(b_f[:], b_f[:], -1.0, None, ALU.mult), reads=['b_f'], writes=['b_f'])

            print('SBUF remaining before setup', nc.sbuf_bytes_remaining)
            with ExitStack() as su:
                sbs = lambda n, s, d: su.enter_context(nc.sbuf_tensor(n, s, d))
                wqT = sbs("wqT", [128, 4, 2, 256], BF16)
                wkT = sbs("wkT", [128, 4, 2, 256], BF16)
                wvT = sbs("wvT", [128, 4, 2, 256], BF16)
                wif = sbs("wif", [128, 24, 8], BF16)
                for nm, t_ in (('ml_wqT', wqT), ('ml_wkT', wkT), ('ml_wvT', wvT)):
                    for h in range(4):
                        P.dma('pool', t_[:, h, :, :], I[nm][h].rearrange("(k p) e -> p k e", p=128), writes=[nm])
                P.dma('pool', wif[:], I['ml_w_if'].rearrange("(k p) g -> p k g", p=128), writes=['wif'])
                for h in range(4):
                    for dt_ in range(2):
                        o_ = pb[0][:, 0:8]
                        i_ = 0
                        for (wT, nm, off) in ((wqT, 'ml_wqT', 0), (wkT, 'ml_wkT', 8)):
                            for ke in range(2):
                                mm(o_, wT[:, h, ke, dt_ * 128:(dt_ + 1) * 128], wif[:, off + h * 2 + ke, :],
                                   i_ == 0, i_ == 3, [nm, 'wif'], ['pb0'])
                                i_ += 1
                        P.op('act', lambda h=h, dt_=dt_: A.copy(wcif[:, h * 2 + dt_, :], pb[0][:, 0:8]),
                             reads=['pb0'], writes=['wcif'])
                        o2 = pb[1][:, 0:8]
                        for ke in range(2):
                            mm(o2, wvT[:, h, ke, dt_ * 128:(dt_ + 1) * 128], wif[:, 16 + h * 2 + ke, :],
                               ke == 0, ke == 1, ['ml_wvT', 'wif'], ['pb1'])
                        P.op('act', lambda h=h, dt_=dt_: A.copy(wxif[:, h * 2 + dt_, :], pb[1][:, 0:8]),
                             reads=['pb1'], writes=['wxif'])

                def S(n, shp):
                    return sbs(n, shp, F32)

                def trig(name, ang_ap, shp, key):
                    cosT = S(name + "_cos", shp)
                    sinT = S(name + "_sin", shp)
                    sh = S(name + "_sh", shp)
                    acc = S(name + "_acc", shp)
                    tmp = S(name + "_tm", shp)
                    for (dst, shift) in ((cosT, PI / 2), (sinT, 0.0)):
                        P.op('dve', lambda shift=shift: V.tensor_scalar(sh[:], ang_ap, shift, None, ALU.add),
                             reads=[key], writes=[name + 'sh'])
                        P.op('dve', lambda: V.tensor_copy(acc[:], sh[:]), reads=[name + 'sh'], writes=[name + 'win_re'])
                        for m_ in range(1, 7):
                            thr = (2 * m_ - 1) * PI
                            P.op('dve', lambda thr=thr: V.tensor_scalar(tmp[:], sh[:], thr, -2 * PI, ALU.is_ge, ALU.mult),
                                 reads=[name + 'sh'], writes=[name + 'tm'])
                            P.op('dve', lambda: V.tensor_tensor(acc[:], acc[:], tmp[:], ALU.add),
                                 reads=[name + 'win_re', name + 'tm'], writes=[name + 'win_re'])
                        P.op('act', lambda dst=dst: A.activation(out=dst[:], in_=acc[:], func=AF.Sin),
                             reads=[name + 'win_re'], writes=[name + ('c' if dst is cosT else 's')])
                    return cosT, sinT

                shA = [128, 4, 64]
                lamreA = S("lamreA", shA); lamimA = S("lamimA", shA); dtA = S("dtA", shA)
                breA = S("breA", shA); bimA = S("bimA", shA)
                P.dma('sp', lamreA[:], I['lamre_A'][:, :, :], writes=['lamreA'])
                P.dma('sp', lamimA[:], I['lamim_A'][:, :, :], writes=['lamimA'])
                P.dma('sp', dtA[:], I['logdt_A'][:, :, :], writes=['dtA'])
                P.dma('sp', breA[:], I['bre_A'][:, :, :], writes=['breA'])
                P.dma('sp', bimA[:], I['bim_A'][:, :, :], writes=['bimA'])
                P.op('act', lambda: A.activation(out=dtA[:], in_=dtA[:], func=AF.Exp), reads=['dtA'], writes=['dtA'])
                magA = S("magA", shA); angA = S("angA", shA)
                P.op('dve', lambda: V.tensor_tensor(magA[:], lamreA[:], dtA[:], ALU.mult), reads=['lamreA', 'dtA'], writes=['magA'])
                P.op('act', lambda: A.activation(out=magA[:], in_=magA[:], func=AF.Exp), reads=['magA'], writes=['magA'])
                P.op('dve', lambda: V.tensor_tensor(angA[:], lamimA[:], dtA[:], ALU.mult), reads=['lamimA', 'dtA'], writes=['angA'])
                cosA, sinA = trig("tA", angA[:], shA, 'angA')
                abre = S("abre", shA); abim = S("abim", shA)
                P.op('dve', lambda: V.tensor_tensor(abre[:], magA[:], cosA[:], ALU.mult), reads=['magA', 'tAc'], writes=['abre'])
                P.op('dve', lambda: V.tensor_tensor(abim[:], magA[:], sinA[:], ALU.mult), reads=['magA', 'tAs'], writes=['abim'])
                den = S("denA", shA); t1 = S("t1A", shA); t2 = S("t2A", shA); kre = S("kreA", shA); kim = S("kimA", shA)
                P.op('dve', lambda: V.tensor_tensor(den[:], lamreA[:], lamreA[:], ALU.mult), reads=['lamreA'], writes=['denA'])
                P.op('dve', lambda: V.tensor_tensor(t1[:], lamimA[:], lamimA[:], ALU.mult), reads=['lamimA'], writes=['t1A'])
                P.op('dve', lambda: V.tensor_tensor(den[:], den[:], t1[:], ALU.add), reads=['denA', 't1A'], writes=['denA'])
                P.op('dve', lambda: V.reciprocal(den[:], den[:]), reads=['denA'], writes=['denA'])
                P.op('dve', lambda: V.tensor_scalar(abre[:], abre[:], -1.0, None, ALU.add), reads=['abre'], writes=['abre'])
                P.op('dve', lambda: V.tensor_tensor(t1[:], abre[:], lamreA[:], ALU.mult), reads=['abre', 'lamreA'], writes=['t1A'])
                P.op('dve', lambda: V.tensor_tensor(t2[:], abim[:], lamimA[:], ALU.mult), reads=['abim', 'lamimA'], writes=['t2A'])
                P.op('dve', lambda: V.tensor_tensor(t1[:], t1[:], t2[:], ALU.add), reads=['t1A', 't2A'], writes=['t1A'])
                P.op('dve', lambda: V.tensor_tensor(kre[:], t1[:], den[:], ALU.mult), reads=['t1A', 'denA'], writes=['kreA'])
                P.op('dve', lambda: V.tensor_tensor(t1[:], abim[:], lamreA[:], ALU.mult), reads=['abim', 'lamreA'], writes=['t1A'])
                P.op('dve', lambda: V.tensor_tensor(t2[:], abre[:], lamimA[:], ALU.mult), reads=['abre', 'lamimA'], writes=['t2A'])
                P.op('dve', lambda: V.tensor_tensor(t1[:], t1[:], t2[:], ALU.subtract), reads=['t1A', 't2A'], writes=['t1A'])
                P.op('dve', lambda: V.tensor_tensor(kim[:], t1[:], den[:], ALU.mult), reads=['t1A', 'denA'], writes=['kimA'])
                bbre = S("bbre", shA); bbim = S("bbim", shA)
                P.op('dve', lambda: V.tensor_tensor(t1[:], kre[:], breA[:], ALU.mult), reads=['kreA', 'breA'], writes=['t1A'])
                P.op('dve', lambda: V.tensor_tensor(t2[:], kim[:], bimA[:], ALU.mult), reads=['kimA', 'bimA'], writes=['t2A'])
                P.op('dve', lambda: V.tensor_tensor(bbre[:], t1[:], t2[:], ALU.subtract), reads=['t1A', 't2A'], writes=['bbre'])
                P.op('dve', lambda: V.tensor_tensor(t1[:], kre[:], bimA[:], ALU.mult), reads=['kreA', 'bimA'], writes=['t1A'])
                P.op('dve', lambda: V.tensor_tensor(t2[:], kim[:], breA[:], ALU.mult), reads=['kimA', 'breA'], writes=['t2A'])
                P.op('dve', lambda: V.tensor_tensor(bbim[:], t1[:], t2[:], ALU.add), reads=['t1A', 't2A'], writes=['bbim'])
                maskA = S("maskA", [128, 8])
                P.dma('sp', maskA[:], I['c_maskA'][:, :], writes=['maskA'])
                for j in range(16):
                    kc, jl = j // 4, j % 4
                    for h in range(2):
                        g8 = 2 * jl + h
                        for part, src, key in ((0, bbre, 'bbre'), (1, bbim, 'bbim')):
                            P.op('dve', lambda j=j, kc=kc, h=h, g8=g8, part=part, src=src: V.tensor_scalar(
                                s5B[:, part, j, h * 64:(h + 1) * 64], src[:, kc, :], maskA[:, g8:g8 + 1], None, ALU.mult),
                                reads=[key, 'maskA'], writes=['s5B'])
                creS = S("creS", [128, 16, 16]); cimS = S("cimS", [128, 16, 16])
                P.dma('sp', creS[:], I['cre_S'][:, :, :], writes=['creS'])
                P.dma('sp', cimS[:], I['cim_S'][:, :, :], writes=['cimS'])
                P.op('pool', lambda: G.memset(s5C[:], 0.0), writes=['s5C'])
                for j in range(16):
                    jl = j % 4
                    for h in range(2):
                        g8 = 2 * jl + h
                        P.op('dve', lambda j=j, h=h, g8=g8: V.tensor_copy(
                            s5C[h * 64:(h + 1) * 64, 0, j, g8 * 16:(g8 + 1) * 16], creS[h * 64:(h + 1) * 64, j, :]),
                            reads=['creS'], writes=['s5C'])
                        P.op('dve', lambda j=j, h=h, g8=g8: V.tensor_scalar(
                            s5C[h * 64:(h + 1) * 64, 1, j, g8 * 16:(g8 + 1) * 16], cimS[h * 64:(h + 1) * 64, j, :],
                            -1.0, None, ALU.mult), reads=['cimS'], writes=['s5C'])
                shS = [128, 16]
                lamreS = S("lamreS", shS); lamimS = S("lamimS", shS); dtS = S("dtS", shS)
                P.dma('sp', lamreS[:], I['lamre_S'][:, :], writes=['lamreS'])
                P.dma('sp', lamimS[:], I['lamim_S'][:, :], writes=['lamimS'])
                P.dma('sp', dtS[:], I['logdt_S'][:, :], writes=['dtS'])
                P.op('act', lambda: A.activation(out=dtS[:], in_=dtS[:], func=AF.Exp), reads=['dtS'], writes=['dtS'])
                rmag = S("rmag", shS); angS = S("angS", shS)
                P.op('dve', lambda: V.tensor_tensor(rmag[:], lamreS[:], dtS[:], ALU.mult), reads=['lamreS', 'dtS'], writes=['rmag'])
                P.op('act', lambda: A.activation(out=rmag[:], in_=rmag[:], func=AF.Exp), reads=['rmag'], writes=['rmag'])
                P.op('dve', lambda: V.tensor_tensor(angS[:], lamimS[:], dtS[:], ALU.mult), reads=['lamimS', 'dtS'], writes=['angS'])
                cosS, sinS = trig("tS", angS[:], shS, 'angS')
                onesL = S("onesL", [128, LR])
                P.op('pool', lambda: G.memset(onesL[:], 1.0), writes=['onesL'])
                for j in range(16):
                    P.op('dve', lambda j=j: V.tensor_scalar(rtab[:, j, :], onesL[:, 0:2], rmag[:, j:j + 1], None, ALU.mult),
                         reads=['rmag', 'onesL'], writes=['rtab'])
                for j in range(16):
                    P.op('dve', lambda j=j: V.tensor_scalar(rtab0[:, j, :], onesL[:], rmag[:, j:j + 1], None, ALU.mult),
                         reads=['rmag', 'onesL'], writes=['rtab0'])
                P.op('pool', lambda: G.memset(rtab0[:, :, 0:1], 0.0), reads=['rtab0'], writes=['rtab0'])
                P.op('dve', lambda: V.tensor_copy(tabc[:, :, 0:1], cosS[:].unsqueeze(2)), reads=['tSc'], writes=['tabc'])
                P.op('dve', lambda: V.tensor_copy(tabs[:, :, 0:1], sinS[:].unsqueeze(2)), reads=['tSs'], writes=['tabs'])
                d1 = S("dbl1", [128, LR // 2])
                m_ = 1
                while m_ < LR:
                    for j in range(16):
                        cm, sm = tabc[:, j, m_ - 1:m_], tabs[:, j, m_ - 1:m_]
                        lo_c, lo_s = tabc[:, j, 0:m_], tabs[:, j, 0:m_]
                        hi_c, hi_s = tabc[:, j, m_:2 * m_], tabs[:, j, m_:2 * m_]
                        a1 = d1[:, 0:m_]
                        P.op('dve', lambda: V.tensor_scalar(a1, lo_s, sm, None, ALU.mult), reads=['tabs'], writes=['dbl1'])
                        P.op('dve', lambda: V.scalar_tensor_tensor(hi_c, lo_c, cm, a1, ALU.mult, ALU.subtract),
                             reads=['tabc', 'dbl1'], writes=['tabc'])
                        P.op('dve', lambda: V.tensor_scalar(a1, lo_c, sm, None, ALU.mult), reads=['tabc', 'tabs'], writes=['dbl1'])
                        P.op('dve', lambda: V.scalar_tensor_tensor(hi_s, lo_s, cm, a1, ALU.mult, ALU.add),
                             reads=['tabs', 'tabc', 'dbl1'], writes=['tabs'])
                    m_ *= 2
                P.barrier()
            print('SBUF remaining after weights/setup', nc.sbuf_bytes_remaining)
            L_ = {}
            hin = sb("hin", [128, D], F32)
            ss = sb("ss", [128, 4], F32)
            ub = sb("ub", [128, D], BF16)
            uT = sb("uT", [128, 8, NP], BF16)
            gated = sb("gated", [128, 12, NP], BF16)
            ua = sb("ua", [128, 4, NP], BF16)
            sza = sb("sza", [128, 4, NP], BF16)
            tA = sb("tA_", [128, NP], F32)
            tB = sb("tB_", [128, NP], F32)
            wre = sb("wre", [128, NP // LR, 4, LR], F32)
            wim = sb("wim", [128, NP // LR, 4, LR], F32)
            tA2 = sb("tA2_", [128, NP], F32)
            tB2 = sb("tB2_", [128, NP], F32)
            hsre = sb("hsre", [128, 4, NP], BF16)
            hsim = sb("hsim", [128, 4, NP], BF16)
            Hre = sb("Hre", [128, 16], F32)
            Him = sb("Him", [128, 16], F32)
            ch1 = sb("ch1", [128, 4], F32)
            ch2 = sb("ch2", [128, 4], F32)
            ch3 = sb("ch3", [128, 4], F32)
            ch4 = sb("ch4", [128, 4], F32)
            y1 = sb("y1", [128, NP], F32)
            y2 = sb("y2", [128, NP + 1], F32)
            y3 = sb("y3", [128, NP + 1], F32)
            yg = sb("yg", [128, 4, NP], BF16)
            xb = sb("xb", [128, 8, 3 + NP], BF16)
            szb = sb("szb", [128, 8, NP], BF16)
            acc = sb("acc", [128, NP], F32)
            xc = sb("xc", [128, 8, NP], BF16)
            qT = uT
            kT = sb("kT", [128, 8, NP], BF16)
            vtok = sb("vtok", [128, 4, 264], BF16)
            ktok = sb("ktok", [128, 4, 256], BF16)
            Cst = sb("Cst", [128, 4, 2, 264], F32)
            Cbf = sb("Cbf", [128, 4, 2, 264], BF16)
            r_ig = sb("r_ig", [4, NP], F32)
            r_lf = sb("r_lf", [4, NP], F32)
            Mfull = sb("Mfull", [4, NP + 1], F32)
            negM = sb("negM", [4, NP + 1], F32)
            r_wi = sb("r_wi", [4, NP], F32)
            r_g = sb("r_g", [4, NP], F32)
            r_emm = sb("r_emm", [4, NP], F32)
            r_wend = sb("r_wend", [4, NP], F32)
            r_e = r_wend
            mcarry = sb("mcarry", [4, 1], F32)
            dg4 = sb("dg4", [4, 4], F32)
            cols = sb("cols", [128, 20], F32)
            Esb = y1[:, 0:128]
            Dm = y1[:, 128:256]
            PT = sb("PT", [128, 128], BF16)
            kw = sb("kw", [128, 256], BF16)
            inter = y2
            tot = y3
            st6 = sb("st6", [128, 6], F32)
            st2 = sb("st2", [128, 8], F32)
            hn = ub
            ytmp = sb("ytmp", [128, 128], F32)
            xbo = sb("xbo", [128, 8, 3], F32)

            print('SBUF remaining after L0 activations', nc.sbuf_bytes_remaining)
            P.op('pool', lambda: G.memset(vtok[:], 1.0), writes=['vtok'])

            mmreg = [(pb[0], 'pb0'), (pb[1], 'pb1')]
            mmi = [0]

            def nextmm():
                r_ = mmreg[mmi[0] % 2]
                mmi[0] += 1
                return r_

            def inproj(f, n):
                ps, key = nextmm()
                for k in range(8):
                    mm(ps[:, 0:n], w_in[:, k, f * 128:(f + 1) * 128], uT[:, k, 0:n], k == 0, k == 7, ['w_in', 'uT'], [key])
                return ps, key

            for (g, si, pcs) in seqs:
                if g == 'p':
                    P.op('pool', lambda: G.memset(Hre[:], 0.0), writes=['Hre'])
                    P.op('pool', lambda: G.memset(Him[:], 0.0), writes=['Him'])
                    P.op('pool', lambda: G.memset(Cst[:], 0.0), writes=['Cst%d' % h_ for h_ in range(4)])
                    P.op('pool', lambda: G.memset(mcarry[:], 0.0), writes=['mcarry'])
                    P.op('pool', lambda: G.memset(xb[:, :, 0:3], 0.0), writes=['xb'])
                else:
                    P.dma('sp', Hre[:], I['s5re_in'][si, :].rearrange("(j q) -> q j", q=128), writes=['Hre'])
                    P.dma('sp', Him[:], I['s5im_in'][si, :].rearrange("(j q) -> q j", q=128), writes=['Him'])
                    for h in range(4):
                        P.dma('sp', Cst[:, h, :, 0:256], I['mlc_in'][si, h].rearrange("(k p) e -> p k e", p=128), writes=['Cst%d' % h])
                        P.dma('sp', Cst[:, h, :, 256], I['mln_in'][si, h].rearrange("(k p) -> p k", p=128), writes=['Cst%d' % h])
                    P.dma('sp', mcarry[:], I['mlm_in'][si, :].rearrange("(p o) -> p o", o=1), writes=['mcarry'])
                    for t_ in range(3):
                        P.dma('pool', xb[:, :, t_], I['mlconv_in'][si, t_].rearrange("(k p) -> p k", p=128), writes=['xb'])
                P.op('act', lambda: A.copy(Cbf[:], Cst[:]), reads=['Cst%d' % h_ for h_ in range(4)], writes=['Cbf%d' % h_ for h_ in range(4)])

                for (t0, n, src, _, _) in pcs:
                    tp = min(n, 128)
                    ntt = (n + 127) // 128
                    L = tp
                    nch = n // L
                    for tt in range(ntt):
                        P.dma('sp', hin[0:tp, :], src[tt * 128:tt * 128 + tp, :], writes=['hin'])
                        rmsnorm_T(hin, tp, tt, gcol, ub, uT, ss)
                    for f in range(4):
                        ps, key = inproj(f, n)
                        P.op('act', lambda f=f, ps=ps: A.copy(ua[:, f, 0:n], ps[:, 0:n]), reads=[key], writes=['ua'])
                    for f in range(4):
                        ps, key = inproj(4 + f, n)
                        P.op('act', lambda f=f, ps=ps: A.activation(out=sza[:, f, 0:n], in_=ps[:, 0:n], func=AF.Silu),
                             reads=[key], writes=['sza'])

                    def gen_s5():
                        Ls = min(n, LR)
                        ncs = n // Ls
                        def v3(ap2):
                            return ap2.rearrange("p (c l) -> p c l", c=ncs)

                        for kc in range(4):
                            tcs, tss = [], []
                            for jl in range(4):
                                j = 4 * kc + jl
                                sreg = (pb[4], 'pb4') if j % 2 == 0 else (pb[5], 'pb5')
                                pre, pim = sreg[0][:, 0:n], sreg[0][:, 256:256 + n]
                                mm(pre, s5B[:, 0, j, :], ua[:, kc, 0:n], True, True, ['s5B', 'ua'], [sreg[1]])
                                mm(pim, s5B[:, 1, j, :], ua[:, kc, 0:n], True, True, ['s5B', 'ua'], [sreg[1]])
                                tc_ = tabc[:, j, 0:Ls].unsqueeze(1).to_broadcast([128, ncs, Ls])
                                ts_ = tabs[:, j, 0:Ls].unsqueeze(1).to_broadcast([128, ncs, Ls])
                                tcs.append(tc_); tss.append(ts_)
                                P.op('dve', lambda pre=pre, tc_=tc_: V.tensor_tensor(v3(tA[:, 0:n]), v3(pre), tc_, ALU.mult),
                                     reads=[sreg[1], 'tabc'], writes=['tA'])
                                P.op('dve', lambda pim=pim, ts_=ts_: V.tensor_tensor(v3(tB[:, 0:n]), v3(pim), ts_, ALU.mult),
                                     reads=[sreg[1], 'tabs'], writes=['tB'])
                                P.op('pool', lambda jl=jl: G.tensor_tensor(wre[:, 0:ncs, jl, 0:Ls], v3(tA[:, 0:n]), v3(tB[:, 0:n]), ALU.add),
                                     reads=['tA', 'tB'], writes=['wre%d' % jl])
                                P.op('dve', lambda pim=pim, tc_=tc_: V.tensor_tensor(v3(tA2[:, 0:n]), v3(pim), tc_, ALU.mult),
                                     reads=[sreg[1], 'tabc'], writes=['tA2'])
                                P.op('dve', lambda pre=pre, ts_=ts_: V.tensor_tensor(v3(tB2[:, 0:n]), v3(pre), ts_, ALU.mult),
                                     reads=[sreg[1], 'tabs'], writes=['tB2'])
                                P.op('pool', lambda jl=jl: G.tensor_tensor(wim[:, 0:ncs, jl, 0:Ls], v3(tA2[:, 0:n]), v3(tB2[:, 0:n]), ALU.subtract),
                                     reads=['tA2', 'tB2'], writes=['wim%d' % jl])
                                if jl % 2 == 1:
                                    yield
                            j0 = 4 * kc
                            for c in range(ncs):
                                cs = slice(c * Ls, (c + 1) * Ls)
                                if c == 0:
                                    i4r, i4i, rk = Hre[:, j0:j0 + 4], Him[:, j0:j0 + 4], ['Hre', 'Him']
                                else:
                                    i4r, i4i, rk = ch1[:, 0:4], ch2[:, 0:4], ['ch1', 'ch2']
                                w4r_ = ['wre%d' % i_ for i_ in range(4)]
                                w4i_ = ['wim%d' % i_ for i_ in range(4)]
                                r4 = rtab[:, j0:j0 + 4, 1]
                                c0 = c * Ls
                                P.op('dve', lambda: V.tensor_tensor(ch3[:, 0:4], i4r, r4, ALU.mult), reads=rk + ['rtab'], writes=['ch3'])
                                P.op('dve', lambda: V.tensor_tensor(ch4[:, 0:4], i4i, r4, ALU.mult), reads=rk + ['rtab'], writes=['ch4'])
                                P.op('dve', lambda: V.tensor_tensor(wre[:, c, :, 0], wre[:, c, :, 0], ch3[:, 0:4], ALU.add), reads=w4r_ + ['ch3'], writes=w4r_)
                                P.op('dve', lambda: V.tensor_tensor(wim[:, c, :, 0], wim[:, c, :, 0], ch4[:, 0:4], ALU.add), reads=w4i_ + ['ch4'], writes=w4i_)
                                if Ls == LR:
                                    d0 = rtab0[:, j0:j0 + 4, :].rearrange("p j l -> p (j l)")
                                    P.op('dve', lambda: V.tensor_tensor_scan(wre[:, c, :, :].rearrange("p j l -> p (j l)"), d0,
                                                                             wre[:, c, :, :].rearrange("p j l -> p (j l)"), 0.0, ALU.mult, ALU.add),
                                         reads=w4r_ + ['rtab0'], writes=w4r_)
                                    P.op('dve', lambda: V.tensor_tensor_scan(wim[:, c, :, :].rearrange("p j l -> p (j l)"), d0,
                                                                             wim[:, c, :, :].rearrange("p j l -> p (j l)"), 0.0, ALU.mult, ALU.add),
                                         reads=w4i_ + ['rtab0'], writes=w4i_)
                                else:
                                    for jl in range(4):
                                        P.op('dve', lambda jl=jl: V.tensor_tensor_scan(wre[:, c, jl, 0:Ls], rtab0[:, j0 + jl, 0:Ls], wre[:, c, jl, 0:Ls], 0.0,
                                                                                       ALU.mult, ALU.add), reads=w4r_ + ['rtab0'], writes=w4r_)
                                        P.op('dve', lambda jl=jl: V.tensor_tensor_scan(wim[:, c, jl, 0:Ls], rtab0[:, j0 + jl, 0:Ls], wim[:, c, jl, 0:Ls], 0.0,
                                                                                       ALU.mult, ALU.add), reads=w4i_ + ['rtab0'], writes=w4i_)
                                e_ = (c + 1) * Ls - 1
                                last = (c == ncs - 1)
                                wlr, wli = wre[:, c, :, Ls - 1], wim[:, c, :, Ls - 1]
                                cL4, sL4 = tabc[:, j0:j0 + 4, Ls - 1], tabs[:, j0:j0 + 4, Ls - 1]
                                dre = Hre[:, j0:j0 + 4] if last else ch1[:, 0:4]
                                dim_ = Him[:, j0:j0 + 4] if last else ch2[:, 0:4]
                                kre_ = 'Hre' if last else 'ch1'
                                kim_ = 'Him' if last else 'ch2'
                                w4r = ['wre%d' % i_ for i_ in range(4)]
                                w4i = ['wim%d' % i_ for i_ in range(4)]
                                P.op('dve', lambda: V.tensor_tensor(ch3[:, 0:4], wli, sL4, ALU.mult), reads=w4i + ['tabs'], writes=['ch3'])
                                P.op('dve', lambda: V.tensor_tensor(ch4[:, 0:4], wlr, sL4, ALU.mult), reads=w4r + ['tabs'], writes=['ch4'])
                                P.op('dve', lambda: V.tensor_tensor(dre, wlr, cL4, ALU.mult), reads=w4r + ['tabc', kre_], writes=[kre_])
                                P.op('dve', lambda: V.tensor_tensor(dim_, wli, cL4, ALU.mult), reads=w4i + ['tabc', kim_], writes=[kim_])
                                P.op('dve', lambda: V.tensor_tensor(dre, dre, ch3[:, 0:4], ALU.subtract), reads=[kre_, 'ch3'], writes=[kre_])
                                P.op('dve', lambda: V.tensor_tensor(dim_, dim_, ch4[:, 0:4], ALU.add), reads=[kim_, 'ch4'], writes=[kim_])
                                if c % 2 == 1 or last:
                                    yield
                            for jl in range(4):
                                tc_, ts_ = tcs[jl], tss[jl]
                                P.op('dve', lambda jl=jl, tc_=tc_: V.tensor_tensor(v3(tA[:, 0:n]), wre[:, 0:ncs, jl, 0:Ls], tc_, ALU.mult),
                                     reads=['wre%d' % jl, 'tabc'], writes=['tA'])
                                P.op('dve', lambda jl=jl, ts_=ts_: V.tensor_tensor(v3(tB[:, 0:n]), wim[:, 0:ncs, jl, 0:Ls], ts_, ALU.mult),
                                     reads=['wim%d' % jl, 'tabs'], writes=['tB'])
                                P.op('pool', lambda jl=jl: G.tensor_tensor(hsre[:, jl, 0:n], tA[:, 0:n], tB[:, 0:n], ALU.subtract),
                                     reads=['tA', 'tB'], writes=['hsre%d' % jl])
                                P.op('dve', lambda jl=jl, tc_=tc_: V.tensor_tensor(v3(tA2[:, 0:n]), wim[:, 0:ncs, jl, 0:Ls], tc_, ALU.mult),
                                     reads=['wim%d' % jl, 'tabc'], writes=['tA2'])
                                P.op('dve', lambda jl=jl, ts_=ts_: V.tensor_tensor(v3(tB2[:, 0:n]), wre[:, 0:ncs, jl, 0:Ls], ts_, ALU.mult),
                                     reads=['wre%d' % jl, 'tabs'], writes=['tB2'])
                                P.op('pool', lambda jl=jl: G.tensor_tensor(hsim[:, jl, 0:n], tA2[:, 0:n], tB2[:, 0:n], ALU.add),
                                     reads=['tA2', 'tB2'], writes=['hsim%d' % jl])
                                if jl % 2 == 1:
                                    yield
                            ps, key = nextmm()
                            for jl in range(4):
                                j = 4 * kc + jl
                                mm(ps[:, 0:n], s5C[:, 0, j, :], hsre[:, jl, 0:n], jl == 0, False, ['s5C', 'hsre%d' % jl], [key])
                                mm(ps[:, 0:n], s5C[:, 1, j, :], hsim[:, jl, 0:n], False, jl == 3, ['s5C', 'hsim%d' % jl], [key])
                            P.op('dve', lambda kc=kc, ps=ps: V.scalar_tensor_tensor(y1[:, 0:n], ua[:, kc, 0:n], dcol[:, kc:kc + 1],
                                                                                   ps[:, 0:n], ALU.mult, ALU.add),
                                 reads=['ua', 'dcol', key], writes=['y1'])
                            P.op('pool', lambda: G.tensor_tensor(y2[:, 0:n], y1[:, 0:n], y1[:, 0:n], ALU.mult), reads=['y1'], writes=['y2'])
                            P.op('pool', lambda: G.tensor_scalar(y2[:, 0:n], y2[:, 0:n], 0.044715, 1.0, ALU.mult, ALU.add),
                                 reads=['y2'], writes=['y2'])
                            P.op('pool', lambda: G.tensor_tensor(y2[:, 0:n], y2[:, 0:n], y1[:, 0:n], ALU.mult), reads=['y2', 'y1'], writes=['y2'])
                            P.op('act', lambda: A.activation(out=y3[:, 0:n], in_=y2[:, 0:n], func=AF.Tanh, scale=math.sqrt(2.0 / PI)),
                                 reads=['y2'], writes=['y3'])
                            P.op('dve', lambda: V.tensor_scalar(y3[:, 0:n], y3[:, 0:n], 0.5, 0.5, ALU.mult, ALU.add), reads=['y3'], writes=['y3'])
                            P.op('dve', lambda kc=kc: V.tensor_tensor(yg[:, kc, 0:n], y3[:, 0:n], y1[:, 0:n], ALU.mult),
                                 reads=['y3', 'y1'], writes=['yg%d' % kc])
                            yield
                        for m_ in range(4):
                            ps, key = nextmm()
                            for kk in range(4):
                                mm(ps[:, 0:n], w_glu[:, kk, m_ * 128:(m_ + 1) * 128], yg[:, kk, 0:n], kk == 0, kk == 3,
                                   ['w_glu', 'yg%d' % kk], [key])
                            P.op('act', lambda m_=m_, ps=ps: A.activation(out=y3[:, 0:n], in_=ps[:, 0:n], func=AF.Tanh, scale=0.5,
                                                                          bias=bgluh[:, m_:m_ + 1]),
                                 reads=[key, 'bgluh'], writes=['y3'])
                            P.op('dve', lambda: V.tensor_scalar(y3[:, 0:n], y3[:, 0:n], 0.5, 0.5, ALU.mult, ALU.add), reads=['y3'], writes=['y3'])
                            P.op('dve', lambda m_=m_: V.tensor_tensor(y3[:, 0:n], y3[:, 0:n], yg[:, m_, 0:n], ALU.mult),
                                 reads=['y3', 'yg%d' % m_], writes=['y3'])
                            P.op('dve', lambda m_=m_: V.tensor_tensor(gated[:, m_, 0:n], y3[:, 0:n], sza[:, m_, 0:n], ALU.mult),
                                 reads=['y3', 'sza'], writes=['gated%d' % m_])
                            yield

                    def gen_ml():
                        for f in range(8):
                            ps, key = inproj(8 + f, n)
                            P.op('act', lambda f=f, ps=ps: A.copy(xb[:, f, 3:3 + n], ps[:, 0:n]), reads=[key], writes=['xb'])
                            if f % 2 == 1:
                                yield
                        for f in range(8):
                            ps, key = inproj(16 + f, n)
                            P.op('act', lambda f=f, ps=ps: A.activation(out=szb[:, f, 0:n], in_=ps[:, 0:n], func=AF.Silu),
                                 reads=[key], writes=['szb'])
                            if f % 2 == 1:
                                yield
                        for k in range(8):
                            P.op('dve', lambda k=k: V.tensor_scalar(acc[:, 0:n], xb[:, k, 0:n], convw[:, k, 0:1], convb[:, k:k + 1],
                                                                    ALU.mult, ALU.add),
                                 reads=['xb', 'convw', 'convb'], writes=['acc'])
                            for tap in range(1, 4):
                                P.op('dve', lambda k=k, tap=tap: V.scalar_tensor_tensor(
                                    acc[:, 0:n], xb[:, k, tap:tap + n], convw[:, k, tap:tap + 1], acc[:, 0:n], ALU.mult, ALU.add),
                                    reads=['xb', 'convw', 'acc'], writes=['acc'])
                            P.op('act', lambda k=k: A.activation(out=xc[:, k, 0:n], in_=acc[:, 0:n], func=AF.Silu),
                                 reads=['acc'], writes=['xc'])
                            yield
                        gip, gikey = nextmm()
                        gfp, gfkey = nextmm()
                        gi, gf = gip[0:4, 0:n], gfp[0:4, 0:n]
                        for gsel, o_, gkey_ in ((0, gi, gikey), (4, gf, gfkey)):
                            for k in range(8):
                                mm(o_, wcif[:, k, gsel:gsel + 4], xc[:, k, 0:n], k == 0, False, ['wcif', 'xc'], [gkey_])
                            for k in range(8):
                                mm(o_, wxif[:, k, gsel:gsel + 4], xb[:, k, 3:3 + n], False, k == 7, ['wxif', 'xb'], [gkey_])
                        P.op('act', lambda: A.activation(out=r_ig[:, 0:n], in_=gi, func=AF.Identity, bias=b_i[:, 0:1]),
                             reads=[gikey, 'b_i'], writes=['r_ig'])
                        P.op('act', lambda: A.activation(out=r_e[:, 0:n], in_=gf, func=AF.Exp, scale=-1.0, bias=b_f[:, 0:1]),
                             reads=[gfkey, 'b_f'], writes=['r_wend'])
                        P.op('act', lambda: A.activation(out=r_e[:, 0:n], in_=r_e[:, 0:n], func=AF.Ln, bias=1.0),
                             reads=['r_wend'], writes=['r_wend'])
                        P.op('dve', lambda: V.tensor_tensor_scan(r_lf[:, 0:n], onesrow[:, 0:n], r_e[:, 0:n], 0.0, ALU.mult, ALU.add),
                             reads=['r_wend', 'onesrow'], writes=['r_lf'])
                        P.op('dve', lambda: V.tensor_tensor(r_ig[:, 0:n], r_ig[:, 0:n], r_lf[:, 0:n], ALU.add),
                             reads=['r_ig', 'r_lf'], writes=['r_ig'])
                        P.op('dve', lambda: V.tensor_copy(Mfull[:, 0:1], mcarry[:, 0:1]), reads=['mcarry'], writes=['Mfull'])
                        P.op('dve', lambda: V.tensor_tensor_scan(Mfull[:, 1:n + 1], onesrow[:, 0:n], r_ig[:, 0:n], mcarry[:, 0:1],
                                                                 ALU.mult, ALU.max),
                             reads=['r_ig', 'onesrow', 'mcarry', 'Mfull'], writes=['Mfull'])
                        P.op('dve', lambda: V.tensor_scalar(negM[:, 0:n + 1], Mfull[:, 0:n + 1], -1.0, None, ALU.mult),
                             reads=['Mfull'], writes=['negM'])
                        P.op('dve', lambda: V.tensor_tensor(r_emm[:, 0:n], r_lf[:, 0:n], Mfull[:, 1:n + 1], ALU.subtract),
                             reads=['r_lf', 'Mfull'], writes=['r_emm'])
                        P.op('act', lambda: A.activation(out=r_emm[:, 0:n], in_=r_emm[:, 0:n], func=AF.Exp), reads=['r_emm'], writes=['r_emm'])
                        P.op('dve', lambda: V.tensor_tensor(mcarry[:, 0:1], Mfull[:, n:n + 1], r_lf[:, n - 1:n], ALU.subtract),
                             reads=['Mfull', 'r_lf'], writes=['mcarry'])
                        for c in range(nch):
                            cs = slice(c * L, (c + 1) * L)
                            P.op('act', lambda c=c, cs=cs: A.activation(out=r_wi[:, cs], in_=Mfull[:, 1 + c * L:1 + (c + 1) * L], func=AF.Exp,
                                                                        scale=-1.0, bias=Mfull[:, c * L:c * L + 1]),
                                 reads=['Mfull'], writes=['r_wi'])
                            P.op('act', lambda c=c, cs=cs: A.activation(out=r_g[:, cs], in_=Mfull[:, 1 + c * L:1 + (c + 1) * L], func=AF.Exp,
                                                                        scale=-1.0, bias=Mfull[:, (c + 1) * L:(c + 1) * L + 1]),
                                 reads=['Mfull'], writes=['r_g'])
                            P.op('act', lambda c=c, cs=cs: A.activation(out=r_wend[:, cs], in_=r_ig[:, cs], func=AF.Exp,
                                                                        bias=negM[:, (c + 1) * L:(c + 1) * L + 1]),
                                 reads=['r_ig', 'negM'], writes=['r_wend'])
                        yield
                        for h in range(4):
                            for e in range(2):
                                for (w_, dst, sc, wkey) in ((wq, qT, 1.0, 'ml_wq'), (wk, kT, 1.0 / 16.0, 'ml_wk')):
                                    ps, key = nextmm()
                                    for kd in range(2):
                                        mm(ps[:, 0:n], w_[:, h, kd, e * 128:(e + 1) * 128], xc[:, 2 * h + kd, 0:n], kd == 0, kd == 1,
                                           [wkey, 'xc'], [key])
                                    P.op('act', lambda ps=ps, dst=dst, h=h, e=e, sc=sc: A.mul(dst[:, 2 * h + e, 0:n], ps[:, 0:n], sc),
                                         reads=[key], writes=['uT'] if dst is qT else ['kT'])
                        for c in range(nch):
                            cs = slice(c * L, (c + 1) * L)
                            tt = c
                            ts0 = c * 128
                            for h in range(4):
                                ps, key = nextmm()
                                for kd in range(2):
                                    mm(ps[0:tp, 0:256], xb[:, 2 * h + kd, 3 + ts0:3 + ts0 + tp], wv[:, h, kd, :], kd == 0, kd == 1,
                                       ['xb', 'ml_wv'], [key])
                                P.op('act', lambda ps=ps, h=h: A.copy(vtok[0:tp, h, 0:256], ps[0:tp, 0:256]),
                                     reads=[key], writes=['vtok'])
                                ps, key = nextmm()
                                for kd in range(2):
                                    mm(ps[0:tp, 0:256], xc[:, 2 * h + kd, ts0:ts0 + tp], wk[:, h, kd, :], kd == 0, kd == 1,
                                       ['xc', 'ml_wk'], [key])
                                P.op('act', lambda ps=ps, h=h: A.mul(ktok[0:tp, h, :], ps[0:tp, 0:256], 1.0 / 16.0),
                                     reads=[key], writes=['ktok'])
                            pc = pb[6]
                            for i_, row in enumerate((r_wi, r_emm, r_wend, r_g)):
                                mm(pc[0:L, 384 + 4 * i_:388 + 4 * i_], row[0:4, cs], identf[0:4, 0:4], True, True,
                                   ['r_wi', 'r_emm', 'r_wend', 'r_g', 'identf'], ['pb6c'])
                            P.op('dve', lambda c=c: V.tensor_scalar(dg4[:, :], identf[0:4, 0:4], r_wi[0:4, (c + 1) * L - 1:(c + 1) * L], None, ALU.mult),
                                 reads=['r_wi', 'identf'], writes=['dg4'])
                            mm(pc[:, 400:404], ones4[0:4, :], dg4[0:4, :], True, True, ['ones4', 'dg4'], ['pb6c'])
                            P.op('dve', lambda pc=pc: V.tensor_copy(cols[:, 0:20], pc[:, 384:404]), reads=['pb6c'], writes=['cols'])
                            yield
                            for h in range(4):
                                stp, stkey = nextmm()
                                ST = stp[0:L, 0:L]
                                for ke in range(2):
                                    mm(ST, kT[:, 2 * h + ke, cs], qT[:, 2 * h + ke, cs], ke == 0, ke == 1, ['kT', 'uT'], [stkey])
                                P.op('dve', lambda ST=ST, h=h: V.scalar_tensor_tensor(PT[0:L, 0:L], ST, cols[0:L, 8 + h:9 + h], negmask[0:L, 0:L],
                                                                                     ALU.mult, ALU.mult),
                                     reads=[stkey, 'cols', 'negmask'], writes=['PT'])
                                mm(pb[7][0:L, 0:257], PT[0:L, 0:L], vtok[0:L, h, 0:257], True, True, ['PT', 'vtok'], ['pb7'])
                                for kd in range(2):
                                    mm(pb[3][0:L, 0:257], qT[:, 2 * h + kd, cs], Cbf[:, h, kd, 0:257], kd == 0, kd == 1, ['uT', 'Cbf%d' % h], ['pb3'])
                                P.op('act', lambda h=h: A.activation(out=inter[0:L, 0:257], in_=pb[3][0:L, 0:257], func=AF.Copy,
                                                                     scale=cols[0:L, h:h + 1]),
                                     reads=['pb3', 'cols'], writes=['y2'])
                                P.op('dve', lambda h=h: V.scalar_tensor_tensor(tot[0:L, 0:257], pb[7][0:L, 0:257], cols[0:L, 12 + h:13 + h],
                                                                            inter[0:L, 0:257], ALU.mult, ALU.add),
                                     reads=['pb7', 'y2', 'cols'], writes=['y3'])
                                P.op('dve', lambda: V.scalar_tensor_tensor(st2[0:L, 2:3], tot[0:L, 256:257], -1.0, tot[0:L, 256:257],
                                                                           ALU.mult, ALU.max), reads=['y3'], writes=['st2c'])
                                P.op('dve', lambda h=h: V.tensor_tensor(st2[0:L, 2:3], st2[0:L, 2:3], cols[0:L, 4 + h:5 + h], ALU.max),
                                     reads=['st2c', 'cols'], writes=['st2c'])
                                P.op('dve', lambda: V.reciprocal(st2[0:L, 3:4], st2[0:L, 2:3]), reads=['st2c'], writes=['st2d'])
                                P.op('dve', lambda: V.bn_stats(st6[0:L, :], tot[0:L, 0:256]), reads=['y3'], writes=['st6'])
                                P.op('dve', lambda: V.bn_aggr(st2[0:L, 4:6], st6[0:L, :]), reads=['st6'], writes=['st2e'])
                                P.op('dve', lambda: V.tensor_tensor(st2[0:L, 6:7], st2[0:L, 3:4], st2[0:L, 3:4], ALU.mult),
                                     reads=['st2d'], writes=['st2f'])
                                P.op('dve', lambda: V.tensor_scalar(st2[0:L, 6:7], st2[0:L, 6:7], st2[0:L, 5:6], EPS, ALU.mult, ALU.add),
                                     reads=['st2f', 'st2e'], writes=['st2f'])
                                P.op('pool', lambda: G.tensor_tensor(st2[0:L, 7:8], st2[0:L, 6:7], negh[0:L, :], ALU.pow),
                                     reads=['st2f', 'negh'], writes=['st2g'])
                                P.op('dve', lambda: V.tensor_tensor(st2[0:L, 7:8], st2[0:L, 7:8], st2[0:L, 3:4], ALU.mult),
                                     reads=['st2g', 'st2d'], writes=['st2g'])
                                P.op('dve', lambda tt=tt, h=h: V.tensor_scalar(hn[0:L, h * 256:(h + 1) * 256], tot[0:L, 0:256],
                                                                              st2[0:L, 4:5], st2[0:L, 7:8], ALU.subtract, ALU.mult),
                                     reads=['y3', 'st2e', 'st2g'], writes=['ub'])
                                P.op('act', lambda tt=tt, h=h: A.activation(out=kw[0:L, :], in_=ktok[0:L, h, :], func=AF.Copy, scale=cols[0:L, 8 + h:9 + h]),
                                     reads=['ktok', 'cols'], writes=['kw'])
                                for kd in range(2):
                                    cup = (pb[4], 'pb4') if kd == 0 else (pb[5], 'pb5')
                                    mm(cup[0][:, 0:257], kw[0:L, kd * 128:(kd + 1) * 128], vtok[0:L, h, 0:257], True, True, ['kw', 'vtok'], [cup[1]])
                                    P.op('dve', lambda h=h, kd=kd, cup=cup: V.scalar_tensor_tensor(
                                        Cst[:, h, kd, 0:257], Cst[:, h, kd, 0:257], cols[:, 16 + h:17 + h], cup[0][:, 0:257], ALU.mult, ALU.add),
                                        reads=['Cst%d' % h, 'cols', cup[1]], writes=['Cst%d' % h])
                                P.op('act', lambda h=h: A.copy(Cbf[:, h, :, :], Cst[:, h, :, :]), reads=['Cst%d' % h], writes=['Cbf%d' % h])
                                yield
                            pT = pb[2][:].bitcast(BF16)
                            for k in range(8):
                                P.op('pe', lambda k=k: PL.transpose(pT[:, k * 128:k * 128 + tp], hn[0:tp, k * 128:(k + 1) * 128],
                                                                    identb[0:tp, 0:tp]),
                                     reads=['ub', 'identb'], writes=['pb2'])
                            for k in range(8):
                                P.op('dve', lambda k=k: V.tensor_scalar(ytmp[:, 0:tp], pT[:, k * 128:k * 128 + tp], mlnw[:, k:k + 1], None, ALU.mult),
                                     reads=['pb2', 'mlnw'], writes=['ytmp'])
                                P.op('dve', lambda k=k: V.scalar_tensor_tensor(ytmp[:, 0:tp], xc[:, k, ts0:ts0 + tp], skipc[:, k:k + 1],
                                                                               ytmp[:, 0:tp], ALU.mult, ALU.add),
                                     reads=['xc', 'skipc', 'ytmp'], writes=['ytmp'])
                                P.op('dve', lambda k=k: V.tensor_tensor(gated[:, 4 + k, ts0:ts0 + tp], ytmp[:, 0:tp], szb[:, k, ts0:ts0 + tp], ALU.mult),
                                     reads=['ytmp', 'szb'], writes=['gatedB'])
                            yield
                        P.op('dve', lambda n=n: V.tensor_copy(xbo[:, :, :], xb[:, :, n:n + 3]), reads=['xb'], writes=['xbo'])
                        P.op('dve', lambda n=n: V.tensor_copy(xb[:, :, 0:3], xbo[:, :, :]), reads=['xbo'], writes=['xb'])
                    gens = [gen_s5(), gen_ml()]
                    while gens:
                        for g_ in list(gens):
                            try:
                                next(g_)
                            except StopIteration:
                                gens.remove(g_)
                    for tt in range(ntt):
                        ts0 = tt * 128
                        P.dma('sp', hin[0:tp, :], src[ts0:ts0 + tp, :], writes=['hin'])
                        for nh in range(2):
                            po = ((pb[7], 'pb7'), (pb[3], 'pb3'), (pb[4], 'pb4'), (pb[5], 'pb5'))[2 * (tt % 2) + nh]
                            for cch in range(12):
                                gk = ('gated%d' % cch) if cch < 4 else 'gatedB'
                                mm(po[0][0:tp, :], gated[:, cch, ts0:ts0 + tp], w_out[:, cch, nh * 512:(nh + 1) * 512], cch == 0, cch == 11,
                                   [gk, 'w_out'], [po[1]])
                            P.op('dve', lambda tt=tt, nh=nh, po=po: V.tensor_tensor(hin[0:tp, nh * 512:(nh + 1) * 512],
                                                                                   hin[0:tp, nh * 512:(nh + 1) * 512], po[0][0:tp, :], ALU.add),
                                 reads=['hin', po[1]], writes=['hin'])
                        P.dma('sp', h1rows(g, si, t0 + ts0, tp), hin[0:tp, :], reads=['hin'], writes=['h1'])
                P.enabled = True
                sfx = '_' + g
                P.dma('sp', O['o_s5re' + sfx][si, :].rearrange("(j q) -> q j", q=128), Hre[:], reads=['Hre'], writes=['o1'])
                P.dma('sp', O['o_s5im' + sfx][si, :].rearrange("(j q) -> q j", q=128), Him[:], reads=['Him'], writes=['o2'])
                for h in range(4):
                    P.dma('sp', O['o_mlc' + sfx][si, h].rearrange("(k p) e -> p k e", p=128), Cst[:, h, :, 0:256],
                          reads=['Cst%d' % h], writes=['o3'])
                    P.dma('sp', O['o_mln' + sfx][si, h].rearrange("(k p) -> p k", p=128), Cst[:, h, :, 256],
                          reads=['Cst%d' % h], writes=['o4'])
                P.dma('sp', O['o_mlm' + sfx][si, :].rearrange("(p o) -> p o", o=1), mcarry[:], reads=['mcarry'], writes=['o5'])
                for t_ in range(3):
                    P.dma('sp', O['o_mlconv' + sfx][si, t_].rearrange("(k p) -> p k", p=128), xbo[:, :, t_], reads=['xbo'], writes=['o6'])
            P.barrier()


        if cfg.layers >= 2:
          with ExitStack() as l1:
            sb = lambda n, s, d: l1.enter_context(nc.sbuf_tensor(n, s, d))
            NP = NPIECE
            w_in = sb("w_in1", [128, 8, 5120], BF16)
            w_sw = sb("w_sw1", [128, 8, 1024], BF16)
            w_out = sb("w_out1", [128, 16, D], BF16)
            lwa = sb("lwa", [128, 8, 128], BF16)
            lwx = sb("lwx", [128, 8, 128], BF16)
            for k in range(8):
                P.dma('pool', w_in[:, k, :], I['od_w_in'][k * 128:(k + 1) * 128, :], writes=['w_in'])
                P.dma('pool', w_sw[:, k, :], I['od_w_qksw'][k * 128:(k + 1) * 128, :], writes=['w_sw'])
            for k in range(16):
                P.dma('pool', w_out[:, k, :], I['od_w_out'][k * 128:(k + 1) * 128, :], writes=['w_out'])
            P.dma('pool', lwa[:], I['lru_w_a'].rearrange("h d e -> d h e"), writes=['lwa'])
            P.dma('pool', lwx[:], I['lru_w_x'].rearrange("h d e -> d h e"), writes=['lwx'])
            gcol = sb("gcol1", [128, 8], F32)
            rnw = sb("rnw", [128, 8], F32)
            cw = sb("cw1", [128, 8, 4], F32)
            cb = sb("cb1", [128, 8], F32)
            bah = sb("bah", [128, 8], F32)
            bxh = sb("bxh", [128, 8], F32)
            ccol = sb("ccol", [128, 8], F32)
            ccol2 = sb("ccol2", [128, 8], F32)
            fnw = sb("fnw", [128, D], F32)
            rmaskL = sb("rmaskL", [128, 4, 128], F32)
            rmaskS = sb("rmaskS", [128, 4, 128], F32)
            rxiL = sb("rxiL", [128, 4, 128], F32)
            rxiS = sb("rxiS", [128, 4, 16], F32)
            rzL = sb("rzL", [128, 4], F32)
            rzS = sb("rzS", [128, 4], F32)
            rglL = sb("rglL", [128, 4], F32)
            rglS = sb("rglS", [128, 4], F32)
            col = lambda nm: I[nm].rearrange("(k p) -> p k", p=128)
            P.dma('sp', gcol[:], I['norm_w'][1, :].rearrange("(k p) -> p k", p=128), writes=['gcol1'])
            P.dma('sp', rnw[:], col('ret_norm_w'), writes=['rnw'])
            for tap in range(4):
                P.dma('sp', cw[:, :, tap], I['lru_conv_w'][tap, :].rearrange("(k p) -> p k", p=128), writes=['cw1'])
            P.dma('sp', cb[:], col('lru_conv_b'), writes=['cb1'])
            P.dma('sp', bah[:], col('lru_b_a'), writes=['bah'])
            P.dma('sp', bxh[:], col('lru_b_x'), writes=['bxh'])
            P.dma('sp', ccol[:], col('lru_lambda'), writes=['ccol'])
            P.dma('sp', fnw[:], I['fnw_b'][:, :], writes=['fnw'])
            P.dma('sp', rmaskL[:], I['c_rmask128'][:, :, :], writes=['rmaskL'])
            P.dma('sp', rmaskS[:], I['c_rmask16'][:, :, :], writes=['rmaskS'])
            P.dma('sp', rxiL[:], I['c_rxi128'][:, :, :], writes=['rxiL'])
            P.dma('sp', rxiS[:], I['c_rxi16'][:, :, :], writes=['rxiS'])
            P.dma('sp', rzL[:], I['c_rzeta128'][:, :], writes=['rzL'])
            P.dma('sp', rzS[:], I['c_rzeta16'][:, :], writes=['rzS'])
            P.dma('sp', rglL[:], I['c_rgl128'][:, :], writes=['rglL'])
            P.dma('sp', rglS[:], I['c_rgl16'][:, :], writes=['rglS'])
            P.op('dve', lambda: V.tensor_scalar(bah[:], bah[:], 0.5, None, ALU.mult), reads=['bah'], writes=['bah'])
            P.op('dve', lambda: V.tensor_scalar(bxh[:], bxh[:], 0.5, None, ALU.mult), reads=['bxh'], writes=['bxh'])
            P.op('act', lambda: A.activation(out=ccol[:], in_=ccol[:], func=AF.Exp, scale=-1.0), reads=['ccol'], writes=['ccol'])
            P.op('act', lambda: A.activation(out=ccol[:], in_=ccol[:], func=AF.Ln, bias=1.0), reads=['ccol'], writes=['ccol'])
            P.op('dve', lambda: V.tensor_scalar(ccol2[:], ccol[:], -4.0, None, ALU.mult), reads=['ccol'], writes=['ccol2'])
            P.op('dve', lambda: V.tensor_scalar(ccol[:], ccol[:], -8.0, None, ALU.mult), reads=['ccol'], writes=['ccol'])

            hin = sb("hin1", [128, D], F32)
            ss = sb("ss1", [128, 4], F32)
            ub = sb("ub1", [128, D], BF16)
            hn = ub
            uT = sb("uT1", [128, 8, NP], BF16)
            gated = sb("gated1", [128, 16, NP], BF16)
            cosF = sb("cosF", [128, NP], F32)
            sinF = sb("sinF", [128, NP], F32)
            qT = sb("qT1", [128, 4, NP], BF16)
            kT = sb("kT1", [128, 4, NP], BF16)
            qxi = sb("qxi", [128, 128], BF16)
            kz = sb("kz", [128, 128], BF16)
            vtok = sb("vtok1", [128, 4, 256], BF16)
            szc = sb("szc", [128, 8, NP], BF16)
            xd = sb("xd", [128, 8, 3 + NP], BF16)
            szd = sb("szd", [128, 8, NP], BF16)
            Sst = sb("Sst", [128, 4, 256], F32)
            Sbf = sb("Sbf", [128, 4, 256], BF16)
            hst = sb("hst", [128, 8], F32)
            f1 = sb("f1", [128, NP], F32)
            LXC = [sb("lx%d" % i_, [128, NP], BF16) for i_ in range(2)]
            LS = [dict(xcb=LXC[i_ % 2], f2=sb("lf2_%d" % i_, [128, NP], F32), f3=sb("lf3_%d" % i_, [128, NP], F32),
                       f4=sb("lf4_%d" % i_, [128, NP], F32)) for i_ in range(4)]
            PTb = sb("PT1b", [128, 128], BF16)
            qxib = sb("qxib", [128, 128], BF16)
            kzb = sb("kzb", [128, 128], BF16)
            st6c = sb("st6c", [128, 6], F32)
            PT = sb("PT1", [128, 128], BF16)
            st6 = sb("st6b", [128, 6], F32)
            st2 = sb("st2b", [128, 8], F32)
            xdo = sb("xdo", [128, 8, 3], F32)
            yout = sb("yout", [128, D], F32)

            print('SBUF remaining after L1 activations', nc.sbuf_bytes_remaining)
            mmreg = [(pb[0], 'pb0'), (pb[1], 'pb1'), (pb[4], 'pb4'), (pb[5], 'pb5')]
            mmi = [0]

            def nextmm():
                r_ = mmreg[mmi[0] % 4]
                mmi[0] += 1
                return r_

            def inproj(f, n, wt=None, wkey='w_in'):
                ps, key = nextmm()
                wt = w_in if wt is None else wt
                for k in range(8):
                    mm(ps[:, 0:n], wt[:, k, f * 128:(f + 1) * 128], uT[:, k, 0:n], k == 0, k == 7, [wkey, 'uT'], [key])
                return ps, key

            for (g, si, pcs) in seqs:
                if g == 'p':
                    P.op('pool', lambda: G.memset(Sst[:], 0.0), writes=['Sst%d' % h_ for h_ in range(4)])
                    P.op('pool', lambda: G.memset(hst[:], 0.0), writes=['hst'])
                    P.op('pool', lambda: G.memset(xd[:, :, 0:3], 0.0), writes=['xd'])
                else:
                    for h in range(4):
                        P.dma('sp', Sst[:, h, :], I['ret_in'][si, h], writes=['Sst%d' % h])
                    P.dma('sp', hst[:], I['lruh_in'][si, :].rearrange("(k p) -> p k", p=128), writes=['hst'])
                    for t_ in range(3):
                        P.dma('pool', xd[:, :, t_], I['lruconv_in'][si, t_].rearrange("(k p) -> p k", p=128), writes=['xd'])
                P.op('act', lambda: A.copy(Sbf[:], Sst[:]), reads=['Sst%d' % h_ for h_ in range(4)], writes=['Sbf%d' % h_ for h_ in range(4)])
                for (t0, n, src, dst, _) in pcs:
                    tp = min(n, 128)
                    ntt = (n + 127) // 128
                    L = tp
                    nch = n // L
                    rmask, rxi, rz, rgl = (rmaskL, rxiL, rzL, rglL) if L == 128 else (rmaskS, rxiS, rzS, rglS)
                    rp0 = t0 if g == 'p' else T
                    P.dma('sp', cosF[:, 0:n], I['ropeF_c'][:, rp0:rp0 + n], writes=['cosF'])
                    P.dma('sp', sinF[:, 0:n], I['ropeF_s'][:, rp0:rp0 + n], writes=['sinF'])
                    for tt in range(ntt):
                        P.dma('sp', hin[0:tp, :], h1rows(g, si, t0 + tt * 128, tp), reads=['h1'], writes=['hin1'])
                        rmsnorm_T(hin, tp, tt, gcol, ub, uT, ss, hkey='hin1', gkey='gcol1')
                    for (base, dstT, sc, dkey) in ((0, qT, 1.0, 'qT1'), (4, kT, 1.0 / math.sqrt(128.0), 'kT1')):
                        for h in range(4):
                            psA, keyA = inproj(base + h, n)
                            psB, keyB = inproj(base + h, n, wt=w_sw, wkey='w_sw')
                            P.op('dve', lambda psA=psA: V.tensor_tensor(yout[:, 0:n], psA[:, 0:n], cosF[:, 0:n], ALU.mult),
                                 reads=[keyA, 'cosF'], writes=['yout'])
                            P.op('dve', lambda psB=psB: V.tensor_tensor(yout[:, 256:256 + n], psB[:, 0:n], sinF[:, 0:n], ALU.mult),
                                 reads=[keyB, 'sinF'], writes=['yout'])
                            P.op('dve', lambda: V.tensor_tensor(yout[:, 0:n], yout[:, 0:n], yout[:, 256:256 + n], ALU.add), reads=['yout', 'yout'], writes=['yout'])
                            P.op('act', lambda h=h, dstT=dstT, sc=sc: A.mul(dstT[:, h, 0:n], yout[:, 0:n], sc), reads=['yout'], writes=[dkey])
                    for f in range(8):
                        ps, key = inproj(16 + f, n)
                        P.op('act', lambda f=f, ps=ps: A.activation(out=szc[:, f, 0:n], in_=ps[:, 0:n], func=AF.Silu), reads=[key], writes=['szc'])
                    for f in range(8):
                        ps, key = inproj(24 + f, n)
                        P.op('act', lambda f=f, ps=ps: A.copy(xd[:, f, 3:3 + n], ps[:, 0:n]), reads=[key], writes=['xd'])
                    for f in range(8):
                        ps, key = inproj(32 + f, n)
                        P.op('act', lambda f=f, ps=ps: A.activation(out=szd[:, f, 0:n], in_=ps[:, 0:n], func=AF.Silu), reads=[key], writes=['szd'])
                    def gen_ret():
                        for c in range(nch):
                            cs = slice(c * L, (c + 1) * L)
                            ts0 = c * 128
                            for h in range(4):
                                ps, key = nextmm()
                                for k in range(8):
                                    mm(ps[0:tp, 0:256], uT[:, k, ts0:ts0 + tp], w_in[:, k, 1024 + h * 256:1024 + (h + 1) * 256], k == 0, k == 7,
                                       ['uT', 'w_in'], [key])
                                P.op('act', lambda ps=ps, h=h: A.copy(vtok[0:tp, h, :], ps[0:tp, 0:256]), reads=[key], writes=['vtok1'])
                            yield
                            for h in range(4):
                                par = h % 2
                                PT_, qxi_, kz_, st6_ = (PT, qxi, kz, st6) if par == 0 else (PTb, qxib, kzb, st6c)
                                pk = '_%d' % par
                                so = 4 * par
                                po_, pokey = (pb[7], 'pb7') if par == 0 else (pb[6], 'pb6')
                                stp, stkey = nextmm()
                                ST = stp[0:L, 0:L]
                                mm(ST, kT[:, h, cs], qT[:, h, cs], True, True, ['kT1', 'qT1'], [stkey])
                                P.op('dve', lambda: V.tensor_tensor(PT_[0:L, 0:L], ST, rmask[0:L, h, 0:L], ALU.mult),
                                     reads=[stkey, 'rmaskL', 'rmaskS'], writes=['PT1' + pk])
                                P.op('dve', lambda: V.tensor_tensor(qxi_[:, 0:L], qT[:, h, cs], rxi[:, h, 0:L], ALU.mult),
                                     reads=['qT1', 'rxiL', 'rxiS'], writes=['qxi' + pk])
                                mm(po_[0:L, 0:256], PT_[0:L, 0:L], vtok[0:L, h, :], True, False, ['PT1' + pk, 'vtok1'], [pokey])
                                mm(po_[0:L, 0:256], qxi_[:, 0:L], Sbf[:, h, :], False, True, ['qxi' + pk, 'Sbf%d' % h], [pokey])
                                pT = pb[2][:].bitcast(BF16)
                                P.op('pe', lambda: PL.transpose(pT[0:L, 0:128], kT[:, h, cs], identb[:, :]),
                                     reads=['kT1', 'identb'], writes=['pb2'])
                                P.op('act', lambda: A.activation(out=kz_[0:L, :], in_=pT[0:L, 0:128], func=AF.Copy, scale=rz[0:L, h:h + 1]),
                                     reads=['pb2', 'rzL', 'rzS'], writes=['kz' + pk])
                                sup, supkey = nextmm()
                                mm(sup[:, 0:256], kz_[0:L, :], vtok[0:L, h, :], True, True, ['kz' + pk, 'vtok1'], [supkey])
                                P.op('dve', lambda: V.bn_stats(st6_[0:L, :], po_[0:L, 0:256]), reads=[pokey], writes=['st6' + pk])
                                P.op('dve', lambda: V.bn_aggr(st2[0:L, so:so + 2], st6_[0:L, :]), reads=['st6' + pk], writes=['st2x' + pk])
                                P.op('dve', lambda: V.tensor_scalar(st2[0:L, so + 2:so + 3], st2[0:L, so + 1:so + 2], EPS, None, ALU.add),
                                     reads=['st2x' + pk], writes=['st2y' + pk])
                                P.op('pool', lambda: G.tensor_tensor(st2[0:L, so + 3:so + 4], st2[0:L, so + 2:so + 3], negh[0:L, :], ALU.pow),
                                     reads=['st2y' + pk, 'negh'], writes=['st2z' + pk])
                                P.op('dve', lambda: V.scalar_tensor_tensor(Sst[:, h, :], Sst[:, h, :], rgl[:, h:h + 1], sup[:, 0:256], ALU.mult, ALU.add),
                                     reads=['Sst%d' % h, 'rglL', 'rglS', supkey], writes=['Sst%d' % h])
                                P.op('act', lambda: A.copy(Sbf[:, h, :], Sst[:, h, :]), reads=['Sst%d' % h], writes=['Sbf%d' % h])
                                P.op('dve', lambda: V.tensor_scalar(hn[0:L, h * 256:(h + 1) * 256], po_[0:L, 0:256], st2[0:L, so:so + 1], st2[0:L, so + 3:so + 4],
                                                                    ALU.subtract, ALU.mult),
                                     reads=[pokey, 'st2x' + pk, 'st2z' + pk], writes=['ub'])
                                yield
                            pT = pb[2][:].bitcast(BF16)
                            for k in range(8):
                                P.op('pe', lambda k=k: PL.transpose(pT[:, k * 128:k * 128 + tp], hn[0:tp, k * 128:(k + 1) * 128], identb[0:tp, 0:tp]),
                                     reads=['ub', 'identb'], writes=['pb2'])
                            for k in range(8):
                                P.op('dve', lambda k=k: V.scalar_tensor_tensor(gated[:, k, ts0:ts0 + tp], pT[:, k * 128:k * 128 + tp], rnw[:, k:k + 1],
                                                                               szc[:, k, ts0:ts0 + tp], ALU.mult, ALU.mult),
                                     reads=['pb2', 'rnw', 'szc'], writes=['gatedA'])
                            yield
                    def gen_lru():
                        def ph1(k):
                            B_ = LS[k % 4]; sfx_ = '_%d' % (k % 4)
                            xcb, f2, f3, f4 = B_['xcb'], B_['f2'], B_['f3'], B_['f4']
                            P.op('dve', lambda: V.tensor_scalar(f1[:, 0:n], xd[:, k, 0:n], cw[:, k, 0:1], cb[:, k:k + 1], ALU.mult, ALU.add),
                                 reads=['xd', 'cw1', 'cb1'], writes=['f1'])
                            for tap in range(1, 4):
                                P.op('dve', lambda tap=tap: V.scalar_tensor_tensor(f1[:, 0:n], xd[:, k, tap:tap + n], cw[:, k, tap:tap + 1], f1[:, 0:n],
                                                                                   ALU.mult, ALU.add),
                                     reads=['xd', 'cw1', 'f1'], writes=['f1'])
                            P.op('act', lambda: A.copy(xcb[:, 0:n], f1[:, 0:n]), reads=['f1'], writes=['xcb_%d' % (k % 2)])
                            psr, keyr = nextmm()
                            mm(psr[:, 0:n], lwa[:, k, :], xcb[:, 0:n], True, True, ['lwa', 'xcb_%d' % (k % 2)], [keyr])
                            psi, keyi = nextmm()
                            mm(psi[:, 0:n], lwx[:, k, :], xcb[:, 0:n], True, True, ['lwx', 'xcb_%d' % (k % 2)], [keyi])
                            P.op('act', lambda: A.activation(out=f2[:, 0:n], in_=psr[:, 0:n], func=AF.Tanh, scale=0.5, bias=bah[:, k:k + 1]),
                                 reads=[keyr, 'bah'], writes=['f2' + sfx_])
                            P.op('act', lambda: A.activation(out=f3[:, 0:n], in_=psi[:, 0:n], func=AF.Tanh, scale=0.5, bias=bxh[:, k:k + 1]),
                                 reads=[keyi, 'bxh'], writes=['f3' + sfx_])
                            P.op('act', lambda: A.activation(out=f4[:, 0:n], in_=f2[:, 0:n], func=AF.Exp, scale=ccol2[:, k:k + 1], bias=ccol2[:, k:k + 1]),
                                 reads=['f2' + sfx_, 'ccol2'], writes=['f4' + sfx_])
                            P.op('pool', lambda: G.tensor_scalar(f3[:, 0:n], f3[:, 0:n], 0.5, 0.5, ALU.mult, ALU.add), reads=['f3' + sfx_], writes=['f3' + sfx_])
                            P.op('pool', lambda: G.tensor_tensor(f3[:, 0:n], f3[:, 0:n], xcb[:, 0:n], ALU.mult), reads=['f3' + sfx_, 'xcb_%d' % (k % 2)], writes=['f3' + sfx_])
                            P.op('pool', lambda: G.tensor_tensor(f2[:, 0:n], f4[:, 0:n], f4[:, 0:n], ALU.mult), reads=['f4' + sfx_, 'f2' + sfx_], writes=['f2' + sfx_])
                            P.op('pool', lambda: G.tensor_scalar(f2[:, 0:n], f2[:, 0:n], -1.0, 1.0, ALU.mult, ALU.add), reads=['f2' + sfx_], writes=['f2' + sfx_])

                        def ph2(k):
                            B_ = LS[k % 4]; sfx_ = '_%d' % (k % 4)
                            f2 = B_['f2']
                            P.op('act', lambda: A.activation(out=f2[:, 0:n], in_=f2[:, 0:n], func=AF.Ln), reads=['f2' + sfx_], writes=['f2' + sfx_])
                            P.op('act', lambda: A.activation(out=f2[:, 0:n], in_=f2[:, 0:n], func=AF.Exp, scale=0.5), reads=['f2' + sfx_], writes=['f2' + sfx_])

                        def ph3(k):
                            B_ = LS[k % 4]; sfx_ = '_%d' % (k % 4)
                            xcb, f2, f3, f4 = B_['xcb'], B_['f2'], B_['f3'], B_['f4']
                            P.op('dve', lambda: V.tensor_tensor(f3[:, 0:n], f3[:, 0:n], f2[:, 0:n], ALU.mult), reads=['f3' + sfx_, 'f2' + sfx_], writes=['f3' + sfx_])
                            P.op('dve', lambda: V.tensor_tensor_scan(f2[:, 0:n], f4[:, 0:n], f3[:, 0:n], hst[:, k:k + 1], ALU.mult, ALU.add),
                                 reads=['f4' + sfx_, 'f3' + sfx_, 'hst', 'f2' + sfx_], writes=['f2' + sfx_])
                            P.op('dve', lambda: V.tensor_copy(hst[:, k:k + 1], f2[:, n - 1:n]), reads=['f2' + sfx_], writes=['hst'])
                            P.op('dve', lambda: V.tensor_tensor(gated[:, 8 + k, 0:n], f2[:, 0:n], szd[:, k, 0:n], ALU.mult),
                                 reads=['f2' + sfx_, 'szd'], writes=['gatedB%d' % k])

                        for q_ in range(2):
                            for i_ in range(4):
                                ph1(4 * q_ + i_)
                                yield
                            for i_ in range(4):
                                ph2(4 * q_ + i_)
                            yield
                            for i_ in range(4):
                                ph3(4 * q_ + i_)
                                if i_ % 2 == 1:
                                    yield
                    gens = [gen_ret(), gen_lru()]
                    while gens:
                        for g_ in list(gens):
                            try:
                                next(g_)
                            except StopIteration:
                                gens.remove(g_)
                    P.op('dve', lambda n=n: V.tensor_copy(xdo[:, :, :], xd[:, :, n:n + 3]), reads=['xd'], writes=['xdo'])
                    P.op('dve', lambda: V.tensor_copy(xd[:, :, 0:3], xdo[:, :, :]), reads=['xdo'], writes=['xd'])
                    for tt in range(ntt):
                        ts0 = tt * 128
                        P.dma('sp', hin[0:tp, :], h1rows(g, si, t0 + ts0, tp), reads=['h1'], writes=['hin1'])
                        for nh in range(2):
                            po = ((pb[7], 'pb7'), (pb[3], 'pb3'), (pb[6], 'pb6'), (pb[2], 'pb2'))[2 * (tt % 2) + nh]
                            for cch in range(16):
                                gk = 'gatedA' if cch < 8 else 'gatedB%d' % (cch - 8)
                                mm(po[0][0:tp, :], gated[:, cch, ts0:ts0 + tp], w_out[:, cch, nh * 512:(nh + 1) * 512], cch == 0, cch == 15,
                                   [gk, 'w_out'], [po[1]])
                            P.op('dve', lambda nh=nh, po=po: V.tensor_tensor(hin[0:tp, nh * 512:(nh + 1) * 512],
                                                                             hin[0:tp, nh * 512:(nh + 1) * 512], po[0][0:tp, :], ALU.add),
                                 reads=['hin1', po[1]], writes=['hin1'])
                        if not (g == 'p' and t0 == 0):
                            P.op('pool', lambda: G.memset(ss[0:tp, 0:1], 0.0), writes=['ss'])
                            P.op('act', lambda: A.activation(out=ub[0:tp, :], in_=hin[0:tp, :], func=AF.Square, accum_out=ss[0:tp, 0:1]),
                                 reads=['hin1'], writes=['ub', 'ss'])
                            P.op('dve', lambda: V.tensor_scalar(ss[0:tp, 1:2], ss[0:tp, 0:1], 1.0 / D, EPS, ALU.mult, ALU.add), reads=['ss'], writes=['ss'])
                            P.op('pool', lambda: G.tensor_tensor(ss[0:tp, 2:3], ss[0:tp, 1:2], negh[0:tp, :], ALU.pow), reads=['ss', 'negh'], writes=['ss'])
                            P.op('dve', lambda: V.scalar_tensor_tensor(yout[0:tp, :], hin[0:tp, :], ss[0:tp, 2:3], fnw[0:tp, :], ALU.mult, ALU.mult),
                                 reads=['hin1', 'ss', 'fnw'], writes=['yout'])
                            P.dma('sp', dst[ts0:ts0 + tp, :], yout[0:tp, :], reads=['yout'], writes=['yo'])
                sfx = '_' + g
                for h in range(4):
                    P.dma('sp', O['o_ret' + sfx][si, h], Sst[:, h, :], reads=['Sst%d' % h], writes=['o7'])
                P.dma('sp', O['o_lruh' + sfx][si, :].rearrange("(k p) -> p k", p=128), hst[:], reads=['hst'], writes=['o8'])
                for t_ in range(3):
                    P.dma('sp', O['o_lruconv' + sfx][si, t_].rearrange("(k p) -> p k", p=128), xdo[:, :, t_], reads=['xdo'], writes=['o9'])
            P.barrier()
        P.barrier()
    return nc


def host_layout(inp):
    f = lambda a: np.ascontiguousarray(np.asarray(a, dtype=np.float32))
    d = {}
    d['norm_w'] = f(inp['norm_w'])
    d['final_norm_w'] = f(inp['final_norm_w'])
    d['ev_w_in'] = f(inp['ev_w_in'][0])
    d['ev_w_out'] = f(inp['ev_w_out'][0])
    lre, lim, ldt = inp['s5_lambda_re'][0], inp['s5_lambda_im'][0], inp['s5_log_dt'][0]
    def A_from_gp(x):
        x4 = np.asarray(x).reshape(4, 8, 64)
        o = np.broadcast_to(x4[:, :, None, :], (4, 8, 16, 64))
        return f(np.transpose(o, (1, 2, 0, 3)).reshape(128, 4, 64))
    d['lamre_A'] = A_from_gp(lre)
    d['lamim_A'] = A_from_gp(lim)
    d['logdt_A'] = A_from_gp(np.broadcast_to(np.asarray(ldt)[:, None], (32, 64)))
    def A_from_gpc(x):
        x5 = np.asarray(x).reshape(4, 8, 64, 16)
        return f(np.transpose(x5, (1, 3, 0, 2)).reshape(128, 4, 64))
    d['bre_A'] = A_from_gpc(inp['s5_b_re'][0])
    d['bim_A'] = A_from_gpc(inp['s5_b_im'][0])
    def S_from_gp(x):
        x3 = np.asarray(x).reshape(16, 2, 64)
        return f(np.transpose(x3, (1, 2, 0)).reshape(128, 16))
    d['lamre_S'] = S_from_gp(lre)
    d['lamim_S'] = S_from_gp(lim)
    d['logdt_S'] = S_from_gp(np.broadcast_to(np.asarray(ldt)[:, None], (32, 64)))
    def S_from_gcp(x):
        x4 = np.asarray(x).reshape(16, 2, 16, 64)
        return f(np.transpose(x4, (1, 3, 0, 2)).reshape(128, 16, 16))
    d['cre_S'] = S_from_gcp(inp['s5_c_re'][0])
    d['cim_S'] = S_from_gcp(inp['s5_c_im'][0])
    d['s5_d'] = f(inp['s5_d'][0]); d['s5_w_glu'] = f(inp['s5_w_glu'][0]); d['s5_b_glu'] = f(inp['s5_b_glu'][0])
    d['ml_conv_w'] = f(inp['ml_conv_w'][0]); d['ml_conv_b'] = f(inp['ml_conv_b'][0])
    for k in ('ml_wq', 'ml_wk', 'ml_wv'):
        d[k] = f(inp[k][0])
        d[k + 'T'] = f(np.transpose(np.asarray(inp[k][0]), (0, 2, 1)))
    d['ml_w_if'] = f(inp['ml_w_if'][0]); d['ml_b_if'] = f(inp['ml_b_if'][0])
    d['ml_norm_w'] = f(inp['ml_norm_w'][0]); d['ml_skip'] = f(inp['ml_skip'][0])
    w1 = np.asarray(inp['od_w_in'][0])
    d['od_w_in'] = f(w1)
    qk = w1[:, 0:1024].reshape(D, 8, 2, 64)
    d['od_w_qksw'] = f(qk[:, :, ::-1, :].reshape(D, 1024))
    d['fnw_b'] = f(np.broadcast_to(np.asarray(inp['final_norm_w'])[None, :], (128, D)))
    d['od_w_out'] = f(inp['od_w_out'][0]); d['ret_norm_w'] = f(inp['ret_norm_w'][0])
    d['lru_conv_w'] = f(inp['lru_conv_w'][0]); d['lru_conv_b'] = f(inp['lru_conv_b'][0])
    d['lru_w_a'] = f(inp['lru_w_a'][0]); d['lru_b_a'] = f(inp['lru_b_a'][0])
    d['lru_w_x'] = f(inp['lru_w_x'][0]); d['lru_b_x'] = f(inp['lru_b_x'][0]); d['lru_lambda'] = f(inp['lru_lambda'][0])
    d['meta'] = f(inp['meta'])
    return d


def make_in_maps(cfg, inputs, ncores):
    shared = host_layout(inputs)
    shared.update(host_consts())
    T = 16 + cfg.T_x
    pos = np.concatenate([np.arange(T, dtype=np.float32), (16 + cfg.past) + np.arange(16, dtype=np.float32)])
    cf, sf, ct, st = rope_tables(pos)
    shared['ropeF_c'], shared['ropeF_s'], shared['ropeT_c'], shared['ropeT_s'] = cf, sf, np.ascontiguousarray(ct), np.ascontiguousarray(st)
    f = lambda a: np.ascontiguousarray(np.asarray(a, dtype=np.float32))
    maps = []
    for c in range(ncores):
        ps = slice(c * cfg.n_p, (c + 1) * cfg.n_p)
        ss_ = slice(c * cfg.n_s, (c + 1) * cfg.n_s)
        m = dict(shared)
        m['xp'] = f(inputs['x_prompt'][ps])
        m['xs'] = f(inputs['x_sample'][ss_])
        m['s5re_in'] = f(inputs['state_s5_re'][0][ss_]).reshape(cfg.n_s, 2048)
        m['s5im_in'] = f(inputs['state_s5_im'][0][ss_]).reshape(cfg.n_s, 2048)
        m['mlc_in'] = f(inputs['state_ml_c'][0][ss_])
        m['mln_in'] = f(inputs['state_ml_n'][0][ss_])
        m['mlm_in'] = f(inputs['state_ml_m'][0][ss_])
        m['mlconv_in'] = f(inputs['state_ml_conv'][0][ss_])
        m['ret_in'] = f(inputs['state_ret'][0][ss_])
        m['lruh_in'] = f(inputs['state_lru_h'][0][ss_])
        m['lruconv_in'] = f(inputs['state_lru_conv'][0][ss_])
        maps.append(m)
    return maps


def gather(cfg, results):
    cat = lambda k: np.concatenate([np.asarray(r[k]) for r in results], axis=0)
    outs = [cat('yp'), cat('ys')]
    for g in ('p', 's'):
        n = cat('o_s5re_' + g).shape[0]
        outs += [cat('o_s5re_' + g).reshape(1, n, 32, 64), cat('o_s5im_' + g).reshape(1, n, 32, 64),
                 cat('o_mlc_' + g)[None], cat('o_mln_' + g)[None], cat('o_mlm_' + g)[None], cat('o_mlconv_' + g)[None],
                 cat('o_ret_' + g)[None], cat('o_lruh_' + g)[None], cat('o_lruconv_' + g)[None]]
    return tuple(np.ascontiguousarray(o.astype(np.float32)) for o in outs)


def kernel(**inputs):
    ncores = 8
    cfg = Cfg(n_p=2, T_x=4096, n_s=2, past=2048, layers=2)
    nc = build(cfg)
    maps = make_in_maps(cfg, inputs, ncores)
    res = run_bass_kernel_spmd(nc, maps, core_ids=list(range(ncores)))
    return gather(cfg, res.results)
```

```python
import math
from contextlib import ExitStack
import numpy as np
import concourse.bass as bass
import concourse.mybir as mybir
from concourse.bass_utils import run_bass_kernel_spmd

F32 = mybir.dt.float32
BF16 = mybir.dt.bfloat16
AF = mybir.ActivationFunctionType
ALU = mybir.AluOpType

D = 1024
NPIECE = 256
EPS = 1e-6
PI = math.pi


class Prog:
    def __init__(self, nc, ndma=6):
        self.nc = nc
        self.e = dict(pe=nc.tensor, dve=nc.vector, act=nc.scalar, pool=nc.gpsimd, sp=nc.sync)
        self.streams = {k: [] for k in self.e}
        self.cnt = {k: 0 for k in self.e}
        self.waited = {k: {} for k in self.e}
        self.lastw = {}
        self.reads = {}
        self.sems = {}
        self.ndma = ndma
        self.dmai = {q: 0 for q in ('sp', 'act', 'pool')}
        self.dma_last = {}
        self.enabled = True
        self.semnames = ['c_' + k for k in ('pe', 'dve', 'act', 'pool')] + \
            ['d_%s_%d' % (q, j) for q in ('sp', 'act', 'pool') for j in range(ndma)]

    def _wait(self, eng, tok):
        sem, val = tok
        if self.waited[eng].get(sem, 0) >= val:
            return
        self.waited[eng][sem] = val
        self.e[eng].wait_ge(self.sems[sem], val)

    def _deps(self, eng, reads, writes):
        deps = []
        own = 'c_' + eng
        for k in reads:
            t = self.lastw.get(k)
            if t is not None:
                if not (eng == 'pe' and t[0] == own):
                    deps.append(t)
        for k in writes:
            t = self.lastw.get(k)
            if t is not None and t[0] != own:
                deps.append(t)
            for t in self.reads.get(k, ()):
                if t[0] != own:
                    deps.append(t)
        for t in deps:
            self._wait(eng, t)

    def _commit(self, tok, reads, writes):
        for k in writes:
            self.lastw[k] = tok
            self.reads[k] = []
        for k in reads:
            if k not in writes:
                self.reads.setdefault(k, []).append(tok)

    def op(self, eng, fn, reads=(), writes=()):
        if not self.enabled:
            return None
        self._deps(eng, reads, writes)
        self.cnt[eng] += 1
        tok = ('c_' + eng, self.cnt[eng])
        fn().then_inc(self.sems['c_' + eng], 1)
        self._commit(tok, reads, writes)
        return tok

    def dma(self, q, out, in_, reads=(), writes=(), **kw):
        if not self.enabled:
            return None
        self._deps(q, reads, writes)
        i = self.dmai[q]
        self.dmai[q] += 1
        j, r = i % self.ndma, i // self.ndma
        sem = 'd_%s_%d' % (q, j)
        if r > 0:
            self._wait(q, (sem, 16 * r))
        e = self.e[q]
        e.dma_start(out=out, in_=in_, **kw).then_inc(self.sems[sem], 16)
        tok = (sem, 16 * (r + 1))
        self.dma_last[sem] = tok
        self._commit(tok, reads, writes)
        return tok

    def barrier(self):
        toks = set(self.lastw.values())
        for lst in self.reads.values():
            toks.update(lst)
        toks.update(self.dma_last.values())
        for k in ('pe', 'dve', 'act', 'pool'):
            if self.cnt[k] > 0:
                toks.add(('c_' + k, self.cnt[k]))
        for eng in self.e:
            for t in toks:
                self._wait(eng, t)
        self.lastw = {}
        self.reads = {}

    def emit(self, block):
        sems = self.sems

        def run(engname, engobj):
            for it in self.streams[engname]:
                if it[0] == 'w':
                    engobj.wait_ge(sems[it[1]], it[2])
                else:
                    inst = it[1]()
                    inst.then_inc(sems[it[2]], it[3])

        @block.tensor
        def _(x):
            run('pe', x)

        @block.vector
        def _(x):
            run('dve', x)

        @block.scalar
        def _(x):
            run('act', x)

        @block.gpsimd
        def _(x):
            run('pool', x)

        @block.sync
        def _(x):
            run('sp', x)


def host_consts():
    c = {}
    c['c_identb'] = np.eye(128, dtype=np.float32)
    c['c_identf'] = np.eye(128, dtype=np.float32)
    s = np.arange(128)
    c['c_negmask'] = np.where(s[:, None] <= s[None, :], 1.0, 0.0).astype(np.float32)
    sel = np.zeros((4, 4, 128), np.float32)
    for h in range(4):
        sel[h, h, :] = 1.0
    c['c_sel'] = sel
    c['c_ones4'] = np.ones((4, 128), np.float32)
    mA = np.zeros((128, 8), np.float32)
    for g8 in range(8):
        mA[g8 * 16:(g8 + 1) * 16, g8] = 1.0
    c['c_maskA'] = mA
    lg = np.log1p(-np.exp2(-5.0 - np.arange(4, dtype=np.float64)))
    for L in (16, 128):
        idx = np.arange(L, dtype=np.float64)
        diff = idx[None, :] - idx[:, None]
        dm = np.where(diff >= 0, np.exp(lg[:, None, None] * np.maximum(diff, 0.0)), 0.0)
        dmp = np.zeros((128, 4, 128), np.float32)
        dmp[:L, :, :L] = np.transpose(dm, (1, 0, 2))
        c['c_rmask%d' % L] = dmp
        xi = np.exp(lg[:, None] * (idx + 1.0))
        c['c_rxi%d' % L] = np.ascontiguousarray(np.broadcast_to(xi[None], (128, 4, L))).astype(np.float32)
        zeta = np.exp(lg[:, None] * (L - 1.0 - idx))
        zp = np.zeros((128, 4), np.float32)
        zp[:L] = zeta.T
        c['c_rzeta%d' % L] = zp
        c['c_rgl%d' % L] = np.ascontiguousarray(np.broadcast_to(np.exp(lg * L)[None], (128, 4))).astype(np.float32)
    return c


def rope_tables(pos):
    half = 64
    inv = (10000.0 ** (-np.arange(half, dtype=np.float32) / half)).astype(np.float32)
    ang = pos.astype(np.float32)[:, None] * inv[None, :]
    cos = np.cos(ang).astype(np.float32)
    sin = np.sin(ang).astype(np.float32)
    cf = np.concatenate([cos.T, cos.T], axis=0)
    sf = np.concatenate([-sin.T, sin.T], axis=0)
    return np.ascontiguousarray(cf), np.ascontiguousarray(sf), cos, sin


class Cfg:
    def __init__(self, n_p=2, T_x=4096, n_s=2, past=2048, layers=2, debug=False, stage=99):
        self.stage = stage
        self.n_p, self.T_x, self.n_s, self.past, self.layers, self.debug = n_p, T_x, n_s, past, layers, debug
        assert T_x % NPIECE == 0


IN_SPECS = None


def in_specs(cfg):
    n_p, T_x, n_s = cfg.n_p, cfg.T_x, cfg.n_s
    T = 16 + T_x
    sp = {
        'xp': [n_p, T_x, D], 'xs': [n_s, 16, D], 'meta': [16, D],
        's5re_in': [n_s, 2048], 's5im_in': [n_s, 2048],
        'mlc_in': [n_s, 4, 256, 256], 'mln_in': [n_s, 4, 256], 'mlm_in': [n_s, 4], 'mlconv_in': [n_s, 3, D],
        'ret_in': [n_s, 4, 128, 256], 'lruh_in': [n_s, D], 'lruconv_in': [n_s, 3, D],
        'norm_w': [2, D], 'final_norm_w': [D],
        'ev_w_in': [D, 3072], 'ev_w_out': [1536, D],
        'lamre_A': [128, 4, 64], 'lamim_A': [128, 4, 64], 'logdt_A': [128, 4, 64],
        'lamre_S': [128, 16], 'lamim_S': [128, 16], 'logdt_S': [128, 16],
        'bre_A': [128, 4, 64], 'bim_A': [128, 4, 64], 'cre_S': [128, 16, 16], 'cim_S': [128, 16, 16],
        's5_d': [512], 's5_w_glu': [512, 512], 's5_b_glu': [512],
        'ml_conv_w': [4, D], 'ml_conv_b': [D], 'ml_wq': [4, 256, 256], 'ml_wk': [4, 256, 256], 'ml_wv': [4, 256, 256],
        'ml_wqT': [4, 256, 256], 'ml_wkT': [4, 256, 256], 'ml_wvT': [4, 256, 256],
        'ml_w_if': [3072, 8], 'ml_b_if': [8], 'ml_norm_w': [D], 'ml_skip': [D],
        'od_w_in': [D, 5120], 'od_w_qksw': [D, 1024], 'fnw_b': [128, D], 'od_w_out': [2048, D], 'ret_norm_w': [D],
        'lru_conv_w': [4, D], 'lru_conv_b': [D], 'lru_w_a': [8, 128, 128], 'lru_b_a': [D],
        'lru_w_x': [8, 128, 128], 'lru_b_x': [D], 'lru_lambda': [D],
        'ropeF_c': [128, T + 16], 'ropeF_s': [128, T + 16], 'ropeT_c': [T + 16, 64], 'ropeT_s': [T + 16, 64],
    }
    for k, v in host_consts().items():
        sp[k] = list(v.shape)
    return sp


def out_specs(cfg):
    n_p, T_x, n_s = cfg.n_p, cfg.T_x, cfg.n_s
    o = {'yp': [n_p, T_x, D], 'ys': [n_s, 16, D]}
    for g, n in (('p', n_p), ('s', n_s)):
        o['o_s5re_' + g] = [n, 2048]
        o['o_s5im_' + g] = [n, 2048]
        o['o_mlc_' + g] = [n, 4, 256, 256]
        o['o_mln_' + g] = [n, 4, 256]
        o['o_mlm_' + g] = [n, 4]
        o['o_mlconv_' + g] = [n, 3, D]
        o['o_ret_' + g] = [n, 4, 128, 256]
        o['o_lruh_' + g] = [n, D]
        o['o_lruconv_' + g] = [n, 3, D]
    return o


def build(cfg):
    nc = bass.Bass("TRN2", target_bir_lowering=False)
    n_p, T_x, n_s = cfg.n_p, cfg.T_x, cfg.n_s
    T = 16 + T_x
    I = {k: nc.dram_tensor(k, v, F32, kind="ExternalInput").ap() for k, v in in_specs(cfg).items()}
    O = {k: nc.dram_tensor(k, v, F32, kind="ExternalOutput").ap() for k, v in out_specs(cfg).items()}
    hk = "ExternalOutput" if cfg.debug else "Internal"
    h1p = nc.dram_tensor("h1p", [n_p, T, D], F32, kind=hk).ap()
    h1s = nc.dram_tensor("h1s", [n_s, 16, D], F32, kind=hk).ap()
    dbg_o = nc.dram_tensor("dbg_o", [4, 128, 256], F32, kind=hk).ap()

    seqs = []
    for i in range(n_p):
        pcs = [(0, 16, I['meta'][:, :], O['yp'], None)]
        for t in range(0, T_x, NPIECE):
            pcs.append((16 + t, NPIECE, I['xp'][i, t:t + NPIECE, :], O['yp'][i, t:t + NPIECE, :], None))
        seqs.append(('p', i, pcs))
    for i in range(n_s):
        seqs.append(('s', i, [(0, 16, I['xs'][i, :, :], O['ys'][i, :, :], None)]))

    def h1rows(g, i, t0, n):
        return (h1p if g == 'p' else h1s)[i, t0:t0 + n, :]

    with ExitStack() as top:
        top.enter_context(nc.allow_non_contiguous_dma(reason='small strided parameter/state layouts'))
        P = Prog(nc)
        for s in P.semnames:
            P.sems[s] = top.enter_context(nc.semaphore(s))
        pb = [top.enter_context(nc.psum_tensor("pb%d" % i, [128, 512], F32)) for i in range(8)]

        V, G, A, PL = nc.vector, nc.gpsimd, nc.scalar, nc.tensor

        def mm(out, lhsT, rhs, start, stop, reads, writes):
            P.op('pe', lambda: PL.matmul(out, lhsT, rhs, start=start, stop=stop), reads=reads, writes=writes)

        sb0 = lambda n, s, d: top.enter_context(nc.sbuf_tensor(n, s, d))
        identb = sb0("identb", [128, 128], BF16)
        identf = sb0("identf", [128, 128], F32)
        negh = sb0("negh", [128, 1], F32)
        P.dma('pool', identb[:], I['c_identb'][:, :], writes=['identb'])
        P.dma('sp', identf[:], I['c_identf'][:, :], writes=['identf'])
        P.op('pool', lambda: G.memset(negh[:], -0.5), writes=['negh'])

        def rmsnorm_T(hin, tp, tt, gcol, ub, uT, ss, hkey='hin', gkey='gcol'):
            P.op('pool', lambda: G.memset(ss[0:tp, 0:1], 0.0), writes=['ss'])
            P.op('act', lambda: A.activation(out=ub[0:tp, :], in_=hin[0:tp, :], func=AF.Square, accum_out=ss[0:tp, 0:1]),
                 reads=[hkey], writes=['ub', 'ss'])
            P.op('dve', lambda: V.tensor_scalar(ss[0:tp, 1:2], ss[0:tp, 0:1], 1.0 / D, EPS, ALU.mult, ALU.add),
                 reads=['ss'], writes=['ss'])
            P.op('pool', lambda: G.tensor_tensor(ss[0:tp, 2:3], ss[0:tp, 1:2], negh[0:tp, :], ALU.pow),
                 reads=['ss', 'negh'], writes=['ss'])
            P.op('act', lambda: A.activation(out=ub[0:tp, :], in_=hin[0:tp, :], func=AF.Copy, scale=ss[0:tp, 2:3]),
                 reads=[hkey, 'ss'], writes=['ub'])
            pT = pb[2][:].bitcast(BF16)
            for k in range(8):
                P.op('pe', lambda k=k: PL.transpose(pT[:, k * 128:k * 128 + tp], ub[0:tp, k * 128:(k + 1) * 128],
                                                    identb[0:tp, 0:tp]),
                     reads=['ub', 'identb'], writes=['pb2'])
            for k in range(8):
                P.op('act', lambda k=k: A.activation(out=uT[:, k, tt * 128:tt * 128 + tp], in_=pT[:, k * 128:k * 128 + tp],
                                                     func=AF.Copy, scale=gcol[:, k:k + 1]),
                     reads=['pb2', gkey], writes=['uT'])

        with ExitStack() as l0:
            sb = lambda n, s, d: l0.enter_context(nc.sbuf_tensor(n, s, d))
            NP = NPIECE
            w_in = sb("w_in0", [128, 8, 3072], BF16)
            w_out = sb("w_out0", [128, 12, D], BF16)
            w_glu = sb("w_glu", [128, 4, 512], BF16)
            wq = sb("wq", [128, 4, 2, 256], BF16)
            wk = sb("wk", [128, 4, 2, 256], BF16)
            wv = sb("wv", [128, 4, 2, 256], BF16)
            wcif = sb("wcif", [128, 8, 8], BF16)
            wxif = sb("wxif", [128, 8, 8], BF16)
            s5B = sb("s5B", [128, 2, 16, 128], BF16)
            s5C = sb("s5C", [128, 2, 16, 128], BF16)
            LR = 64
            tabc = sb("tabc", [128, 16, LR], F32)
            tabs = sb("tabs", [128, 16, LR], F32)
            rtab = sb("rtab", [128, 16, 2], F32)
            rtab0 = sb("rtab0", [128, 16, LR], F32)
            gcol = sb("gcol0", [128, 8], F32)
            dcol = sb("dcol", [128, 4], F32)
            bgluh = sb("bgluh", [128, 4], F32)
            convw = sb("convw", [128, 8, 4], F32)
            convb = sb("convb", [128, 8], F32)
            mlnw = sb("mlnw", [128, 8], F32)
            skipc = sb("skipc", [128, 8], F32)
            b_i = sb("b_i", [4, 1], F32)
            b_f = sb("b_f", [4, 1], F32)
            negmask = sb("negmask", [128, 128], F32)
            sel = sb("sel", [4, 4, 128], F32)
            ones4 = sb("ones4", [4, 128], F32)
            onesrow = sb("onesrow", [4, NP], F32)

            for k in range(8):
                P.dma('pool', w_in[:, k, :], I['ev_w_in'][k * 128:(k + 1) * 128, :], writes=['w_in'])
            for k in range(12):
                P.dma('pool', w_out[:, k, :], I['ev_w_out'][k * 128:(k + 1) * 128, :], writes=['w_out'])
            P.dma('pool', w_glu[:], I['s5_w_glu'].rearrange("(k p) f -> p k f", p=128), writes=['w_glu'])
            for nm, t_ in (('ml_wq', wq), ('ml_wk', wk), ('ml_wv', wv)):
                for h in range(4):
                    P.dma('pool', t_[:, h, :, :], I[nm][h].rearrange("(k p) e -> p k e", p=128), writes=[nm])
            P.dma('sp', gcol[:], I['norm_w'][0, :].rearrange("(k p) -> p k", p=128), writes=['gcol'])
            P.dma('sp', dcol[:], I['s5_d'].rearrange("(k p) -> p k", p=128), writes=['dcol'])
            P.dma('sp', bgluh[:], I['s5_b_glu'].rearrange("(k p) -> p k", p=128), writes=['bgluh'])
            for tap in range(4):
                P.dma('sp', convw[:, :, tap], I['ml_conv_w'][tap, :].rearrange("(k p) -> p k", p=128), writes=['convw'])
            P.dma('sp', convb[:], I['ml_conv_b'].rearrange("(k p) -> p k", p=128), writes=['convb'])
            P.dma('sp', mlnw[:], I['ml_norm_w'].rearrange("(k p) -> p k", p=128), writes=['mlnw'])
            P.dma('sp', skipc[:], I['ml_skip'].rearrange("(k p) -> p k", p=128), writes=['skipc'])
            P.dma('sp', b_i[:], I['ml_b_if'][0:4].rearrange("(p o) -> p o", o=1), writes=['b_i'])
            P.dma('sp', b_f[:], I['ml_b_if'][4:8].rearrange("(p o) -> p o", o=1), writes=['b_f'])
            P.dma('sp', negmask[:], I['c_negmask'][:, :], writes=['negmask'])
            P.dma('sp', sel[:], I['c_sel'][:, :, :], writes=['sel'])
            P.dma('sp', ones4[:], I['c_ones4'][:, :], writes=['ones4'])
            P.op('pool', lambda: G.memset(onesrow[:], 1.0), writes=['onesrow'])
            P.op('dve', lambda: V.tensor_scalar(bgluh[:], bgluh[:], 0.5, None, ALU.mult), reads=['bgluh'], writes=['bgluh'])
            P.op('dve', lambda: V.tensor_scalar(b_f[:], b_f[:], -1.0, None, ALU.mult), reads=['b_f'], writes=['b_f'])

            print('SBUF remaining before setup', nc.sbuf_bytes_remaining)
            with ExitStack() as su:
                sbs = lambda n, s, d: su.enter_context(nc.sbuf_tensor(n, s, d))
                wqT = sbs("wqT", [128, 4, 2, 256], BF16)
                wkT = sbs("wkT", [128, 4, 2, 256], BF16)
                wvT = sbs("wvT", [128, 4, 2, 256], BF16)
                wif = sbs("wif", [128, 24, 8], BF16)
                for nm, t_ in (('ml_wqT', wqT), ('ml_wkT', wkT), ('ml_wvT', wvT)):
                    for h in range(4):
                        P.dma('pool', t_[:, h, :, :], I[nm][h].rearrange("(k p) e -> p k e", p=128), writes=[nm])
                P.dma('pool', wif[:], I['ml_w_if'].rearrange("(k p) g -> p k g", p=128), writes=['wif'])
                for h in range(4):
                    for dt_ in range(2):
                        o_ = pb[0][:, 0:8]
                        i_ = 0
                        for (wT, nm, off) in ((wqT, 'ml_wqT', 0), (wkT, 'ml_wkT', 8)):
                            for ke in range(2):
                                mm(o_, wT[:, h, ke, dt_ * 128:(dt_ + 1) * 128], wif[:, off + h * 2 + ke, :],
                                   i_ == 0, i_ == 3, [nm, 'wif'], ['pb0'])
                                i_ += 1
                        P.op('act', lambda h=h, dt_=dt_: A.copy(wcif[:, h * 2 + dt_, :], pb[0][:, 0:8]),
                             reads=['pb0'], writes=['wcif'])
                        o2 = pb[1][:, 0:8]
                        for ke in range(2):
                            mm(o2, wvT[:, h, ke, dt_ * 128:(dt_ + 1) * 128], wif[:, 16 + h * 2 + ke, :],
                               ke == 0, ke == 1, ['ml_wvT', 'wif'], ['pb1'])
                        P.op('act', lambda h=h, dt_=dt_: A.copy(wxif[:, h * 2 + dt_, :], pb[1][:, 0:8]),
                             reads=['pb1'], writes=['wxif'])

                def S(n, shp):
                    return sbs(n, shp, F32)

                def trig(name, ang_ap, shp, key):
                    cosT = S(name + "_cos", shp)
                    sinT = S(name + "_sin", shp)
                    sh = S(name + "_sh", shp)
                    acc = S(name + "_acc", shp)
                    tmp = S(name + "_tm", shp)
                    for (dst, shift) in ((cosT, PI / 2), (sinT, 0.0)):
                        P.op('dve', lambda shift=shift: V.tensor_scalar(sh[:], ang_ap, shift, None, ALU.add),
                             reads=[key], writes=[name + 'sh'])
                        P.op('dve', lambda: V.tensor_copy(acc[:], sh[:]), reads=[name + 'sh'], writes=[name + 'win_re'])
                        for m_ in range(1, 7):
                            thr = (2 * m_ - 1) * PI
                            P.op('dve', lambda thr=thr: V.tensor_scalar(tmp[:], sh[:], thr, -2 * PI, ALU.is_ge, ALU.mult),
                                 reads=[name + 'sh'], writes=[name + 'tm'])
                            P.op('dve', lambda: V.tensor_tensor(acc[:], acc[:], tmp[:], ALU.add),
                                 reads=[name + 'win_re', name + 'tm'], writes=[name + 'win_re'])
                        P.op('act', lambda dst=dst: A.activation(out=dst[:], in_=acc[:], func=AF.Sin),
                             reads=[name + 'win_re'], writes=[name + ('c' if dst is cosT else 's')])
                    return cosT, sinT

                shA = [128, 4, 64]
                lamreA = S("lamreA", shA); lamimA = S("lamimA", shA); dtA = S("dtA", shA)
                breA = S("breA", shA); bimA = S("bimA", shA)
                P.dma('sp', lamreA[:], I['lamre_A'][:, :, :], writes=['lamreA'])
                P.dma('sp', lamimA[:], I['lamim_A'][:, :, :], writes=['lamimA'])
                P.dma('sp', dtA[:], I['logdt_A'][:, :, :], writes=['dtA'])
                P.dma('sp', breA[:], I['bre_A'][:, :, :], writes=['breA'])
                P.dma('sp', bimA[:], I['bim_A'][:, :, :], writes=['bimA'])
                P.op('act', lambda: A.activation(out=dtA[:], in_=dtA[:], func=AF.Exp), reads=['dtA'], writes=['dtA'])
                magA = S("magA", shA); angA = S("angA", shA)
                P.op('dve', lambda: V.tensor_tensor(magA[:], lamreA[:], dtA[:], ALU.mult), reads=['lamreA', 'dtA'], writes=['magA'])
                P.op('act', lambda: A.activation(out=magA[:], in_=magA[:], func=AF.Exp), reads=['magA'], writes=['magA'])
                P.op('dve', lambda: V.tensor_tensor(angA[:], lamimA[:], dtA[:], ALU.mult), reads=['lamimA', 'dtA'], writes=['angA'])
                cosA, sinA = trig("tA", angA[:], shA, 'angA')
                abre = S("abre", shA); abim = S("abim", shA)
                P.op('dve', lambda: V.tensor_tensor(abre[:], magA[:], cosA[:], ALU.mult), reads=['magA', 'tAc'], writes=['abre'])
                P.op('dve', lambda: V.tensor_tensor(abim[:], magA[:], sinA[:], ALU.mult), reads=['magA', 'tAs'], writes=['abim'])
                den = S("denA", shA); t1 = S("t1A", shA); t2 = S("t2A", shA); kre = S("kreA", shA); kim = S("kimA", shA)
                P.op('dve', lambda: V.tensor_tensor(den[:], lamreA[:], lamreA[:], ALU.mult), reads=['lamreA'], writes=['denA'])
                P.op('dve', lambda: V.tensor_tensor(t1[:], lamimA[:], lamimA[:], ALU.mult), reads=['lamimA'], writes=['t1A'])
                P.op('dve', lambda: V.tensor_tensor(den[:], den[:], t1[:], ALU.add), reads=['denA', 't1A'], writes=['denA'])
                P.op('dve', lambda: V.reciprocal(den[:], den[:]), reads=['denA'], writes=['denA'])
                P.op('dve', lambda: V.tensor_scalar(abre[:], abre[:], -1.0, None, ALU.add), reads=['abre'], writes=['abre'])
                P.op('dve', lambda: V.tensor_tensor(t1[:], abre[:], lamreA[:], ALU.mult), reads=['abre', 'lamreA'], writes=['t1A'])
                P.op('dve', lambda: V.tensor_tensor(t2[:], abim[:], lamimA[:], ALU.mult), reads=['abim', 'lamimA'], writes=['t2A'])
                P.op('dve', lambda: V.tensor_tensor(t1[:], t1[:], t2[:], ALU.add), reads=['t1A', 't2A'], writes=['t1A'])
                P.op('dve', lambda: V.tensor_tensor(kre[:], t1[:], den[:], ALU.mult), reads=['t1A', 'denA'], writes=['kreA'])
                P.op('dve', lambda: V.tensor_tensor(t1[:], abim[:], lamreA[:], ALU.mult), reads=['abim', 'lamreA'], writes=['t1A'])
                P.op('dve', lambda: V.tensor_tensor(t2[:], abre[:], lamimA[:], ALU.mult), reads=['abre', 'lamimA'], writes=['t2A'])
                P.op('dve', lambda: V.tensor_tensor(t1[:], t1[:], t2[:], ALU.subtract), reads=['t1A', 't2A'], writes=['t1A'])
                P.op('dve', lambda: V.tensor_tensor(kim[:], t1[:], den[:], ALU.mult), reads=['t1A', 'denA'], writes=['kimA'])
                bbre = S("bbre", shA); bbim = S("bbim", shA)
                P.op('dve', lambda: V.tensor_tensor(t1[:], kre[:], breA[:], ALU.mult), reads=['kreA', 'breA'], writes=['t1A'])
                P.op('dve', lambda: V.tensor_tensor(t2[:], kim[:], bimA[:], ALU.mult), reads=['kimA', 'bimA'], writes=['t2A'])
                P.op('dve', lambda: V.tensor_tensor(bbre[:], t1[:], t2[:], ALU.subtract), reads=['t1A', 't2A'], writes=['bbre'])
                P.op('dve', lambda: V.tensor_tensor(t1[:], kre[:], bimA[:], ALU.mult), reads=['kreA', 'bimA'], writes=['t1A'])
                P.op('dve', lambda: V.tensor_tensor(t2[:], kim[:], breA[:], ALU.mult), reads=['kimA', 'breA'], writes=['t2A'])
                P.op('dve', lambda: V.tensor_tensor(bbim[:], t1[:], t2[:], ALU.add), reads=['t1A', 't2A'], writes=['bbim'])
                maskA = S("maskA", [128, 8])
                P.dma('sp', maskA[:], I['c_maskA'][:, :], writes=['maskA'])
                for j in range(16):
                    kc, jl = j // 4, j % 4
                    for h in range(2):
                        g8 = 2 * jl + h
                        for part, src, key in ((0, bbre, 'bbre'), (1, bbim, 'bbim')):
                            P.op('dve', lambda j=j, kc=kc, h=h, g8=g8, part=part, src=src: V.tensor_scalar(
                                s5B[:, part, j, h * 64:(h + 1) * 64], src[:, kc, :], maskA[:, g8:g8 + 1], None, ALU.mult),
                                reads=[key, 'maskA'], writes=['s5B'])
                creS = S("creS", [128, 16, 16]); cimS = S("cimS", [128, 16, 16])
                P.dma('sp', creS[:], I['cre_S'][:, :, :], writes=['creS'])
                P.dma('sp', cimS[:], I['cim_S'][:, :, :], writes=['cimS'])
                P.op('pool', lambda: G.memset(s5C[:], 0.0), writes=['s5C'])
                for j in range(16):
                    jl = j % 4
                    for h in range(2):
                        g8 = 2 * jl + h
                        P.op('dve', lambda j=j, h=h, g8=g8: V.tensor_copy(
                            s5C[h * 64:(h + 1) * 64, 0, j, g8 * 16:(g8 + 1) * 16], creS[h * 64:(h + 1) * 64, j, :]),
                            reads=['creS'], writes=['s5C'])
                        P.op('dve', lambda j=j, h=h, g8=g8: V.tensor_scalar(
                            s5C[h * 64:(h + 1) * 64, 1, j, g8 * 16:(g8 + 1) * 16], cimS[h * 64:(h + 1) * 64, j, :],
                            -1.0, None, ALU.mult), reads=['cimS'], writes=['s5C'])
                shS = [128, 16]
                lamreS = S("lamreS", shS); lamimS = S("lamimS", shS); dtS = S("dtS", shS)
                P.dma('sp', lamreS[:], I['lamre_S'][:, :], writes=['lamreS'])
                P.dma('sp', lamimS[:], I['lamim_S'][:, :], writes=['lamimS'])
                P.dma('sp', dtS[:], I['logdt_S'][:, :], writes=['dtS'])
                P.op('act', lambda: A.activation(out=dtS[:], in_=dtS[:], func=AF.Exp), reads=['dtS'], writes=['dtS'])
                rmag = S("rmag", shS); angS = S("angS", shS)
                P.op('dve', lambda: V.tensor_tensor(rmag[:], lamreS[:], dtS[:], ALU.mult), reads=['lamreS', 'dtS'], writes=['rmag'])
                P.op('act', lambda: A.activation(out=rmag[:], in_=rmag[:], func=AF.Exp), reads=['rmag'], writes=['rmag'])
                P.op('dve', lambda: V.tensor_tensor(angS[:], lamimS[:], dtS[:], ALU.mult), reads=['lamimS', 'dtS'], writes=['angS'])
                cosS, sinS = trig("tS", angS[:], shS, 'angS')
                onesL = S("onesL", [128, LR])
                P.op('pool', lambda: G.memset(onesL[:], 1.0), writes=['onesL'])
                for j in range(16):
                    P.op('dve', lambda j=j: V.tensor_scalar(rtab[:, j, :], onesL[:, 0:2], rmag[:, j:j + 1], None, ALU.mult),
                         reads=['rmag', 'onesL'], writes=['rtab'])
                for j in range(16):
                    P.op('dve', lambda j=j: V.tensor_scalar(rtab0[:, j, :], onesL[:], rmag[:, j:j + 1], None, ALU.mult),
                         reads=['rmag', 'onesL'], writes=['rtab0'])
                P.op('pool', lambda: G.memset(rtab0[:, :, 0:1], 0.0), reads=['rtab0'], writes=['rtab0'])
                P.op('dve', lambda: V.tensor_copy(tabc[:, :, 0:1], cosS[:].unsqueeze(2)), reads=['tSc'], writes=['tabc'])
                P.op('dve', lambda: V.tensor_copy(tabs[:, :, 0:1], sinS[:].unsqueeze(2)), reads=['tSs'], writes=['tabs'])
                d1 = S("dbl1", [128, LR // 2])
                m_ = 1
                while m_ < LR:
                    for j in range(16):
                        cm, sm = tabc[:, j, m_ - 1:m_], tabs[:, j, m_ - 1:m_]
                        lo_c, lo_s = tabc[:, j, 0:m_], tabs[:, j, 0:m_]
                        hi_c, hi_s = tabc[:, j, m_:2 * m_], tabs[:, j, m_:2 * m_]
                        a1 = d1[:, 0:m_]
                        P.op('dve', lambda: V.tensor_scalar(a1, lo_s, sm, None, ALU.mult), reads=['tabs'], writes=['dbl1'])
                        P.op('dve', lambda: V.scalar_tensor_tensor(hi_c, lo_c, cm, a1, ALU.mult, ALU.subtract),
                             reads=['tabc', 'dbl1'], writes=['tabc'])
                        P.op('dve', lambda: V.tensor_scalar(a1, lo_c, sm, None, ALU.mult), reads=['tabc', 'tabs'], writes=['dbl1'])
                        P.op('dve', lambda: V.scalar_tensor_tensor(hi_s, lo_s, cm, a1, ALU.mult, ALU.add),
                             reads=['tabs', 'tabc', 'dbl1'], writes=['tabs'])
                    m_ *= 2
                P.barrier()
            print('SBUF remaining after weights/setup', nc.sbuf_bytes_remaining)
            L_ = {}
            hin = sb("hin", [128, D], F32)
            ss = sb("ss", [128, 4], F32)
            ub = sb("ub", [128, D], BF16)
            uT = sb("uT", [128, 8, NP], BF16)
            gated = sb("gated", [128, 12, NP], BF16)
            ua = sb("ua", [128, 4, NP], BF16)
            sza = sb("sza", [128, 4, NP], BF16)
            tA = sb("tA_", [128, NP], F32)
            tB = sb("tB_", [128, NP], F32)
            wre = sb("wre", [128, NP // LR, 4, LR], F32)
            wim = sb("wim", [128, NP // LR, 4, LR], F32)
            tA2 = sb("tA2_", [128, NP], F32)
            tB2 = sb("tB2_", [128, NP], F32)
            hsre = sb("hsre", [128, 4, NP], BF16)
            hsim = sb("hsim", [128, 4, NP], BF16)
            Hre = sb("Hre", [128, 16], F32)
            Him = sb("Him", [128, 16], F32)
            ch1 = sb("ch1", [128, 4], F32)
            ch2 = sb("ch2", [128, 4], F32)
            ch3 = sb("ch3", [128, 4], F32)
            ch4 = sb("ch4", [128, 4], F32)
            y1 = sb("y1", [128, NP], F32)
            y2 = sb("y2", [128, NP + 1], F32)
            y3 = sb("y3", [128, NP + 1], F32)
            yg = sb("yg", [128, 4, NP], BF16)
            xb = sb("xb", [128, 8, 3 + NP], BF16)
            szb = sb("szb", [128, 8, NP], BF16)
            acc = sb("acc", [128, NP], F32)
            xc = sb("xc", [128, 8, NP], BF16)
            qT = uT
            kT = sb("kT", [128, 8, NP], BF16)
            vtok = sb("vtok", [128, 4, 264], BF16)
            ktok = sb("ktok", [128, 4, 256], BF16)
            Cst = sb("Cst", [128, 4, 2, 264], F32)
            Cbf = sb("Cbf", [128, 4, 2, 264], BF16)
            r_ig = sb("r_ig", [4, NP], F32)
            r_lf = sb("r_lf", [4, NP], F32)
            Mfull = sb("Mfull", [4, NP + 1], F32)
            negM = sb("negM", [4, NP + 1], F32)
            r_wi = sb("r_wi", [4, NP], F32)
            r_g = sb("r_g", [4, NP], F32)
            r_emm = sb("r_emm", [4, NP], F32)
            r_wend = sb("r_wend", [4, NP], F32)
            r_e = r_wend
            mcarry = sb("mcarry", [4, 1], F32)
            dg4 = sb("dg4", [4, 4], F32)
            cols = sb("cols", [128, 20], F32)
            Esb = y1[:, 0:128]
            Dm = y1[:, 128:256]
            PT = sb("PT", [128, 128], BF16)
            kw = sb("kw", [128, 256], BF16)
            inter = y2
            tot = y3
            st6 = sb("st6", [128, 6], F32)
            st2 = sb("st2", [128, 8], F32)
            hn = ub
            ytmp = sb("ytmp", [128, 128], F32)
            xbo = sb("xbo", [128, 8, 3], F32)

            print('SBUF remaining after L0 activations', nc.sbuf_bytes_remaining)
            P.op('pool', lambda: G.memset(vtok[:], 1.0), writes=['vtok'])

            mmreg = [(pb[0], 'pb0'), (pb[1], 'pb1')]
            mmi = [0]

            def nextmm():
                r_ = mmreg[mmi[0] % 2]
                mmi[0] += 1
                return r_

            def inproj(f, n):
                ps, key = nextmm()
                for k in range(8):
                    mm(ps[:, 0:n], w_in[:, k, f * 128:(f + 1) * 128], uT[:, k, 0:n], k == 0, k == 7, ['w_in', 'uT'], [key])
                return ps, key

            for (g, si, pcs) in seqs:
                if g == 'p':
                    P.op('pool', lambda: G.memset(Hre[:], 0.0), writes=['Hre'])
                    P.op('pool', lambda: G.memset(Him[:], 0.0), writes=['Him'])
                    P.op('pool', lambda: G.memset(Cst[:], 0.0), writes=['Cst%d' % h_ for h_ in range(4)])
                    P.op('pool', lambda: G.memset(mcarry[:], 0.0), writes=['mcarry'])
                    P.op('pool', lambda: G.memset(xb[:, :, 0:3], 0.0), writes=['xb'])
                else:
                    P.dma('sp', Hre[:], I['s5re_in'][si, :].rearrange("(j q) -> q j", q=128), writes=['Hre'])
                    P.dma('sp', Him[:], I['s5im_in'][si, :].rearrange("(j q) -> q j", q=128), writes=['Him'])
                    for h in range(4):
                        P.dma('sp', Cst[:, h, :, 0:256], I['mlc_in'][si, h].rearrange("(k p) e -> p k e", p=128), writes=['Cst%d' % h])
                        P.dma('sp', Cst[:, h, :, 256], I['mln_in'][si, h].rearrange("(k p) -> p k", p=128), writes=['Cst%d' % h])
                    P.dma('sp', mcarry[:], I['mlm_in'][si, :].rearrange("(p o) -> p o", o=1), writes=['mcarry'])
                    for t_ in range(3):
                        P.dma('pool', xb[:, :, t_], I['mlconv_in'][si, t_].rearrange("(k p) -> p k", p=128), writes=['xb'])
                P.op('act', lambda: A.copy(Cbf[:], Cst[:]), reads=['Cst%d' % h_ for h_ in range(4)], writes=['Cbf%d' % h_ for h_ in range(4)])

                def piece_head(t0, n, src):
                    tp = min(n, 128)
                    ntt = (n + 127) // 128
                    L = tp
                    nch = n // L
                    for tt in range(ntt):
                        P.dma('sp', hin[0:tp, :], src[tt * 128:tt * 128 + tp, :], writes=['hin'])
                        rmsnorm_T(hin, tp, tt, gcol, ub, uT, ss)
                    for f in range(4):
                        ps, key = inproj(f, n)
                        P.op('act', lambda f=f, ps=ps: A.copy(ua[:, f, 0:n], ps[:, 0:n]), reads=[key], writes=['ua'])
                    for f in range(4):
                        ps, key = inproj(4 + f, n)
                        P.op('act', lambda f=f, ps=ps: A.activation(out=sza[:, f, 0:n], in_=ps[:, 0:n], func=AF.Silu),
                             reads=[key], writes=['sza'])

                def piece_middle(t0, n, src, extra):
                    tp = min(n, 128)
                    ntt = (n + 127) // 128
                    L = tp
                    nch = n // L
                    def gen_s5():
                        Ls = min(n, LR)
                        ncs = n // Ls
                        def v3(ap2):
                            return ap2.rearrange("p (c l) -> p c l", c=ncs)

                        for kc in range(4):
                            tcs, tss = [], []
                            for jl in range(4):
                                j = 4 * kc + jl
                                sreg = (pb[4], 'pb4') if j % 2 == 0 else (pb[5], 'pb5')
                                pre, pim = sreg[0][:, 0:n], sreg[0][:, 256:256 + n]
                                mm(pre, s5B[:, 0, j, :], ua[:, kc, 0:n], True, True, ['s5B', 'ua'], [sreg[1]])
                                mm(pim, s5B[:, 1, j, :], ua[:, kc, 0:n], True, True, ['s5B', 'ua'], [sreg[1]])
                                tc_ = tabc[:, j, 0:Ls].unsqueeze(1).to_broadcast([128, ncs, Ls])
                                ts_ = tabs[:, j, 0:Ls].unsqueeze(1).to_broadcast([128, ncs, Ls])
                                tcs.append(tc_); tss.append(ts_)
                                P.op('dve', lambda pre=pre, tc_=tc_: V.tensor_tensor(v3(tA[:, 0:n]), v3(pre), tc_, ALU.mult),
                                     reads=[sreg[1], 'tabc'], writes=['tA'])
                                P.op('dve', lambda pim=pim, ts_=ts_: V.tensor_tensor(v3(tB[:, 0:n]), v3(pim), ts_, ALU.mult),
                                     reads=[sreg[1], 'tabs'], writes=['tB'])
                                P.op('pool', lambda jl=jl: G.tensor_tensor(wre[:, 0:ncs, jl, 0:Ls], v3(tA[:, 0:n]), v3(tB[:, 0:n]), ALU.add),
                                     reads=['tA', 'tB'], writes=['wre%d' % jl])
                                P.op('dve', lambda pim=pim, tc_=tc_: V.tensor_tensor(v3(tA2[:, 0:n]), v3(pim), tc_, ALU.mult),
                                     reads=[sreg[1], 'tabc'], writes=['tA2'])
                                P.op('dve', lambda pre=pre, ts_=ts_: V.tensor_tensor(v3(tB2[:, 0:n]), v3(pre), ts_, ALU.mult),
                                     reads=[sreg[1], 'tabs'], writes=['tB2'])
                                P.op('pool', lambda jl=jl: G.tensor_tensor(wim[:, 0:ncs, jl, 0:Ls], v3(tA2[:, 0:n]), v3(tB2[:, 0:n]), ALU.subtract),
                                     reads=['tA2', 'tB2'], writes=['wim%d' % jl])
                                if jl % 2 == 1:
                                    yield
                            j0 = 4 * kc
                            for c in range(ncs):
                                cs = slice(c * Ls, (c + 1) * Ls)
                                if c == 0:
                                    i4r, i4i, rk = Hre[:, j0:j0 + 4], Him[:, j0:j0 + 4], ['Hre', 'Him']
                                else:
                                    i4r, i4i, rk = ch1[:, 0:4], ch2[:, 0:4], ['ch1', 'ch2']
                                w4r_ = ['wre%d' % i_ for i_ in range(4)]
                                w4i_ = ['wim%d' % i_ for i_ in range(4)]
                                r4 = rtab[:, j0:j0 + 4, 1]
                                c0 = c * Ls
                                P.op('dve', lambda: V.tensor_tensor(ch3[:, 0:4], i4r, r4, ALU.mult), reads=rk + ['rtab'], writes=['ch3'])
                                P.op('dve', lambda: V.tensor_tensor(ch4[:, 0:4], i4i, r4, ALU.mult), reads=rk + ['rtab'], writes=['ch4'])
                                P.op('dve', lambda: V.tensor_tensor(wre[:, c, :, 0], wre[:, c, :, 0], ch3[:, 0:4], ALU.add), reads=w4r_ + ['ch3'], writes=w4r_)
                                P.op('dve', lambda: V.tensor_tensor(wim[:, c, :, 0], wim[:, c, :, 0], ch4[:, 0:4], ALU.add), reads=w4i_ + ['ch4'], writes=w4i_)
                                if Ls == LR:
                                    d0 = rtab0[:, j0:j0 + 4, :].rearrange("p j l -> p (j l)")
                                    P.op('dve', lambda: V.tensor_tensor_scan(wre[:, c, :, :].rearrange("p j l -> p (j l)"), d0,
                                                                             wre[:, c, :, :].rearrange("p j l -> p (j l)"), 0.0, ALU.mult, ALU.add),
                                         reads=w4r_ + ['rtab0'], writes=w4r_)
                                    P.op('dve', lambda: V.tensor_tensor_scan(wim[:, c, :, :].rearrange("p j l -> p (j l)"), d0,
                                                                             wim[:, c, :, :].rearrange("p j l -> p (j l)"), 0.0, ALU.mult, ALU.add),
                                         reads=w4i_ + ['rtab0'], writes=w4i_)
                                else:
                                    for jl in range(4):
                                        P.op('dve', lambda jl=jl: V.tensor_tensor_scan(wre[:, c, jl, 0:Ls], rtab0[:, j0 + jl, 0:Ls], wre[:, c, jl, 0:Ls], 0.0,
                                                                                       ALU.mult, ALU.add), reads=w4r_ + ['rtab0'], writes=w4r_)
                                        P.op('dve', lambda jl=jl: V.tensor_tensor_scan(wim[:, c, jl, 0:Ls], rtab0[:, j0 + jl, 0:Ls], wim[:, c, jl, 0:Ls], 0.0,
                                                                                       ALU.mult, ALU.add), reads=w4i_ + ['rtab0'], writes=w4i_)
                                e_ = (c + 1) * Ls - 1
                                last = (c == ncs - 1)
                                wlr, wli = wre[:, c, :, Ls - 1], wim[:, c, :, Ls - 1]
                                cL4, sL4 = tabc[:, j0:j0 + 4, Ls - 1], tabs[:, j0:j0 + 4, Ls - 1]
                                dre = Hre[:, j0:j0 + 4] if last else ch1[:, 0:4]
                                dim_ = Him[:, j0:j0 + 4] if last else ch2[:, 0:4]
                                kre_ = 'Hre' if last else 'ch1'
                                kim_ = 'Him' if last else 'ch2'
                                w4r = ['wre%d' % i_ for i_ in range(4)]
                                w4i = ['wim%d' % i_ for i_ in range(4)]
                                P.op('dve', lambda: V.tensor_tensor(ch3[:, 0:4], wli, sL4, ALU.mult), reads=w4i + ['tabs'], writes=['ch3'])
                                P.op('dve', lambda: V.tensor_tensor(ch4[:, 0:4], wlr, sL4, ALU.mult), reads=w4r + ['tabs'], writes=['ch4'])
                                P.op('dve', lambda: V.tensor_tensor(dre, wlr, cL4, ALU.mult), reads=w4r + ['tabc', kre_], writes=[kre_])
                                P.op('dve', lambda: V.tensor_tensor(dim_, wli, cL4, ALU.mult), reads=w4i + ['tabc', kim_], writes=[kim_])
                                P.op('dve', lambda: V.tensor_tensor(dre, dre, ch3[:, 0:4], ALU.subtract), reads=[kre_, 'ch3'], writes=[kre_])
                                P.op('dve', lambda: V.tensor_tensor(dim_, dim_, ch4[:, 0:4], ALU.add), reads=[kim_, 'ch4'], writes=[kim_])
                                if c % 2 == 1 or last:
                                    yield
                            for jl in range(4):
                                tc_, ts_ = tcs[jl], tss[jl]
                                P.op('dve', lambda jl=jl, tc_=tc_: V.tensor_tensor(v3(tA[:, 0:n]), wre[:, 0:ncs, jl, 0:Ls], tc_, ALU.mult),
                                     reads=['wre%d' % jl, 'tabc'], writes=['tA'])
                                P.op('dve', lambda jl=jl, ts_=ts_: V.tensor_tensor(v3(tB[:, 0:n]), wim[:, 0:ncs, jl, 0:Ls], ts_, ALU.mult),
                                     reads=['wim%d' % jl, 'tabs'], writes=['tB'])
                                P.op('pool', lambda jl=jl: G.tensor_tensor(hsre[:, jl, 0:n], tA[:, 0:n], tB[:, 0:n], ALU.subtract),
                                     reads=['tA', 'tB'], writes=['hsre%d' % jl])
                                P.op('dve', lambda jl=jl, tc_=tc_: V.tensor_tensor(v3(tA2[:, 0:n]), wim[:, 0:ncs, jl, 0:Ls], tc_, ALU.mult),
                                     reads=['wim%d' % jl, 'tabc'], writes=['tA2'])
                                P.op('dve', lambda jl=jl, ts_=ts_: V.tensor_tensor(v3(tB2[:, 0:n]), wre[:, 0:ncs, jl, 0:Ls], ts_, ALU.mult),
                                     reads=['wre%d' % jl, 'tabs'], writes=['tB2'])
                                P.op('pool', lambda jl=jl: G.tensor_tensor(hsim[:, jl, 0:n], tA2[:, 0:n], tB2[:, 0:n], ALU.add),
                                     reads=['tA2', 'tB2'], writes=['hsim%d' % jl])
                                if jl % 2 == 1:
                                    yield
                            ps, key = nextmm()
                            for jl in range(4):
                                j = 4 * kc + jl
                                mm(ps[:, 0:n], s5C[:, 0, j, :], hsre[:, jl, 0:n], jl == 0, False, ['s5C', 'hsre%d' % jl], [key])
                                mm(ps[:, 0:n], s5C[:, 1, j, :], hsim[:, jl, 0:n], False, jl == 3, ['s5C', 'hsim%d' % jl], [key])
                            P.op('dve', lambda kc=kc, ps=ps: V.scalar_tensor_tensor(y1[:, 0:n], ua[:, kc, 0:n], dcol[:, kc:kc + 1],
                                                                                   ps[:, 0:n], ALU.mult, ALU.add),
                                 reads=['ua', 'dcol', key], writes=['y1'])
                            P.op('pool', lambda: G.tensor_tensor(y2[:, 0:n], y1[:, 0:n], y1[:, 0:n], ALU.mult), reads=['y1'], writes=['y2'])
                            P.op('pool', lambda: G.tensor_scalar(y2[:, 0:n], y2[:, 0:n], 0.044715, 1.0, ALU.mult, ALU.add),
                                 reads=['y2'], writes=['y2'])
                            P.op('pool', lambda: G.tensor_tensor(y2[:, 0:n], y2[:, 0:n], y1[:, 0:n], ALU.mult), reads=['y2', 'y1'], writes=['y2'])
                            P.op('act', lambda: A.activation(out=y3[:, 0:n], in_=y2[:, 0:n], func=AF.Tanh, scale=math.sqrt(2.0 / PI)),
                                 reads=['y2'], writes=['y3'])
                            P.op('dve', lambda: V.tensor_scalar(y3[:, 0:n], y3[:, 0:n], 0.5, 0.5, ALU.mult, ALU.add), reads=['y3'], writes=['y3'])
                            P.op('dve', lambda kc=kc: V.tensor_tensor(yg[:, kc, 0:n], y3[:, 0:n], y1[:, 0:n], ALU.mult),
                                 reads=['y3', 'y1'], writes=['yg%d' % kc])
                            yield
                        for m_ in range(4):
                            ps, key = nextmm()
                            for kk in range(4):
                                mm(ps[:, 0:n], w_glu[:, kk, m_ * 128:(m_ + 1) * 128], yg[:, kk, 0:n], kk == 0, kk == 3,
                                   ['w_glu', 'yg%d' % kk], [key])
                            P.op('act', lambda m_=m_, ps=ps: A.activation(out=y3[:, 0:n], in_=ps[:, 0:n], func=AF.Tanh, scale=0.5,
                                                                          bias=bgluh[:, m_:m_ + 1]),
                                 reads=[key, 'bgluh'], writes=['y3'])
                            P.op('dve', lambda: V.tensor_scalar(y3[:, 0:n], y3[:, 0:n], 0.5, 0.5, ALU.mult, ALU.add), reads=['y3'], writes=['y3'])
                            P.op('dve', lambda m_=m_: V.tensor_tensor(y3[:, 0:n], y3[:, 0:n], yg[:, m_, 0:n], ALU.mult),
                                 reads=['y3', 'yg%d' % m_], writes=['y3'])
                            P.op('dve', lambda m_=m_: V.tensor_tensor(gated[:, m_, 0:n], y3[:, 0:n], sza[:, m_, 0:n], ALU.mult),
                                 reads=['y3', 'sza'], writes=['gated%d' % m_])
                            yield

                    def gen_ml():
                        for f in range(8):
                            ps, key = inproj(8 + f, n)
                            P.op('act', lambda f=f, ps=ps: A.copy(xb[:, f, 3:3 + n], ps[:, 0:n]), reads=[key], writes=['xb'])
                            if f % 2 == 1:
                                yield
                        for f in range(8):
                            ps, key = inproj(16 + f, n)
                            P.op('act', lambda f=f, ps=ps: A.activation(out=szb[:, f, 0:n], in_=ps[:, 0:n], func=AF.Silu),
                                 reads=[key], writes=['szb'])
                            if f % 2 == 1:
                                yield
                        for k in range(8):
                            P.op('dve', lambda k=k: V.tensor_scalar(acc[:, 0:n], xb[:, k, 0:n], convw[:, k, 0:1], convb[:, k:k + 1],
                                                                    ALU.mult, ALU.add),
                                 reads=['xb', 'convw', 'convb'], writes=['acc'])
                            for tap in range(1, 4):
                                P.op('dve', lambda k=k, tap=tap: V.scalar_tensor_tensor(
                                    acc[:, 0:n], xb[:, k, tap:tap + n], convw[:, k, tap:tap + 1], acc[:, 0:n], ALU.mult, ALU.add),
                                    reads=['xb', 'convw', 'acc'], writes=['acc'])
                            P.op('act', lambda k=k: A.activation(out=xc[:, k, 0:n], in_=acc[:, 0:n], func=AF.Silu),
                                 reads=['acc'], writes=['xc'])
                            yield
                        gip, gikey = nextmm()
                        gfp, gfkey = nextmm()
                        gi, gf = gip[0:4, 0:n], gfp[0:4, 0:n]
                        for gsel, o_, gkey_ in ((0, gi, gikey), (4, gf, gfkey)):
                            for k in range(8):
                                mm(o_, wcif[:, k, gsel:gsel + 4], xc[:, k, 0:n], k == 0, False, ['wcif', 'xc'], [gkey_])
                            for k in range(8):
                                mm(o_, wxif[:, k, gsel:gsel + 4], xb[:, k, 3:3 + n], False, k == 7, ['wxif', 'xb'], [gkey_])
                        P.op('act', lambda: A.activation(out=r_ig[:, 0:n], in_=gi, func=AF.Identity, bias=b_i[:, 0:1]),
                             reads=[gikey, 'b_i'], writes=['r_ig'])
                        P.op('act', lambda: A.activation(out=r_e[:, 0:n], in_=gf, func=AF.Exp, scale=-1.0, bias=b_f[:, 0:1]),
                             reads=[gfkey, 'b_f'], writes=['r_wend'])
                        P.op('act', lambda: A.activation(out=r_e[:, 0:n], in_=r_e[:, 0:n], func=AF.Ln, bias=1.0),
                             reads=['r_wend'], writes=['r_wend'])
                        P.op('dve', lambda: V.tensor_tensor_scan(r_lf[:, 0:n], onesrow[:, 0:n], r_e[:, 0:n], 0.0, ALU.mult, ALU.add),
                             reads=['r_wend', 'onesrow'], writes=['r_lf'])
                        P.op('dve', lambda: V.tensor_tensor(r_ig[:, 0:n], r_ig[:, 0:n], r_lf[:, 0:n], ALU.add),
                             reads=['r_ig', 'r_lf'], writes=['r_ig'])
                        P.op('dve', lambda: V.tensor_copy(Mfull[:, 0:1], mcarry[:, 0:1]), reads=['mcarry'], writes=['Mfull'])
                        P.op('dve', lambda: V.tensor_tensor_scan(Mfull[:, 1:n + 1], onesrow[:, 0:n], r_ig[:, 0:n], mcarry[:, 0:1],
                                                                 ALU.mult, ALU.max),
                             reads=['r_ig', 'onesrow', 'mcarry', 'Mfull'], writes=['Mfull'])
                        P.op('dve', lambda: V.tensor_scalar(negM[:, 0:n + 1], Mfull[:, 0:n + 1], -1.0, None, ALU.mult),
                             reads=['Mfull'], writes=['negM'])
                        P.op('dve', lambda: V.tensor_tensor(r_emm[:, 0:n], r_lf[:, 0:n], Mfull[:, 1:n + 1], ALU.subtract),
                             reads=['r_lf', 'Mfull'], writes=['r_emm'])
                        P.op('act', lambda: A.activation(out=r_emm[:, 0:n], in_=r_emm[:, 0:n], func=AF.Exp), reads=['r_emm'], writes=['r_emm'])
                        P.op('dve', lambda: V.tensor_tensor(mcarry[:, 0:1], Mfull[:, n:n + 1], r_lf[:, n - 1:n], ALU.subtract),
                             reads=['Mfull', 'r_lf'], writes=['mcarry'])
                        for c in range(nch):
                            cs = slice(c * L, (c + 1) * L)
                            P.op('act', lambda c=c, cs=cs: A.activation(out=r_wi[:, cs], in_=Mfull[:, 1 + c * L:1 + (c + 1) * L], func=AF.Exp,
                                                                        scale=-1.0, bias=Mfull[:, c * L:c * L + 1]),
                                 reads=['Mfull'], writes=['r_wi'])
                            P.op('act', lambda c=c, cs=cs: A.activation(out=r_g[:, cs], in_=Mfull[:, 1 + c * L:1 + (c + 1) * L], func=AF.Exp,
                                                                        scale=-1.0, bias=Mfull[:, (c + 1) * L:(c + 1) * L + 1]),
                                 reads=['Mfull'], writes=['r_g'])
                            P.op('act', lambda c=c, cs=cs: A.activation(out=r_wend[:, cs], in_=r_ig[:, cs], func=AF.Exp,
                                                                        bias=negM[:, (c + 1) * L:(c + 1) * L + 1]),
                                 reads=['r_ig', 'negM'], writes=['r_wend'])
                        yield
                        for h in range(4):
                            for e in range(2):
                                for (w_, dst, sc, wkey) in ((wq, qT, 1.0, 'ml_wq'), (wk, kT, 1.0 / 16.0, 'ml_wk')):
                                    ps, key = nextmm()
                                    for kd in range(2):
                                        mm(ps[:, 0:n], w_[:, h, kd, e * 128:(e + 1) * 128], xc[:, 2 * h + kd, 0:n], kd == 0, kd == 1,
                                           [wkey, 'xc'], [key])
                                    P.op('act', lambda ps=ps, dst=dst, h=h, e=e, sc=sc: A.mul(dst[:, 2 * h + e, 0:n], ps[:, 0:n], sc),
                                         reads=[key], writes=['uT'] if dst is qT else ['kT'])
                        for c in range(nch):
                            cs = slice(c * L, (c + 1) * L)
                            tt = c
                            ts0 = c * 128
                            for h in range(4):
                                ps, key = nextmm()
                                for kd in range(2):
                                    mm(ps[0:tp, 0:256], xb[:, 2 * h + kd, 3 + ts0:3 + ts0 + tp], wv[:, h, kd, :], kd == 0, kd == 1,
                                       ['xb', 'ml_wv'], [key])
                                P.op('act', lambda ps=ps, h=h: A.copy(vtok[0:tp, h, 0:256], ps[0:tp, 0:256]),
                                     reads=[key], writes=['vtok'])
                                ps, key = nextmm()
                                for kd in range(2):
                                    mm(ps[0:tp, 0:256], xc[:, 2 * h + kd, ts0:ts0 + tp], wk[:, h, kd, :], kd == 0, kd == 1,
                                       ['xc', 'ml_wk'], [key])
                                P.op('act', lambda ps=ps, h=h: A.mul(ktok[0:tp, h, :], ps[0:tp, 0:256], 1.0 / 16.0),
                                     reads=[key], writes=['ktok'])
                            pc = pb[6]
                            for i_, row in enumerate((r_wi, r_emm, r_wend, r_g)):
                                mm(pc[0:L, 384 + 4 * i_:388 + 4 * i_], row[0:4, cs], identf[0:4, 0:4], True, True,
                                   ['r_wi', 'r_emm', 'r_wend', 'r_g', 'identf'], ['pb6c'])
                            P.op('dve', lambda c=c: V.tensor_scalar(dg4[:, :], identf[0:4, 0:4], r_wi[0:4, (c + 1) * L - 1:(c + 1) * L], None, ALU.mult),
                                 reads=['r_wi', 'identf'], writes=['dg4'])
                            mm(pc[:, 400:404], ones4[0:4, :], dg4[0:4, :], True, True, ['ones4', 'dg4'], ['pb6c'])
                            P.op('dve', lambda pc=pc: V.tensor_copy(cols[:, 0:20], pc[:, 384:404]), reads=['pb6c'], writes=['cols'])
                            yield
                            for h in range(4):
                                stp, stkey = nextmm()
                                ST = stp[0:L, 0:L]
                                for ke in range(2):
                                    mm(ST, kT[:, 2 * h + ke, cs], qT[:, 2 * h + ke, cs], ke == 0, ke == 1, ['kT', 'uT'], [stkey])
                                P.op('dve', lambda ST=ST, h=h: V.scalar_tensor_tensor(PT[0:L, 0:L], ST, cols[0:L, 8 + h:9 + h], negmask[0:L, 0:L],
                                                                                     ALU.mult, ALU.mult),
                                     reads=[stkey, 'cols', 'negmask'], writes=['PT'])
                                mm(pb[7][0:L, 0:257], PT[0:L, 0:L], vtok[0:L, h, 0:257], True, True, ['PT', 'vtok'], ['pb7'])
                                for kd in range(2):
                                    mm(pb[3][0:L, 0:257], qT[:, 2 * h + kd, cs], Cbf[:, h, kd, 0:257], kd == 0, kd == 1, ['uT', 'Cbf%d' % h], ['pb3'])
                                P.op('act', lambda h=h: A.activation(out=inter[0:L, 0:257], in_=pb[3][0:L, 0:257], func=AF.Copy,
                                                                     scale=cols[0:L, h:h + 1]),
                                     reads=['pb3', 'cols'], writes=['y2'])
                                P.op('dve', lambda h=h: V.scalar_tensor_tensor(tot[0:L, 0:257], pb[7][0:L, 0:257], cols[0:L, 12 + h:13 + h],
                                                                            inter[0:L, 0:257], ALU.mult, ALU.add),
                                     reads=['pb7', 'y2', 'cols'], writes=['y3'])
                                P.op('dve', lambda: V.scalar_tensor_tensor(st2[0:L, 2:3], tot[0:L, 256:257], -1.0, tot[0:L, 256:257],
                                                                           ALU.mult, ALU.max), reads=['y3'], writes=['st2c'])
                                P.op('dve', lambda h=h: V.tensor_tensor(st2[0:L, 2:3], st2[0:L, 2:3], cols[0:L, 4 + h:5 + h], ALU.max),
                                     reads=['st2c', 'cols'], writes=['st2c'])
                                P.op('dve', lambda: V.reciprocal(st2[0:L, 3:4], st2[0:L, 2:3]), reads=['st2c'], writes=['st2d'])
                                P.op('dve', lambda: V.bn_stats(st6[0:L, :], tot[0:L, 0:256]), reads=['y3'], writes=['st6'])
                                P.op('dve', lambda: V.bn_aggr(st2[0:L, 4:6], st6[0:L, :]), reads=['st6'], writes=['st2e'])
                                P.op('dve', lambda: V.tensor_tensor(st2[0:L, 6:7], st2[0:L, 3:4], st2[0:L, 3:4], ALU.mult),
                                     reads=['st2d'], writes=['st2f'])
                                P.op('dve', lambda: V.tensor_scalar(st2[0:L, 6:7], st2[0:L, 6:7], st2[0:L, 5:6], EPS, ALU.mult, ALU.add),
                                     reads=['st2f', 'st2e'], writes=['st2f'])
                                P.op('pool', lambda: G.tensor_tensor(st2[0:L, 7:8], st2[0:L, 6:7], negh[0:L, :], ALU.pow),
                                     reads=['st2f', 'negh'], writes=['st2g'])
                                P.op('dve', lambda: V.tensor_tensor(st2[0:L, 7:8], st2[0:L, 7:8], st2[0:L, 3:4], ALU.mult),
                                     reads=['st2g', 'st2d'], writes=['st2g'])
                                P.op('dve', lambda tt=tt, h=h: V.tensor_scalar(hn[0:L, h * 256:(h + 1) * 256], tot[0:L, 0:256],
                                                                              st2[0:L, 4:5], st2[0:L, 7:8], ALU.subtract, ALU.mult),
                                     reads=['y3', 'st2e', 'st2g'], writes=['ub'])
                                P.op('act', lambda tt=tt, h=h: A.activation(out=kw[0:L, :], in_=ktok[0:L, h, :], func=AF.Copy, scale=cols[0:L, 8 + h:9 + h]),
                                     reads=['ktok', 'cols'], writes=['kw'])
                                for kd in range(2):
                                    cup = (pb[4], 'pb4') if kd == 0 else (pb[5], 'pb5')
                                    mm(cup[0][:, 0:257], kw[0:L, kd * 128:(kd + 1) * 128], vtok[0:L, h, 0:257], True, True, ['kw', 'vtok'], [cup[1]])
                                    P.op('dve', lambda h=h, kd=kd, cup=cup: V.scalar_tensor_tensor(
                                        Cst[:, h, kd, 0:257], Cst[:, h, kd, 0:257], cols[:, 16 + h:17 + h], cup[0][:, 0:257], ALU.mult, ALU.add),
                                        reads=['Cst%d' % h, 'cols', cup[1]], writes=['Cst%d' % h])
                                P.op('act', lambda h=h: A.copy(Cbf[:, h, :, :], Cst[:, h, :, :]), reads=['Cst%d' % h], writes=['Cbf%d' % h])
                                yield
                            pT = pb[2][:].bitcast(BF16)
                            for k in range(8):
                                P.op('pe', lambda k=k: PL.transpose(pT[:, k * 128:k * 128 + tp], hn[0:tp, k * 128:(k + 1) * 128],
                                                                    identb[0:tp, 0:tp]),
                                     reads=['ub', 'identb'], writes=['pb2'])
                            for k in range(8):
                                P.op('dve', lambda k=k: V.tensor_scalar(ytmp[:, 0:tp], pT[:, k * 128:k * 128 + tp], mlnw[:, k:k + 1], None, ALU.mult),
                                     reads=['pb2', 'mlnw'], writes=['ytmp'])
                                P.op('dve', lambda k=k: V.scalar_tensor_tensor(ytmp[:, 0:tp], xc[:, k, ts0:ts0 + tp], skipc[:, k:k + 1],
                                                                               ytmp[:, 0:tp], ALU.mult, ALU.add),
                                     reads=['xc', 'skipc', 'ytmp'], writes=['ytmp'])
                                P.op('dve', lambda k=k: V.tensor_tensor(gated[:, 4 + k, ts0:ts0 + tp], ytmp[:, 0:tp], szb[:, k, ts0:ts0 + tp], ALU.mult),
                                     reads=['ytmp', 'szb'], writes=['gatedB'])
                            yield
                        P.op('dve', lambda n=n: V.tensor_copy(xbo[:, :, :], xb[:, :, n:n + 3]), reads=['xb'], writes=['xbo'])
                        P.op('dve', lambda n=n: V.tensor_copy(xb[:, :, 0:3], xbo[:, :, :]), reads=['xbo'], writes=['xb'])
                    gens = [gen_s5(), gen_ml()] + list(extra)
                    while gens:
                        for g_ in list(gens):
                            try:
                                next(g_)
                            except StopIteration:
                                gens.remove(g_)
                def gen_tail(t0, n, src):
                    tp = min(n, 128)
                    ntt = (n + 127) // 128
                    L = tp
                    nch = n // L
                    for tt in range(ntt):
                        ts0 = tt * 128
                        P.dma('sp', hin[0:tp, :], src[ts0:ts0 + tp, :], writes=['hin'])
                        pos_ = []
                        for nh in range(2):
                            po = ((pb[7], 'pb7'), (pb[3], 'pb3'), (pb[4], 'pb4'), (pb[5], 'pb5'))[2 * (tt % 2) + nh]
                            for cch in range(12):
                                gk = ('gated%d' % cch) if cch < 4 else 'gatedB'
                                mm(po[0][0:tp, :], gated[:, cch, ts0:ts0 + tp], w_out[:, cch, nh * 512:(nh + 1) * 512], cch == 0, cch == 11,
                                   [gk, 'w_out'], [po[1]])
                            pos_.append(po)
                            yield
                        for nh in range(2):
                            po = pos_[nh]
                            P.op('dve', lambda nh=nh, po=po: V.tensor_tensor(hin[0:tp, nh * 512:(nh + 1) * 512],
                                                                             hin[0:tp, nh * 512:(nh + 1) * 512], po[0][0:tp, :], ALU.add),
                                 reads=['hin', po[1]], writes=['hin'])
                        P.dma('sp', h1rows(g, si, t0 + ts0, tp), hin[0:tp, :], reads=['hin'], writes=['h1'])
                        yield
                prev_ = None
                piece_head(*pcs[0][:3])
                for ip_, (t0, n, src, _, _) in enumerate(pcs):
                    extra_ = [gen_tail(*prev_)] if prev_ is not None else []
                    piece_middle(t0, n, src, extra_)
                    if ip_ + 1 < len(pcs):
                        piece_head(*pcs[ip_ + 1][:3])
                    prev_ = (t0, n, src)
                for _ in gen_tail(*prev_):
                    pass
                P.enabled = True
                sfx = '_' + g
                P.dma('sp', O['o_s5re' + sfx][si, :].rearrange("(j q) -> q j", q=128), Hre[:], reads=['Hre'], writes=['o1'])
                P.dma('sp', O['o_s5im' + sfx][si, :].rearrange("(j q) -> q j", q=128), Him[:], reads=['Him'], writes=['o2'])
                for h in range(4):
                    P.dma('sp', O['o_mlc' + sfx][si, h].rearrange("(k p) e -> p k e", p=128), Cst[:, h, :, 0:256],
                          reads=['Cst%d' % h], writes=['o3'])
                    P.dma('sp', O['o_mln' + sfx][si, h].rearrange("(k p) -> p k", p=128), Cst[:, h, :, 256],
                          reads=['Cst%d' % h], writes=['o4'])
                P.dma('sp', O['o_mlm' + sfx][si, :].rearrange("(p o) -> p o", o=1), mcarry[:], reads=['mcarry'], writes=['o5'])
                for t_ in range(3):
                    P.dma('sp', O['o_mlconv' + sfx][si, t_].rearrange("(k p) -> p k", p=128), xbo[:, :, t_], reads=['xbo'], writes=['o6'])
            P.barrier()


        if cfg.layers >= 2:
          with ExitStack() as l1:
            sb = lambda n, s, d: l1.enter_context(nc.sbuf_tensor(n, s, d))
            NP = NPIECE
            w_in = sb("w_in1", [128, 8, 5120], BF16)
            w_sw = sb("w_sw1", [128, 8, 1024], BF16)
            w_out = sb("w_out1", [128, 16, D], BF16)
            lwa = sb("lwa", [128, 8, 128], BF16)
            lwx = sb("lwx", [128, 8, 128], BF16)
            for k in range(8):
                P.dma('pool', w_in[:, k, :], I['od_w_in'][k * 128:(k + 1) * 128, :], writes=['w_in'])
                P.dma('pool', w_sw[:, k, :], I['od_w_qksw'][k * 128:(k + 1) * 128, :], writes=['w_sw'])
            for k in range(16):
                P.dma('pool', w_out[:, k, :], I['od_w_out'][k * 128:(k + 1) * 128, :], writes=['w_out'])
            P.dma('pool', lwa[:], I['lru_w_a'].rearrange("h d e -> d h e"), writes=['lwa'])
            P.dma('pool', lwx[:], I['lru_w_x'].rearrange("h d e -> d h e"), writes=['lwx'])
            gcol = sb("gcol1", [128, 8], F32)
            rnw = sb("rnw", [128, 8], F32)
            cw = sb("cw1", [128, 8, 4], F32)
            cb = sb("cb1", [128, 8], F32)
            bah = sb("bah", [128, 8], F32)
            bxh = sb("bxh", [128, 8], F32)
            ccol = sb("ccol", [128, 8], F32)
            ccol2 = sb("ccol2", [128, 8], F32)
            fnw = sb("fnw", [128, D], F32)
            rmaskL = sb("rmaskL", [128, 4, 128], F32)
            rmaskS = sb("rmaskS", [128, 4, 128], F32)
            rxiL = sb("rxiL", [128, 4, 128], F32)
            rxiS = sb("rxiS", [128, 4, 16], F32)
            rzL = sb("rzL", [128, 4], F32)
            rzS = sb("rzS", [128, 4], F32)
            rglL = sb("rglL", [128, 4], F32)
            rglS = sb("rglS", [128, 4], F32)
            col = lambda nm: I[nm].rearrange("(k p) -> p k", p=128)
            P.dma('sp', gcol[:], I['norm_w'][1, :].rearrange("(k p) -> p k", p=128), writes=['gcol1'])
            P.dma('sp', rnw[:], col('ret_norm_w'), writes=['rnw'])
            for tap in range(4):
                P.dma('sp', cw[:, :, tap], I['lru_conv_w'][tap, :].rearrange("(k p) -> p k", p=128), writes=['cw1'])
            P.dma('sp', cb[:], col('lru_conv_b'), writes=['cb1'])
            P.dma('sp', bah[:], col('lru_b_a'), writes=['bah'])
            P.dma('sp', bxh[:], col('lru_b_x'), writes=['bxh'])
            P.dma('sp', ccol[:], col('lru_lambda'), writes=['ccol'])
            P.dma('sp', fnw[:], I['fnw_b'][:, :], writes=['fnw'])
            P.dma('sp', rmaskL[:], I['c_rmask128'][:, :, :], writes=['rmaskL'])
            P.dma('sp', rmaskS[:], I['c_rmask16'][:, :, :], writes=['rmaskS'])
            P.dma('sp', rxiL[:], I['c_rxi128'][:, :, :], writes=['rxiL'])
            P.dma('sp', rxiS[:], I['c_rxi16'][:, :, :], writes=['rxiS'])
            P.dma('sp', rzL[:], I['c_rzeta128'][:, :], writes=['rzL'])
            P.dma('sp', rzS[:], I['c_rzeta16'][:, :], writes=['rzS'])
            P.dma('sp', rglL[:], I['c_rgl128'][:, :], writes=['rglL'])
            P.dma('sp', rglS[:], I['c_rgl16'][:, :], writes=['rglS'])
            P.op('dve', lambda: V.tensor_scalar(bah[:], bah[:], 0.5, None, ALU.mult), reads=['bah'], writes=['bah'])
            P.op('dve', lambda: V.tensor_scalar(bxh[:], bxh[:], 0.5, None, ALU.mult), reads=['bxh'], writes=['bxh'])
            P.op('act', lambda: A.activation(out=ccol[:], in_=ccol[:], func=AF.Exp, scale=-1.0), reads=['ccol'], writes=['ccol'])
            P.op('act', lambda: A.activation(out=ccol[:], in_=ccol[:], func=AF.Ln, bias=1.0), reads=['ccol'], writes=['ccol'])
            P.op('dve', lambda: V.tensor_scalar(ccol2[:], ccol[:], -4.0, None, ALU.mult), reads=['ccol'], writes=['ccol2'])
            P.op('dve', lambda: V.tensor_scalar(ccol[:], ccol[:], -8.0, None, ALU.mult), reads=['ccol'], writes=['ccol'])

            hin = sb("hin1", [128, D], F32)
            ss = sb("ss1", [128, 4], F32)
            ub = sb("ub1", [128, D], BF16)
            hn = ub
            uT = sb("uT1", [128, 8, NP], BF16)
            gated = sb("gated1", [128, 16, NP], BF16)
            cosF = sb("cosF", [128, NP], F32)
            sinF = sb("sinF", [128, NP], F32)
            qT = sb("qT1", [128, 4, NP], BF16)
            kT = sb("kT1", [128, 4, NP], BF16)
            qxi = sb("qxi", [128, 128], BF16)
            kz = sb("kz", [128, 128], BF16)
            vtok = sb("vtok1", [128, 4, 256], BF16)
            szc = sb("szc", [128, 8, NP], BF16)
            xd = sb("xd", [128, 8, 3 + NP], BF16)
            szd = sb("szd", [128, 8, NP], BF16)
            Sst = sb("Sst", [128, 4, 256], F32)
            Sbf = sb("Sbf", [128, 4, 256], BF16)
            hst = sb("hst", [128, 8], F32)
            f1 = sb("f1", [128, NP], F32)
            LXC = [sb("lx%d" % i_, [128, NP], BF16) for i_ in range(2)]
            LS = [dict(xcb=LXC[i_ % 2], f2=sb("lf2_%d" % i_, [128, NP], F32), f3=sb("lf3_%d" % i_, [128, NP], F32),
                       f4=sb("lf4_%d" % i_, [128, NP], F32)) for i_ in range(4)]
            PTb = sb("PT1b", [128, 128], BF16)
            qxib = sb("qxib", [128, 128], BF16)
            kzb = sb("kzb", [128, 128], BF16)
            st6c = sb("st6c", [128, 6], F32)
            PT = sb("PT1", [128, 128], BF16)
            st6 = sb("st6b", [128, 6], F32)
            st2 = sb("st2b", [128, 8], F32)
            xdo = sb("xdo", [128, 8, 3], F32)
            yout = sb("yout", [128, D], F32)

            print('SBUF remaining after L1 activations', nc.sbuf_bytes_remaining)
            mmreg = [(pb[0], 'pb0'), (pb[1], 'pb1'), (pb[4], 'pb4'), (pb[5], 'pb5')]
            mmi = [0]

            def nextmm():
                r_ = mmreg[mmi[0] % 4]
                mmi[0] += 1
                return r_

            def inproj(f, n, wt=None, wkey='w_in'):
                ps, key = nextmm()
                wt = w_in if wt is None else wt
                for k in range(8):
                    mm(ps[:, 0:n], wt[:, k, f * 128:(f + 1) * 128], uT[:, k, 0:n], k == 0, k == 7, [wkey, 'uT'], [key])
                return ps, key

            for (g, si, pcs) in seqs:
                if g == 'p':
                    P.op('pool', lambda: G.memset(Sst[:], 0.0), writes=['Sst%d' % h_ for h_ in range(4)])
                    P.op('pool', lambda: G.memset(hst[:], 0.0), writes=['hst'])
                    P.op('pool', lambda: G.memset(xd[:, :, 0:3], 0.0), writes=['xd'])
                else:
                    for h in range(4):
                        P.dma('sp', Sst[:, h, :], I['ret_in'][si, h], writes=['Sst%d' % h])
                    P.dma('sp', hst[:], I['lruh_in'][si, :].rearrange("(k p) -> p k", p=128), writes=['hst'])
                    for t_ in range(3):
                        P.dma('pool', xd[:, :, t_], I['lruconv_in'][si, t_].rearrange("(k p) -> p k", p=128), writes=['xd'])
                P.op('act', lambda: A.copy(Sbf[:], Sst[:]), reads=['Sst%d' % h_ for h_ in range(4)], writes=['Sbf%d' % h_ for h_ in range(4)])
                for (t0, n, src, dst, _) in pcs:
                    tp = min(n, 128)
                    ntt = (n + 127) // 128
                    L = tp
                    nch = n // L
                    rmask, rxi, rz, rgl = (rmaskL, rxiL, rzL, rglL) if L == 128 else (rmaskS, rxiS, rzS, rglS)
                    rp0 = t0 if g == 'p' else T
                    P.dma('sp', cosF[:, 0:n], I['ropeF_c'][:, rp0:rp0 + n], writes=['cosF'])
                    P.dma('sp', sinF[:, 0:n], I['ropeF_s'][:, rp0:rp0 + n], writes=['sinF'])
                    for tt in range(ntt):
                        P.dma('sp', hin[0:tp, :], h1rows(g, si, t0 + tt * 128, tp), reads=['h1'], writes=['hin1'])
                        rmsnorm_T(hin, tp, tt, gcol, ub, uT, ss, hkey='hin1', gkey='gcol1')
                    for (base, dstT, sc, dkey) in ((0, qT, 1.0, 'qT1'), (4, kT, 1.0 / math.sqrt(128.0), 'kT1')):
                        for h in range(4):
                            psA, keyA = inproj(base + h, n)
                            psB, keyB = inproj(base + h, n, wt=w_sw, wkey='w_sw')
                            P.op('dve', lambda psA=psA: V.tensor_tensor(yout[:, 0:n], psA[:, 0:n], cosF[:, 0:n], ALU.mult),
                                 reads=[keyA, 'cosF'], writes=['yout'])
                            P.op('dve', lambda psB=psB: V.tensor_tensor(yout[:, 256:256 + n], psB[:, 0:n], sinF[:, 0:n], ALU.mult),
                                 reads=[keyB, 'sinF'], writes=['yout'])
                            P.op('dve', lambda: V.tensor_tensor(yout[:, 0:n], yout[:, 0:n], yout[:, 256:256 + n], ALU.add), reads=['yout', 'yout'], writes=['yout'])
                            P.op('act', lambda h=h, dstT=dstT, sc=sc: A.mul(dstT[:, h, 0:n], yout[:, 0:n], sc), reads=['yout'], writes=[dkey])
                    for f in range(8):
                        ps, key = inproj(16 + f, n)
                        P.op('act', lambda f=f, ps=ps: A.activation(out=szc[:, f, 0:n], in_=ps[:, 0:n], func=AF.Silu), reads=[key], writes=['szc'])
                    for f in range(8):
                        ps, key = inproj(24 + f, n)
                        P.op('act', lambda f=f, ps=ps: A.copy(xd[:, f, 3:3 + n], ps[:, 0:n]), reads=[key], writes=['xd'])
                    for f in range(8):
                        ps, key = inproj(32 + f, n)
                        P.op('act', lambda f=f, ps=ps: A.activation(out=szd[:, f, 0:n], in_=ps[:, 0:n], func=AF.Silu), reads=[key], writes=['szd'])
                    def gen_ret():
                        for c in range(nch):
                            cs = slice(c * L, (c + 1) * L)
                            ts0 = c * 128
                            for h in range(4):
                                ps, key = nextmm()
                                for k in range(8):
                                    mm(ps[0:tp, 0:256], uT[:, k, ts0:ts0 + tp], w_in[:, k, 1024 + h * 256:1024 + (h + 1) * 256], k == 0, k == 7,
                                       ['uT', 'w_in'], [key])
                                P.op('act', lambda ps=ps, h=h: A.copy(vtok[0:tp, h, :], ps[0:tp, 0:256]), reads=[key], writes=['vtok1'])
                            yield
                            for h in range(4):
                                par = h % 2
                                PT_, qxi_, kz_, st6_ = (PT, qxi, kz, st6) if par == 0 else (PTb, qxib, kzb, st6c)
                                pk = '_%d' % par
                                so = 4 * par
                                po_, pokey = (pb[7], 'pb7') if par == 0 else (pb[6], 'pb6')
                                stp, stkey = nextmm()
                                ST = stp[0:L, 0:L]
                                mm(ST, kT[:, h, cs], qT[:, h, cs], True, True, ['kT1', 'qT1'], [stkey])
                                P.op('dve', lambda: V.tensor_tensor(PT_[0:L, 0:L], ST, rmask[0:L, h, 0:L], ALU.mult),
                                     reads=[stkey, 'rmaskL', 'rmaskS'], writes=['PT1' + pk])
                                P.op('dve', lambda: V.tensor_tensor(qxi_[:, 0:L], qT[:, h, cs], rxi[:, h, 0:L], ALU.mult),
                                     reads=['qT1', 'rxiL', 'rxiS'], writes=['qxi' + pk])
                                mm(po_[0:L, 0:256], PT_[0:L, 0:L], vtok[0:L, h, :], True, False, ['PT1' + pk, 'vtok1'], [pokey])
                                mm(po_[0:L, 0:256], qxi_[:, 0:L], Sbf[:, h, :], False, True, ['qxi' + pk, 'Sbf%d' % h], [pokey])
                                pT = pb[2][:].bitcast(BF16)
                                P.op('pe', lambda: PL.transpose(pT[0:L, 0:128], kT[:, h, cs], identb[:, :]),
                                     reads=['kT1', 'identb'], writes=['pb2'])
                                P.op('act', lambda: A.activation(out=kz_[0:L, :], in_=pT[0:L, 0:128], func=AF.Copy, scale=rz[0:L, h:h + 1]),
                                     reads=['pb2', 'rzL', 'rzS'], writes=['kz' + pk])
                                sup, supkey = nextmm()
                                mm(sup[:, 0:256], kz_[0:L, :], vtok[0:L, h, :], True, True, ['kz' + pk, 'vtok1'], [supkey])
                                P.op('dve', lambda: V.bn_stats(st6_[0:L, :], po_[0:L, 0:256]), reads=[pokey], writes=['st6' + pk])
                                P.op('dve', lambda: V.bn_aggr(st2[0:L, so:so + 2], st6_[0:L, :]), reads=['st6' + pk], writes=['st2x' + pk])
                                P.op('dve', lambda: V.tensor_scalar(st2[0:L, so + 2:so + 3], st2[0:L, so + 1:so + 2], EPS, None, ALU.add),
                                     reads=['st2x' + pk], writes=['st2y' + pk])
                                P.op('pool', lambda: G.tensor_tensor(st2[0:L, so + 3:so + 4], st2[0:L, so + 2:so + 3], negh[0:L, :], ALU.pow),
                                     reads=['st2y' + pk, 'negh'], writes=['st2z' + pk])
                                P.op('dve', lambda: V.scalar_tensor_tensor(Sst[:, h, :], Sst[:, h, :], rgl[:, h:h + 1], sup[:, 0:256], ALU.mult, ALU.add),
                                     reads=['Sst%d' % h, 'rglL', 'rglS', supkey], writes=['Sst%d' % h])
                                P.op('act', lambda: A.copy(Sbf[:, h, :], Sst[:, h, :]), reads=['Sst%d' % h], writes=['Sbf%d' % h])
                                P.op('dve', lambda: V.tensor_scalar(hn[0:L, h * 256:(h + 1) * 256], po_[0:L, 0:256], st2[0:L, so:so + 1], st2[0:L, so + 3:so + 4],
                                                                    ALU.subtract, ALU.mult),
                                     reads=[pokey, 'st2x' + pk, 'st2z' + pk], writes=['ub'])
                                yield
                            pT = pb[2][:].bitcast(BF16)
                            for k in range(8):
                                P.op('pe', lambda k=k: PL.transpose(pT[:, k * 128:k * 128 + tp], hn[0:tp, k * 128:(k + 1) * 128], identb[0:tp, 0:tp]),
                                     reads=['ub', 'identb'], writes=['pb2'])
                            for k in range(8):
                                P.op('dve', lambda k=k: V.scalar_tensor_tensor(gated[:, k, ts0:ts0 + tp], pT[:, k * 128:k * 128 + tp], rnw[:, k:k + 1],
                                                                               szc[:, k, ts0:ts0 + tp], ALU.mult, ALU.mult),
                                     reads=['pb2', 'rnw', 'szc'], writes=['gatedA'])
                            yield
                    def gen_lru():
                        def ph1(k):
                            B_ = LS[k % 4]; sfx_ = '_%d' % (k % 4)
                            xcb, f2, f3, f4 = B_['xcb'], B_['f2'], B_['f3'], B_['f4']
                            P.op('dve', lambda: V.tensor_scalar(f1[:, 0:n], xd[:, k, 0:n], cw[:, k, 0:1], cb[:, k:k + 1], ALU.mult, ALU.add),
                                 reads=['xd', 'cw1', 'cb1'], writes=['f1'])
                            for tap in range(1, 4):
                                P.op('dve', lambda tap=tap: V.scalar_tensor_tensor(f1[:, 0:n], xd[:, k, tap:tap + n], cw[:, k, tap:tap + 1], f1[:, 0:n],
                                                                                   ALU.mult, ALU.add),
                                     reads=['xd', 'cw1', 'f1'], writes=['f1'])
                            P.op('act', lambda: A.copy(xcb[:, 0:n], f1[:, 0:n]), reads=['f1'], writes=['xcb_%d' % (k % 2)])
                            psr, keyr = nextmm()
                            mm(psr[:, 0:n], lwa[:, k, :], xcb[:, 0:n], True, True, ['lwa', 'xcb_%d' % (k % 2)], [keyr])
                            psi, keyi = nextmm()
                            mm(psi[:, 0:n], lwx[:, k, :], xcb[:, 0:n], True, True, ['lwx', 'xcb_%d' % (k % 2)], [keyi])
                            P.op('act', lambda: A.activation(out=f2[:, 0:n], in_=psr[:, 0:n], func=AF.Tanh, scale=0.5, bias=bah[:, k:k + 1]),
                                 reads=[keyr, 'bah'], writes=['f2' + sfx_])
                            P.op('act', lambda: A.activation(out=f3[:, 0:n], in_=psi[:, 0:n], func=AF.Tanh, scale=0.5, bias=bxh[:, k:k + 1]),
                                 reads=[keyi, 'bxh'], writes=['f3' + sfx_])
                            P.op('act', lambda: A.activation(out=f4[:, 0:n], in_=f2[:, 0:n], func=AF.Exp, scale=ccol2[:, k:k + 1], bias=ccol2[:, k:k + 1]),
                                 reads=['f2' + sfx_, 'ccol2'], writes=['f4' + sfx_])
                            P.op('pool', lambda: G.tensor_scalar(f3[:, 0:n], f3[:, 0:n], 0.5, 0.5, ALU.mult, ALU.add), reads=['f3' + sfx_], writes=['f3' + sfx_])
                            P.op('pool', lambda: G.tensor_tensor(f3[:, 0:n], f3[:, 0:n], xcb[:, 0:n], ALU.mult), reads=['f3' + sfx_, 'xcb_%d' % (k % 2)], writes=['f3' + sfx_])
                            P.op('pool', lambda: G.tensor_tensor(f2[:, 0:n], f4[:, 0:n], f4[:, 0:n], ALU.mult), reads=['f4' + sfx_, 'f2' + sfx_], writes=['f2' + sfx_])
                            P.op('pool', lambda: G.tensor_scalar(f2[:, 0:n], f2[:, 0:n], -1.0, 1.0, ALU.mult, ALU.add), reads=['f2' + sfx_], writes=['f2' + sfx_])

                        def ph2(k):
                            B_ = LS[k % 4]; sfx_ = '_%d' % (k % 4)
                            f2 = B_['f2']
                            P.op('act', lambda: A.activation(out=f2[:, 0:n], in_=f2[:, 0:n], func=AF.Ln), reads=['f2' + sfx_], writes=['f2' + sfx_])
                            P.op('act', lambda: A.activation(out=f2[:, 0:n], in_=f2[:, 0:n], func=AF.Exp, scale=0.5), reads=['f2' + sfx_], writes=['f2' + sfx_])

                        def ph3(k):
                            B_ = LS[k % 4]; sfx_ = '_%d' % (k % 4)
                            xcb, f2, f3, f4 = B_['xcb'], B_['f2'], B_['f3'], B_['f4']
                            P.op('dve', lambda: V.tensor_tensor(f3[:, 0:n], f3[:, 0:n], f2[:, 0:n], ALU.mult), reads=['f3' + sfx_, 'f2' + sfx_], writes=['f3' + sfx_])
                            P.op('dve', lambda: V.tensor_tensor_scan(f2[:, 0:n], f4[:, 0:n], f3[:, 0:n], hst[:, k:k + 1], ALU.mult, ALU.add),
                                 reads=['f4' + sfx_, 'f3' + sfx_, 'hst', 'f2' + sfx_], writes=['f2' + sfx_])
                            P.op('dve', lambda: V.tensor_copy(hst[:, k:k + 1], f2[:, n - 1:n]), reads=['f2' + sfx_], writes=['hst'])
                            P.op('dve', lambda: V.tensor_tensor(gated[:, 8 + k, 0:n], f2[:, 0:n], szd[:, k, 0:n], ALU.mult),
                                 reads=['f2' + sfx_, 'szd'], writes=['gatedB%d' % k])

                        for q_ in range(2):
                            for i_ in range(4):
                                ph1(4 * q_ + i_)
                                yield
                            for i_ in range(4):
                                ph2(4 * q_ + i_)
                            yield
                            for i_ in range(4):
                                ph3(4 * q_ + i_)
                                if i_ % 2 == 1:
                                    yield
                    gens = [gen_ret(), gen_lru()]
                    while gens:
                        for g_ in list(gens):
                            try:
                                next(g_)
                            except StopIteration:
                                gens.remove(g_)
                    P.op('dve', lambda n=n: V.tensor_copy(xdo[:, :, :], xd[:, :, n:n + 3]), reads=['xd'], writes=['xdo'])
                    P.op('dve', lambda: V.tensor_copy(xd[:, :, 0:3], xdo[:, :, :]), reads=['xdo'], writes=['xd'])
                    for tt in range(ntt):
                        ts0 = tt * 128
                        P.dma('sp', hin[0:tp, :], h1rows(g, si, t0 + ts0, tp), reads=['h1'], writes=['hin1'])
                        for nh in range(2):
                            po = ((pb[7], 'pb7'), (pb[3], 'pb3'), (pb[6], 'pb6'), (pb[2], 'pb2'))[2 * (tt % 2) + nh]
                            for cch in range(16):
                                gk = 'gatedA' if cch < 8 else 'gatedB%d' % (cch - 8)
                                mm(po[0][0:tp, :], gated[:, cch, ts0:ts0 + tp], w_out[:, cch, nh * 512:(nh + 1) * 512], cch == 0, cch == 15,
                                   [gk, 'w_out'], [po[1]])
                            P.op('dve', lambda nh=nh, po=po: V.tensor_tensor(hin[0:tp, nh * 512:(nh + 1) * 512],
                                                                             hin[0:tp, nh * 512:(nh + 1) * 512], po[0][0:tp, :], ALU.add),
                                 reads=['hin1', po[1]], writes=['hin1'])
                        if not (g == 'p' and t0 == 0):
                            P.op('pool', lambda: G.memset(ss[0:tp, 0:1], 0.0), writes=['ss'])
                            P.op('act', lambda: A.activation(out=ub[0:tp, :], in_=hin[0:tp, :], func=AF.Square, accum_out=ss[0:tp, 0:1]),
                                 reads=['hin1'], writes=['ub', 'ss'])
                            P.op('dve', lambda: V.tensor_scalar(ss[0:tp, 1:2], ss[0:tp, 0:1], 1.0 / D, EPS, ALU.mult, ALU.add), reads=['ss'], writes=['ss'])
                            P.op('pool', lambda: G.tensor_tensor(ss[0:tp, 2:3], ss[0:tp, 1:2], negh[0:tp, :], ALU.pow), reads=['ss', 'negh'], writes=['ss'])
                            P.op('dve', lambda: V.scalar_tensor_tensor(yout[0:tp, :], hin[0:tp, :], ss[0:tp, 2:3], fnw[0:tp, :], ALU.mult, ALU.mult),
                                 reads=['hin1', 'ss', 'fnw'], writes=['yout'])
                            P.dma('sp', dst[ts0:ts0 + tp, :], yout[0:tp, :], reads=['yout'], writes=['yo'])
                sfx = '_' + g
                for h in range(4):
                    P.dma('sp', O['o_ret' + sfx][si, h], Sst[:, h, :], reads=['Sst%d' % h], writes=['o7'])
                P.dma('sp', O['o_lruh' + sfx][si, :].rearrange("(k p) -> p k", p=128), hst[:], reads=['hst'], writes=['o8'])
                for t_ in range(3):
                    P.dma('sp', O['o_lruconv' + sfx][si, t_].rearrange("(k p) -> p k", p=128), xdo[:, :, t_], reads=['xdo'], writes=['o9'])
            P.barrier()
        P.barrier()
    return nc


def host_layout(inp):
    f = lambda a: np.ascontiguousarray(np.asarray(a, dtype=np.float32))
    d = {}
    d['norm_w'] = f(inp['norm_w'])
    d['final_norm_w'] = f(inp['final_norm_w'])
    d['ev_w_in'] = f(inp['ev_w_in'][0])
    d['ev_w_out'] = f(inp['ev_w_out'][0])
    lre, lim, ldt = inp['s5_lambda_re'][0], inp['s5_lambda_im'][0], inp['s5_log_dt'][0]
    def A_from_gp(x):
        x4 = np.asarray(x).reshape(4, 8, 64)
        o = np.broadcast_to(x4[:, :, None, :], (4, 8, 16, 64))
        return f(np.transpose(o, (1, 2, 0, 3)).reshape(128, 4, 64))
    d['lamre_A'] = A_from_gp(lre)
    d['lamim_A'] = A_from_gp(lim)
    d['logdt_A'] = A_from_gp(np.broadcast_to(np.asarray(ldt)[:, None], (32, 64)))
    def A_from_gpc(x):
        x5 = np.asarray(x).reshape(4, 8, 64, 16)
        return f(np.transpose(x5, (1, 3, 0, 2)).reshape(128, 4, 64))
    d['bre_A'] = A_from_gpc(inp['s5_b_re'][0])
    d['bim_A'] = A_from_gpc(inp['s5_b_im'][0])
    def S_from_gp(x):
        x3 = np.asarray(x).reshape(16, 2, 64)
        return f(np.transpose(x3, (1, 2, 0)).reshape(128, 16))
    d['lamre_S'] = S_from_gp(lre)
    d['lamim_S'] = S_from_gp(lim)
    d['logdt_S'] = S_from_gp(np.broadcast_to(np.asarray(ldt)[:, None], (32, 64)))
    def S_from_gcp(x):
        x4 = np.asarray(x).reshape(16, 2, 16, 64)
        return f(np.transpose(x4, (1, 3, 0, 2)).reshape(128, 16, 16))
    d['cre_S'] = S_from_gcp(inp['s5_c_re'][0])
    d['cim_S'] = S_from_gcp(inp['s5_c_im'][0])
    d['s5_d'] = f(inp['s5_d'][0]); d['s5_w_glu'] = f(inp['s5_w_glu'][0]); d['s5_b_glu'] = f(inp['s5_b_glu'][0])
    d['ml_conv_w'] = f(inp['ml_conv_w'][0]); d['ml_conv_b'] = f(inp['ml_conv_b'][0])
    for k in ('ml_wq', 'ml_wk', 'ml_wv'):
        d[k] = f(inp[k][0])
        d[k + 'T'] = f(np.transpose(np.asarray(inp[k][0]), (0, 2, 1)))
    d['ml_w_if'] = f(inp['ml_w_if'][0]); d['ml_b_if'] = f(inp['ml_b_if'][0])
    d['ml_norm_w'] = f(inp['ml_norm_w'][0]); d['ml_skip'] = f(inp['ml_skip'][0])
    w1 = np.asarray(inp['od_w_in'][0])
    d['od_w_in'] = f(w1)
    qk = w1[:, 0:1024].reshape(D, 8, 2, 64)
    d['od_w_qksw'] = f(qk[:, :, ::-1, :].reshape(D, 1024))
    d['fnw_b'] = f(np.broadcast_to(np.asarray(inp['final_norm_w'])[None, :], (128, D)))
    d['od_w_out'] = f(inp['od_w_out'][0]); d['ret_norm_w'] = f(inp['ret_norm_w'][0])
    d['lru_conv_w'] = f(inp['lru_conv_w'][0]); d['lru_conv_b'] = f(inp['lru_conv_b'][0])
    d['lru_w_a'] = f(inp['lru_w_a'][0]); d['lru_b_a'] = f(inp['lru_b_a'][0])
    d['lru_w_x'] = f(inp['lru_w_x'][0]); d['lru_b_x'] = f(inp['lru_b_x'][0]); d['lru_lambda'] = f(inp['lru_lambda'][0])
    d['meta'] = f(inp['meta'])
    return d


def make_in_maps(cfg, inputs, ncores):
    shared = host_layout(inputs)
    shared.update(host_consts())
    T = 16 + cfg.T_x
    pos = np.concatenate([np.arange(T, dtype=np.float32), (16 + cfg.past) + np.arange(16, dtype=np.float32)])
    cf, sf, ct, st = rope_tables(pos)
    shared['ropeF_c'], shared['ropeF_s'], shared['ropeT_c'], shared['ropeT_s'] = cf, sf, np.ascontiguousarray(ct), np.ascontiguousarray(st)
    f = lambda a: np.ascontiguousarray(np.asarray(a, dtype=np.float32))
    maps = []
    for c in range(ncores):
        ps = slice(c * cfg.n_p, (c + 1) * cfg.n_p)
        ss_ = slice(c * cfg.n_s, (c + 1) * cfg.n_s)
        m = dict(shared)
        m['xp'] = f(inputs['x_prompt'][ps])
        m['xs'] = f(inputs['x_sample'][ss_])
        m['s5re_in'] = f(inputs['state_s5_re'][0][ss_]).reshape(cfg.n_s, 2048)
        m['s5im_in'] = f(inputs['state_s5_im'][0][ss_]).reshape(cfg.n_s, 2048)
        m['mlc_in'] = f(inputs['state_ml_c'][0][ss_])
        m['mln_in'] = f(inputs['state_ml_n'][0][ss_])
        m['mlm_in'] = f(inputs['state_ml_m'][0][ss_])
        m['mlconv_in'] = f(inputs['state_ml_conv'][0][ss_])
        m['ret_in'] = f(inputs['state_ret'][0][ss_])
        m['lruh_in'] = f(inputs['state_lru_h'][0][ss_])
        m['lruconv_in'] = f(inputs['state_lru_conv'][0][ss_])
        maps.append(m)
    return maps


def gather(cfg, results):
    cat = lambda k: np.concatenate([np.asarray(r[k]) for r in results], axis=0)
    outs = [cat('yp'), cat('ys')]
    for g in ('p', 's'):
        n = cat('o_s5re_' + g).shape[0]
        outs += [cat('o_s5re_' + g).reshape(1, n, 32, 64), cat('o_s5im_' + g).reshape(1, n, 32, 64),
                 cat('o_mlc_' + g)[None], cat('o_mln_' + g)[None], cat('o_mlm_' + g)[None], cat('o_mlconv_' + g)[None],
                 cat('o_ret_' + g)[None], cat('o_lruh_' + g)[None], cat('o_lruconv_' + g)[None]]
    return tuple(np.ascontiguousarray(o.astype(np.float32)) for o in outs)


def kernel(**inputs):
    ncores = 8
    cfg = Cfg(n_p=2, T_x=4096, n_s=2, past=2048, layers=2)
    nc = build(cfg)
    maps = make_in_maps(cfg, inputs, ncores)
    res = run_bass_kernel_spmd(nc, maps, core_ids=list(range(ncores)))
    return gather(cfg, res.results)
```
